# Optimizing a Trainium2 kernel written in Bass

```python
import math
import jax
import jax.numpy as jnp
from jax import lax
import numpy as np

D_MODEL = 2048
BATCH = 1
SEQ = 8192
DEPTH = 2

GRID_W = 64
N_MEM = 256
EPS = 1e-6

SSM_GROUP = 16
SSM_STATE = 64
SSM_GROUPS = 48
SSM_WIDTH = SSM_GROUPS * SSM_GROUP

DN_HEADS = 6
DN_HEAD_DIM = 128
DN_WIDTH = DN_HEADS * DN_HEAD_DIM
DN_CONV = 5
DN_CHUNK = 64

ATT_HEADS = 8
ATT_KV_HEADS = 2
ATT_HEAD_DIM = 128
ATT_WIDTH = ATT_HEADS * ATT_HEAD_DIM
ATT_KV_WIDTH = ATT_KV_HEADS * ATT_HEAD_DIM
ATT_BLOCK = 128
ROPE_THETA = 10000.0

MEM_HEADS = 4
MEM_HEAD_DIM = 128
MEM_WIDTH = MEM_HEADS * MEM_HEAD_DIM

N_BRANCH = 4
BRANCH_WIDTHS = (SSM_WIDTH, DN_WIDTH, ATT_WIDTH, MEM_WIDTH)
BRANCH_OFFSETS = (0, SSM_WIDTH, SSM_WIDTH + DN_WIDTH, SSM_WIDTH + DN_WIDTH + ATT_WIDTH)
BRANCH_TOTAL = SSM_WIDTH + DN_WIDTH + ATT_WIDTH + MEM_WIDTH

IN_SPLITS = (
    SSM_WIDTH, SSM_WIDTH,
    DN_WIDTH, DN_WIDTH, DN_WIDTH, 2 * DN_HEADS, 2 * DN_HEADS, DN_WIDTH,
    ATT_WIDTH, ATT_KV_WIDTH, ATT_KV_WIDTH, ATT_WIDTH,
    MEM_WIDTH, MEM_WIDTH,
    N_BRANCH * D_MODEL,
)
IN_WIDTH = (2 * SSM_WIDTH + 4 * DN_WIDTH + 4 * DN_HEADS + 2 * ATT_WIDTH
            + 2 * ATT_KV_WIDTH + 2 * MEM_WIDTH + N_BRANCH * D_MODEL)

kernel_name = "hybrid_gated_s5_deltanet_gridattn_encoder"


def rmsnorm(x, g):
    xf = x.astype(jnp.float32)
    y = xf * lax.rsqrt(jnp.mean(xf * xf, axis=-1, keepdims=True) + EPS)
    return (y * g.astype(jnp.float32)).astype(x.dtype)


def l2norm(x):
    return x * lax.rsqrt(jnp.sum(x * x, axis=-1, keepdims=True) + EPS)


def _cmul(ar, ai, br, bi):
    return ar * br - ai * bi, ar * bi + ai * br


def _ssm_combine(e1, e2):
    a1r, a1i, b1r, b1i = e1
    a2r, a2i, b2r, b2i = e2
    ar, ai = _cmul(a2r, a2i, a1r, a1i)
    br, bi = _cmul(a2r, a2i, b1r, b1i)
    return ar, ai, br + b2r, bi + b2i


def s5_direction(u, a_re, a_im, log_step, b_re, b_im, c_re, c_im, reverse):
    step = jnp.exp(log_step)[:, None]
    mag = jnp.exp(a_re * step)
    lam_re = mag * jnp.cos(a_im * step)
    lam_im = mag * jnp.sin(a_im * step)
    den = a_re * a_re + a_im * a_im
    nr = lam_re - 1.0
    ni = lam_im
    coef_re = (nr * a_re + ni * a_im) / den
    coef_im = (ni * a_re - nr * a_im) / den
    bb_re = coef_re[..., None] * b_re - coef_im[..., None] * b_im
    bb_im = coef_re[..., None] * b_im + coef_im[..., None] * b_re
    bu_re = jnp.einsum("blgp,gnp->blgn", u, bb_re)
    bu_im = jnp.einsum("blgp,gnp->blgn", u, bb_im)
    lr = jnp.broadcast_to(lam_re, bu_re.shape)
    li = jnp.broadcast_to(lam_im, bu_re.shape)
    _, _, s_re, s_im = lax.associative_scan(
        _ssm_combine, (lr, li, bu_re, bu_im), reverse=reverse, axis=1)
    return (jnp.einsum("blgn,gpn->blgp", s_re, c_re)
            - jnp.einsum("blgn,gpn->blgp", s_im, c_im))


def s5_mixer(u, a_re, a_im, log_step, b_re, b_im, c_re, c_im, d, w_glu, b_glu):
    dtype = u.dtype
    bsz, seq, _ = u.shape
    f = lambda t: t.astype(jnp.float32)
    ug = f(u).reshape(bsz, seq, SSM_GROUPS, SSM_GROUP)
    y = s5_direction(ug, f(a_re[0]), f(a_im[0]), f(log_step[0]), f(b_re[0]), f(b_im[0]),
                     f(c_re[0]), f(c_im[0]), reverse=False)
    y = y + s5_direction(ug, f(a_re[1]), f(a_im[1]), f(log_step[1]), f(b_re[1]), f(b_im[1]),
                         f(c_re[1]), f(c_im[1]), reverse=True)
    y = y + f(d).reshape(SSM_GROUPS, SSM_GROUP) * ug
    y = jax.nn.gelu(y.reshape(bsz, seq, SSM_WIDTH))
    y = y * jax.nn.sigmoid(y @ f(w_glu) + f(b_glu))
    return y.astype(dtype)


def short_conv(x, w):
    ch = x.shape[-1]
    rhs = jnp.transpose(w)[:, None, :].astype(x.dtype)
    return lax.conv_general_dilated(
        x, rhs, window_strides=(1,), padding=[(DN_CONV // 2, DN_CONV // 2)],
        dimension_numbers=("NWC", "WIO", "NWC"), feature_group_count=ch)


def gated_delta_rule(q, k, v, beta, g):
    b, h, l, dk = q.shape
    dv = v.shape[-1]
    c = DN_CHUNK
    n = l // c
    q = q.reshape(b, h, n, c, dk)
    k = k.reshape(b, h, n, c, dk)
    v = v.reshape(b, h, n, c, dv)
    beta = beta.reshape(b, h, n, c)
    g = jnp.cumsum(g.reshape(b, h, n, c), axis=-1)
    idx = jnp.arange(c)
    incl = idx[:, None] >= idx[None, :]
    strict = idx[:, None] > idx[None, :]
    diff = g[..., :, None] - g[..., None, :]
    decay = jnp.where(incl, jnp.exp(jnp.where(incl, diff, 0.0)), 0.0)
    k_beta = k * beta[..., None]
    lower = jnp.where(strict, jnp.einsum("bhncd,bhnsd->bhncs", k_beta, k) * decay, 0.0)
    eye = jnp.eye(c, dtype=q.dtype)
    rhs = jnp.concatenate([v * beta[..., None], k_beta * jnp.exp(g)[..., None]], axis=-1)
    sol = lax.linalg.triangular_solve(eye + lower, rhs, left_side=True, lower=True,
                                      unit_diagonal=True)
    u_c = sol[..., :dv]
    w_c = sol[..., dv:]
    intra = jnp.einsum("bhncd,bhnsd->bhncs", q, k) * decay
    q_dec = q * jnp.exp(g)[..., None]
    k_dec = k * jnp.exp(g[..., -1:] - g)[..., None]
    g_last = jnp.exp(g[..., -1])
    xs = (jnp.moveaxis(u_c, 2, 0), jnp.moveaxis(w_c, 2, 0), jnp.moveaxis(q_dec, 2, 0),
          jnp.moveaxis(k_dec, 2, 0), jnp.moveaxis(intra, 2, 0), jnp.moveaxis(g_last, 2, 0))

    def step(state, inp):
        u_i, w_i, qd_i, kd_i, a_i, gl_i = inp
        v_new = u_i - jnp.einsum("bhck,bhkv->bhcv", w_i, state)
        o = (jnp.einsum("bhck,bhkv->bhcv", qd_i, state)
             + jnp.einsum("bhcs,bhsv->bhcv", a_i, v_new))
        state = state * gl_i[..., None, None] + jnp.einsum("bhck,bhcv->bhkv", kd_i, v_new)
        return state, o

    s0 = jnp.zeros((b, h, dk, dv), q.dtype)
    _, o = lax.scan(step, s0, xs)
    return jnp.moveaxis(o, 0, 2).reshape(b, h, l, dv)


def deltanet_mixer(q, k, v, a_logit, b_logit, conv_w, a_log, dt_bias, norm_g):
    dtype = q.dtype
    bsz, seq, _ = q.shape
    qkv = jax.nn.silu(short_conv(jnp.concatenate([q, k, v], axis=-1), conv_w))
    qkv = qkv.astype(jnp.float32).reshape(bsz, seq, 3, DN_HEADS, DN_HEAD_DIM)
    qh = jnp.transpose(l2norm(qkv[:, :, 0]) * (DN_HEAD_DIM ** -0.5), (0, 2, 1, 3))
    kh = jnp.transpose(l2norm(qkv[:, :, 1]), (0, 2, 1, 3))
    vh = jnp.transpose(qkv[:, :, 2], (0, 2, 1, 3))
    a4 = a_logit.astype(jnp.float32).reshape(bsz, seq, 2, DN_HEADS)
    b4 = b_logit.astype(jnp.float32).reshape(bsz, seq, 2, DN_HEADS)
    beta = jax.nn.sigmoid(b4)
    g = -jnp.exp(a_log.astype(jnp.float32)) * jax.nn.softplus(a4 + dt_bias.astype(jnp.float32))
    beta_f = jnp.transpose(beta[:, :, 0], (0, 2, 1))
    beta_b = jnp.transpose(beta[:, :, 1], (0, 2, 1))
    g_f = jnp.transpose(g[:, :, 0], (0, 2, 1))
    g_b = jnp.transpose(g[:, :, 1], (0, 2, 1))
    o_f = gated_delta_rule(qh, kh, vh, beta_f, g_f)
    o_b = jnp.flip(gated_delta_rule(jnp.flip(qh, 2), jnp.flip(kh, 2), jnp.flip(vh, 2),
                                    jnp.flip(beta_b, 2), jnp.flip(g_b, 2)), 2)
    o = jnp.transpose(o_f + o_b, (0, 2, 1, 3))
    o = rmsnorm(o, norm_g)
    return o.reshape(bsz, seq, DN_WIDTH).astype(dtype)


def axial_rope(rows):
    row = jnp.repeat(jnp.arange(rows), GRID_W).astype(jnp.float32)
    col = jnp.tile(jnp.arange(GRID_W), rows).astype(jnp.float32)
    axis_dim = ATT_HEAD_DIM // 2
    freqs = ROPE_THETA ** (-jnp.arange(0, axis_dim, 2, dtype=jnp.float32) / axis_dim)
    ang = jnp.concatenate([row[:, None] * freqs, col[:, None] * freqs], axis=-1)
    return jnp.cos(ang), jnp.sin(ang)


def apply_rope(x, cos, sin):
    xp = x.reshape(x.shape[:-1] + (x.shape[-1] // 2, 2))
    x0, x1 = xp[..., 0], xp[..., 1]
    c = cos[None, :, None, :]
    s = sin[None, :, None, :]
    return jnp.stack([x0 * c - x1 * s, x0 * s + x1 * c], axis=-1).reshape(x.shape)


def grid_attention(q, k, v, qn_g, kn_g, cos, sin):
    dtype = q.dtype
    bsz, seq, _ = q.shape
    grp = ATT_HEADS // ATT_KV_HEADS
    qh = rmsnorm(q.reshape(bsz, seq, ATT_HEADS, ATT_HEAD_DIM), qn_g).astype(jnp.float32)
    kh = rmsnorm(k.reshape(bsz, seq, ATT_KV_HEADS, ATT_HEAD_DIM), kn_g).astype(jnp.float32)
    vh = v.reshape(bsz, seq, ATT_KV_HEADS, ATT_HEAD_DIM).astype(jnp.float32)
    qh = apply_rope(qh, cos, sin) * (ATT_HEAD_DIM ** -0.5)
    kh = apply_rope(kh, cos, sin)
    nblk = seq // ATT_BLOCK
    qb = qh.reshape(bsz, nblk, ATT_BLOCK, ATT_KV_HEADS, grp, ATT_HEAD_DIM)
    qb = jnp.transpose(qb, (1, 0, 2, 3, 4, 5))

    def block(qi):
        s = jnp.einsum("bqhgd,bkhd->bhgqk", qi, kh)
        p = jax.nn.softmax(s, axis=-1)
        return jnp.einsum("bhgqk,bkhd->bqhgd", p, vh)

    o = lax.map(block, qb)
    o = jnp.transpose(o, (1, 0, 2, 3, 4, 5)).reshape(bsz, seq, ATT_WIDTH)
    return o.astype(dtype)


def memory_attention(q, mem_n, w_kv):
    dtype = q.dtype
    bsz, seq, _ = q.shape
    kv = mem_n @ w_kv
    km = kv[..., :MEM_WIDTH].reshape(bsz, -1, MEM_HEADS, MEM_HEAD_DIM).astype(jnp.float32)
    vm = kv[..., MEM_WIDTH:].reshape(bsz, -1, MEM_HEADS, MEM_HEAD_DIM).astype(jnp.float32)
    qh = q.reshape(bsz, seq, MEM_HEADS, MEM_HEAD_DIM).astype(jnp.float32)
    s = jnp.einsum("bqhd,bkhd->bhqk", qh, km) * (MEM_HEAD_DIM ** -0.5)
    p = jax.nn.softmax(s, axis=-1)
    o = jnp.einsum("bhqk,bkhd->bqhd", p, vm).reshape(bsz, seq, MEM_WIDTH)
    return o.astype(dtype)


def setup_inputs(seed: int = 0) -> dict:
    key = jax.random.key(seed)
    ks = jax.random.split(key, 32)
    f32 = jnp.float32

    def nrm(k, shape, scale):
        return jax.random.normal(k, shape, f32) * scale

    x = nrm(ks[0], (BATCH, SEQ, D_MODEL), 1.0)
    mem = nrm(ks[1], (BATCH, N_MEM, D_MODEL), 1.0)
    norm_g = 1.0 + nrm(ks[2], (DEPTH, D_MODEL), 0.02)
    w_in = nrm(ks[3], (DEPTH, D_MODEL, IN_WIDTH), D_MODEL ** -0.5)
    ssm_shape = (DEPTH, 2, SSM_GROUPS, SSM_STATE)
    ssm_a_re = -0.5 + nrm(ks[4], ssm_shape, 0.01)
    ssm_a_im = jnp.pi * jnp.arange(SSM_STATE, dtype=f32) + nrm(ks[5], ssm_shape, 0.01)
    ssm_log_step = jax.random.uniform(ks[6], (DEPTH, 2, SSM_GROUPS), f32,
                                      math.log(1e-3), math.log(1e-1))
    ssm_b_re = nrm(ks[7], (DEPTH, 2, SSM_GROUPS, SSM_STATE, SSM_GROUP), (2 * SSM_GROUP) ** -0.5)
    ssm_b_im = nrm(ks[8], (DEPTH, 2, SSM_GROUPS, SSM_STATE, SSM_GROUP), (2 * SSM_GROUP) ** -0.5)
    ssm_c_re = nrm(ks[9], (DEPTH, 2, SSM_GROUPS, SSM_GROUP, SSM_STATE), SSM_STATE ** -0.5)
    ssm_c_im = nrm(ks[10], (DEPTH, 2, SSM_GROUPS, SSM_GROUP, SSM_STATE), SSM_STATE ** -0.5)
    ssm_d = nrm(ks[11], (DEPTH, SSM_WIDTH), 1.0)
    ssm_w_glu = nrm(ks[12], (DEPTH, SSM_WIDTH, SSM_WIDTH), SSM_WIDTH ** -0.5)
    ssm_b_glu = nrm(ks[13], (DEPTH, SSM_WIDTH), 0.02)
    dn_conv = nrm(ks[14], (DEPTH, 3 * DN_WIDTH, DN_CONV), DN_CONV ** -0.5)
    dn_a_log = jnp.log(jax.random.uniform(ks[15], (DEPTH, 2, DN_HEADS), f32, 1.0, 16.0))
    dt = jnp.exp(jax.random.uniform(ks[16], (DEPTH, 2, DN_HEADS), f32,
                                    math.log(1e-3), math.log(1e-1)))
    dn_dt_bias = dt + jnp.log(-jnp.expm1(-dt))
    dn_norm_g = 1.0 + nrm(ks[17], (DEPTH, DN_HEAD_DIM), 0.02)
    attn_q_norm = 1.0 + nrm(ks[18], (DEPTH, ATT_HEAD_DIM), 0.02)
    attn_k_norm = 1.0 + nrm(ks[19], (DEPTH, ATT_HEAD_DIM), 0.02)
    mem_norm_g = 1.0 + nrm(ks[20], (DEPTH, D_MODEL), 0.02)
    w_mem_kv = nrm(ks[21], (DEPTH, D_MODEL, 2 * MEM_WIDTH), D_MODEL ** -0.5)
    bks = jax.random.split(ks[22], N_BRANCH)
    w_branch = jnp.concatenate(
        [nrm(bks[i], (DEPTH, BRANCH_WIDTHS[i], D_MODEL), BRANCH_WIDTHS[i] ** -0.5)
         for i in range(N_BRANCH)], axis=1)
    w_out = nrm(ks[23], (DEPTH, D_MODEL, D_MODEL), D_MODEL ** -0.5)
    final_norm_g = 1.0 + nrm(ks[24], (D_MODEL,), 0.02)
    return {
        "x": x, "mem": mem, "norm_g": norm_g, "w_in": w_in,
        "ssm_a_re": ssm_a_re, "ssm_a_im": ssm_a_im, "ssm_log_step": ssm_log_step,
        "ssm_b_re": ssm_b_re, "ssm_b_im": ssm_b_im, "ssm_c_re": ssm_c_re,
        "ssm_c_im": ssm_c_im, "ssm_d": ssm_d, "ssm_w_glu": ssm_w_glu,
        "ssm_b_glu": ssm_b_glu, "dn_conv": dn_conv, "dn_a_log": dn_a_log,
        "dn_dt_bias": dn_dt_bias, "dn_norm_g": dn_norm_g, "attn_q_norm": attn_q_norm,
        "attn_k_norm": attn_k_norm, "mem_norm_g": mem_norm_g, "w_mem_kv": w_mem_kv,
        "w_branch": w_branch, "w_out": w_out, "final_norm_g": final_norm_g,
    }


def reference(x, mem, norm_g, w_in, ssm_a_re, ssm_a_im, ssm_log_step, ssm_b_re, ssm_b_im,
              ssm_c_re, ssm_c_im, ssm_d, ssm_w_glu, ssm_b_glu, dn_conv, dn_a_log,
              dn_dt_bias, dn_norm_g, attn_q_norm, attn_k_norm, mem_norm_g, w_mem_kv,
              w_branch, w_out, final_norm_g):
    bsz, seq, _ = x.shape
    rows = seq // GRID_W
    cos, sin = axial_rope(rows)
    split_at = [int(i) for i in np.cumsum(IN_SPLITS)[:-1]]
    for layer in range(DEPTH):
        xn = rmsnorm(x, norm_g[layer])
        h = xn @ w_in[layer]
        (u_a, z_a, dq, dk, dv, da, db, z_b, aq, ak, av, z_c, mq, z_m,
         gate_logits) = jnp.split(h, split_at, axis=-1)

        y_a = s5_mixer(u_a, ssm_a_re[layer], ssm_a_im[layer], ssm_log_step[layer],
                       ssm_b_re[layer], ssm_b_im[layer], ssm_c_re[layer], ssm_c_im[layer],
                       ssm_d[layer], ssm_w_glu[layer], ssm_b_glu[layer]) * jax.nn.silu(z_a)
        y_b = deltanet_mixer(dq, dk, dv, da, db, dn_conv[layer], dn_a_log[layer],
                             dn_dt_bias[layer], dn_norm_g[layer]) * jax.nn.silu(z_b)
        y_c = grid_attention(aq, ak, av, attn_q_norm[layer], attn_k_norm[layer],
                             cos, sin) * jax.nn.silu(z_c)
        y_m = memory_attention(mq, rmsnorm(mem, mem_norm_g[layer]),
                               w_mem_kv[layer]) * jax.nn.silu(z_m)

        gates = jax.nn.sigmoid(gate_logits.reshape(bsz, seq, N_BRANCH, D_MODEL))
        merged = jnp.zeros_like(x)
        for bi, y_br in enumerate((y_a, y_b, y_c, y_m)):
            off = BRANCH_OFFSETS[bi]
            w_b = w_branch[layer, off:off + BRANCH_WIDTHS[bi]]
            merged = merged + gates[:, :, bi] * (y_br @ w_b)
        x = x + merged @ w_out[layer]
    return rmsnorm(x, final_norm_g)
```

```python
from contextlib import ExitStack
import os
import numpy as np
import concourse.bass as bass
import concourse.mybir as mybir
from concourse.bass_utils import run_bass_kernel_spmd

F32 = mybir.dt.float32
BF16 = mybir.dt.bfloat16
I32 = mybir.dt.int32
AF = mybir.ActivationFunctionType
ALU = mybir.AluOpType
AX = mybir.AxisListType

NCORES = 8
D = 2048
SEQ = 8192
TPC = SEQ // NCORES
EPS = 1e-6
TWO_PI = float(2 * np.pi)


class Buf:
    def __init__(self, name, t=None):
        self.name = name
        self.t = t
        self.w = None
        self.r = []
        self.dsem = None
        self.dcnt = 0

    def __getitem__(self, k):
        return self.t[k]


class Prog:
    ENG = ("sp", "act", "dve", "pool", "pe")

    def __init__(self):
        self.nc = bass.Bass("TRN2", target_bir_lowering=False)
        self.ctx = ExitStack()
        self.streams = {e: [] for e in self.ENG}
        self.seq = {e: 0 for e in self.ENG}
        self.esem = {e: self.ctx.enter_context(self.nc.semaphore("es_" + e)) for e in self.ENG}
        self.store_bufs = []
        self.nid = 0

    def dram_in(self, name, shape, dt=F32):
        return self.nc.dram_tensor(name, list(shape), dt, kind="ExternalInput").ap()

    def dram_out(self, name, shape, dt=F32):
        return self.nc.dram_tensor(name, list(shape), dt, kind="ExternalOutput").ap()

    def sbuf(self, name, shape, dt=F32):
        t = self.ctx.enter_context(self.nc.sbuf_tensor("sb_" + name, list(shape), dt))
        esz = 2 if dt == BF16 else 4
        nbytes = int(np.prod(shape[1:])) * esz
        rem = (-nbytes) % 64
        if rem > 32:
            self.ctx.enter_context(self.nc.sbuf_tensor("pad_" + name, [shape[0], 8], F32))
        elif 0 < rem <= 32 and ((nbytes + 31) // 32 * 32) % 64 != 0:
            self.ctx.enter_context(self.nc.sbuf_tensor("pad_" + name, [shape[0], 8], F32))
        return Buf(name, t)

    def psum(self, name, shape, dt=F32):
        t = self.ctx.enter_context(self.nc.psum_tensor("ps_" + name, list(shape), dt))
        return Buf(name, t)

    def _deps(self, reads, writes):
        toks = []
        for b in reads:
            if b.w is not None:
                toks.append((b.w[0], b.w[1], "raw:" + str(b.w[2])))
        for b in writes:
            if b.w is not None:
                toks.append(b.w)
            toks.extend(b.r)
        return toks

    def op(self, eng, fn, reads=(), writes=()):
        toks = self._deps(reads, writes)
        self.seq[eng] += 1
        tok = (self.esem[eng], self.seq[eng], eng)
        self.streams[eng].append((toks, fn, (self.esem[eng], 1)))
        for b in reads:
            b.r.append(tok)
        for b in writes:
            b.w = tok
            b.r = []

    def dma(self, eng, pairs, owner, reads=(), writes=()):
        if owner.dsem is None:
            self.nid += 1
            owner.dsem = self.ctx.enter_context(self.nc.semaphore("ds%d" % self.nid))
        toks = self._deps(reads, writes)
        owner.dcnt += len(pairs)
        tok = (owner.dsem, 16 * owner.dcnt, "dma")

        def fn(e, pairs=pairs):
            return [e.dma_start(out=o, in_=i) for (o, i) in pairs]

        self.streams[eng].append((toks, fn, (owner.dsem, 16)))
        for b in reads:
            b.r.append(tok)
        for b in writes:
            b.w = tok
            b.r = []

    def load(self, eng, buf, out_ap, in_ap):
        self.dma(eng, [(out_ap, in_ap)], buf, writes=[buf])

    def store(self, eng, buf, out_ap, in_ap):
        if buf not in self.store_bufs:
            self.store_bufs.append(buf)
        self.dma(eng, [(out_ap, in_ap)], buf, reads=[buf])

    def finish(self):
        final = [(b.dsem, 16 * b.dcnt, "dma") for b in self.store_bufs]
        self.streams["sp"].append((final, None, None))
        streams = self.streams

        def emit(name, e):
            waited = {}
            for toks, fn, inc in streams[name]:
                for (sem, val, teng) in toks:
                    if teng == name or (teng == "raw:" + name and name == "pe"):
                        continue
                    k = id(sem)
                    if waited.get(k, 0) >= val:
                        continue
                    e.wait_ge(sem, val)
                    waited[k] = val
                if fn is None:
                    continue
                ins = fn(e)
                if isinstance(ins, list):
                    for i_ in ins:
                        i_.then_inc(inc[0], inc[1])
                else:
                    ins.then_inc(inc[0], inc[1])

        with self.nc.Block() as block:
            @block.sync
            def _(e):
                emit("sp", e)

            @block.scalar
            def _(e):
                emit("act", e)

            @block.vector
            def _(e):
                emit("dve", e)

            @block.gpsimd
            def _(e):
                emit("pool", e)

            @block.tensor
            def _(e):
                emit("pe", e)
        self.ctx.close()
        return self.nc


def run(prog, in_maps):
    nc = prog.finish()
    n = int(os.environ.get("DBG_CORES", NCORES))
    res = run_bass_kernel_spmd(nc, in_maps[:n], core_ids=list(range(n)))
    out = list(res.results)
    while len(out) < NCORES:
        out.append(out[0])
    return out


def emit_norm_T(P, x_dram, gb, ident, xnT, ntiles, tag, single=False, bufs=None):
    if bufs is None:
        nb = 1 if single else 2
        bufs = dict(xt=[P.sbuf(f"{tag}_xt{i}", [128, D]) for i in range(nb)],
                    junk=P.sbuf(f"{tag}_junk", [128, D], BF16),
                    xn=[P.sbuf(f"{tag}_xn{i}", [128, D]) for i in range(nb)],
                    ss=[P.sbuf(f"{tag}_ss{i}", [128, 16]) for i in range(nb)],
                    tp=[P.psum(f"{tag}_tp{i}", [128, 512]) for i in range(2)])
    xt, junk, xn, ss, tp = bufs["xt"], bufs["junk"], bufs["xn"], bufs["ss"], bufs["tp"]
    nb = len(xt)
    tcount = 0
    for i in range(ntiles):
        b = i % nb
        P.load("sp", xt[b], xt[b][:], x_dram[i * 128:(i + 1) * 128, :])
        P.op("dve", lambda e, b=b: e.memset(ss[b][:], 0.0), writes=[ss[b]])
        P.op("act", lambda e, b=b: e.activation(out=junk[:], in_=xt[b][:], func=AF.Square,
                                                accum_out=ss[b][:, 0:1]),
             reads=[xt[b]], writes=[junk, ss[b]])
        P.op("act", lambda e, b=b: e.activation(out=ss[b][:, 1:2], in_=ss[b][:, 0:1], func=AF.Sqrt,
                                                scale=1.0 / D, bias=EPS),
             reads=[ss[b]], writes=[ss[b]])
        P.op("dve", lambda e, b=b: e.reciprocal(out=ss[b][:, 1:2], in_=ss[b][:, 1:2]),
             reads=[ss[b]], writes=[ss[b]])
        P.op("dve", lambda e, b=b: e.scalar_tensor_tensor(out=xn[b][:], in0=xt[b][:], scalar=ss[b][:, 1:2],
                                                          in1=gb[:], op0=ALU.mult, op1=ALU.mult),
             reads=[xt[b], ss[b], gb], writes=[xn[b]])
        for kk in range(4):
            pb = tp[tcount % 2]
            tcount += 1
            for j in range(4):
                k = kk * 4 + j
                P.op("pe", lambda e, b=b, k=k, j=j, pb=pb: e.transpose(pb[:, j * 128:(j + 1) * 128],
                                                                      xn[b][:, k * 128:(k + 1) * 128], ident[:]),
                     reads=[xn[b], ident], writes=[pb])
            eng = "act" if kk % 2 == 0 else "dve"
            if eng == "act":
                P.op("act", lambda e, kk=kk, i=i, pb=pb: e.copy(
                    out=xnT[:, kk * 4:(kk + 1) * 4, i * 128:(i + 1) * 128],
                    in_=pb[:].rearrange("p (a b) -> p a b", a=4)), reads=[pb], writes=[xnT])
            else:
                P.op("dve", lambda e, kk=kk, i=i, pb=pb: e.tensor_copy(
                    out=xnT[:, kk * 4:(kk + 1) * 4, i * 128:(i + 1) * 128],
                    in_=pb[:].rearrange("p (a b) -> p a b", a=4)), reads=[pb], writes=[xnT])

    return bufs


NA = 4632
NA_MAIN = 4608


def build_A():
    P = Prog()
    x_d = P.dram_in("x", [TPC, D])
    gb_d = P.dram_in("gb", [128, D])
    w_d = P.dram_in("wA", [D, NA])
    id_d = P.dram_in("ident", [128, 128])
    pos_d = P.dram_in("pos", [128, TPC // 128, 64])
    frq_d = P.dram_in("frq", [128, 64])
    qg_d = P.dram_in("qg", [128, 128])
    kg_d = P.dram_in("kg", [128, 128])
    out_d = P.dram_out("hA", [TPC, NA])
    NT = TPC // 128

    ident = P.sbuf("ident", [128, 128])
    gb = P.sbuf("gb", [128, D])
    qg = P.sbuf("qg", [128, 128])
    kg = P.sbuf("kg", [128, 128])
    pos = P.sbuf("pos", [128, NT, 64])
    frq = P.sbuf("frq", [128, 64])
    P.load("sp", ident, ident[:], id_d)
    P.load("sp", gb, gb[:], gb_d)
    P.load("sp", qg, qg[:], qg_d)
    P.load("sp", kg, kg[:], kg_d)
    P.load("sp", pos, pos[:], pos_d)
    P.load("sp", frq, frq[:], frq_d)

    ang = P.sbuf("ang", [128, NT, 64])
    tmpf = P.sbuf("tmpf", [128, NT, 64])
    tmpi = P.sbuf("tmpi", [128, NT, 64], I32)
    cosT = P.sbuf("cosT", [128, NT, 64])
    sinT = P.sbuf("sinT", [128, NT, 64])
    P.op("dve", lambda e: e.tensor_tensor(out=ang[:], in0=pos[:], in1=frq[:].unsqueeze(1).to_broadcast([128, NT, 64]),
                                          op=ALU.mult), reads=[pos, frq], writes=[ang])
    for (dst, shift) in ((sinT, 0.0), (cosT, float(np.pi / 2))):
        if shift != 0.0:
            P.op("dve", lambda e, shift=shift: e.tensor_scalar(out=ang[:], in0=ang[:], scalar1=shift, scalar2=None,
                                                                op0=ALU.add), reads=[ang], writes=[ang])
        P.op("dve", lambda e: e.tensor_scalar(out=tmpf[:], in0=ang[:], scalar1=1.0 / TWO_PI, scalar2=None,
                                              op0=ALU.mult), reads=[ang], writes=[tmpf])
        P.op("dve", lambda e: e.tensor_copy(out=tmpi[:], in_=tmpf[:]), reads=[tmpf], writes=[tmpi])
        P.op("dve", lambda e: e.tensor_copy(out=tmpf[:], in_=tmpi[:]), reads=[tmpi], writes=[tmpf])
        P.op("dve", lambda e: e.scalar_tensor_tensor(out=tmpf[:], in0=tmpf[:], scalar=-TWO_PI, in1=ang[:],
                                                     op0=ALU.mult, op1=ALU.add), reads=[tmpf, ang], writes=[tmpf])
        P.op("act", lambda e, dst=dst: e.activation(out=dst[:], in_=tmpf[:], func=AF.Sin),
             reads=[tmpf], writes=[dst])

    xnT = P.sbuf("xnT", [128, 16, TPC], BF16)
    emit_norm_T(P, x_d, gb, ident, xnT, NT, "nA")

    wblk = [P.sbuf(f"wblk{i}", [128, 16, 512], BF16) for i in range(2)]
    pp = [P.psum(f"pp{i}", [128, 512]) for i in range(2)]
    ot = [P.sbuf(f"ot{i}", [128, 512]) for i in range(3)]
    ss4 = P.sbuf("ss4", [128, 8])
    junk2 = P.sbuf("junk2", [128, 128])
    t1 = P.sbuf("rt1", [128, 4, 64])
    t2 = P.sbuf("rt2", [128, 4, 64])
    qn = P.sbuf("qn", [128, 512])
    w_v = w_d.rearrange("(c p) n -> p c n", p=128)
    nblk = 10
    cnt = 0
    for cb in range(nblk):
        wb = wblk[cb % 2]
        c0 = cb * 512
        ncol = 512 if cb < 9 else NA - NA_MAIN
        P.dma("pool", [(wb[:, k, 0:ncol], w_v[:, k, c0:c0 + ncol]) for k in range(16)], wb, writes=[wb])
        for i in range(NT):
            ps = pp[cnt % 2]
            o = ot[cnt % 3]
            cnt += 1
            for k in range(16):
                P.op("pe", lambda e, ps=ps, k=k, i=i, wb=wb, ncol=ncol: e.matmul(
                    ps[:, 0:ncol], xnT[:, k, i * 128:(i + 1) * 128], wb[:, k, 0:ncol],
                    start=(k == 0), stop=(k == 15)), reads=[xnT, wb], writes=[ps])
            if cb in (6, 7) or cb == 8:
                nh = 4 if cb in (6, 7) else 2
                g = qg if cb in (6, 7) else kg
                sc = (128.0 ** -0.5) if cb in (6, 7) else 1.0
                P.op("dve", lambda e: e.memset(ss4[:], 0.0), writes=[ss4])
                for h in range(nh):
                    P.op("act", lambda e, ps=ps, h=h: e.activation(out=junk2[:], in_=ps[:, h * 128:(h + 1) * 128],
                                                                   func=AF.Square, accum_out=ss4[:, h:h + 1]),
                         reads=[ps], writes=[junk2, ss4])
                P.op("act", lambda e, nh=nh: e.activation(out=ss4[:, 4:4 + nh], in_=ss4[:, 0:nh], func=AF.Sqrt,
                                                          scale=1.0 / 128, bias=EPS), reads=[ss4], writes=[ss4])
                P.op("dve", lambda e, nh=nh: e.reciprocal(out=ss4[:, 4:4 + nh], in_=ss4[:, 4:4 + nh]),
                     reads=[ss4], writes=[ss4])
                if sc != 1.0:
                    P.op("dve", lambda e, nh=nh, sc=sc: e.tensor_scalar(out=ss4[:, 4:4 + nh], in0=ss4[:, 4:4 + nh],
                                                                        scalar1=sc, scalar2=None, op0=ALU.mult),
                         reads=[ss4], writes=[ss4])
                for h in range(nh):
                    P.op("dve", lambda e, ps=ps, h=h, g=g: e.scalar_tensor_tensor(
                        out=qn[:, h * 128:(h + 1) * 128], in0=ps[:, h * 128:(h + 1) * 128],
                        scalar=ss4[:, 4 + h:5 + h], in1=g[:], op0=ALU.mult, op1=ALU.mult),
                         reads=[ps, ss4, g], writes=[qn])
                if nh < 4:
                    P.op("act", lambda e, ps=ps, o=o: e.copy(out=o[:, 256:512], in_=ps[:, 256:512]),
                         reads=[ps], writes=[o])
                W = nh * 128
                qv = qn[:, 0:W].rearrange("p (h i two) -> p h i two", h=nh, two=2)
                ov = o[:, 0:W].rearrange("p (h i two) -> p h i two", h=nh, two=2)
                x0 = qv[:, :, :, 0]
                x1 = qv[:, :, :, 1]
                cb_ = cosT[:, i, :].unsqueeze(1).to_broadcast([128, nh, 64])
                sb_ = sinT[:, i, :].unsqueeze(1).to_broadcast([128, nh, 64])
                a1 = t1[:, 0:nh, :]
                a2 = t2[:, 0:nh, :]
                P.op("dve", lambda e, x0=x0, cb_=cb_, a1=a1: e.tensor_tensor(out=a1, in0=x0, in1=cb_, op=ALU.mult),
                     reads=[qn, cosT], writes=[t1])
                P.op("pool", lambda e, x1=x1, sb_=sb_, a2=a2: e.tensor_tensor(out=a2, in0=x1, in1=sb_, op=ALU.mult),
                     reads=[qn, sinT], writes=[t2])
                P.op("dve", lambda e, ov=ov, a1=a1, a2=a2: e.tensor_tensor(out=ov[:, :, :, 0], in0=a1, in1=a2,
                                                                          op=ALU.subtract),
                     reads=[t1, t2], writes=[o])
                P.op("dve", lambda e, x0=x0, sb_=sb_, a1=a1: e.tensor_tensor(out=a1, in0=x0, in1=sb_, op=ALU.mult),
                     reads=[qn, sinT], writes=[t1])
                P.op("pool", lambda e, x1=x1, cb_=cb_, a2=a2: e.tensor_tensor(out=a2, in0=x1, in1=cb_, op=ALU.mult),
                     reads=[qn, cosT], writes=[t2])
                P.op("dve", lambda e, ov=ov, a1=a1, a2=a2: e.tensor_tensor(out=ov[:, :, :, 1], in0=a1, in1=a2,
                                                                          op=ALU.add),
                     reads=[t1, t2], writes=[o])
            else:
                if cnt % 2 == 0:
                    P.op("act", lambda e, ps=ps, o=o, ncol=ncol: e.copy(out=o[:, 0:ncol], in_=ps[:, 0:ncol]),
                         reads=[ps], writes=[o])
                else:
                    P.op("dve", lambda e, ps=ps, o=o, ncol=ncol: e.tensor_copy(out=o[:, 0:ncol], in_=ps[:, 0:ncol]),
                         reads=[ps], writes=[o])
            P.store("sp", o, out_d[i * 128:(i + 1) * 128, c0:c0 + ncol], o[:, 0:ncol])
    return P


COLS_A = np.concatenate([
    np.arange(0, 768),
    np.arange(1536, 3840),
    np.arange(4632, 6168),
    np.arange(3840, 3864),
])


def rope_consts():
    t = np.arange(SEQ)
    row = (t // 64).astype(np.float32)
    col = (t % 64).astype(np.float32)
    pos = np.concatenate([np.repeat(row[:, None], 32, 1), np.repeat(col[:, None], 32, 1)], axis=1)
    freqs = (10000.0 ** (-np.arange(0, 64, 2, dtype=np.float32) / 64)).astype(np.float32)
    frq = np.concatenate([freqs, freqs])[None, :].repeat(128, 0).astype(np.float32)
    return pos.astype(np.float32), frq


def bcast_rows(v, n=128):
    return np.ascontiguousarray(np.broadcast_to(np.asarray(v, np.float32)[None, :], (n, v.shape[0])))


def run_A(x, norm_g, w_in_l, qn_g, kn_g):
    P = build_A()
    pos, frq = rope_consts()
    wA = np.ascontiguousarray(w_in_l[:, COLS_A])
    common = dict(gb=bcast_rows(norm_g), wA=wA, ident=np.eye(128, dtype=np.float32), frq=frq,
                  qg=bcast_rows(qn_g), kg=bcast_rows(kn_g))
    maps = []
    for c in range(NCORES):
        pc = pos[c * TPC:(c + 1) * TPC].reshape(TPC // 128, 128, 64).transpose(1, 0, 2)
        maps.append(dict(common, x=np.ascontiguousarray(x[c * TPC:(c + 1) * TPC]), pos=np.ascontiguousarray(pc)))
    res = run(P, maps)
    return np.concatenate([r["hA"] for r in res], axis=0)


def build_ATT():
    P = Prog()
    qT_d = P.dram_in("qT", [128, SEQ])
    kT_d = P.dram_in("kT", [128, SEQ])
    v_d = P.dram_in("v", [128, SEQ // 128, 128])
    out_d = P.dram_out("oT", [128, SEQ])
    qT = P.sbuf("qT", [128, SEQ], BF16)
    kT = P.sbuf("kT", [128, SEQ], BF16)
    v = P.sbuf("v", [128, SEQ // 128, 128], BF16)
    ones = P.sbuf("ones", [128, 128], BF16)
    P.op("dve", lambda e: e.memset(ones[:], 1.0), writes=[ones])
    for j in range(4):
        sl = slice(j * 2048, (j + 1) * 2048)
        P.dma("pool", [(kT[:, sl], kT_d[:, sl])], kT, writes=[kT])
        P.dma("pool", [(qT[:, sl], qT_d[:, sl])], qT, writes=[qT])
        P.dma("pool", [(v[:, j * 16:(j + 1) * 16, :], v_d[:, j * 16:(j + 1) * 16, :])], v, writes=[v])
    ps_s = [P.psum(f"s{i}", [128, 512]) for i in range(2)]
    ps_o = [P.psum(f"o{i}", [128, 512]) for i in range(2)]
    ps_d = [P.psum(f"d{i}", [128, 512]) for i in range(2)]
    pt = [P.sbuf(f"pt{i}", [128, 512], BF16) for i in range(3)]
    rd = P.sbuf("rd", [128, 512])
    ot = [P.sbuf(f"ot{i}", [128, 512]) for i in range(2)]
    NKT = SEQ // 128
    cnt = 0
    for qb in range(SEQ // 512):
        po = ps_o[qb % 2]
        pd = ps_d[qb % 2]
        for kt in range(NKT):
            s = ps_s[cnt % 2]
            p = pt[cnt % 3]
            cnt += 1
            P.op("pe", lambda e, s=s, kt=kt, qb=qb: e.matmul(s[:], kT[:, kt * 128:(kt + 1) * 128],
                                                          qT[:, qb * 512:(qb + 1) * 512], start=True, stop=True),
                 reads=[kT, qT], writes=[s])
            P.op("act", lambda e, s=s, p=p: e.activation(out=p[:], in_=s[:], func=AF.Exp), reads=[s], writes=[p])
            P.op("pe", lambda e, po=po, p=p, kt=kt: e.matmul(po[:], v[:, kt, :], p[:], start=(kt == 0),
                                                          stop=(kt == NKT - 1)), reads=[v, p], writes=[po])
            P.op("pe", lambda e, pd=pd, p=p, kt=kt: e.matmul(pd[:], ones[:], p[:], start=(kt == 0),
                                                          stop=(kt == NKT - 1)), reads=[ones, p], writes=[pd])
        o = ot[qb % 2]
        P.op("dve", lambda e, pd=pd: e.reciprocal(out=rd[:], in_=pd[:]), reads=[pd], writes=[rd])
        P.op("dve", lambda e, po=po, o=o: e.tensor_tensor(out=o[:], in0=po[:], in1=rd[:], op=ALU.mult),
             reads=[po, rd], writes=[o])
        P.store("sp", o, out_d[:, qb * 512:(qb + 1) * 512], o[:])
    return P


def run_ATT(hA):
    P = build_ATT()
    q = hA[:, 3072:4096]
    k = hA[:, 4096:4352]
    vv = hA[:, 4352:4608]
    maps = []
    for c in range(NCORES):
        kv = c // 4
        maps.append(dict(
            qT=np.ascontiguousarray(q[:, c * 128:(c + 1) * 128].T),
            kT=np.ascontiguousarray(k[:, kv * 128:(kv + 1) * 128].T),
            v=np.ascontiguousarray(vv[:, kv * 128:(kv + 1) * 128].reshape(SEQ // 128, 128, 128).transpose(1, 0, 2)),
        ))
    res = run(P, maps)
    return np.concatenate([r["oT"].T for r in res], axis=1)


S5C = 512


def _range_reduce_sin(P, dst, src, tmpf, tmpi, shift):
    P.op("dve", lambda e: e.tensor_scalar(out=tmpf[:], in0=src[:], scalar1=shift, scalar2=1.0 / TWO_PI,
                                          op0=ALU.add, op1=ALU.mult), reads=[src], writes=[tmpf])
    P.op("dve", lambda e: e.tensor_copy(out=tmpi[:], in_=tmpf[:]), reads=[tmpf], writes=[tmpi])
    P.op("dve", lambda e: e.tensor_copy(out=tmpf[:], in_=tmpi[:]), reads=[tmpi], writes=[tmpf])
    P.op("dve", lambda e: e.scalar_tensor_tensor(out=tmpf[:], in0=tmpf[:], scalar=-TWO_PI, in1=src[:],
                                                 op0=ALU.mult, op1=ALU.add), reads=[tmpf, src], writes=[tmpf])
    if shift != 0.0:
        P.op("dve", lambda e: e.tensor_scalar(out=tmpf[:], in0=tmpf[:], scalar1=shift, scalar2=None, op0=ALU.add),
             reads=[tmpf], writes=[tmpf])
    P.op("act", lambda e: e.activation(out=dst[:], in_=tmpf[:], func=AF.Sin), reads=[tmpf], writes=[dst])


def build_S5():
    P = Prog()
    NU = 6
    NCH = SEQ // S5C
    uT_d = P.dram_in("uT", [NU, 32, SEQ])
    prm_d = P.dram_in("prm", [NU, 128, 3])
    b_d = P.dram_in("bmat", [NU, 128, 64])
    c_d = P.dram_in("cmat", [NU, 128, 64])
    jidx_d = P.dram_in("jidx", [128, S5C + 1])
    id_d = P.dram_in("ident", [128, 128])
    out_d = P.dram_out("yT", [NU, 32, SEQ])

    ident = P.sbuf("ident", [128, 128])
    jidx = P.sbuf("jidx", [128, S5C + 1])
    P.load("sp", ident, ident[:], id_d)
    P.load("sp", jidx, jidx[:], jidx_d)
    prm = P.sbuf("prm", [128, 3])
    bm = P.sbuf("bm", [128, 64])
    cm = P.sbuf("cm", [128, 64])
    sc = P.sbuf("sc", [128, 16])
    s1f = P.sbuf("s1f", [128, 1])
    s1i = P.sbuf("s1i", [128, 1], I32)
    th = P.sbuf("th", [128, 1])
    ph = P.sbuf("ph", [128, S5C + 1])
    tmpf = P.sbuf("tmpf", [128, S5C + 1])
    tmpi = P.sbuf("tmpi", [128, S5C + 1], I32)
    Pc = P.sbuf("Pc", [128, S5C + 1])
    Ps = P.sbuf("Ps", [128, S5C + 1])
    rt = P.sbuf("rt", [128, S5C])
    bb = P.sbuf("bb", [128, 64])
    bt1 = P.sbuf("bt1", [128, 32])
    BT = P.sbuf("BT", [32, 256], BF16)
    CT = P.sbuf("CT", [128, 96], BF16)
    pst = P.psum("pst", [32, 256])
    ps_re = [P.psum(f"psre{i}", [128, S5C]) for i in range(2)]
    ps_im = [P.psum(f"psim{i}", [128, S5C]) for i in range(2)]
    ps_y = [P.psum(f"psy{i}", [32, S5C]) for i in range(2)]
    ut = [P.sbuf(f"ut{i}", [32, S5C], BF16) for i in range(2)]
    m = [P.sbuf(f"m{i}", [128, S5C]) for i in range(4)]
    cre = P.sbuf("cre_", [128, S5C])
    cim = P.sbuf("cim_", [128, S5C])
    zre = [P.sbuf(f"zre{i}", [128, S5C]) for i in range(2)]
    zim = [P.sbuf(f"zim{i}", [128, S5C]) for i in range(2)]
    nn = [P.sbuf(f"nn{i}", [128, S5C], BF16) for i in range(4)]
    init = P.sbuf("init", [128, 4])
    yt = [P.sbuf(f"yt{i}", [32, S5C]) for i in range(2)]

    def col(t, j):
        return t[:, j:j + 1]

    for u in range(NU):
        P.load("sp", prm, prm[:], prm_d[u])
        P.load("sp", bm, bm[:], b_d[u])
        P.load("sp", cm, cm[:], c_d[u])
        P.op("act", lambda e: e.activation(out=col(sc, 0), in_=col(prm, 2), func=AF.Exp), reads=[prm], writes=[sc])
        P.op("dve", lambda e: e.tensor_tensor(out=col(sc, 10), in0=col(prm, 0), in1=col(sc, 0), op=ALU.mult),
             reads=[prm, sc], writes=[sc])
        P.op("act", lambda e: e.activation(out=col(sc, 1), in_=col(sc, 10), func=AF.Exp), reads=[sc], writes=[sc])
        P.op("dve", lambda e: e.tensor_tensor(out=col(sc, 2), in0=col(prm, 1), in1=col(sc, 0), op=ALU.mult),
             reads=[prm, sc], writes=[sc])
        P.op("dve", lambda e: e.tensor_scalar(out=s1f[:], in0=col(sc, 2), scalar1=1.0 / TWO_PI, scalar2=None,
                                              op0=ALU.mult), reads=[sc], writes=[s1f])
        P.op("dve", lambda e: e.tensor_copy(out=s1i[:], in_=s1f[:]), reads=[s1f], writes=[s1i])
        P.op("dve", lambda e: e.tensor_copy(out=s1f[:], in_=s1i[:]), reads=[s1i], writes=[s1f])
        P.op("dve", lambda e: e.scalar_tensor_tensor(out=th[:], in0=s1f[:], scalar=-TWO_PI, in1=col(sc, 2),
                                                     op0=ALU.mult, op1=ALU.add), reads=[s1f, sc], writes=[th])
        P.op("dve", lambda e: e.tensor_scalar(out=ph[:], in0=jidx[:], scalar1=th[:, 0:1], scalar2=None,
                                              op0=ALU.mult), reads=[jidx, th], writes=[ph])
        _range_reduce_sin(P, Ps, ph, tmpf, tmpi, 0.0)
        _range_reduce_sin(P, Pc, ph, tmpf, tmpi, float(np.pi / 2))
        P.op("dve", lambda e: e.tensor_tensor(out=col(sc, 5), in0=col(sc, 1), in1=col(Pc, 1), op=ALU.mult),
             reads=[sc, Pc], writes=[sc])
        P.op("dve", lambda e: e.tensor_scalar(out=col(sc, 5), in0=col(sc, 5), scalar1=-1.0, scalar2=None,
                                              op0=ALU.add), reads=[sc], writes=[sc])
        P.op("dve", lambda e: e.tensor_tensor(out=col(sc, 6), in0=col(sc, 1), in1=col(Ps, 1), op=ALU.mult),
             reads=[sc, Ps], writes=[sc])
        P.op("dve", lambda e: e.tensor_tensor(out=col(sc, 7), in0=col(prm, 0), in1=col(prm, 0), op=ALU.mult),
             reads=[prm], writes=[sc])
        P.op("dve", lambda e: e.scalar_tensor_tensor(out=col(sc, 7), in0=col(prm, 1), scalar=col(prm, 1),
                                                     in1=col(sc, 7), op0=ALU.mult, op1=ALU.add),
             reads=[prm, sc], writes=[sc])
        P.op("dve", lambda e: e.reciprocal(out=col(sc, 7), in_=col(sc, 7)), reads=[sc], writes=[sc])
        P.op("dve", lambda e: e.tensor_tensor(out=col(sc, 10), in0=col(sc, 5), in1=col(prm, 0), op=ALU.mult),
             reads=[sc, prm], writes=[sc])
        P.op("dve", lambda e: e.scalar_tensor_tensor(out=col(sc, 10), in0=col(sc, 6), scalar=col(prm, 1),
                                                     in1=col(sc, 10), op0=ALU.mult, op1=ALU.add),
             reads=[sc, prm], writes=[sc])
        P.op("dve", lambda e: e.tensor_tensor(out=col(sc, 8), in0=col(sc, 10), in1=col(sc, 7), op=ALU.mult),
             reads=[sc], writes=[sc])
        P.op("dve", lambda e: e.tensor_tensor(out=col(sc, 11), in0=col(sc, 5), in1=col(prm, 1), op=ALU.mult),
             reads=[sc, prm], writes=[sc])
        P.op("dve", lambda e: e.scalar_tensor_tensor(out=col(sc, 11), in0=col(sc, 6), scalar=col(prm, 0),
                                                     in1=col(sc, 11), op0=ALU.mult, op1=ALU.subtract),
             reads=[sc, prm], writes=[sc])
        P.op("dve", lambda e: e.tensor_tensor(out=col(sc, 9), in0=col(sc, 11), in1=col(sc, 7), op=ALU.mult),
             reads=[sc], writes=[sc])
        P.op("dve", lambda e: e.tensor_scalar(out=bt1[:], in0=bm[:, 32:64], scalar1=col(sc, 9), scalar2=None,
                                              op0=ALU.mult), reads=[bm, sc], writes=[bt1])
        P.op("dve", lambda e: e.scalar_tensor_tensor(out=bb[:, 0:32], in0=bm[:, 0:32], scalar=col(sc, 8),
                                                     in1=bt1[:], op0=ALU.mult, op1=ALU.subtract),
             reads=[bm, sc, bt1], writes=[bb])
        P.op("dve", lambda e: e.tensor_scalar(out=bt1[:], in0=bm[:, 0:32], scalar1=col(sc, 9), scalar2=None,
                                              op0=ALU.mult), reads=[bm, sc, bb], writes=[bt1])
        P.op("dve", lambda e: e.scalar_tensor_tensor(out=bb[:, 32:64], in0=bm[:, 32:64], scalar=col(sc, 8),
                                                     in1=bt1[:], op0=ALU.mult, op1=ALU.add),
             reads=[bm, sc, bt1], writes=[bb])
        P.op("pe", lambda e: e.transpose(pst[:, 0:128], bb[:, 0:32], ident[:]), reads=[bb, ident], writes=[pst])
        P.op("pe", lambda e: e.transpose(pst[:, 128:256], bb[:, 32:64], ident[:]), reads=[bb, ident], writes=[pst])
        P.op("dve", lambda e: e.tensor_copy(out=BT[:], in_=pst[:]), reads=[pst], writes=[BT])
        P.op("dve", lambda e: e.tensor_copy(out=CT[:, 0:32], in_=cm[:, 0:32]), reads=[cm], writes=[CT])
        P.op("dve", lambda e: e.tensor_scalar(out=CT[:, 32:96], in0=cm[:, 0:64], scalar1=-1.0, scalar2=None,
                                              op0=ALU.mult), reads=[cm], writes=[CT])
        P.op("dve", lambda e: e.tensor_scalar(out=rt[:], in0=jidx[:, 0:S5C], scalar1=0.0, scalar2=col(sc, 1),
                                              op0=ALU.mult, op1=ALU.add), reads=[jidx, sc], writes=[rt])
        for ch in range(NCH):
            b = ch % 2
            tsl = slice(ch * S5C, (ch + 1) * S5C)
            P.dma("pool", [(ut[b][:], uT_d[u, :, tsl])], ut[b], writes=[ut[b]])
            pr, pi_ = ps_re[b], ps_im[b]
            P.op("pe", lambda e, pr=pr, b=b: e.matmul(pr[:], BT[:, 0:128], ut[b][:], start=True, stop=True),
                 reads=[BT, ut[b]], writes=[pr])
            P.op("pe", lambda e, pi_=pi_, b=b: e.matmul(pi_[:], BT[:, 128:256], ut[b][:], start=True, stop=True),
                 reads=[BT, ut[b]], writes=[pi_])
            PcS, PsS = Pc[:, 0:S5C], Ps[:, 0:S5C]
            P.op("dve", lambda e, pr=pr: e.tensor_tensor(out=m[0][:], in0=pr[:], in1=PcS, op=ALU.mult),
                 reads=[pr, Pc], writes=[m[0]])
            P.op("dve", lambda e, pi_=pi_: e.tensor_tensor(out=m[1][:], in0=pi_[:], in1=PsS, op=ALU.mult),
                 reads=[pi_, Ps], writes=[m[1]])
            P.op("pool", lambda e: e.tensor_tensor(out=cre[:], in0=m[0][:], in1=m[1][:], op=ALU.add),
                 reads=[m[0], m[1]], writes=[cre])
            P.op("dve", lambda e, pi_=pi_: e.tensor_tensor(out=m[2][:], in0=pi_[:], in1=PcS, op=ALU.mult),
                 reads=[pi_, Pc], writes=[m[2]])
            P.op("dve", lambda e, pr=pr: e.tensor_tensor(out=m[3][:], in0=pr[:], in1=PsS, op=ALU.mult),
                 reads=[pr, Ps], writes=[m[3]])
            P.op("pool", lambda e: e.tensor_tensor(out=cim[:], in0=m[2][:], in1=m[3][:], op=ALU.subtract),
                 reads=[m[2], m[3]], writes=[cim])
            if ch == 0:
                P.op("dve", lambda e: e.memset(init[:], 0.0), writes=[init])
            else:
                pzr, pzi = zre[1 - b], zim[1 - b]
                L = S5C - 1
                P.op("dve", lambda e, pzi=pzi: e.tensor_tensor(out=col(init, 2), in0=col(pzi, L), in1=col(Ps, S5C),
                                                              op=ALU.mult), reads=[pzi, Ps], writes=[init])
                P.op("dve", lambda e, pzr=pzr: e.scalar_tensor_tensor(out=col(init, 0), in0=col(pzr, L),
                                                                     scalar=col(Pc, S5C), in1=col(init, 2),
                                                                     op0=ALU.mult, op1=ALU.subtract),
                     reads=[pzr, Pc, init], writes=[init])
                P.op("dve", lambda e, pzr=pzr: e.tensor_tensor(out=col(init, 3), in0=col(pzr, L), in1=col(Ps, S5C),
                                                              op=ALU.mult), reads=[pzr, Ps], writes=[init])
                P.op("dve", lambda e, pzi=pzi: e.scalar_tensor_tensor(out=col(init, 1), in0=col(pzi, L),
                                                                     scalar=col(Pc, S5C), in1=col(init, 3),
                                                                     op0=ALU.mult, op1=ALU.add),
                     reads=[pzi, Pc, init], writes=[init])
            zr, zi = zre[b], zim[b]
            P.op("dve", lambda e, zr=zr: e.tensor_tensor_scan(out=zr[:], data0=rt[:], data1=cre[:],
                                                             initial=col(init, 0), op0=ALU.mult, op1=ALU.add),
                 reads=[rt, cre, init], writes=[zr])
            P.op("dve", lambda e, zi=zi: e.tensor_tensor_scan(out=zi[:], data0=rt[:], data1=cim[:],
                                                             initial=col(init, 1), op0=ALU.mult, op1=ALU.add),
                 reads=[rt, cim, init], writes=[zi])
            P.op("pool", lambda e, zr=zr: e.tensor_tensor(out=nn[0][:], in0=zr[:], in1=PcS, op=ALU.mult),
                 reads=[zr, Pc], writes=[nn[0]])
            P.op("pool", lambda e, zi=zi: e.tensor_tensor(out=nn[1][:], in0=zi[:], in1=PsS, op=ALU.mult),
                 reads=[zi, Ps], writes=[nn[1]])
            P.op("pool", lambda e, zi=zi: e.tensor_tensor(out=nn[2][:], in0=zi[:], in1=PcS, op=ALU.mult),
                 reads=[zi, Pc], writes=[nn[2]])
            P.op("dve", lambda e, zr=zr: e.tensor_tensor(out=nn[3][:], in0=zr[:], in1=PsS, op=ALU.mult),
                 reads=[zr, Ps], writes=[nn[3]])
            py = ps_y[b]
            lts = [CT[:, 0:32], CT[:, 32:64], CT[:, 64:96], CT[:, 64:96]]
            for q in range(4):
                P.op("pe", lambda e, py=py, q=q, lt=lts[q]: e.matmul(py[:], lt, nn[q][:], start=(q == 0),
                                                                    stop=(q == 3)), reads=[CT, nn[q]], writes=[py])
            y = yt[b]
            P.op("act", lambda e, py=py, y=y: e.copy(out=y[:], in_=py[:]), reads=[py], writes=[y])
            P.store("sp", y, out_d[u, :, tsl], y[:])
    return P


def run_S5(hA, a_re, a_im, log_step, b_re, b_im, c_re, c_im):
    P = build_S5()
    u = hA[:, 0:768]
    uT = np.ascontiguousarray(u.T)
    uTr = np.ascontiguousarray(uT[:, ::-1])
    jidx = bcast_rows(np.arange(S5C + 1, dtype=np.float32))
    maps = []
    for c in range(NCORES):
        uTc = np.zeros((6, 32, SEQ), np.float32)
        prm = np.zeros((6, 128, 3), np.float32)
        bmat = np.zeros((6, 128, 64), np.float32)
        cmat = np.zeros((6, 128, 64), np.float32)
        for pq in range(3):
            for d in range(2):
                un = pq * 2 + d
                for gg in range(2):
                    g = c * 6 + pq * 2 + gg
                    src = uTr if d == 1 else uT
                    uTc[un, gg * 16:(gg + 1) * 16] = src[g * 16:(g + 1) * 16]
                    rs = slice(gg * 64, (gg + 1) * 64)
                    prm[un, rs, 0] = a_re[d, g]
                    prm[un, rs, 1] = a_im[d, g]
                    prm[un, rs, 2] = log_step[d, g]
                    bmat[un, rs, gg * 16:(gg + 1) * 16] = b_re[d, g]
                    bmat[un, rs, 32 + gg * 16:32 + (gg + 1) * 16] = b_im[d, g]
                    cmat[un, rs, gg * 16:(gg + 1) * 16] = c_re[d, g].T
                    cmat[un, rs, 32 + gg * 16:32 + (gg + 1) * 16] = c_im[d, g].T
        maps.append(dict(uT=uTc, prm=prm, bmat=bmat, cmat=cmat, jidx=jidx, ident=np.eye(128, dtype=np.float32)))
    res = run(P, maps)
    yf = np.zeros((SEQ, 768), np.float32)
    yb = np.zeros((SEQ, 768), np.float32)
    for c in range(NCORES):
        yT = res[c]["yT"]
        for pq in range(3):
            cs = slice((c * 6 + pq * 2) * 16, (c * 6 + pq * 2 + 2) * 16)
            yf[:, cs] = yT[pq * 2].T
            yb[:, cs] = yT[pq * 2 + 1].T[::-1]
    return yf, yb


DNC = 128
NDC = SEQ // DNC


def build_DN():
    P = Prog()
    NU = 2
    xin_d = P.dram_in("xin", [NU, 3, 128, SEQ + 4])
    cw_d = P.dram_in("cw", [NU, 128, 15])
    ab_d = P.dram_in("ab", [NU, 128, 2, NDC])
    hp_d = P.dram_in("hp", [NU, 128, 2])
    id_d = P.dram_in("ident", [128, 128])
    tri_d = P.dram_in("triu", [128, 128])
    mb_d = P.dram_in("maskb", [128, 128])
    m0_d = P.dram_in("msk0", [128, 128])
    mT_d = P.dram_in("mskT", [128, 6, 128])
    out_d = P.dram_out("o", [NU, SEQ, 128])

    ident = P.sbuf("ident", [128, 128])
    identb = P.sbuf("identb", [128, 128], BF16)
    triu = P.sbuf("triu", [128, 128])
    maskb = P.sbuf("maskb", [128, 128])
    onesf = P.sbuf("onesf", [128, 128])
    P.load("sp", ident, ident[:], id_d)
    P.load("sp", triu, triu[:], tri_d)
    P.load("sp", maskb, maskb[:], mb_d)
    onesb = P.sbuf("onesb", [128, 128], BF16)
    triub = P.sbuf("triub", [128, 128], BF16)
    P.op("dve", lambda e: e.memset(onesf[:], 1.0), writes=[onesf])
    P.op("dve", lambda e: e.memset(onesb[:], 1.0), writes=[onesb])
    P.op("dve", lambda e: e.tensor_copy(out=triub[:], in_=triu[:]), reads=[triu], writes=[triub])
    P.op("dve", lambda e: e.tensor_copy(out=identb[:], in_=ident[:]), reads=[ident], writes=[identb])

    qT = P.sbuf("qT", [128, SEQ], BF16)
    kT = P.sbuf("kT", [128, SEQ], BF16)
    vT = P.sbuf("vT", [128, SEQ], BF16)
    cw = P.sbuf("cw", [128, 15])
    ab = P.sbuf("ab", [128, 2, NDC])
    hp = P.sbuf("hp", [128, 16])
    PW = 2048
    xp = [P.sbuf(f"xp{i}", [128, PW + 4]) for i in range(2)]
    acc = P.sbuf("acc", [128, PW])
    sq = P.sbuf("sq", [128, PW], BF16)
    ghl = P.sbuf("ghl", [128, 2, NDC], BF16)
    gtmp = P.sbuf("gtmp", [128, NDC])
    dgh = P.sbuf("dgh", [128, 128], BF16)
    dgl = P.sbuf("dgl", [128, 128], BF16)
    dgt = P.sbuf("dgt", [128, 128])
    rn = P.sbuf("rn", [128, 512])
    pss = [P.psum(f"pss{i}", [128, 512]) for i in range(2)]

    tb = {n: P.sbuf("tb_" + n, [128, NDC]) for n in ("g", "beta", "gc", "egc", "negegc", "egl", "ekd", "tmp")}
    pt64 = pss[0]

    ptr = P.psum("ptr", [128, 256], BF16)
    KV = P.sbuf("KV", [128, 256], BF16)
    dg = P.sbuf("dg", [128, 128])
    kTc = P.sbuf("kTc", [128, 128], BF16)
    qTc = P.sbuf("qTc", [128, 128], BF16)
    Winvw = P.sbuf("Winvw", [128, 128], BF16)
    pg = P.psum("pg", [128, 128])
    xe = P.sbuf("xe", [128, 128])
    E = P.sbuf("E", [128, 128])
    Es = P.sbuf("Es", [128, 128])
    pkk = P.psum("pkk", [128, 256])
    AT = P.sbuf("AT", [128, 128], BF16)
    X = [P.sbuf(f"X{i}", [128, 128], BF16) for i in range(2)]
    XT = [P.sbuf(f"XT{i}", [128, 128], BF16) for i in range(2)]
    W = [P.sbuf(f"W{i}", [128, 128]) for i in range(2)]
    Wb = [P.sbuf(f"Wb{i}", [128, 128], BF16) for i in range(2)]
    Xw = [P.sbuf(f"Xw{i}", [128, 128], BF16) for i in range(2)]
    XTw = [P.sbuf(f"XTw{i}", [128, 128], BF16) for i in range(2)]
    x0f = P.sbuf("x0f", [128, 128])
    UT = P.sbuf("UT", [128, 128])
    G32 = P.sbuf("G32", [128, 128])
    Gm = P.sbuf("Gm", [128, 128], BF16)
    Gw = P.sbuf("Gw", [128, 128], BF16)
    GTw = P.sbuf("GTw", [128, 128], BF16)
    Ysb = P.sbuf("Ysb", [128, 128], BF16)
    CT = [P.sbuf(f"CTl{i}", [128, 128], BF16) for i in range(6)]
    msk0 = P.sbuf("msk0", [128, 128])
    mskT = P.sbuf("mskT", [128, 6, 128])
    P.load("sp", msk0, msk0[:], m0_d)
    P.load("sp", mskT, mskT[:], mT_d)
    pX = P.psum("pX", [128, 256])
    pW = P.psum("pW", [128, 128])
    S = P.sbuf("S", [128, 128])
    Sb = P.sbuf("Sb", [128, 128], BF16)
    pks = P.psum("pks", [128, 256])
    Rp = P.sbuf("Rp", [128, 128], BF16)
    vnew = P.sbuf("vnew", [128, 128], BF16)
    oq = P.sbuf("oq", [128, 128])
    ot = [P.sbuf(f"ot{i}", [128, 128]) for i in range(2)]
    Kd = P.sbuf("Kd", [128, 128], BF16)

    def col(t, j):
        return t[:, j:j + 1]

    for u in range(NU):
        P.load("sp", cw, cw[:], cw_d[u])
        P.load("sp", ab, ab[:], ab_d[u])
        P.load("sp", hp, hp[:, 0:2], hp_d[u])
        cnt = 0
        for ti, dst in enumerate((qT, kT, vT)):
            for pc in range(SEQ // PW):
                x_ = xp[cnt % 2]
                cnt += 1
                P.load("sp", x_, x_[:], xin_d[u, ti, :, pc * PW:pc * PW + PW + 4])
                P.op("act", lambda e, x_=x_, ti=ti: e.activation(out=acc[:], in_=x_[:, 0:PW], func=AF.Copy,
                                                                scale=col(cw, ti * 5)), reads=[x_, cw], writes=[acc])
                for k in range(1, 5):
                    P.op("dve", lambda e, x_=x_, ti=ti, k=k: e.scalar_tensor_tensor(
                        out=acc[:], in0=x_[:, k:k + PW], scalar=col(cw, ti * 5 + k), in1=acc[:],
                        op0=ALU.mult, op1=ALU.add), reads=[x_, cw, acc], writes=[acc])
                dsl = slice(pc * PW, (pc + 1) * PW)
                if ti == 2:
                    P.op("act", lambda e, dsl=dsl: e.activation(out=vT[:, dsl], in_=acc[:], func=AF.Silu),
                         reads=[acc], writes=[vT])
                    continue
                P.op("act", lambda e: e.activation(out=acc[:], in_=acc[:], func=AF.Silu), reads=[acc], writes=[acc])
                P.op("pool", lambda e: e.tensor_tensor(out=sq[:], in0=acc[:], in1=acc[:], op=ALU.mult),
                     reads=[acc], writes=[sq])
                for j in range(PW // 512):
                    ps = pss[j % 2]
                    js = slice(j * 512, (j + 1) * 512)
                    P.op("pe", lambda e, ps=ps, js=js: e.matmul(ps[:], onesb[:], sq[:, js], start=True, stop=True),
                         reads=[onesb, sq], writes=[ps])
                    P.op("act", lambda e, ps=ps: e.activation(out=rn[:], in_=ps[:], func=AF.Sqrt, bias=EPS),
                         reads=[ps], writes=[rn])
                    P.op("dve", lambda e: e.reciprocal(out=rn[:], in_=rn[:]), reads=[rn], writes=[rn])
                    scl = (128.0 ** -0.5) if ti == 0 else 1.0
                    P.op("dve", lambda e, js=js, dst=dst, pc=pc, j=j, scl=scl: e.scalar_tensor_tensor(
                        out=dst[:, pc * PW + j * 512:pc * PW + (j + 1) * 512], in0=acc[:, js], scalar=scl, in1=rn[:],
                        op0=ALU.mult, op1=ALU.mult), reads=[acc, rn], writes=[dst])
        P.op("act", lambda e: e.activation(out=tb["tmp"][:], in_=ab[:, 0, :], func=AF.Exp, bias=col(hp, 1)),
             reads=[ab, hp], writes=[tb["tmp"]])
        P.op("act", lambda e: e.activation(out=tb["tmp"][:], in_=tb["tmp"][:], func=AF.Ln, bias=1.0),
             reads=[tb["tmp"]], writes=[tb["tmp"]])
        P.op("act", lambda e: e.activation(out=col(hp, 2), in_=col(hp, 0), func=AF.Exp), reads=[hp], writes=[hp])
        P.op("dve", lambda e: e.tensor_scalar(out=col(hp, 3), in0=col(hp, 2), scalar1=-1.0, scalar2=None,
                                              op0=ALU.mult), reads=[hp], writes=[hp])
        P.op("dve", lambda e: e.tensor_scalar(out=tb["g"][:], in0=tb["tmp"][:], scalar1=col(hp, 3), scalar2=None,
                                              op0=ALU.mult), reads=[tb["tmp"], hp], writes=[tb["g"]])
        P.op("act", lambda e: e.activation(out=tb["beta"][:], in_=ab[:, 1, :], func=AF.Sigmoid),
             reads=[ab], writes=[tb["beta"]])
        P.op("dve", lambda e: e.tensor_copy(out=ghl[:, 0, :], in_=tb["g"][:]), reads=[tb["g"]], writes=[ghl])
        P.op("dve", lambda e: e.tensor_copy(out=gtmp[:], in_=ghl[:, 0, :]), reads=[ghl], writes=[gtmp])
        P.op("dve", lambda e: e.tensor_tensor(out=ghl[:, 1, :], in0=tb["g"][:], in1=gtmp[:], op=ALU.subtract),
             reads=[tb["g"], gtmp], writes=[ghl])
        for hl in range(2):
            P.op("pe", lambda e, hl=hl: e.matmul(pt64[:, 0:NDC], triub[:], ghl[:, hl, :], start=(hl == 0),
                                                 stop=(hl == 1)), reads=[triub, ghl], writes=[pt64])
        for hl in range(2):
            P.op("pe", lambda e, hl=hl: e.matmul(pt64[:, NDC:2 * NDC], onesb[:], ghl[:, hl, :], start=(hl == 0),
                                                 stop=(hl == 1)), reads=[onesb, ghl], writes=[pt64])
        P.op("dve", lambda e: e.tensor_copy(out=tb["gc"][:], in_=pt64[:, 0:NDC]), reads=[pt64], writes=[tb["gc"]])
        P.op("dve", lambda e: e.tensor_copy(out=gtmp[:], in_=pt64[:, NDC:2 * NDC]), reads=[pt64], writes=[gtmp])
        P.op("act", lambda e: e.activation(out=tb["egc"][:], in_=tb["gc"][:], func=AF.Exp),
             reads=[tb["gc"]], writes=[tb["egc"]])
        P.op("dve", lambda e: e.tensor_scalar(out=tb["negegc"][:], in0=tb["egc"][:], scalar1=-1.0, scalar2=None,
                                              op0=ALU.mult), reads=[tb["egc"]], writes=[tb["negegc"]])
        P.op("act", lambda e: e.activation(out=tb["egl"][:], in_=gtmp[:], func=AF.Exp),
             reads=[gtmp], writes=[tb["egl"]])
        P.op("dve", lambda e: e.tensor_tensor(out=tb["tmp"][:], in0=gtmp[:], in1=tb["gc"][:],
                                              op=ALU.subtract), reads=[gtmp, tb["gc"]], writes=[tb["tmp"]])
        P.op("act", lambda e: e.activation(out=tb["ekd"][:], in_=tb["tmp"][:], func=AF.Exp),
             reads=[tb["tmp"]], writes=[tb["ekd"]])
        P.op("dve", lambda e: e.memset(S[:], 0.0), writes=[S])
        P.op("dve", lambda e: e.memset(Sb[:], 0.0), writes=[Sb])
        for c in range(int(os.environ.get('DN_NCH', NDC))):
            csl = slice(c * DNC, (c + 1) * DNC)
            gcc, bec = col(tb["gc"], c), col(tb["beta"], c)
            P.op("pe", lambda e, csl=csl: e.transpose(ptr[:, 0:128], kT[:, csl], identb[:]),
                 reads=[kT, identb], writes=[ptr])
            P.op("pe", lambda e, csl=csl: e.transpose(ptr[:, 128:256], vT[:, csl], identb[:]),
                 reads=[vT, identb], writes=[ptr])
            P.op("act", lambda e: e.copy(out=KV[:], in_=ptr[:]), reads=[ptr], writes=[KV])
            if int(os.environ.get('DN_STOP', 9)) <= 1:
                continue
            P.op("dve", lambda e, gcc=gcc: e.tensor_scalar(out=dg[:], in0=ident[:], scalar1=gcc, scalar2=None,
                                                           op0=ALU.mult), reads=[ident, tb["gc"]], writes=[dg])
            P.op("dve", lambda e: e.tensor_copy(out=dgh[:], in_=dg[:]), reads=[dg], writes=[dgh])
            P.op("pool", lambda e: e.tensor_copy(out=dgt[:], in_=dgh[:]), reads=[dgh], writes=[dgt])
            P.op("pool", lambda e: e.tensor_tensor(out=dgl[:], in0=dg[:], in1=dgt[:], op=ALU.subtract),
                 reads=[dg, dgt], writes=[dgl])
            P.op("pe", lambda e: e.matmul(pg[:], onesb[:], dgh[:], start=True, stop=False),
                 reads=[onesb, dgh], writes=[pg])
            P.op("pe", lambda e: e.matmul(pg[:], onesb[:], dgl[:], start=False, stop=True),
                 reads=[onesb, dgl], writes=[pg])
            P.op("dve", lambda e, gcc=gcc: e.tensor_scalar(out=xe[:], in0=pg[:], scalar1=gcc, scalar2=0.0,
                                                           op0=ALU.subtract, op1=ALU.min),
                 reads=[pg, tb["gc"]], writes=[xe])
            P.op("pool", lambda e: e.tensor_tensor(out=xe[:], in0=xe[:], in1=maskb[:], op=ALU.add),
                 reads=[xe, maskb], writes=[xe])
            P.op("act", lambda e: e.activation(out=E[:], in_=xe[:], func=AF.Exp), reads=[xe], writes=[E])
            if int(os.environ.get('DN_STOP', 9)) <= 2:
                continue
            P.op("pool", lambda e, csl=csl: e.tensor_copy(out=kTc[:], in_=kT[:, csl]), reads=[kT], writes=[kTc])
            P.op("pe", lambda e, csl=csl: e.matmul(pkk[:, 0:128], kT[:, csl], kTc[:], start=True, stop=True),
                 reads=[kT, kTc], writes=[pkk])
            P.op("pool", lambda e, csl=csl: e.tensor_copy(out=qTc[:], in_=qT[:, csl]), reads=[qT], writes=[qTc])
            P.op("pe", lambda e, csl=csl: e.matmul(pkk[:, 128:256], kT[:, csl], qTc[:], start=True, stop=True),
                 reads=[kT, qTc], writes=[pkk])
            P.op("pool", lambda e: e.tensor_tensor(out=Es[:], in0=E[:], in1=ident[:], op=ALU.subtract),
                 reads=[E, ident], writes=[Es])
            P.op("dve", lambda e, bec=bec: e.scalar_tensor_tensor(out=x0f[:], in0=pkk[:, 0:128], scalar=bec,
                                                                 in1=Es[:], op0=ALU.mult, op1=ALU.mult),
                 reads=[pkk, tb["beta"], Es], writes=[x0f])
            P.op("dve", lambda e: e.tensor_tensor(out=AT[:], in0=pkk[:, 128:256], in1=E[:], op=ALU.mult),
                 reads=[pkk, E], writes=[AT])
            if int(os.environ.get('DN_STOP', 9)) <= 3:
                continue
            P.op("pe", lambda e: e.transpose(pss[1][:, 0:128], x0f[:], ident[:]), reads=[x0f, ident], writes=[pss[1]])
            P.op("dve", lambda e: e.tensor_copy(out=UT[:], in_=pss[1][:, 0:128]), reads=[pss[1]], writes=[UT])
            for lv in range(1, 7):
                eng = "pool" if lv % 2 else "dve"
                P.op(eng, lambda e, lv=lv: e.tensor_tensor(out=CT[lv - 1][:], in0=UT[:], in1=mskT[:, lv - 1, :],
                                                           op=ALU.mult), reads=[UT, mskT], writes=[CT[lv - 1]])
            P.op("dve", lambda e: e.tensor_tensor(out=G32[:], in0=x0f[:], in1=msk0[:], op=ALU.mult),
                 reads=[x0f, msk0], writes=[G32])
            P.op("dve", lambda e: e.tensor_tensor(out=G32[:], in0=ident[:], in1=G32[:], op=ALU.subtract),
                 reads=[ident, G32], writes=[G32])
            P.op("act", lambda e: e.copy(out=Gm[:], in_=G32[:]), reads=[G32], writes=[Gm])
            P.op("pool", lambda e: e.tensor_copy(out=Gw[:], in_=G32[:]), reads=[G32], writes=[Gw])
            for lv in range(1, int(os.environ.get('DN_LVL', 7))):
                P.op("pe", lambda e, lv=lv: e.matmul(pX[:, 0:128], CT[lv - 1][:], Gm[:], start=True, stop=True),
                     reads=[CT[lv - 1], Gm], writes=[pX])
                P.op("pe", lambda e: e.transpose(ptr[:, 0:128], Gw[:], identb[:]), reads=[Gw, identb], writes=[ptr])
                P.op("act", lambda e: e.copy(out=Ysb[:], in_=pX[:, 0:128]), reads=[pX], writes=[Ysb])
                P.op("act", lambda e: e.copy(out=GTw[:], in_=ptr[:, 0:128]), reads=[ptr], writes=[GTw])
                P.op("pe", lambda e: e.matmul(pW[:], GTw[:], Ysb[:], start=True, stop=True),
                     reads=[GTw, Ysb], writes=[pW])
                P.op("dve", lambda e: e.tensor_tensor(out=G32[:], in0=G32[:], in1=pW[:], op=ALU.subtract),
                     reads=[G32, pW], writes=[G32])
                P.op("act", lambda e: e.copy(out=Gm[:], in_=G32[:]), reads=[G32], writes=[Gm])
                P.op("pool", lambda e: e.tensor_copy(out=Gw[:], in_=G32[:]), reads=[G32], writes=[Gw])
            Winv = Gw
            if int(os.environ.get('DN_STOP', 9)) <= 4:
                continue
            P.op("pe", lambda e, csl=csl: e.matmul(pks[:, 0:128], kT[:, csl], Sb[:], start=True, stop=True),
                 reads=[kT, Sb], writes=[pks])
            P.op("pe", lambda e, csl=csl: e.matmul(pks[:, 128:256], qT[:, csl], Sb[:], start=True, stop=True),
                 reads=[qT, Sb], writes=[pks])
            P.op("dve", lambda e, c=c: e.scalar_tensor_tensor(out=Rp[:], in0=pks[:, 0:128], scalar=col(tb["negegc"], c),
                                                              in1=KV[:, 128:256], op0=ALU.mult, op1=ALU.add),
                 reads=[pks, tb["negegc"], KV], writes=[Rp])
            P.op("pe", lambda e, Winv=Winv: e.matmul(pg[:], Winv[:], Rp[:], start=True, stop=True),
                 reads=[Winv, Rp], writes=[pg])
            P.op("dve", lambda e, bec=bec: e.tensor_scalar(out=vnew[:], in0=pg[:], scalar1=bec, scalar2=None,
                                                           op0=ALU.mult), reads=[pg, tb["beta"]], writes=[vnew])
            P.op("pe", lambda e: e.matmul(pW[:], AT[:], vnew[:], start=True, stop=True),
                 reads=[AT, vnew], writes=[pW])
            P.op("dve", lambda e, c=c: e.tensor_scalar(out=oq[:], in0=pks[:, 128:256], scalar1=col(tb["egc"], c),
                                                       scalar2=None, op0=ALU.mult), reads=[pks, tb["egc"]], writes=[oq])
            o = ot[c % 2]
            P.op("dve", lambda e, o=o: e.tensor_tensor(out=o[:], in0=pW[:], in1=oq[:], op=ALU.add),
                 reads=[pW, oq], writes=[o])
            P.store("sp", o, out_d[u, csl, :], o[:])
            P.op("pool", lambda e, c=c: e.tensor_scalar(out=Kd[:], in0=KV[:, 0:128], scalar1=col(tb["ekd"], c),
                                                        scalar2=None, op0=ALU.mult), reads=[KV, tb["ekd"]], writes=[Kd])
            P.op("pe", lambda e: e.matmul(pX[:, 0:128], Kd[:], vnew[:], start=True, stop=True),
                 reads=[Kd, vnew], writes=[pX])
            P.op("dve", lambda e, c=c: e.scalar_tensor_tensor(out=S[:], in0=S[:], scalar=col(tb["egl"], c),
                                                              in1=pX[:, 0:128], op0=ALU.mult, op1=ALU.add),
                 reads=[S, tb["egl"], pX], writes=[S])
            P.op("act", lambda e: e.copy(out=Sb[:], in_=S[:]), reads=[S], writes=[Sb])
    return P


DN_UNITS = [(h, d) for h in range(6) for d in range(2)]


def run_DN(hA, conv_w, a_log, dt_bias):
    P = build_DN()
    qkv = hA[:, 768:3072]
    da = hA[:, 4608:4620]
    db = hA[:, 4620:4632]
    ii = np.arange(128)
    triu = (ii[:, None] <= ii[None, :]).astype(np.float32)
    maskb = np.where(ii[None, :] >= ii[:, None], 0.0, -30000.0).astype(np.float32)
    msk0 = np.zeros((128, 128), np.float32)
    mskT = np.zeros((128, 6, 128), np.float32)
    for lv in range(7):
        b = 1 << lv
        jj, i2 = np.meshgrid(ii, ii, indexing="ij")
        m = ((jj // (2 * b) == i2 // (2 * b)) & (jj % (2 * b) < b) & (i2 % (2 * b) >= b)).astype(np.float32)
        if lv == 0:
            msk0 = m
        else:
            mskT[:, lv - 1, :] = m.T
    units = DN_UNITS + DN_UNITS[:4]
    maps = []
    for c in range(NCORES):
        xin = np.zeros((2, 3, 128, SEQ + 4), np.float32)
        cw = np.zeros((2, 128, 15), np.float32)
        ab = np.zeros((2, 128, 2, NDC), np.float32)
        hp = np.zeros((2, 128, 2), np.float32)
        for s in range(2):
            h, d = units[c * 2 + s]
            for ti in range(3):
                cs = slice(ti * 768 + h * 128, ti * 768 + (h + 1) * 128)
                xt = qkv[:, cs].T
                w = conv_w[cs]
                if d == 1:
                    xt = xt[:, ::-1]
                    w = w[:, ::-1]
                xin[s, ti, :, 2:2 + SEQ] = xt
                cw[s, :, ti * 5:(ti + 1) * 5] = w
            av = da[:, d * 6 + h]
            bv = db[:, d * 6 + h]
            if d == 1:
                av = av[::-1]
                bv = bv[::-1]
            ab[s, :, 0, :] = av.reshape(NDC, 128).T
            ab[s, :, 1, :] = bv.reshape(NDC, 128).T
            hp[s, :, 0] = a_log[d, h]
            hp[s, :, 1] = dt_bias[d, h]
        maps.append(dict(xin=xin, cw=cw, ab=ab, hp=hp, ident=np.eye(128, dtype=np.float32), triu=triu, maskb=maskb,
                         msk0=msk0, mskT=mskT))
    res = run(P, maps)
    of = np.zeros((SEQ, 768), np.float32)
    ob = np.zeros((SEQ, 768), np.float32)
    for idx, (h, d) in enumerate(DN_UNITS):
        o = res[idx // 2]["o"][idx % 2]
        if d == 0:
            of[:, h * 128:(h + 1) * 128] = o
        else:
            ob[:, h * 128:(h + 1) * 128] = o[::-1]
    return of, ob


COLS_Z = np.concatenate([np.arange(768, 1536), np.arange(3864, 4632), np.arange(6168, 7192),
                         np.arange(7704, 8216), np.arange(7192, 7704)])
NZ = 3584


def build_C1():
    P = Prog()
    x_d = P.dram_in("x", [TPC, D])
    gb_d = P.dram_in("gb", [128, D])
    id_d = P.dram_in("ident", [128, 128])
    wz_d = P.dram_in("wz", [D, NZ])
    mem_d = P.dram_in("mem", [256, D])
    mgb_d = P.dram_in("mgb", [128, D])
    wkv_d = P.dram_in("wkv", [D, 1024])
    s5_d = P.dram_in("s5", [3, 768, TPC])
    sd_d = P.dram_in("sd", [128, 16])
    wglu_d = P.dram_in("wglu", [768, 768])
    dn_d = P.dram_in("dn", [2, 768, TPC])
    oc_d = P.dram_in("oc", [1024, TPC])
    y_d = P.dram_out("yT", [24, 128, TPC], BF16)

    ident = P.sbuf("ident", [128, 128])
    gb = P.sbuf("gb", [128, D])
    sd = P.sbuf("sd", [128, 16])
    onesb = P.sbuf("onesb", [128, 128], BF16)
    P.load("sp", ident, ident[:], id_d)
    P.load("sp", gb, gb[:], gb_d)
    P.load("sp", sd, sd[:], sd_d)
    P.op("dve", lambda e: e.memset(onesb[:], 1.0), writes=[onesb])
    xnT = P.sbuf("xnT", [128, 16, TPC], BF16)
    nb = emit_norm_T(P, x_d, gb, ident, xnT, TPC // 128, "nC", single=True)
    memnT = P.sbuf("memnT", [128, 16, 256], BF16)
    P.load("sp", gb, gb[:], mgb_d)
    emit_norm_T(P, mem_d, gb, ident, memnT, 2, "nC", bufs=nb)

    pp = [P.psum(f"pp{i}", [128, 512]) for i in range(2)]
    pa = [P.psum(f"pa{i}", [128, 512]) for i in range(2)]
    po = P.psum("po", [128, 512])
    pd = P.psum("pd", [128, 512])

    wk = P.sbuf("wk", [128, 16, 512], BF16)
    KmT = P.sbuf("KmT", [128, 4, 256], BF16)
    Vm = P.sbuf("Vm", [128, 2, 512], BF16)
    wkv_v = wkv_d.rearrange("(c p) n -> p c n", p=128)
    P.dma("pool", [(wk[:, k, :], wkv_v[:, k, 0:512]) for k in range(16)], wk, writes=[wk])
    for h in range(4):
        ps = pp[h % 2]
        for k in range(16):
            P.op("pe", lambda e, ps=ps, k=k, h=h: e.matmul(ps[:, 0:256], wk[:, k, h * 128:(h + 1) * 128],
                                                          memnT[:, k, :], start=(k == 0), stop=(k == 15)),
                 reads=[wk, memnT], writes=[ps])
        P.op("act", lambda e, ps=ps, h=h: e.copy(out=KmT[:, h, :], in_=ps[:, 0:256]), reads=[ps], writes=[KmT])
    P.dma("pool", [(wk[:, k, :], wkv_v[:, k, 512:1024]) for k in range(16)], wk, writes=[wk])
    for mt in range(2):
        ps = pp[mt % 2]
        for k in range(16):
            P.op("pe", lambda e, ps=ps, k=k, mt=mt: e.matmul(ps[:], memnT[:, k, mt * 128:(mt + 1) * 128],
                                                            wk[:, k, :], start=(k == 0), stop=(k == 15)),
                 reads=[wk, memnT], writes=[ps])
        P.op("act", lambda e, ps=ps, mt=mt: e.copy(out=Vm[:, mt, :], in_=ps[:]), reads=[ps], writes=[Vm])

    wj = [P.sbuf(f"wj{i}", [128, 16, 128], BF16) for i in range(2)]
    sz = [P.sbuf(f"sz{i}", [128, TPC], BF16) for i in range(2)]
    wz_v = wz_d.rearrange("(c p) n -> p c n", p=128)
    cnt = {"w": 0, "p": 0}

    def projT(col0, dst_ap_fn, func, dst_buf):
        w = wj[cnt["w"] % 2]
        cnt["w"] += 1
        P.dma("pool", [(w[:, k, :], wz_v[:, k, col0:col0 + 128]) for k in range(16)], w, writes=[w])
        for half in range(2):
            ps = pp[cnt["p"] % 2]
            cnt["p"] += 1
            hs = slice(half * 512, (half + 1) * 512)
            for k in range(16):
                P.op("pe", lambda e, ps=ps, k=k, w=w, hs=hs: e.matmul(ps[:], w[:, k, :], xnT[:, k, hs],
                                                                     start=(k == 0), stop=(k == 15)),
                     reads=[w, xnT], writes=[ps])
            P.op("act", lambda e, ps=ps, hs=hs: e.activation(out=dst_ap_fn(hs), in_=ps[:], func=func),
                 reads=[ps], writes=[dst_buf])

    def proj_silu(col0):
        z = sz[cnt["w"] % 2]
        projT(col0, lambda hs, z=z: z[:, hs], AF.Silu, z)
        return z

    f1 = [P.sbuf(f"f1_{i}", [128, TPC]) for i in range(2)]
    f2 = [P.sbuf(f"f2_{i}", [128, TPC]) for i in range(2)]
    f3 = P.sbuf("f3", [128, TPC])
    f4 = P.sbuf("f4", [128, TPC])
    yo = [P.sbuf(f"yo{i}", [128, TPC], BF16) for i in range(2)]
    sqb = P.sbuf("sqb", [128, TPC], BF16)
    rn = P.sbuf("rn", [128, 512])
    sg = P.sbuf("sg", [128, 512])
    ycnt = {"n": 0}

    def next_yo():
        y = yo[ycnt["n"] % 2]
        ycnt["n"] += 1
        return y

    GY = P.sbuf("GY", [128, 6, TPC], BF16)
    wglu = P.sbuf("wglu", [128, 6, 768], BF16)
    wglu_v = wglu_d.rearrange("(c p) n -> p c n", p=128)
    P.dma("pool", [(wglu[:, k, :], wglu_v[:, k, :]) for k in range(6)], wglu, writes=[wglu])
    for j in range(6):
        a, b_ = f1[j % 2], f2[j % 2]
        rs = slice(j * 128, (j + 1) * 128)
        P.load("sp", a, a[:], s5_d[0, rs, :])
        P.load("sp", b_, b_[:], s5_d[1, rs, :])
        P.load("sp", f3, f3[:], s5_d[2, rs, :])
        P.op("pool", lambda e, a=a, b_=b_: e.tensor_tensor(out=a[:], in0=a[:], in1=b_[:], op=ALU.add),
             reads=[a, b_], writes=[a])
        P.op("dve", lambda e, a=a, j=j: e.scalar_tensor_tensor(out=a[:], in0=f3[:], scalar=sd[:, j:j + 1], in1=a[:],
                                                              op0=ALU.mult, op1=ALU.add), reads=[f3, sd, a], writes=[a])
        P.op("pool", lambda e, a=a, b_=b_: e.tensor_tensor(out=b_[:], in0=a[:], in1=a[:], op=ALU.mult),
             reads=[a], writes=[b_])
        P.op("dve", lambda e, b_=b_: e.tensor_scalar(out=b_[:], in0=b_[:], scalar1=0.044715, scalar2=1.0,
                                                     op0=ALU.mult, op1=ALU.add), reads=[b_], writes=[b_])
        P.op("pool", lambda e, a=a, b_=b_: e.tensor_tensor(out=b_[:], in0=b_[:], in1=a[:], op=ALU.mult),
             reads=[a, b_], writes=[b_])
        P.op("act", lambda e, b_=b_: e.activation(out=f4[:], in_=b_[:], func=AF.Sigmoid, scale=1.5957691216057308),
             reads=[b_], writes=[f4])
        P.op("dve", lambda e, a=a, j=j: e.tensor_tensor(out=GY[:, j, :], in0=a[:], in1=f4[:], op=ALU.mult),
             reads=[a, f4], writes=[GY])
    for j in range(6):
        z = proj_silu(0 + j * 128)
        y = next_yo()
        for half in range(2):
            hs = slice(half * 512, (half + 1) * 512)
            ps = pa[half]
            for k in range(6):
                P.op("pe", lambda e, ps=ps, k=k, j=j, hs=hs: e.matmul(ps[:], wglu[:, k, j * 128:(j + 1) * 128],
                                                                     GY[:, k, hs], start=(k == 0), stop=(k == 5)),
                     reads=[wglu, GY], writes=[ps])
            P.op("act", lambda e, ps=ps, j=j: e.activation(out=sg[:], in_=ps[:], func=AF.Sigmoid,
                                                           bias=sd[:, 6 + j:7 + j]), reads=[ps, sd], writes=[sg])
            P.op("dve", lambda e, j=j, hs=hs: e.tensor_tensor(out=sg[:], in0=sg[:], in1=GY[:, j, hs], op=ALU.mult),
                 reads=[sg, GY], writes=[sg])
            P.op("dve", lambda e, y=y, z=z, hs=hs: e.tensor_tensor(out=y[:, hs], in0=sg[:], in1=z[:, hs], op=ALU.mult),
                 reads=[sg, z], writes=[y])
        P.store("sp", y, y_d[j], y[:])
    for h in range(6):
        a, b_ = f1[h % 2], f2[h % 2]
        rs = slice(h * 128, (h + 1) * 128)
        P.load("sp", a, a[:], dn_d[0, rs, :])
        P.load("sp", b_, b_[:], dn_d[1, rs, :])
        P.op("pool", lambda e, a=a, b_=b_: e.tensor_tensor(out=a[:], in0=a[:], in1=b_[:], op=ALU.add),
             reads=[a, b_], writes=[a])
        P.op("pool", lambda e, a=a: e.tensor_tensor(out=sqb[:], in0=a[:], in1=a[:], op=ALU.mult),
             reads=[a], writes=[sqb])
        z = proj_silu(768 + h * 128)
        y = next_yo()
        for half in range(2):
            hs = slice(half * 512, (half + 1) * 512)
            ps = pa[half]
            P.op("pe", lambda e, ps=ps, hs=hs: e.matmul(ps[:], onesb[:], sqb[:, hs], start=True, stop=True),
                 reads=[onesb, sqb], writes=[ps])
            P.op("act", lambda e, ps=ps: e.activation(out=rn[:], in_=ps[:], func=AF.Sqrt, scale=1.0 / 128, bias=EPS),
                 reads=[ps], writes=[rn])
            P.op("dve", lambda e: e.reciprocal(out=rn[:], in_=rn[:]), reads=[rn], writes=[rn])
            P.op("dve", lambda e, a=a, hs=hs: e.scalar_tensor_tensor(out=rn[:], in0=a[:, hs], scalar=sd[:, 12:13],
                                                                    in1=rn[:], op0=ALU.mult, op1=ALU.mult),
                 reads=[a, sd, rn], writes=[rn])
            P.op("dve", lambda e, y=y, z=z, hs=hs: e.tensor_tensor(out=y[:, hs], in0=rn[:], in1=z[:, hs], op=ALU.mult),
                 reads=[rn, z], writes=[y])
        P.store("sp", y, y_d[6 + h], y[:])
    for c in range(8):
        a = f1[c % 2]
        P.load("sp", a, a[:], oc_d[c * 128:(c + 1) * 128, :])
        z = proj_silu(1536 + c * 128)
        y = next_yo()
        P.op("dve", lambda e, a=a, y=y, z=z: e.tensor_tensor(out=y[:], in0=a[:], in1=z[:], op=ALU.mult),
             reads=[a, z], writes=[y])
        P.store("sp", y, y_d[12 + c], y[:])
    QmT = P.sbuf("QmT", [128, TPC], BF16)
    pm = [P.sbuf(f"pm{i}", [128, 512], BF16) for i in range(2)]
    pc = 0
    for h in range(4):
        projT(3072 + h * 128, lambda hs: QmT[:, hs], AF.Copy, QmT)
        z = proj_silu(2560 + h * 128)
        y = next_yo()
        for half in range(2):
            hs = slice(half * 512, (half + 1) * 512)
            for mt in range(2):
                ps = pa[mt]
                p_ = pm[pc % 2]
                pc += 1
                P.op("pe", lambda e, ps=ps, h=h, mt=mt, hs=hs: e.matmul(ps[:], KmT[:, h, mt * 128:(mt + 1) * 128],
                                                                       QmT[:, hs], start=True, stop=True),
                     reads=[KmT, QmT], writes=[ps])
                P.op("act", lambda e, ps=ps, p_=p_: e.activation(out=p_[:], in_=ps[:], func=AF.Exp, scale=128.0 ** -0.5),
                     reads=[ps], writes=[p_])
                P.op("pe", lambda e, p_=p_, h=h, mt=mt: e.matmul(po[:], Vm[:, mt, h * 128:(h + 1) * 128], p_[:],
                                                                start=(mt == 0), stop=(mt == 1)),
                     reads=[Vm, p_], writes=[po])
                P.op("pe", lambda e, p_=p_, mt=mt: e.matmul(pd[:], onesb[:], p_[:], start=(mt == 0), stop=(mt == 1)),
                     reads=[onesb, p_], writes=[pd])
            P.op("dve", lambda e: e.reciprocal(out=rn[:], in_=pd[:]), reads=[pd], writes=[rn])
            P.op("dve", lambda e: e.tensor_tensor(out=rn[:], in0=po[:], in1=rn[:], op=ALU.mult),
                 reads=[po, rn], writes=[rn])
            P.op("dve", lambda e, y=y, z=z, hs=hs: e.tensor_tensor(out=y[:, hs], in0=rn[:], in1=z[:, hs], op=ALU.mult),
                 reads=[rn, z], writes=[y])
        P.store("sp", y, y_d[20 + h], y[:])
    return P


def run_C1(x, norm_g, w_in_l, mem, mem_g, w_kv, hA, yf, yb, ssm_d, w_glu, b_glu, of, ob, dn_g, yc):
    P = build_C1()
    sd = np.zeros((128, 16), np.float32)
    sd[:, 0:6] = np.asarray(ssm_d, np.float32).reshape(6, 128).T
    sd[:, 6:12] = np.asarray(b_glu, np.float32).reshape(6, 128).T
    sd[:, 12] = np.asarray(dn_g, np.float32)
    common = dict(gb=bcast_rows(norm_g), ident=np.eye(128, dtype=np.float32),
                  wz=np.ascontiguousarray(w_in_l[:, COLS_Z]), mem=np.ascontiguousarray(mem),
                  mgb=bcast_rows(mem_g), wkv=np.ascontiguousarray(w_kv), sd=sd,
                  wglu=np.ascontiguousarray(w_glu))
    maps = []
    for c in range(NCORES):
        ts = slice(c * TPC, (c + 1) * TPC)
        s5 = np.stack([yf[ts].T, yb[ts].T, hA[ts, 0:768].T]).astype(np.float32)
        dn = np.stack([of[ts].T, ob[ts].T]).astype(np.float32)
        maps.append(dict(common, x=np.ascontiguousarray(x[ts]), s5=np.ascontiguousarray(s5),
                         dn=np.ascontiguousarray(dn), oc=np.ascontiguousarray(yc[ts].T)))
    res = run(P, maps)
    return [r["yT"] for r in res]


BR_CHUNKS = [(0, 6), (6, 12), (12, 20), (20, 24)]


def build_C2():
    P = Prog()
    x_d = P.dram_in("x", [TPC, D])
    gb_d = P.dram_in("gb", [128, D])
    id_d = P.dram_in("ident", [128, 128])
    wg_d = P.dram_in("wg", [D, 4 * D])
    y_d = P.dram_in("yT", [24, 128, TPC], BF16)
    wbr_d = P.dram_in("wbr", [3072, D])
    wout_d = P.dram_in("wout", [D, D])
    out_d = P.dram_out("xnew", [TPC, D])

    ident = P.sbuf("ident", [128, 128])
    gb = P.sbuf("gb", [128, D])
    P.load("sp", ident, ident[:], id_d)
    P.load("sp", gb, gb[:], gb_d)
    xnT = P.sbuf("xnT", [128, 16, TPC], BF16)
    emit_norm_T(P, x_d, gb, ident, xnT, TPC // 128, "nD", single=True)
    Y = P.sbuf("Y", [128, 24, TPC], BF16)
    for k in range(24):
        P.dma("sp", [(Y[:, k, :], y_d[k])], Y, writes=[Y])
    mT = P.sbuf("mT", [128, 16, TPC], BF16)
    wbr = P.sbuf("wbr", [128, 24, 128], BF16)
    wg = [P.sbuf(f"wg{i}", [128, 16, 128], BF16) for i in range(2)]
    pb = [P.psum(f"pb{i}", [128, 512]) for i in range(2)]
    pg = [P.psum(f"pg{i}", [128, 512]) for i in range(2)]
    sg = [P.sbuf(f"sg{i}", [128, 512]) for i in range(2)]
    acc = P.sbuf("acc", [128, TPC])
    tmp = P.sbuf("tmpm", [128, 512])
    wbr_v = wbr_d.rearrange("(c p) n -> p c n", p=128)
    wg_v = wg_d.rearrange("(c p) n -> p c n", p=128)
    cnt = 0
    for j in range(16):
        js = slice(j * 128, (j + 1) * 128)
        P.dma("pool", [(wbr[:, k, :], wbr_v[:, k, js]) for k in range(24)], wbr, writes=[wbr])
        for b in range(4):
            w = wg[cnt % 2]
            g0 = b * D + j * 128
            P.dma("pool", [(w[:, k, :], wg_v[:, k, g0:g0 + 128]) for k in range(16)], w, writes=[w])
            k0, k1 = BR_CHUNKS[b]
            for half in range(2):
                hs = slice(half * 512, (half + 1) * 512)
                p1, p2, s_ = pb[cnt % 2], pg[cnt % 2], sg[cnt % 2]
                cnt += 1
                for k in range(16):
                    P.op("pe", lambda e, p2=p2, k=k, w=w, hs=hs: e.matmul(p2[:], w[:, k, :], xnT[:, k, hs],
                                                                         start=(k == 0), stop=(k == 15)),
                         reads=[w, xnT], writes=[p2])
                for k in range(k0, k1):
                    P.op("pe", lambda e, p1=p1, k=k, hs=hs, k0=k0, k1=k1: e.matmul(p1[:], wbr[:, k, :], Y[:, k, hs],
                                                                                  start=(k == k0), stop=(k == k1 - 1)),
                         reads=[wbr, Y], writes=[p1])
                P.op("act", lambda e, p2=p2, s_=s_: e.activation(out=s_[:], in_=p2[:], func=AF.Sigmoid),
                     reads=[p2], writes=[s_])
                if b == 0:
                    P.op("dve", lambda e, p1=p1, s_=s_, hs=hs: e.tensor_tensor(out=acc[:, hs], in0=p1[:], in1=s_[:],
                                                                              op=ALU.mult), reads=[p1, s_], writes=[acc])
                else:
                    P.op("dve", lambda e, p1=p1, s_=s_: e.tensor_tensor(out=tmp[:], in0=p1[:], in1=s_[:], op=ALU.mult),
                         reads=[p1, s_], writes=[tmp])
                    P.op("pool", lambda e, hs=hs: e.tensor_tensor(out=acc[:, hs], in0=acc[:, hs], in1=tmp[:],
                                                                  op=ALU.add), reads=[acc, tmp], writes=[acc])
        P.op("act", lambda e, j=j: e.copy(out=mT[:, j, :], in_=acc[:]), reads=[acc], writes=[mT])
    wo_view = Y[:, 0:8, :].rearrange("p a (b c) -> p (a b) c", c=512)
    wout_v = wout_d.rearrange("(c p) n -> p c n", p=128)
    xr = [P.sbuf(f"xr{i}", [128, 512]) for i in range(2)]
    ot = [P.sbuf(f"oo{i}", [128, 512]) for i in range(2)]
    cnt = 0
    for cb in range(4):
        cs = slice(cb * 512, (cb + 1) * 512)
        P.dma("pool", [(wo_view[:, k, :], wout_v[:, k, cs]) for k in range(16)], Y, writes=[Y])
        for i in range(TPC // 128):
            ps = pb[cnt % 2]
            r_ = xr[cnt % 2]
            o_ = ot[cnt % 2]
            cnt += 1
            ts = slice(i * 128, (i + 1) * 128)
            P.load("sp", r_, r_[:], x_d[ts, cs])
            for k in range(16):
                P.op("pe", lambda e, ps=ps, k=k, ts=ts: e.matmul(ps[:], mT[:, k, ts], wo_view[:, k, :],
                                                                start=(k == 0), stop=(k == 15)),
                     reads=[mT, Y], writes=[ps])
            P.op("dve", lambda e, ps=ps, r_=r_, o_=o_: e.tensor_tensor(out=o_[:], in0=ps[:], in1=r_[:], op=ALU.add),
                 reads=[ps, r_], writes=[o_])
            P.store("sp", o_, out_d[ts, cs], o_[:])
    return P


def run_C2(x, norm_g, w_in_l, yTs, w_br, w_o):
    P = build_C2()
    common = dict(gb=bcast_rows(norm_g), ident=np.eye(128, dtype=np.float32),
                  wg=np.ascontiguousarray(w_in_l[:, 8216:]), wbr=np.ascontiguousarray(w_br),
                  wout=np.ascontiguousarray(w_o))
    maps = [dict(common, x=np.ascontiguousarray(x[c * TPC:(c + 1) * TPC]), yT=yTs[c]) for c in range(NCORES)]
    res = run(P, maps)
    return np.concatenate([r["xnew"] for r in res], axis=0)


def build_F():
    P = Prog()
    x_d = P.dram_in("x", [TPC, D])
    gb_d = P.dram_in("gb", [128, D])
    out_d = P.dram_out("y", [TPC, D])
    gb = P.sbuf("gb", [128, D])
    P.load("sp", gb, gb[:], gb_d)
    xt = [P.sbuf(f"xt{i}", [128, D]) for i in range(2)]
    yt = [P.sbuf(f"yt{i}", [128, D]) for i in range(2)]
    junk = P.sbuf("junk", [128, D], BF16)
    ss = [P.sbuf(f"ss{i}", [128, 16]) for i in range(2)]
    for i in range(TPC // 128):
        b = i % 2
        ts = slice(i * 128, (i + 1) * 128)
        P.load("sp", xt[b], xt[b][:], x_d[ts, :])
        P.op("act", lambda e, b=b: e.activation(out=junk[:], in_=xt[b][:], func=AF.Square, accum_out=ss[b][:, 0:1]),
             reads=[xt[b]], writes=[junk, ss[b]])
        P.op("act", lambda e, b=b: e.activation(out=ss[b][:, 1:2], in_=ss[b][:, 0:1], func=AF.Sqrt, scale=1.0 / D,
                                                bias=EPS), reads=[ss[b]], writes=[ss[b]])
        P.op("dve", lambda e, b=b: e.reciprocal(out=ss[b][:, 1:2], in_=ss[b][:, 1:2]), reads=[ss[b]], writes=[ss[b]])
        P.op("dve", lambda e, b=b: e.scalar_tensor_tensor(out=yt[b][:], in0=xt[b][:], scalar=ss[b][:, 1:2], in1=gb[:],
                                                          op0=ALU.mult, op1=ALU.mult),
             reads=[xt[b], ss[b], gb], writes=[yt[b]])
        P.store("sp", yt[b], out_d[ts, :], yt[b][:])
    return P


def run_F(x, g):
    P = build_F()
    maps = [dict(x=np.ascontiguousarray(x[c * TPC:(c + 1) * TPC]), gb=bcast_rows(g)) for c in range(NCORES)]
    res = run(P, maps)
    return np.concatenate([r["y"] for r in res], axis=0)


def layer_forward(xs, L, inp):
    g = lambda k: np.asarray(inp[k][L], np.float32)
    w_in_l = g("w_in")
    hA = run_A(xs, g("norm_g"), w_in_l, g("attn_q_norm"), g("attn_k_norm"))
    yc = run_ATT(hA)
    yf, yb = run_S5(hA, g("ssm_a_re"), g("ssm_a_im"), g("ssm_log_step"), g("ssm_b_re"), g("ssm_b_im"),
                    g("ssm_c_re"), g("ssm_c_im"))
    of, ob = run_DN(hA, g("dn_conv"), g("dn_a_log"), g("dn_dt_bias"))
    yTs = run_C1(xs, g("norm_g"), w_in_l, np.asarray(inp["mem"], np.float32)[0], g("mem_norm_g"), g("w_mem_kv"),
                 hA, yf, yb, g("ssm_d"), g("ssm_w_glu"), g("ssm_b_glu"), of, ob, g("dn_norm_g"), yc)
    return run_C2(xs, g("norm_g"), w_in_l, yTs, g("w_branch"), g("w_out"))


def kernel(**inp):
    xs = np.asarray(inp["x"], np.float32)[0]
    for L in range(2):
        xs = layer_forward(xs, L, inp)
    out = run_F(xs, np.asarray(inp["final_norm_g"], np.float32))
    return out[None].astype(np.float32)
```

```python
from contextlib import ExitStack
import os
import numpy as np
import concourse.bass as bass
import concourse.mybir as mybir
from concourse.bass_utils import run_bass_kernel_spmd

F32 = mybir.dt.float32
BF16 = mybir.dt.bfloat16
I32 = mybir.dt.int32
AF = mybir.ActivationFunctionType
ALU = mybir.AluOpType
AX = mybir.AxisListType

NCORES = 8
D = 2048
SEQ = 8192
TPC = SEQ // NCORES
EPS = 1e-6
TWO_PI = float(2 * np.pi)


class Buf:
    def __init__(self, name, t=None):
        self.name = name
        self.t = t
        self.w = None
        self.r = []
        self.dsem = None
        self.dcnt = 0

    def __getitem__(self, k):
        return self.t[k]


class Prog:
    ENG = ("sp", "act", "dve", "pool", "pe")

    def __init__(self):
        self.nc = bass.Bass("TRN2", target_bir_lowering=False)
        self.ctx = ExitStack()
        self.streams = {e: [] for e in self.ENG}
        self.seq = {e: 0 for e in self.ENG}
        self.esem = {e: self.ctx.enter_context(self.nc.semaphore("es_" + e)) for e in self.ENG}
        self.store_bufs = []
        self.nid = 0

    def dram_in(self, name, shape, dt=F32):
        return self.nc.dram_tensor(name, list(shape), dt, kind="ExternalInput").ap()

    def dram_out(self, name, shape, dt=F32):
        return self.nc.dram_tensor(name, list(shape), dt, kind="ExternalOutput").ap()

    def sbuf(self, name, shape, dt=F32):
        t = self.ctx.enter_context(self.nc.sbuf_tensor("sb_" + name, list(shape), dt))
        esz = 2 if dt == BF16 else 4
        nbytes = int(np.prod(shape[1:])) * esz
        rem = (-nbytes) % 64
        if rem > 32:
            self.ctx.enter_context(self.nc.sbuf_tensor("pad_" + name, [shape[0], 8], F32))
        elif 0 < rem <= 32 and ((nbytes + 31) // 32 * 32) % 64 != 0:
            self.ctx.enter_context(self.nc.sbuf_tensor("pad_" + name, [shape[0], 8], F32))
        return Buf(name, t)

    def psum(self, name, shape, dt=F32):
        t = self.ctx.enter_context(self.nc.psum_tensor("ps_" + name, list(shape), dt))
        return Buf(name, t)

    def _deps(self, reads, writes):
        toks = []
        for b in reads:
            if b.w is not None:
                toks.append((b.w[0], b.w[1], "raw:" + str(b.w[2])))
        for b in writes:
            if b.w is not None:
                toks.append(b.w)
            toks.extend(b.r)
        return toks

    def op(self, eng, fn, reads=(), writes=()):
        toks = self._deps(reads, writes)
        self.seq[eng] += 1
        tok = (self.esem[eng], self.seq[eng], eng)
        self.streams[eng].append((toks, fn, (self.esem[eng], 1)))
        for b in reads:
            b.r.append(tok)
        for b in writes:
            b.w = tok
            b.r = []

    def dma(self, eng, pairs, owner, reads=(), writes=()):
        if owner.dsem is None:
            self.nid += 1
            owner.dsem = self.ctx.enter_context(self.nc.semaphore("ds%d" % self.nid))
        toks = self._deps(reads, writes)
        owner.dcnt += len(pairs)
        tok = (owner.dsem, 16 * owner.dcnt, "dma")

        def fn(e, pairs=pairs):
            return [e.dma_start(out=o, in_=i) for (o, i) in pairs]

        self.streams[eng].append((toks, fn, (owner.dsem, 16)))
        for b in reads:
            b.r.append(tok)
        for b in writes:
            b.w = tok
            b.r = []

    def load(self, eng, buf, out_ap, in_ap):
        self.dma(eng, [(out_ap, in_ap)], buf, writes=[buf])

    def store(self, eng, buf, out_ap, in_ap):
        if buf not in self.store_bufs:
            self.store_bufs.append(buf)
        self.dma(eng, [(out_ap, in_ap)], buf, reads=[buf])

    def finish(self):
        final = [(b.dsem, 16 * b.dcnt, "dma") for b in self.store_bufs]
        self.streams["sp"].append((final, None, None))
        streams = self.streams

        def emit(name, e):
            waited = {}
            for toks, fn, inc in streams[name]:
                for (sem, val, teng) in toks:
                    if teng == name or (teng == "raw:" + name and name == "pe"):
                        continue
                    k = id(sem)
                    if waited.get(k, 0) >= val:
                        continue
                    e.wait_ge(sem, val)
                    waited[k] = val
                if fn is None:
                    continue
                ins = fn(e)
                if isinstance(ins, list):
                    for i_ in ins:
                        i_.then_inc(inc[0], inc[1])
                else:
                    ins.then_inc(inc[0], inc[1])

        with self.nc.Block() as block:
            @block.sync
            def _(e):
                emit("sp", e)

            @block.scalar
            def _(e):
                emit("act", e)

            @block.vector
            def _(e):
                emit("dve", e)

            @block.gpsimd
            def _(e):
                emit("pool", e)

            @block.tensor
            def _(e):
                emit("pe", e)
        self.ctx.close()
        return self.nc


def run(prog, in_maps):
    nc = prog.finish()
    n = int(os.environ.get("DBG_CORES", NCORES))
    if os.environ.get("DBG_TRACE"):
        res = run_bass_kernel_spmd(nc, in_maps[:n], core_ids=list(range(n)), trace=True)
        print("DBG_TRACE exec_time_ns", res.exec_time_ns, flush=True)
    else:
        res = run_bass_kernel_spmd(nc, in_maps[:n], core_ids=list(range(n)))
    out = list(res.results)
    while len(out) < NCORES:
        out.append(out[0])
    return out


def emit_norm_T(P, x_dram, gb, ident, xnT, ntiles, tag, single=False, bufs=None):
    if bufs is None:
        nb = 1 if single else 2
        bufs = dict(xt=[P.sbuf(f"{tag}_xt{i}", [128, D]) for i in range(nb)],
                    junk=P.sbuf(f"{tag}_junk", [128, D], BF16),
                    xn=[P.sbuf(f"{tag}_xn{i}", [128, D]) for i in range(nb)],
                    ss=[P.sbuf(f"{tag}_ss{i}", [128, 16]) for i in range(nb)],
                    tp=[P.psum(f"{tag}_tp{i}", [128, 512]) for i in range(2)])
    xt, junk, xn, ss, tp = bufs["xt"], bufs["junk"], bufs["xn"], bufs["ss"], bufs["tp"]
    nb = len(xt)
    tcount = 0
    for i in range(ntiles):
        b = i % nb
        P.load("sp", xt[b], xt[b][:], x_dram[i * 128:(i + 1) * 128, :])
        P.op("dve", lambda e, b=b: e.memset(ss[b][:], 0.0), writes=[ss[b]])
        P.op("act", lambda e, b=b: e.activation(out=junk[:], in_=xt[b][:], func=AF.Square,
                                                accum_out=ss[b][:, 0:1]),
             reads=[xt[b]], writes=[junk, ss[b]])
        P.op("act", lambda e, b=b: e.activation(out=ss[b][:, 1:2], in_=ss[b][:, 0:1], func=AF.Sqrt,
                                                scale=1.0 / D, bias=EPS),
             reads=[ss[b]], writes=[ss[b]])
        P.op("dve", lambda e, b=b: e.reciprocal(out=ss[b][:, 1:2], in_=ss[b][:, 1:2]),
             reads=[ss[b]], writes=[ss[b]])
        P.op("dve", lambda e, b=b: e.scalar_tensor_tensor(out=xn[b][:], in0=xt[b][:], scalar=ss[b][:, 1:2],
                                                          in1=gb[:], op0=ALU.mult, op1=ALU.mult),
             reads=[xt[b], ss[b], gb], writes=[xn[b]])
        for kk in range(4):
            pb = tp[tcount % 2]
            tcount += 1
            for j in range(4):
                k = kk * 4 + j
                P.op("pe", lambda e, b=b, k=k, j=j, pb=pb: e.transpose(pb[:, j * 128:(j + 1) * 128],
                                                                      xn[b][:, k * 128:(k + 1) * 128], ident[:]),
                     reads=[xn[b], ident], writes=[pb])
            eng = "act" if kk % 2 == 0 else "dve"
            if eng == "act":
                P.op("act", lambda e, kk=kk, i=i, pb=pb: e.copy(
                    out=xnT[:, kk * 4:(kk + 1) * 4, i * 128:(i + 1) * 128],
                    in_=pb[:].rearrange("p (a b) -> p a b", a=4)), reads=[pb], writes=[xnT])
            else:
                P.op("dve", lambda e, kk=kk, i=i, pb=pb: e.tensor_copy(
                    out=xnT[:, kk * 4:(kk + 1) * 4, i * 128:(i + 1) * 128],
                    in_=pb[:].rearrange("p (a b) -> p a b", a=4)), reads=[pb], writes=[xnT])

    return bufs


NA = 4632
NA_MAIN = 4608


def build_A():
    P = Prog()
    x_d = P.dram_in("x", [TPC, D])
    gb_d = P.dram_in("gb", [128, D])
    w_d = P.dram_in("wA", [D, NA])
    id_d = P.dram_in("ident", [128, 128])
    pos_d = P.dram_in("pos", [128, TPC // 128, 64])
    frq_d = P.dram_in("frq", [128, 64])
    qg_d = P.dram_in("qg", [128, 128])
    kg_d = P.dram_in("kg", [128, 128])
    out_d = P.dram_out("hA", [TPC, NA])
    NT = TPC // 128

    ident = P.sbuf("ident", [128, 128])
    gb = P.sbuf("gb", [128, D])
    qg = P.sbuf("qg", [128, 128])
    kg = P.sbuf("kg", [128, 128])
    pos = P.sbuf("pos", [128, NT, 64])
    frq = P.sbuf("frq", [128, 64])
    P.load("sp", ident, ident[:], id_d)
    P.load("sp", gb, gb[:], gb_d)
    P.load("sp", qg, qg[:], qg_d)
    P.load("sp", kg, kg[:], kg_d)
    P.load("sp", pos, pos[:], pos_d)
    P.load("sp", frq, frq[:], frq_d)

    ang = P.sbuf("ang", [128, NT, 64])
    tmpf = P.sbuf("tmpf", [128, NT, 64])
    tmpi = P.sbuf("tmpi", [128, NT, 64], I32)
    cosT = P.sbuf("cosT", [128, NT, 64])
    sinT = P.sbuf("sinT", [128, NT, 64])
    P.op("dve", lambda e: e.tensor_tensor(out=ang[:], in0=pos[:], in1=frq[:].unsqueeze(1).to_broadcast([128, NT, 64]),
                                          op=ALU.mult), reads=[pos, frq], writes=[ang])
    for (dst, shift) in ((sinT, 0.0), (cosT, float(np.pi / 2))):
        if shift != 0.0:
            P.op("dve", lambda e, shift=shift: e.tensor_scalar(out=ang[:], in0=ang[:], scalar1=shift, scalar2=None,
                                                                op0=ALU.add), reads=[ang], writes=[ang])
        P.op("dve", lambda e: e.tensor_scalar(out=tmpf[:], in0=ang[:], scalar1=1.0 / TWO_PI, scalar2=None,
                                              op0=ALU.mult), reads=[ang], writes=[tmpf])
        P.op("dve", lambda e: e.tensor_copy(out=tmpi[:], in_=tmpf[:]), reads=[tmpf], writes=[tmpi])
        P.op("dve", lambda e: e.tensor_copy(out=tmpf[:], in_=tmpi[:]), reads=[tmpi], writes=[tmpf])
        P.op("dve", lambda e: e.scalar_tensor_tensor(out=tmpf[:], in0=tmpf[:], scalar=-TWO_PI, in1=ang[:],
                                                     op0=ALU.mult, op1=ALU.add), reads=[tmpf, ang], writes=[tmpf])
        P.op("act", lambda e, dst=dst: e.activation(out=dst[:], in_=tmpf[:], func=AF.Sin),
             reads=[tmpf], writes=[dst])

    xnT = P.sbuf("xnT", [128, 16, TPC], BF16)
    emit_norm_T(P, x_d, gb, ident, xnT, NT, "nA")

    wblk = [P.sbuf(f"wblk{i}", [128, 16, 512], BF16) for i in range(2)]
    pp = [P.psum(f"pp{i}", [128, 512]) for i in range(2)]
    ot = [P.sbuf(f"ot{i}", [128, 512]) for i in range(3)]
    ss4 = P.sbuf("ss4", [128, 8])
    junk2 = P.sbuf("junk2", [128, 128])
    t1 = P.sbuf("rt1", [128, 4, 64])
    t2 = P.sbuf("rt2", [128, 4, 64])
    qn = P.sbuf("qn", [128, 512])
    w_v = w_d.rearrange("(c p) n -> p c n", p=128)
    nblk = 10
    cnt = 0
    for cb in range(nblk):
        wb = wblk[cb % 2]
        c0 = cb * 512
        ncol = 512 if cb < 9 else NA - NA_MAIN
        P.dma("pool", [(wb[:, k, 0:ncol], w_v[:, k, c0:c0 + ncol]) for k in range(16)], wb, writes=[wb])
        for i in range(NT):
            ps = pp[cnt % 2]
            o = ot[cnt % 3]
            cnt += 1
            for k in range(16):
                P.op("pe", lambda e, ps=ps, k=k, i=i, wb=wb, ncol=ncol: e.matmul(
                    ps[:, 0:ncol], xnT[:, k, i * 128:(i + 1) * 128], wb[:, k, 0:ncol],
                    start=(k == 0), stop=(k == 15)), reads=[xnT, wb], writes=[ps])
            if cb in (6, 7) or cb == 8:
                nh = 4 if cb in (6, 7) else 2
                g = qg if cb in (6, 7) else kg
                sc = (128.0 ** -0.5) if cb in (6, 7) else 1.0
                P.op("dve", lambda e: e.memset(ss4[:], 0.0), writes=[ss4])
                for h in range(nh):
                    P.op("act", lambda e, ps=ps, h=h: e.activation(out=junk2[:], in_=ps[:, h * 128:(h + 1) * 128],
                                                                   func=AF.Square, accum_out=ss4[:, h:h + 1]),
                         reads=[ps], writes=[junk2, ss4])
                P.op("act", lambda e, nh=nh: e.activation(out=ss4[:, 4:4 + nh], in_=ss4[:, 0:nh], func=AF.Sqrt,
                                                          scale=1.0 / 128, bias=EPS), reads=[ss4], writes=[ss4])
                P.op("dve", lambda e, nh=nh: e.reciprocal(out=ss4[:, 4:4 + nh], in_=ss4[:, 4:4 + nh]),
                     reads=[ss4], writes=[ss4])
                if sc != 1.0:
                    P.op("dve", lambda e, nh=nh, sc=sc: e.tensor_scalar(out=ss4[:, 4:4 + nh], in0=ss4[:, 4:4 + nh],
                                                                        scalar1=sc, scalar2=None, op0=ALU.mult),
                         reads=[ss4], writes=[ss4])
                for h in range(nh):
                    P.op("dve", lambda e, ps=ps, h=h, g=g: e.scalar_tensor_tensor(
                        out=qn[:, h * 128:(h + 1) * 128], in0=ps[:, h * 128:(h + 1) * 128],
                        scalar=ss4[:, 4 + h:5 + h], in1=g[:], op0=ALU.mult, op1=ALU.mult),
                         reads=[ps, ss4, g], writes=[qn])
                if nh < 4:
                    P.op("act", lambda e, ps=ps, o=o: e.copy(out=o[:, 256:512], in_=ps[:, 256:512]),
                         reads=[ps], writes=[o])
                W = nh * 128
                qv = qn[:, 0:W].rearrange("p (h i two) -> p h i two", h=nh, two=2)
                ov = o[:, 0:W].rearrange("p (h i two) -> p h i two", h=nh, two=2)
                x0 = qv[:, :, :, 0]
                x1 = qv[:, :, :, 1]
                cb_ = cosT[:, i, :].unsqueeze(1).to_broadcast([128, nh, 64])
                sb_ = sinT[:, i, :].unsqueeze(1).to_broadcast([128, nh, 64])
                a1 = t1[:, 0:nh, :]
                a2 = t2[:, 0:nh, :]
                P.op("dve", lambda e, x0=x0, cb_=cb_, a1=a1: e.tensor_tensor(out=a1, in0=x0, in1=cb_, op=ALU.mult),
                     reads=[qn, cosT], writes=[t1])
                P.op("pool", lambda e, x1=x1, sb_=sb_, a2=a2: e.tensor_tensor(out=a2, in0=x1, in1=sb_, op=ALU.mult),
                     reads=[qn, sinT], writes=[t2])
                P.op("dve", lambda e, ov=ov, a1=a1, a2=a2: e.tensor_tensor(out=ov[:, :, :, 0], in0=a1, in1=a2,
                                                                          op=ALU.subtract),
                     reads=[t1, t2], writes=[o])
                P.op("dve", lambda e, x0=x0, sb_=sb_, a1=a1: e.tensor_tensor(out=a1, in0=x0, in1=sb_, op=ALU.mult),
                     reads=[qn, sinT], writes=[t1])
                P.op("pool", lambda e, x1=x1, cb_=cb_, a2=a2: e.tensor_tensor(out=a2, in0=x1, in1=cb_, op=ALU.mult),
                     reads=[qn, cosT], writes=[t2])
                P.op("dve", lambda e, ov=ov, a1=a1, a2=a2: e.tensor_tensor(out=ov[:, :, :, 1], in0=a1, in1=a2,
                                                                          op=ALU.add),
                     reads=[t1, t2], writes=[o])
            else:
                if cnt % 2 == 0:
                    P.op("act", lambda e, ps=ps, o=o, ncol=ncol: e.copy(out=o[:, 0:ncol], in_=ps[:, 0:ncol]),
                         reads=[ps], writes=[o])
                else:
                    P.op("dve", lambda e, ps=ps, o=o, ncol=ncol: e.tensor_copy(out=o[:, 0:ncol], in_=ps[:, 0:ncol]),
                         reads=[ps], writes=[o])
            P.store("sp", o, out_d[i * 128:(i + 1) * 128, c0:c0 + ncol], o[:, 0:ncol])
    return P


COLS_A = np.concatenate([
    np.arange(0, 768),
    np.arange(1536, 3840),
    np.arange(4632, 6168),
    np.arange(3840, 3864),
])


def rope_consts():
    t = np.arange(SEQ)
    row = (t // 64).astype(np.float32)
    col = (t % 64).astype(np.float32)
    pos = np.concatenate([np.repeat(row[:, None], 32, 1), np.repeat(col[:, None], 32, 1)], axis=1)
    freqs = (10000.0 ** (-np.arange(0, 64, 2, dtype=np.float32) / 64)).astype(np.float32)
    frq = np.concatenate([freqs, freqs])[None, :].repeat(128, 0).astype(np.float32)
    return pos.astype(np.float32), frq


def bcast_rows(v, n=128):
    return np.ascontiguousarray(np.broadcast_to(np.asarray(v, np.float32)[None, :], (n, v.shape[0])))


def run_A(x, norm_g, w_in_l, qn_g, kn_g):
    P = build_A()
    pos, frq = rope_consts()
    wA = np.ascontiguousarray(w_in_l[:, COLS_A])
    common = dict(gb=bcast_rows(norm_g), wA=wA, ident=np.eye(128, dtype=np.float32), frq=frq,
                  qg=bcast_rows(qn_g), kg=bcast_rows(kn_g))
    maps = []
    for c in range(NCORES):
        pc = pos[c * TPC:(c + 1) * TPC].reshape(TPC // 128, 128, 64).transpose(1, 0, 2)
        maps.append(dict(common, x=np.ascontiguousarray(x[c * TPC:(c + 1) * TPC]), pos=np.ascontiguousarray(pc)))
    res = run(P, maps)
    return np.concatenate([r["hA"] for r in res], axis=0)


def build_ATT():
    P = Prog()
    qT_d = P.dram_in("qT", [128, SEQ])
    kT_d = P.dram_in("kT", [128, SEQ])
    v_d = P.dram_in("v", [128, SEQ // 128, 128])
    out_d = P.dram_out("oT", [128, SEQ])
    qT = P.sbuf("qT", [128, SEQ], BF16)
    kT = P.sbuf("kT", [128, SEQ], BF16)
    v = P.sbuf("v", [128, SEQ // 128, 128], BF16)
    ones = P.sbuf("ones", [128, 128], BF16)
    P.op("dve", lambda e: e.memset(ones[:], 1.0), writes=[ones])
    for j in range(4):
        sl = slice(j * 2048, (j + 1) * 2048)
        P.dma("pool", [(kT[:, sl], kT_d[:, sl])], kT, writes=[kT])
        P.dma("pool", [(qT[:, sl], qT_d[:, sl])], qT, writes=[qT])
        P.dma("pool", [(v[:, j * 16:(j + 1) * 16, :], v_d[:, j * 16:(j + 1) * 16, :])], v, writes=[v])
    ps_s = [P.psum(f"s{i}", [128, 512]) for i in range(2)]
    ps_o = [P.psum(f"o{i}", [128, 512]) for i in range(2)]
    ps_d = [P.psum(f"d{i}", [128, 512]) for i in range(2)]
    pt = [P.sbuf(f"pt{i}", [128, 512], BF16) for i in range(3)]
    rd = P.sbuf("rd", [128, 512])
    ot = [P.sbuf(f"ot{i}", [128, 512]) for i in range(2)]
    NKT = SEQ // 128
    NQB = SEQ // 512
    ps_s = ps_s + [P.psum("s2", [128, 512])]
    steps = [(qb, kt) for qb in range(NQB) for kt in range(NKT)]

    def emit_S(idx):
        qb, kt = steps[idx]
        s_ = ps_s[idx % 3]
        P.op("pe", lambda e, s_=s_, kt=kt, qb=qb: e.matmul(s_[:], kT[:, kt * 128:(kt + 1) * 128],
                                                        qT[:, qb * 512:(qb + 1) * 512], start=True, stop=True),
             reads=[kT, qT], writes=[s_])

    emit_S(0)
    emit_S(1)
    for idx, (qb, kt) in enumerate(steps):
        po = ps_o[qb % 2]
        pd = ps_d[qb % 2]
        s_ = ps_s[idx % 3]
        p = pt[idx % 3]
        P.op("act", lambda e, s_=s_, p=p: e.activation(out=p[:], in_=s_[:], func=AF.Exp), reads=[s_], writes=[p])
        if idx + 2 < len(steps):
            emit_S(idx + 2)
        P.op("pe", lambda e, po=po, p=p, kt=kt: e.matmul(po[:], v[:, kt, :], p[:], start=(kt == 0),
                                                      stop=(kt == NKT - 1)), reads=[v, p], writes=[po])
        P.op("pe", lambda e, pd=pd, p=p, kt=kt: e.matmul(pd[:], ones[:], p[:], start=(kt == 0),
                                                      stop=(kt == NKT - 1)), reads=[ones, p], writes=[pd])
        if kt == NKT - 1:
            o = ot[qb % 2]
            P.op("dve", lambda e, pd=pd: e.reciprocal(out=rd[:], in_=pd[:]), reads=[pd], writes=[rd])
            P.op("dve", lambda e, po=po, o=o: e.tensor_tensor(out=o[:], in0=po[:], in1=rd[:], op=ALU.mult),
                 reads=[po, rd], writes=[o])
            P.store("sp", o, out_d[:, qb * 512:(qb + 1) * 512], o[:])
    return P


def run_ATT(hA):
    P = build_ATT()
    q = hA[:, 3072:4096]
    k = hA[:, 4096:4352]
    vv = hA[:, 4352:4608]
    maps = []
    for c in range(NCORES):
        kv = c // 4
        maps.append(dict(
            qT=np.ascontiguousarray(q[:, c * 128:(c + 1) * 128].T),
            kT=np.ascontiguousarray(k[:, kv * 128:(kv + 1) * 128].T),
            v=np.ascontiguousarray(vv[:, kv * 128:(kv + 1) * 128].reshape(SEQ // 128, 128, 128).transpose(1, 0, 2)),
        ))
    res = run(P, maps)
    return np.concatenate([r["oT"].T for r in res], axis=1)


S5C = 512


def _range_reduce_sin(P, dst, src, tmpf, tmpi, shift):
    P.op("dve", lambda e: e.tensor_scalar(out=tmpf[:], in0=src[:], scalar1=shift, scalar2=1.0 / TWO_PI,
                                          op0=ALU.add, op1=ALU.mult), reads=[src], writes=[tmpf])
    P.op("dve", lambda e: e.tensor_copy(out=tmpi[:], in_=tmpf[:]), reads=[tmpf], writes=[tmpi])
    P.op("dve", lambda e: e.tensor_copy(out=tmpf[:], in_=tmpi[:]), reads=[tmpi], writes=[tmpf])
    P.op("dve", lambda e: e.scalar_tensor_tensor(out=tmpf[:], in0=tmpf[:], scalar=-TWO_PI, in1=src[:],
                                                 op0=ALU.mult, op1=ALU.add), reads=[tmpf, src], writes=[tmpf])
    if shift != 0.0:
        P.op("dve", lambda e: e.tensor_scalar(out=tmpf[:], in0=tmpf[:], scalar1=shift, scalar2=None, op0=ALU.add),
             reads=[tmpf], writes=[tmpf])
    P.op("act", lambda e: e.activation(out=dst[:], in_=tmpf[:], func=AF.Sin), reads=[tmpf], writes=[dst])


def build_S5():
    P = Prog()
    NU = 6
    NCH = SEQ // S5C
    uT_d = P.dram_in("uT", [NU, 32, SEQ])
    prm_d = P.dram_in("prm", [NU, 128, 3])
    b_d = P.dram_in("bmat", [NU, 128, 64])
    c_d = P.dram_in("cmat", [NU, 128, 64])
    jidx_d = P.dram_in("jidx", [128, S5C + 1])
    id_d = P.dram_in("ident", [128, 128])
    out_d = P.dram_out("yT", [NU, 32, SEQ])

    ident = P.sbuf("ident", [128, 128])
    jidx = P.sbuf("jidx", [128, S5C + 1])
    P.load("sp", ident, ident[:], id_d)
    P.load("sp", jidx, jidx[:], jidx_d)
    prm = P.sbuf("prm", [128, 3])
    bm = P.sbuf("bm", [128, 64])
    cm = P.sbuf("cm", [128, 64])
    sc = P.sbuf("sc", [128, 16])
    s1f = P.sbuf("s1f", [128, 1])
    s1i = P.sbuf("s1i", [128, 1], I32)
    th = P.sbuf("th", [128, 1])
    ph = P.sbuf("ph", [128, S5C + 1])
    tmpf = P.sbuf("tmpf", [128, S5C + 1])
    tmpi = P.sbuf("tmpi", [128, S5C + 1], I32)
    Pc = P.sbuf("Pc", [128, S5C + 1])
    Ps = P.sbuf("Ps", [128, S5C + 1])
    rt = P.sbuf("rt", [128, S5C])
    bb = P.sbuf("bb", [128, 64])
    bt1 = P.sbuf("bt1", [128, 32])
    BT = P.sbuf("BT", [32, 256], BF16)
    CT = P.sbuf("CT", [128, 96], BF16)
    pst = P.psum("pst", [32, 256])
    ps_re = [P.psum(f"psre{i}", [128, S5C]) for i in range(2)]
    ps_im = [P.psum(f"psim{i}", [128, S5C]) for i in range(2)]
    ps_y = [P.psum(f"psy{i}", [32, S5C]) for i in range(2)]
    ut = [P.sbuf(f"ut{i}", [32, S5C], BF16) for i in range(2)]
    m = [P.sbuf(f"m{i}", [128, S5C]) for i in range(4)]
    cre = P.sbuf("cre_", [128, S5C])
    cim = P.sbuf("cim_", [128, S5C])
    zre = [P.sbuf(f"zre{i}", [128, S5C]) for i in range(2)]
    zim = [P.sbuf(f"zim{i}", [128, S5C]) for i in range(2)]
    nn = [P.sbuf(f"nn{i}", [128, S5C], BF16) for i in range(4)]
    init = P.sbuf("init", [128, 4])
    yt = [P.sbuf(f"yt{i}", [32, S5C]) for i in range(2)]

    def col(t, j):
        return t[:, j:j + 1]

    for u in range(NU):
        P.load("sp", prm, prm[:], prm_d[u])
        P.load("sp", bm, bm[:], b_d[u])
        P.load("sp", cm, cm[:], c_d[u])
        P.op("act", lambda e: e.activation(out=col(sc, 0), in_=col(prm, 2), func=AF.Exp), reads=[prm], writes=[sc])
        P.op("dve", lambda e: e.tensor_tensor(out=col(sc, 10), in0=col(prm, 0), in1=col(sc, 0), op=ALU.mult),
             reads=[prm, sc], writes=[sc])
        P.op("act", lambda e: e.activation(out=col(sc, 1), in_=col(sc, 10), func=AF.Exp), reads=[sc], writes=[sc])
        P.op("dve", lambda e: e.tensor_tensor(out=col(sc, 2), in0=col(prm, 1), in1=col(sc, 0), op=ALU.mult),
             reads=[prm, sc], writes=[sc])
        P.op("dve", lambda e: e.tensor_scalar(out=s1f[:], in0=col(sc, 2), scalar1=1.0 / TWO_PI, scalar2=None,
                                              op0=ALU.mult), reads=[sc], writes=[s1f])
        P.op("dve", lambda e: e.tensor_copy(out=s1i[:], in_=s1f[:]), reads=[s1f], writes=[s1i])
        P.op("dve", lambda e: e.tensor_copy(out=s1f[:], in_=s1i[:]), reads=[s1i], writes=[s1f])
        P.op("dve", lambda e: e.scalar_tensor_tensor(out=th[:], in0=s1f[:], scalar=-TWO_PI, in1=col(sc, 2),
                                                     op0=ALU.mult, op1=ALU.add), reads=[s1f, sc], writes=[th])
        P.op("dve", lambda e: e.tensor_scalar(out=ph[:], in0=jidx[:], scalar1=th[:, 0:1], scalar2=None,
                                              op0=ALU.mult), reads=[jidx, th], writes=[ph])
        _range_reduce_sin(P, Ps, ph, tmpf, tmpi, 0.0)
        _range_reduce_sin(P, Pc, ph, tmpf, tmpi, float(np.pi / 2))
        P.op("dve", lambda e: e.tensor_tensor(out=col(sc, 5), in0=col(sc, 1), in1=col(Pc, 1), op=ALU.mult),
             reads=[sc, Pc], writes=[sc])
        P.op("dve", lambda e: e.tensor_scalar(out=col(sc, 5), in0=col(sc, 5), scalar1=-1.0, scalar2=None,
                                              op0=ALU.add), reads=[sc], writes=[sc])
        P.op("dve", lambda e: e.tensor_tensor(out=col(sc, 6), in0=col(sc, 1), in1=col(Ps, 1), op=ALU.mult),
             reads=[sc, Ps], writes=[sc])
        P.op("dve", lambda e: e.tensor_tensor(out=col(sc, 7), in0=col(prm, 0), in1=col(prm, 0), op=ALU.mult),
             reads=[prm], writes=[sc])
        P.op("dve", lambda e: e.scalar_tensor_tensor(out=col(sc, 7), in0=col(prm, 1), scalar=col(prm, 1),
                                                     in1=col(sc, 7), op0=ALU.mult, op1=ALU.add),
             reads=[prm, sc], writes=[sc])
        P.op("dve", lambda e: e.reciprocal(out=col(sc, 7), in_=col(sc, 7)), reads=[sc], writes=[sc])
        P.op("dve", lambda e: e.tensor_tensor(out=col(sc, 10), in0=col(sc, 5), in1=col(prm, 0), op=ALU.mult),
             reads=[sc, prm], writes=[sc])
        P.op("dve", lambda e: e.scalar_tensor_tensor(out=col(sc, 10), in0=col(sc, 6), scalar=col(prm, 1),
                                                     in1=col(sc, 10), op0=ALU.mult, op1=ALU.add),
             reads=[sc, prm], writes=[sc])
        P.op("dve", lambda e: e.tensor_tensor(out=col(sc, 8), in0=col(sc, 10), in1=col(sc, 7), op=ALU.mult),
             reads=[sc], writes=[sc])
        P.op("dve", lambda e: e.tensor_tensor(out=col(sc, 11), in0=col(sc, 5), in1=col(prm, 1), op=ALU.mult),
             reads=[sc, prm], writes=[sc])
        P.op("dve", lambda e: e.scalar_tensor_tensor(out=col(sc, 11), in0=col(sc, 6), scalar=col(prm, 0),
                                                     in1=col(sc, 11), op0=ALU.mult, op1=ALU.subtract),
             reads=[sc, prm], writes=[sc])
        P.op("dve", lambda e: e.tensor_tensor(out=col(sc, 9), in0=col(sc, 11), in1=col(sc, 7), op=ALU.mult),
             reads=[sc], writes=[sc])
        P.op("dve", lambda e: e.tensor_scalar(out=bt1[:], in0=bm[:, 32:64], scalar1=col(sc, 9), scalar2=None,
                                              op0=ALU.mult), reads=[bm, sc], writes=[bt1])
        P.op("dve", lambda e: e.scalar_tensor_tensor(out=bb[:, 0:32], in0=bm[:, 0:32], scalar=col(sc, 8),
                                                     in1=bt1[:], op0=ALU.mult, op1=ALU.subtract),
             reads=[bm, sc, bt1], writes=[bb])
        P.op("dve", lambda e: e.tensor_scalar(out=bt1[:], in0=bm[:, 0:32], scalar1=col(sc, 9), scalar2=None,
                                              op0=ALU.mult), reads=[bm, sc, bb], writes=[bt1])
        P.op("dve", lambda e: e.scalar_tensor_tensor(out=bb[:, 32:64], in0=bm[:, 32:64], scalar=col(sc, 8),
                                                     in1=bt1[:], op0=ALU.mult, op1=ALU.add),
             reads=[bm, sc, bt1], writes=[bb])
        P.op("pe", lambda e: e.transpose(pst[:, 0:128], bb[:, 0:32], ident[:]), reads=[bb, ident], writes=[pst])
        P.op("pe", lambda e: e.transpose(pst[:, 128:256], bb[:, 32:64], ident[:]), reads=[bb, ident], writes=[pst])
        P.op("dve", lambda e: e.tensor_copy(out=BT[:], in_=pst[:]), reads=[pst], writes=[BT])
        P.op("dve", lambda e: e.tensor_copy(out=CT[:, 0:32], in_=cm[:, 0:32]), reads=[cm], writes=[CT])
        P.op("dve", lambda e: e.tensor_scalar(out=CT[:, 32:96], in0=cm[:, 0:64], scalar1=-1.0, scalar2=None,
                                              op0=ALU.mult), reads=[cm], writes=[CT])
        P.op("dve", lambda e: e.tensor_scalar(out=rt[:], in0=jidx[:, 0:S5C], scalar1=0.0, scalar2=col(sc, 1),
                                              op0=ALU.mult, op1=ALU.add), reads=[jidx, sc], writes=[rt])
        for ch in range(NCH):
            b = ch % 2
            tsl = slice(ch * S5C, (ch + 1) * S5C)
            P.dma("pool", [(ut[b][:], uT_d[u, :, tsl])], ut[b], writes=[ut[b]])
            pr, pi_ = ps_re[b], ps_im[b]
            P.op("pe", lambda e, pr=pr, b=b: e.matmul(pr[:], BT[:, 0:128], ut[b][:], start=True, stop=True),
                 reads=[BT, ut[b]], writes=[pr])
            P.op("pe", lambda e, pi_=pi_, b=b: e.matmul(pi_[:], BT[:, 128:256], ut[b][:], start=True, stop=True),
                 reads=[BT, ut[b]], writes=[pi_])
            PcS, PsS = Pc[:, 0:S5C], Ps[:, 0:S5C]
            P.op("dve", lambda e, pr=pr: e.tensor_tensor(out=m[0][:], in0=pr[:], in1=PcS, op=ALU.mult),
                 reads=[pr, Pc], writes=[m[0]])
            P.op("dve", lambda e, pi_=pi_: e.tensor_tensor(out=m[1][:], in0=pi_[:], in1=PsS, op=ALU.mult),
                 reads=[pi_, Ps], writes=[m[1]])
            P.op("pool", lambda e: e.tensor_tensor(out=cre[:], in0=m[0][:], in1=m[1][:], op=ALU.add),
                 reads=[m[0], m[1]], writes=[cre])
            P.op("dve", lambda e, pi_=pi_: e.tensor_tensor(out=m[2][:], in0=pi_[:], in1=PcS, op=ALU.mult),
                 reads=[pi_, Pc], writes=[m[2]])
            P.op("dve", lambda e, pr=pr: e.tensor_tensor(out=m[3][:], in0=pr[:], in1=PsS, op=ALU.mult),
                 reads=[pr, Ps], writes=[m[3]])
            P.op("pool", lambda e: e.tensor_tensor(out=cim[:], in0=m[2][:], in1=m[3][:], op=ALU.subtract),
                 reads=[m[2], m[3]], writes=[cim])
            if ch == 0:
                P.op("dve", lambda e: e.memset(init[:], 0.0), writes=[init])
            else:
                pzr, pzi = zre[1 - b], zim[1 - b]
                L = S5C - 1
                P.op("dve", lambda e, pzi=pzi: e.tensor_tensor(out=col(init, 2), in0=col(pzi, L), in1=col(Ps, S5C),
                                                              op=ALU.mult), reads=[pzi, Ps], writes=[init])
                P.op("dve", lambda e, pzr=pzr: e.scalar_tensor_tensor(out=col(init, 0), in0=col(pzr, L),
                                                                     scalar=col(Pc, S5C), in1=col(init, 2),
                                                                     op0=ALU.mult, op1=ALU.subtract),
                     reads=[pzr, Pc, init], writes=[init])
                P.op("dve", lambda e, pzr=pzr: e.tensor_tensor(out=col(init, 3), in0=col(pzr, L), in1=col(Ps, S5C),
                                                              op=ALU.mult), reads=[pzr, Ps], writes=[init])
                P.op("dve", lambda e, pzi=pzi: e.scalar_tensor_tensor(out=col(init, 1), in0=col(pzi, L),
                                                                     scalar=col(Pc, S5C), in1=col(init, 3),
                                                                     op0=ALU.mult, op1=ALU.add),
                     reads=[pzi, Pc, init], writes=[init])
            zr, zi = zre[b], zim[b]
            P.op("dve", lambda e, zr=zr: e.tensor_tensor_scan(out=zr[:], data0=rt[:], data1=cre[:],
                                                             initial=col(init, 0), op0=ALU.mult, op1=ALU.add),
                 reads=[rt, cre, init], writes=[zr])
            P.op("dve", lambda e, zi=zi: e.tensor_tensor_scan(out=zi[:], data0=rt[:], data1=cim[:],
                                                             initial=col(init, 1), op0=ALU.mult, op1=ALU.add),
                 reads=[rt, cim, init], writes=[zi])
            P.op("pool", lambda e, zr=zr: e.tensor_tensor(out=nn[0][:], in0=zr[:], in1=PcS, op=ALU.mult),
                 reads=[zr, Pc], writes=[nn[0]])
            P.op("pool", lambda e, zi=zi: e.tensor_tensor(out=nn[1][:], in0=zi[:], in1=PsS, op=ALU.mult),
                 reads=[zi, Ps], writes=[nn[1]])
            P.op("pool", lambda e, zi=zi: e.tensor_tensor(out=nn[2][:], in0=zi[:], in1=PcS, op=ALU.mult),
                 reads=[zi, Pc], writes=[nn[2]])
            P.op("dve", lambda e, zr=zr: e.tensor_tensor(out=nn[3][:], in0=zr[:], in1=PsS, op=ALU.mult),
                 reads=[zr, Ps], writes=[nn[3]])
            py = ps_y[b]
            lts = [CT[:, 0:32], CT[:, 32:64], CT[:, 64:96], CT[:, 64:96]]
            for q in range(4):
                P.op("pe", lambda e, py=py, q=q, lt=lts[q]: e.matmul(py[:], lt, nn[q][:], start=(q == 0),
                                                                    stop=(q == 3)), reads=[CT, nn[q]], writes=[py])
            y = yt[b]
            P.op("act", lambda e, py=py, y=y: e.copy(out=y[:], in_=py[:]), reads=[py], writes=[y])
            P.store("sp", y, out_d[u, :, tsl], y[:])
    return P


def run_S5(hA, a_re, a_im, log_step, b_re, b_im, c_re, c_im):
    P = build_S5()
    u = hA[:, 0:768]
    uT = np.ascontiguousarray(u.T)
    uTr = np.ascontiguousarray(uT[:, ::-1])
    jidx = bcast_rows(np.arange(S5C + 1, dtype=np.float32))
    maps = []
    for c in range(NCORES):
        uTc = np.zeros((6, 32, SEQ), np.float32)
        prm = np.zeros((6, 128, 3), np.float32)
        bmat = np.zeros((6, 128, 64), np.float32)
        cmat = np.zeros((6, 128, 64), np.float32)
        for pq in range(3):
            for d in range(2):
                un = pq * 2 + d
                for gg in range(2):
                    g = c * 6 + pq * 2 + gg
                    src = uTr if d == 1 else uT
                    uTc[un, gg * 16:(gg + 1) * 16] = src[g * 16:(g + 1) * 16]
                    rs = slice(gg * 64, (gg + 1) * 64)
                    prm[un, rs, 0] = a_re[d, g]
                    prm[un, rs, 1] = a_im[d, g]
                    prm[un, rs, 2] = log_step[d, g]
                    bmat[un, rs, gg * 16:(gg + 1) * 16] = b_re[d, g]
                    bmat[un, rs, 32 + gg * 16:32 + (gg + 1) * 16] = b_im[d, g]
                    cmat[un, rs, gg * 16:(gg + 1) * 16] = c_re[d, g].T
                    cmat[un, rs, 32 + gg * 16:32 + (gg + 1) * 16] = c_im[d, g].T
        maps.append(dict(uT=uTc, prm=prm, bmat=bmat, cmat=cmat, jidx=jidx, ident=np.eye(128, dtype=np.float32)))
    res = run(P, maps)
    yf = np.zeros((SEQ, 768), np.float32)
    yb = np.zeros((SEQ, 768), np.float32)
    for c in range(NCORES):
        yT = res[c]["yT"]
        for pq in range(3):
            cs = slice((c * 6 + pq * 2) * 16, (c * 6 + pq * 2 + 2) * 16)
            yf[:, cs] = yT[pq * 2].T
            yb[:, cs] = yT[pq * 2 + 1].T[::-1]
    return yf, yb


DNC = 128
NDC = SEQ // DNC


def build_DN():
    P = Prog()
    NU = 2
    xin_d = P.dram_in("xin", [NU, 3, 128, SEQ + 4])
    cw_d = P.dram_in("cw", [NU, 128, 15])
    ab_d = P.dram_in("ab", [NU, 128, 2, NDC])
    hp_d = P.dram_in("hp", [NU, 128, 2])
    id_d = P.dram_in("ident", [128, 128])
    tri_d = P.dram_in("triu", [128, 128])
    mb_d = P.dram_in("maskb", [128, 128])
    m0_d = P.dram_in("msk0", [128, 128])
    mT_d = P.dram_in("mskT", [128, 6, 128])
    out_d = P.dram_out("o", [NU, SEQ, 128])

    ident = P.sbuf("ident", [128, 128])
    identb = P.sbuf("identb", [128, 128], BF16)
    triu = P.sbuf("triu", [128, 128])
    maskb = P.sbuf("maskb", [128, 128])
    onesf = P.sbuf("onesf", [128, 128])
    P.load("sp", ident, ident[:], id_d)
    P.load("sp", triu, triu[:], tri_d)
    P.load("sp", maskb, maskb[:], mb_d)
    onesb = P.sbuf("onesb", [128, 128], BF16)
    triub = P.sbuf("triub", [128, 128], BF16)
    P.op("dve", lambda e: e.memset(onesf[:], 1.0), writes=[onesf])
    P.op("dve", lambda e: e.memset(onesb[:], 1.0), writes=[onesb])
    P.op("dve", lambda e: e.tensor_copy(out=triub[:], in_=triu[:]), reads=[triu], writes=[triub])
    P.op("dve", lambda e: e.tensor_copy(out=identb[:], in_=ident[:]), reads=[ident], writes=[identb])

    qT = P.sbuf("qT", [128, SEQ], BF16)
    kT = P.sbuf("kT", [128, SEQ], BF16)
    vT = P.sbuf("vT", [128, SEQ], BF16)
    cw = P.sbuf("cw", [128, 15])
    ab = P.sbuf("ab", [128, 2, NDC])
    hp = P.sbuf("hp", [128, 16])
    PW = 2048
    xp = [P.sbuf(f"xp{i}", [128, PW + 4]) for i in range(2)]
    acc = P.sbuf("acc", [128, PW])
    sq = P.sbuf("sq", [128, PW], BF16)
    ghl = P.sbuf("ghl", [128, 2, NDC], BF16)
    gtmp = P.sbuf("gtmp", [128, NDC])
    dgh = P.sbuf("dgh", [128, 128], BF16)
    dgl = P.sbuf("dgl", [128, 128], BF16)
    dgt = P.sbuf("dgt", [128, 128])
    rn = P.sbuf("rn", [128, 512])
    pss = [P.psum(f"pss{i}", [128, 512]) for i in range(2)]

    tb = {n: P.sbuf("tb_" + n, [128, NDC]) for n in ("g", "beta", "gc", "egc", "negegc", "egl", "ekd", "tmp")}
    pt64 = pss[0]

    ptr = P.psum("ptr", [128, 256], BF16)
    KV = P.sbuf("KV", [128, 256], BF16)
    dg = P.sbuf("dg", [128, 128])
    kTc = P.sbuf("kTc", [128, 128], BF16)
    qTc = P.sbuf("qTc", [128, 128], BF16)
    Winvw = P.sbuf("Winvw", [128, 128], BF16)
    pg = P.psum("pg", [128, 128])
    xe = P.sbuf("xe", [128, 128])
    E = P.sbuf("E", [128, 128])
    Es = P.sbuf("Es", [128, 128])
    pkk = P.psum("pkk", [128, 256])
    AT = P.sbuf("AT", [128, 128], BF16)
    X = [P.sbuf(f"X{i}", [128, 128], BF16) for i in range(2)]
    XT = [P.sbuf(f"XT{i}", [128, 128], BF16) for i in range(2)]
    W = [P.sbuf(f"W{i}", [128, 128]) for i in range(2)]
    Wb = [P.sbuf(f"Wb{i}", [128, 128], BF16) for i in range(2)]
    Xw = [P.sbuf(f"Xw{i}", [128, 128], BF16) for i in range(2)]
    XTw = [P.sbuf(f"XTw{i}", [128, 128], BF16) for i in range(2)]
    x0f = P.sbuf("x0f", [128, 128])
    UT = P.sbuf("UT", [128, 128])
    G32 = P.sbuf("G32", [128, 128])
    Gm = P.sbuf("Gm", [128, 128], BF16)
    Gw = P.sbuf("Gw", [128, 128], BF16)
    GTw = P.sbuf("GTw", [128, 128], BF16)
    Ysb = P.sbuf("Ysb", [128, 128], BF16)
    CT = [P.sbuf(f"CTl{i}", [128, 128], BF16) for i in range(6)]
    msk0 = P.sbuf("msk0", [128, 128])
    mskT = P.sbuf("mskT", [128, 6, 128])
    P.load("sp", msk0, msk0[:], m0_d)
    P.load("sp", mskT, mskT[:], mT_d)
    pX = P.psum("pX", [128, 256])
    pW = P.psum("pW", [128, 128])
    S = P.sbuf("S", [128, 128])
    Sb = P.sbuf("Sb", [128, 128], BF16)
    pks = P.psum("pks", [128, 256])
    Rp = P.sbuf("Rp", [128, 128], BF16)
    vnew = P.sbuf("vnew", [128, 128], BF16)
    oq = P.sbuf("oq", [128, 128])
    ot = [P.sbuf(f"ot{i}", [128, 128]) for i in range(2)]
    Kd = P.sbuf("Kd", [128, 128], BF16)

    def col(t, j):
        return t[:, j:j + 1]

    for u in range(NU):
        P.load("sp", cw, cw[:], cw_d[u])
        P.load("sp", ab, ab[:], ab_d[u])
        P.load("sp", hp, hp[:, 0:2], hp_d[u])
        cnt = 0
        for ti, dst in enumerate((qT, kT, vT)):
            for pc in range(SEQ // PW):
                x_ = xp[cnt % 2]
                cnt += 1
                P.load("sp", x_, x_[:], xin_d[u, ti, :, pc * PW:pc * PW + PW + 4])
                P.op("act", lambda e, x_=x_, ti=ti: e.activation(out=acc[:], in_=x_[:, 0:PW], func=AF.Copy,
                                                                scale=col(cw, ti * 5)), reads=[x_, cw], writes=[acc])
                for k in range(1, 5):
                    P.op("dve", lambda e, x_=x_, ti=ti, k=k: e.scalar_tensor_tensor(
                        out=acc[:], in0=x_[:, k:k + PW], scalar=col(cw, ti * 5 + k), in1=acc[:],
                        op0=ALU.mult, op1=ALU.add), reads=[x_, cw, acc], writes=[acc])
                dsl = slice(pc * PW, (pc + 1) * PW)
                if ti == 2:
                    P.op("act", lambda e, dsl=dsl: e.activation(out=vT[:, dsl], in_=acc[:], func=AF.Silu),
                         reads=[acc], writes=[vT])
                    continue
                P.op("act", lambda e: e.activation(out=acc[:], in_=acc[:], func=AF.Silu), reads=[acc], writes=[acc])
                P.op("pool", lambda e: e.tensor_tensor(out=sq[:], in0=acc[:], in1=acc[:], op=ALU.mult),
                     reads=[acc], writes=[sq])
                for j in range(PW // 512):
                    ps = pss[j % 2]
                    js = slice(j * 512, (j + 1) * 512)
                    P.op("pe", lambda e, ps=ps, js=js: e.matmul(ps[:], onesb[:], sq[:, js], start=True, stop=True),
                         reads=[onesb, sq], writes=[ps])
                    P.op("act", lambda e, ps=ps: e.activation(out=rn[:], in_=ps[:], func=AF.Sqrt, bias=EPS),
                         reads=[ps], writes=[rn])
                    P.op("dve", lambda e: e.reciprocal(out=rn[:], in_=rn[:]), reads=[rn], writes=[rn])
                    scl = (128.0 ** -0.5) if ti == 0 else 1.0
                    P.op("dve", lambda e, js=js, dst=dst, pc=pc, j=j, scl=scl: e.scalar_tensor_tensor(
                        out=dst[:, pc * PW + j * 512:pc * PW + (j + 1) * 512], in0=acc[:, js], scalar=scl, in1=rn[:],
                        op0=ALU.mult, op1=ALU.mult), reads=[acc, rn], writes=[dst])
        P.op("act", lambda e: e.activation(out=tb["tmp"][:], in_=ab[:, 0, :], func=AF.Exp, bias=col(hp, 1)),
             reads=[ab, hp], writes=[tb["tmp"]])
        P.op("act", lambda e: e.activation(out=tb["tmp"][:], in_=tb["tmp"][:], func=AF.Ln, bias=1.0),
             reads=[tb["tmp"]], writes=[tb["tmp"]])
        P.op("act", lambda e: e.activation(out=col(hp, 2), in_=col(hp, 0), func=AF.Exp), reads=[hp], writes=[hp])
        P.op("dve", lambda e: e.tensor_scalar(out=col(hp, 3), in0=col(hp, 2), scalar1=-1.0, scalar2=None,
                                              op0=ALU.mult), reads=[hp], writes=[hp])
        P.op("dve", lambda e: e.tensor_scalar(out=tb["g"][:], in0=tb["tmp"][:], scalar1=col(hp, 3), scalar2=None,
                                              op0=ALU.mult), reads=[tb["tmp"], hp], writes=[tb["g"]])
        P.op("act", lambda e: e.activation(out=tb["beta"][:], in_=ab[:, 1, :], func=AF.Sigmoid),
             reads=[ab], writes=[tb["beta"]])
        P.op("dve", lambda e: e.tensor_copy(out=ghl[:, 0, :], in_=tb["g"][:]), reads=[tb["g"]], writes=[ghl])
        P.op("dve", lambda e: e.tensor_copy(out=gtmp[:], in_=ghl[:, 0, :]), reads=[ghl], writes=[gtmp])
        P.op("dve", lambda e: e.tensor_tensor(out=ghl[:, 1, :], in0=tb["g"][:], in1=gtmp[:], op=ALU.subtract),
             reads=[tb["g"], gtmp], writes=[ghl])
        for hl in range(2):
            P.op("pe", lambda e, hl=hl: e.matmul(pt64[:, 0:NDC], triub[:], ghl[:, hl, :], start=(hl == 0),
                                                 stop=(hl == 1)), reads=[triub, ghl], writes=[pt64])
        for hl in range(2):
            P.op("pe", lambda e, hl=hl: e.matmul(pt64[:, NDC:2 * NDC], onesb[:], ghl[:, hl, :], start=(hl == 0),
                                                 stop=(hl == 1)), reads=[onesb, ghl], writes=[pt64])
        P.op("dve", lambda e: e.tensor_copy(out=tb["gc"][:], in_=pt64[:, 0:NDC]), reads=[pt64], writes=[tb["gc"]])
        P.op("dve", lambda e: e.tensor_copy(out=gtmp[:], in_=pt64[:, NDC:2 * NDC]), reads=[pt64], writes=[gtmp])
        P.op("act", lambda e: e.activation(out=tb["egc"][:], in_=tb["gc"][:], func=AF.Exp),
             reads=[tb["gc"]], writes=[tb["egc"]])
        P.op("dve", lambda e: e.tensor_scalar(out=tb["negegc"][:], in0=tb["egc"][:], scalar1=-1.0, scalar2=None,
                                              op0=ALU.mult), reads=[tb["egc"]], writes=[tb["negegc"]])
        P.op("act", lambda e: e.activation(out=tb["egl"][:], in_=gtmp[:], func=AF.Exp),
             reads=[gtmp], writes=[tb["egl"]])
        P.op("dve", lambda e: e.tensor_tensor(out=tb["tmp"][:], in0=gtmp[:], in1=tb["gc"][:],
                                              op=ALU.subtract), reads=[gtmp, tb["gc"]], writes=[tb["tmp"]])
        P.op("act", lambda e: e.activation(out=tb["ekd"][:], in_=tb["tmp"][:], func=AF.Exp),
             reads=[tb["tmp"]], writes=[tb["ekd"]])
        P.op("dve", lambda e: e.memset(S[:], 0.0), writes=[S])
        P.op("dve", lambda e: e.memset(Sb[:], 0.0), writes=[Sb])
        for c in range(int(os.environ.get('DN_NCH', NDC))):
            csl = slice(c * DNC, (c + 1) * DNC)
            gcc, bec = col(tb["gc"], c), col(tb["beta"], c)
            P.op("pe", lambda e, csl=csl: e.transpose(ptr[:, 0:128], kT[:, csl], identb[:]),
                 reads=[kT, identb], writes=[ptr])
            P.op("pe", lambda e, csl=csl: e.transpose(ptr[:, 128:256], vT[:, csl], identb[:]),
                 reads=[vT, identb], writes=[ptr])
            P.op("act", lambda e: e.copy(out=KV[:], in_=ptr[:]), reads=[ptr], writes=[KV])
            if int(os.environ.get('DN_STOP', 9)) <= 1:
                continue
            P.op("dve", lambda e, gcc=gcc: e.tensor_scalar(out=dg[:], in0=ident[:], scalar1=gcc, scalar2=None,
                                                           op0=ALU.mult), reads=[ident, tb["gc"]], writes=[dg])
            P.op("dve", lambda e: e.tensor_copy(out=dgh[:], in_=dg[:]), reads=[dg], writes=[dgh])
            P.op("pool", lambda e: e.tensor_copy(out=dgt[:], in_=dgh[:]), reads=[dgh], writes=[dgt])
            P.op("pool", lambda e: e.tensor_tensor(out=dgl[:], in0=dg[:], in1=dgt[:], op=ALU.subtract),
                 reads=[dg, dgt], writes=[dgl])
            P.op("pe", lambda e: e.matmul(pg[:], onesb[:], dgh[:], start=True, stop=False),
                 reads=[onesb, dgh], writes=[pg])
            P.op("pe", lambda e: e.matmul(pg[:], onesb[:], dgl[:], start=False, stop=True),
                 reads=[onesb, dgl], writes=[pg])
            P.op("dve", lambda e, gcc=gcc: e.tensor_scalar(out=xe[:], in0=pg[:], scalar1=gcc, scalar2=0.0,
                                                           op0=ALU.subtract, op1=ALU.min),
                 reads=[pg, tb["gc"]], writes=[xe])
            P.op("pool", lambda e: e.tensor_tensor(out=xe[:], in0=xe[:], in1=maskb[:], op=ALU.add),
                 reads=[xe, maskb], writes=[xe])
            P.op("act", lambda e: e.activation(out=E[:], in_=xe[:], func=AF.Exp), reads=[xe], writes=[E])
            if int(os.environ.get('DN_STOP', 9)) <= 2:
                continue
            P.op("pool", lambda e, csl=csl: e.tensor_copy(out=kTc[:], in_=kT[:, csl]), reads=[kT], writes=[kTc])
            P.op("pe", lambda e, csl=csl: e.matmul(pkk[:, 0:128], kT[:, csl], kTc[:], start=True, stop=True),
                 reads=[kT, kTc], writes=[pkk])
            P.op("pool", lambda e, csl=csl: e.tensor_copy(out=qTc[:], in_=qT[:, csl]), reads=[qT], writes=[qTc])
            P.op("pe", lambda e, csl=csl: e.matmul(pkk[:, 128:256], kT[:, csl], qTc[:], start=True, stop=True),
                 reads=[kT, qTc], writes=[pkk])
            P.op("pool", lambda e: e.tensor_tensor(out=Es[:], in0=E[:], in1=ident[:], op=ALU.subtract),
                 reads=[E, ident], writes=[Es])
            P.op("dve", lambda e, bec=bec: e.scalar_tensor_tensor(out=x0f[:], in0=pkk[:, 0:128], scalar=bec,
                                                                 in1=Es[:], op0=ALU.mult, op1=ALU.mult),
                 reads=[pkk, tb["beta"], Es], writes=[x0f])
            P.op("dve", lambda e: e.tensor_tensor(out=AT[:], in0=pkk[:, 128:256], in1=E[:], op=ALU.mult),
                 reads=[pkk, E], writes=[AT])
            if int(os.environ.get('DN_STOP', 9)) <= 3:
                continue
            P.op("pe", lambda e: e.transpose(pss[1][:, 0:128], x0f[:], ident[:]), reads=[x0f, ident], writes=[pss[1]])
            P.op("dve", lambda e: e.tensor_copy(out=UT[:], in_=pss[1][:, 0:128]), reads=[pss[1]], writes=[UT])
            for lv in range(1, 7):
                eng = "pool" if lv % 2 else "dve"
                P.op(eng, lambda e, lv=lv: e.tensor_tensor(out=CT[lv - 1][:], in0=UT[:], in1=mskT[:, lv - 1, :],
                                                           op=ALU.mult), reads=[UT, mskT], writes=[CT[lv - 1]])
            P.op("dve", lambda e: e.tensor_tensor(out=G32[:], in0=x0f[:], in1=msk0[:], op=ALU.mult),
                 reads=[x0f, msk0], writes=[G32])
            P.op("dve", lambda e: e.tensor_tensor(out=G32[:], in0=ident[:], in1=G32[:], op=ALU.subtract),
                 reads=[ident, G32], writes=[G32])
            P.op("act", lambda e: e.copy(out=Gm[:], in_=G32[:]), reads=[G32], writes=[Gm])
            P.op("pool", lambda e: e.tensor_copy(out=Gw[:], in_=G32[:]), reads=[G32], writes=[Gw])
            for lv in range(1, int(os.environ.get('DN_LVL', 7))):
                P.op("pe", lambda e, lv=lv: e.matmul(pX[:, 0:128], CT[lv - 1][:], Gm[:], start=True, stop=True),
                     reads=[CT[lv - 1], Gm], writes=[pX])
                P.op("pe", lambda e: e.transpose(ptr[:, 0:128], Gw[:], identb[:]), reads=[Gw, identb], writes=[ptr])
                P.op("act", lambda e: e.copy(out=Ysb[:], in_=pX[:, 0:128]), reads=[pX], writes=[Ysb])
                P.op("act", lambda e: e.copy(out=GTw[:], in_=ptr[:, 0:128]), reads=[ptr], writes=[GTw])
                P.op("pe", lambda e: e.matmul(pW[:], GTw[:], Ysb[:], start=True, stop=True),
                     reads=[GTw, Ysb], writes=[pW])
                P.op("dve", lambda e: e.tensor_tensor(out=G32[:], in0=G32[:], in1=pW[:], op=ALU.subtract),
                     reads=[G32, pW], writes=[G32])
                P.op("act", lambda e: e.copy(out=Gm[:], in_=G32[:]), reads=[G32], writes=[Gm])
                P.op("pool", lambda e: e.tensor_copy(out=Gw[:], in_=G32[:]), reads=[G32], writes=[Gw])
            Winv = Gw
            if int(os.environ.get('DN_STOP', 9)) <= 4:
                continue
            P.op("pe", lambda e, csl=csl: e.matmul(pks[:, 0:128], kT[:, csl], Sb[:], start=True, stop=True),
                 reads=[kT, Sb], writes=[pks])
            P.op("pe", lambda e, csl=csl: e.matmul(pks[:, 128:256], qT[:, csl], Sb[:], start=True, stop=True),
                 reads=[qT, Sb], writes=[pks])
            P.op("dve", lambda e, c=c: e.scalar_tensor_tensor(out=Rp[:], in0=pks[:, 0:128], scalar=col(tb["negegc"], c),
                                                              in1=KV[:, 128:256], op0=ALU.mult, op1=ALU.add),
                 reads=[pks, tb["negegc"], KV], writes=[Rp])
            P.op("pe", lambda e, Winv=Winv: e.matmul(pg[:], Winv[:], Rp[:], start=True, stop=True),
                 reads=[Winv, Rp], writes=[pg])
            P.op("dve", lambda e, bec=bec: e.tensor_scalar(out=vnew[:], in0=pg[:], scalar1=bec, scalar2=None,
                                                           op0=ALU.mult), reads=[pg, tb["beta"]], writes=[vnew])
            P.op("pe", lambda e: e.matmul(pW[:], AT[:], vnew[:], start=True, stop=True),
                 reads=[AT, vnew], writes=[pW])
            P.op("dve", lambda e, c=c: e.tensor_scalar(out=oq[:], in0=pks[:, 128:256], scalar1=col(tb["egc"], c),
                                                       scalar2=None, op0=ALU.mult), reads=[pks, tb["egc"]], writes=[oq])
            o = ot[c % 2]
            P.op("dve", lambda e, o=o: e.tensor_tensor(out=o[:], in0=pW[:], in1=oq[:], op=ALU.add),
                 reads=[pW, oq], writes=[o])
            P.store("sp", o, out_d[u, csl, :], o[:])
            P.op("pool", lambda e, c=c: e.tensor_scalar(out=Kd[:], in0=KV[:, 0:128], scalar1=col(tb["ekd"], c),
                                                        scalar2=None, op0=ALU.mult), reads=[KV, tb["ekd"]], writes=[Kd])
            P.op("pe", lambda e: e.matmul(pX[:, 0:128], Kd[:], vnew[:], start=True, stop=True),
                 reads=[Kd, vnew], writes=[pX])
            P.op("dve", lambda e, c=c: e.scalar_tensor_tensor(out=S[:], in0=S[:], scalar=col(tb["egl"], c),
                                                              in1=pX[:, 0:128], op0=ALU.mult, op1=ALU.add),
                 reads=[S, tb["egl"], pX], writes=[S])
            P.op("act", lambda e: e.copy(out=Sb[:], in_=S[:]), reads=[S], writes=[Sb])
    return P


DN_UNITS = [(h, d) for h in range(6) for d in range(2)]


def run_DN(hA, conv_w, a_log, dt_bias):
    P = build_DN()
    qkv = hA[:, 768:3072]
    da = hA[:, 4608:4620]
    db = hA[:, 4620:4632]
    ii = np.arange(128)
    triu = (ii[:, None] <= ii[None, :]).astype(np.float32)
    maskb = np.where(ii[None, :] >= ii[:, None], 0.0, -30000.0).astype(np.float32)
    msk0 = np.zeros((128, 128), np.float32)
    mskT = np.zeros((128, 6, 128), np.float32)
    for lv in range(7):
        b = 1 << lv
        jj, i2 = np.meshgrid(ii, ii, indexing="ij")
        m = ((jj // (2 * b) == i2 // (2 * b)) & (jj % (2 * b) < b) & (i2 % (2 * b) >= b)).astype(np.float32)
        if lv == 0:
            msk0 = m
        else:
            mskT[:, lv - 1, :] = m.T
    units = DN_UNITS + DN_UNITS[:4]
    maps = []
    for c in range(NCORES):
        xin = np.zeros((2, 3, 128, SEQ + 4), np.float32)
        cw = np.zeros((2, 128, 15), np.float32)
        ab = np.zeros((2, 128, 2, NDC), np.float32)
        hp = np.zeros((2, 128, 2), np.float32)
        for s in range(2):
            h, d = units[c * 2 + s]
            for ti in range(3):
                cs = slice(ti * 768 + h * 128, ti * 768 + (h + 1) * 128)
                xt = qkv[:, cs].T
                w = conv_w[cs]
                if d == 1:
                    xt = xt[:, ::-1]
                    w = w[:, ::-1]
                xin[s, ti, :, 2:2 + SEQ] = xt
                cw[s, :, ti * 5:(ti + 1) * 5] = w
            av = da[:, d * 6 + h]
            bv = db[:, d * 6 + h]
            if d == 1:
                av = av[::-1]
                bv = bv[::-1]
            ab[s, :, 0, :] = av.reshape(NDC, 128).T
            ab[s, :, 1, :] = bv.reshape(NDC, 128).T
            hp[s, :, 0] = a_log[d, h]
            hp[s, :, 1] = dt_bias[d, h]
        maps.append(dict(xin=xin, cw=cw, ab=ab, hp=hp, ident=np.eye(128, dtype=np.float32), triu=triu, maskb=maskb,
                         msk0=msk0, mskT=mskT))
    res = run(P, maps)
    of = np.zeros((SEQ, 768), np.float32)
    ob = np.zeros((SEQ, 768), np.float32)
    for idx, (h, d) in enumerate(DN_UNITS):
        o = res[idx // 2]["o"][idx % 2]
        if d == 0:
            of[:, h * 128:(h + 1) * 128] = o
        else:
            ob[:, h * 128:(h + 1) * 128] = o[::-1]
    return of, ob


COLS_Z = np.concatenate([np.arange(768, 1536), np.arange(3864, 4632), np.arange(6168, 7192),
                         np.arange(7704, 8216), np.arange(7192, 7704)])
NZ = 3584


def build_C1():
    P = Prog()
    x_d = P.dram_in("x", [TPC, D])
    gb_d = P.dram_in("gb", [128, D])
    id_d = P.dram_in("ident", [128, 128])
    wz_d = P.dram_in("wz", [D, NZ])
    mem_d = P.dram_in("mem", [256, D])
    mgb_d = P.dram_in("mgb", [128, D])
    wkv_d = P.dram_in("wkv", [D, 1024])
    s5_d = P.dram_in("s5", [3, 768, TPC])
    sd_d = P.dram_in("sd", [128, 16])
    wglu_d = P.dram_in("wglu", [768, 768])
    dn_d = P.dram_in("dn", [2, 768, TPC])
    oc_d = P.dram_in("oc", [1024, TPC])
    y_d = P.dram_out("yT", [24, 128, TPC], BF16)

    ident = P.sbuf("ident", [128, 128])
    gb = P.sbuf("gb", [128, D])
    sd = P.sbuf("sd", [128, 16])
    onesb = P.sbuf("onesb", [128, 128], BF16)
    P.load("sp", ident, ident[:], id_d)
    P.load("sp", gb, gb[:], gb_d)
    P.load("sp", sd, sd[:], sd_d)
    P.op("dve", lambda e: e.memset(onesb[:], 1.0), writes=[onesb])
    xnT = P.sbuf("xnT", [128, 16, TPC], BF16)
    nb = emit_norm_T(P, x_d, gb, ident, xnT, TPC // 128, "nC", single=True)
    memnT = P.sbuf("memnT", [128, 16, 256], BF16)
    P.load("sp", gb, gb[:], mgb_d)
    emit_norm_T(P, mem_d, gb, ident, memnT, 2, "nC", bufs=nb)

    pp = [P.psum(f"pp{i}", [128, 512]) for i in range(2)]
    pa = [P.psum(f"pa{i}", [128, 512]) for i in range(2)]
    po = P.psum("po", [128, 512])
    pd = P.psum("pd", [128, 512])

    wk = P.sbuf("wk", [128, 16, 512], BF16)
    KmT = P.sbuf("KmT", [128, 4, 256], BF16)
    Vm = P.sbuf("Vm", [128, 2, 512], BF16)
    wkv_v = wkv_d.rearrange("(c p) n -> p c n", p=128)
    P.dma("pool", [(wk[:, k, :], wkv_v[:, k, 0:512]) for k in range(16)], wk, writes=[wk])
    for h in range(4):
        ps = pp[h % 2]
        for k in range(16):
            P.op("pe", lambda e, ps=ps, k=k, h=h: e.matmul(ps[:, 0:256], wk[:, k, h * 128:(h + 1) * 128],
                                                          memnT[:, k, :], start=(k == 0), stop=(k == 15)),
                 reads=[wk, memnT], writes=[ps])
        P.op("act", lambda e, ps=ps, h=h: e.copy(out=KmT[:, h, :], in_=ps[:, 0:256]), reads=[ps], writes=[KmT])
    P.dma("pool", [(wk[:, k, :], wkv_v[:, k, 512:1024]) for k in range(16)], wk, writes=[wk])
    for mt in range(2):
        ps = pp[mt % 2]
        for k in range(16):
            P.op("pe", lambda e, ps=ps, k=k, mt=mt: e.matmul(ps[:], memnT[:, k, mt * 128:(mt + 1) * 128],
                                                            wk[:, k, :], start=(k == 0), stop=(k == 15)),
                 reads=[wk, memnT], writes=[ps])
        P.op("act", lambda e, ps=ps, mt=mt: e.copy(out=Vm[:, mt, :], in_=ps[:]), reads=[ps], writes=[Vm])

    wj = [P.sbuf(f"wj{i}", [128, 16, 128], BF16) for i in range(2)]
    sz = [P.sbuf(f"sz{i}", [128, TPC], BF16) for i in range(2)]
    wz_v = wz_d.rearrange("(c p) n -> p c n", p=128)
    cnt = {"w": 0, "p": 0}

    def projT(col0, dst_ap_fn, func, dst_buf):
        w = wj[cnt["w"] % 2]
        cnt["w"] += 1
        P.dma("pool", [(w[:, k, :], wz_v[:, k, col0:col0 + 128]) for k in range(16)], w, writes=[w])
        for half in range(2):
            ps = pp[cnt["p"] % 2]
            cnt["p"] += 1
            hs = slice(half * 512, (half + 1) * 512)
            for k in range(16):
                P.op("pe", lambda e, ps=ps, k=k, w=w, hs=hs: e.matmul(ps[:], w[:, k, :], xnT[:, k, hs],
                                                                     start=(k == 0), stop=(k == 15)),
                     reads=[w, xnT], writes=[ps])
            P.op("act", lambda e, ps=ps, hs=hs: e.activation(out=dst_ap_fn(hs), in_=ps[:], func=func),
                 reads=[ps], writes=[dst_buf])

    def proj_silu(col0):
        z = sz[cnt["w"] % 2]
        projT(col0, lambda hs, z=z: z[:, hs], AF.Silu, z)
        return z

    f1 = [P.sbuf(f"f1_{i}", [128, TPC]) for i in range(2)]
    f2 = [P.sbuf(f"f2_{i}", [128, TPC]) for i in range(2)]
    f3 = P.sbuf("f3", [128, TPC])
    f4 = P.sbuf("f4", [128, TPC])
    yo = [P.sbuf(f"yo{i}", [128, TPC], BF16) for i in range(2)]
    sqb = P.sbuf("sqb", [128, TPC], BF16)
    rn = P.sbuf("rn", [128, 512])
    sg = P.sbuf("sg", [128, 512])
    ycnt = {"n": 0}

    def next_yo():
        y = yo[ycnt["n"] % 2]
        ycnt["n"] += 1
        return y

    GY = P.sbuf("GY", [128, 6, TPC], BF16)
    wglu = P.sbuf("wglu", [128, 6, 768], BF16)
    wglu_v = wglu_d.rearrange("(c p) n -> p c n", p=128)
    P.dma("pool", [(wglu[:, k, :], wglu_v[:, k, :]) for k in range(6)], wglu, writes=[wglu])
    for j in range(6):
        a, b_ = f1[j % 2], f2[j % 2]
        rs = slice(j * 128, (j + 1) * 128)
        P.load("sp", a, a[:], s5_d[0, rs, :])
        P.load("sp", b_, b_[:], s5_d[1, rs, :])
        P.load("sp", f3, f3[:], s5_d[2, rs, :])
        P.op("pool", lambda e, a=a, b_=b_: e.tensor_tensor(out=a[:], in0=a[:], in1=b_[:], op=ALU.add),
             reads=[a, b_], writes=[a])
        P.op("dve", lambda e, a=a, j=j: e.scalar_tensor_tensor(out=a[:], in0=f3[:], scalar=sd[:, j:j + 1], in1=a[:],
                                                              op0=ALU.mult, op1=ALU.add), reads=[f3, sd, a], writes=[a])
        P.op("pool", lambda e, a=a, b_=b_: e.tensor_tensor(out=b_[:], in0=a[:], in1=a[:], op=ALU.mult),
             reads=[a], writes=[b_])
        P.op("dve", lambda e, b_=b_: e.tensor_scalar(out=b_[:], in0=b_[:], scalar1=0.044715, scalar2=1.0,
                                                     op0=ALU.mult, op1=ALU.add), reads=[b_], writes=[b_])
        P.op("pool", lambda e, a=a, b_=b_: e.tensor_tensor(out=b_[:], in0=b_[:], in1=a[:], op=ALU.mult),
             reads=[a, b_], writes=[b_])
        P.op("act", lambda e, b_=b_: e.activation(out=f4[:], in_=b_[:], func=AF.Sigmoid, scale=1.5957691216057308),
             reads=[b_], writes=[f4])
        P.op("dve", lambda e, a=a, j=j: e.tensor_tensor(out=GY[:, j, :], in0=a[:], in1=f4[:], op=ALU.mult),
             reads=[a, f4], writes=[GY])
    for j in range(6):
        z = proj_silu(0 + j * 128)
        y = next_yo()
        for half in range(2):
            hs = slice(half * 512, (half + 1) * 512)
            ps = pa[half]
            for k in range(6):
                P.op("pe", lambda e, ps=ps, k=k, j=j, hs=hs: e.matmul(ps[:], wglu[:, k, j * 128:(j + 1) * 128],
                                                                     GY[:, k, hs], start=(k == 0), stop=(k == 5)),
                     reads=[wglu, GY], writes=[ps])
            P.op("act", lambda e, ps=ps, j=j: e.activation(out=sg[:], in_=ps[:], func=AF.Sigmoid,
                                                           bias=sd[:, 6 + j:7 + j]), reads=[ps, sd], writes=[sg])
            P.op("dve", lambda e, j=j, hs=hs: e.tensor_tensor(out=sg[:], in0=sg[:], in1=GY[:, j, hs], op=ALU.mult),
                 reads=[sg, GY], writes=[sg])
            P.op("dve", lambda e, y=y, z=z, hs=hs: e.tensor_tensor(out=y[:, hs], in0=sg[:], in1=z[:, hs], op=ALU.mult),
                 reads=[sg, z], writes=[y])
        P.store("sp", y, y_d[j], y[:])
    for h in range(6):
        a, b_ = f1[h % 2], f2[h % 2]
        rs = slice(h * 128, (h + 1) * 128)
        P.load("sp", a, a[:], dn_d[0, rs, :])
        P.load("sp", b_, b_[:], dn_d[1, rs, :])
        P.op("pool", lambda e, a=a, b_=b_: e.tensor_tensor(out=a[:], in0=a[:], in1=b_[:], op=ALU.add),
             reads=[a, b_], writes=[a])
        P.op("pool", lambda e, a=a: e.tensor_tensor(out=sqb[:], in0=a[:], in1=a[:], op=ALU.mult),
             reads=[a], writes=[sqb])
        z = proj_silu(768 + h * 128)
        y = next_yo()
        for half in range(2):
            hs = slice(half * 512, (half + 1) * 512)
            ps = pa[half]
            P.op("pe", lambda e, ps=ps, hs=hs: e.matmul(ps[:], onesb[:], sqb[:, hs], start=True, stop=True),
                 reads=[onesb, sqb], writes=[ps])
            P.op("act", lambda e, ps=ps: e.activation(out=rn[:], in_=ps[:], func=AF.Sqrt, scale=1.0 / 128, bias=EPS),
                 reads=[ps], writes=[rn])
            P.op("dve", lambda e: e.reciprocal(out=rn[:], in_=rn[:]), reads=[rn], writes=[rn])
            P.op("dve", lambda e, a=a, hs=hs: e.scalar_tensor_tensor(out=rn[:], in0=a[:, hs], scalar=sd[:, 12:13],
                                                                    in1=rn[:], op0=ALU.mult, op1=ALU.mult),
                 reads=[a, sd, rn], writes=[rn])
            P.op("dve", lambda e, y=y, z=z, hs=hs: e.tensor_tensor(out=y[:, hs], in0=rn[:], in1=z[:, hs], op=ALU.mult),
                 reads=[rn, z], writes=[y])
        P.store("sp", y, y_d[6 + h], y[:])
    for c in range(8):
        a = f1[c % 2]
        P.load("sp", a, a[:], oc_d[c * 128:(c + 1) * 128, :])
        z = proj_silu(1536 + c * 128)
        y = next_yo()
        P.op("dve", lambda e, a=a, y=y, z=z: e.tensor_tensor(out=y[:], in0=a[:], in1=z[:], op=ALU.mult),
             reads=[a, z], writes=[y])
        P.store("sp", y, y_d[12 + c], y[:])
    QmT = P.sbuf("QmT", [128, TPC], BF16)
    pm = [P.sbuf(f"pm{i}", [128, 512], BF16) for i in range(2)]
    pc = 0
    for h in range(4):
        projT(3072 + h * 128, lambda hs: QmT[:, hs], AF.Copy, QmT)
        z = proj_silu(2560 + h * 128)
        y = next_yo()
        for half in range(2):
            hs = slice(half * 512, (half + 1) * 512)
            for mt in range(2):
                ps = pa[mt]
                p_ = pm[pc % 2]
                pc += 1
                P.op("pe", lambda e, ps=ps, h=h, mt=mt, hs=hs: e.matmul(ps[:], KmT[:, h, mt * 128:(mt + 1) * 128],
                                                                       QmT[:, hs], start=True, stop=True),
                     reads=[KmT, QmT], writes=[ps])
                P.op("act", lambda e, ps=ps, p_=p_: e.activation(out=p_[:], in_=ps[:], func=AF.Exp, scale=128.0 ** -0.5),
                     reads=[ps], writes=[p_])
                P.op("pe", lambda e, p_=p_, h=h, mt=mt: e.matmul(po[:], Vm[:, mt, h * 128:(h + 1) * 128], p_[:],
                                                                start=(mt == 0), stop=(mt == 1)),
                     reads=[Vm, p_], writes=[po])
                P.op("pe", lambda e, p_=p_, mt=mt: e.matmul(pd[:], onesb[:], p_[:], start=(mt == 0), stop=(mt == 1)),
                     reads=[onesb, p_], writes=[pd])
            P.op("dve", lambda e: e.reciprocal(out=rn[:], in_=pd[:]), reads=[pd], writes=[rn])
            P.op("dve", lambda e: e.tensor_tensor(out=rn[:], in0=po[:], in1=rn[:], op=ALU.mult),
                 reads=[po, rn], writes=[rn])
            P.op("dve", lambda e, y=y, z=z, hs=hs: e.tensor_tensor(out=y[:, hs], in0=rn[:], in1=z[:, hs], op=ALU.mult),
                 reads=[rn, z], writes=[y])
        P.store("sp", y, y_d[20 + h], y[:])
    return P


def run_C1(x, norm_g, w_in_l, mem, mem_g, w_kv, hA, yf, yb, ssm_d, w_glu, b_glu, of, ob, dn_g, yc):
    P = build_C1()
    sd = np.zeros((128, 16), np.float32)
    sd[:, 0:6] = np.asarray(ssm_d, np.float32).reshape(6, 128).T
    sd[:, 6:12] = np.asarray(b_glu, np.float32).reshape(6, 128).T
    sd[:, 12] = np.asarray(dn_g, np.float32)
    common = dict(gb=bcast_rows(norm_g), ident=np.eye(128, dtype=np.float32),
                  wz=np.ascontiguousarray(w_in_l[:, COLS_Z]), mem=np.ascontiguousarray(mem),
                  mgb=bcast_rows(mem_g), wkv=np.ascontiguousarray(w_kv), sd=sd,
                  wglu=np.ascontiguousarray(w_glu))
    maps = []
    for c in range(NCORES):
        ts = slice(c * TPC, (c + 1) * TPC)
        s5 = np.stack([yf[ts].T, yb[ts].T, hA[ts, 0:768].T]).astype(np.float32)
        dn = np.stack([of[ts].T, ob[ts].T]).astype(np.float32)
        maps.append(dict(common, x=np.ascontiguousarray(x[ts]), s5=np.ascontiguousarray(s5),
                         dn=np.ascontiguousarray(dn), oc=np.ascontiguousarray(yc[ts].T)))
    res = run(P, maps)
    return [r["yT"] for r in res]


BR_CHUNKS = [(0, 6), (6, 12), (12, 20), (20, 24)]


def build_C2():
    P = Prog()
    x_d = P.dram_in("x", [TPC, D])
    gb_d = P.dram_in("gb", [128, D])
    id_d = P.dram_in("ident", [128, 128])
    wg_d = P.dram_in("wg", [D, 4 * D])
    y_d = P.dram_in("yT", [24, 128, TPC], BF16)
    wbr_d = P.dram_in("wbr", [3072, D])
    wout_d = P.dram_in("wout", [D, D])
    out_d = P.dram_out("xnew", [TPC, D])

    ident = P.sbuf("ident", [128, 128])
    gb = P.sbuf("gb", [128, D])
    P.load("sp", ident, ident[:], id_d)
    P.load("sp", gb, gb[:], gb_d)
    xnT = P.sbuf("xnT", [128, 16, TPC], BF16)
    emit_norm_T(P, x_d, gb, ident, xnT, TPC // 128, "nD", single=True)
    Y = P.sbuf("Y", [128, 24, TPC], BF16)
    for k in range(24):
        P.dma("sp", [(Y[:, k, :], y_d[k])], Y, writes=[Y])
    mT = P.sbuf("mT", [128, 16, TPC], BF16)
    wbr = P.sbuf("wbr", [128, 24, 128], BF16)
    wg = [P.sbuf(f"wg{i}", [128, 16, 128], BF16) for i in range(2)]
    pb = [P.psum(f"pb{i}", [128, 512]) for i in range(2)]
    pg = [P.psum(f"pg{i}", [128, 512]) for i in range(2)]
    sg = [P.sbuf(f"sg{i}", [128, 512]) for i in range(2)]
    acc = P.sbuf("acc", [128, TPC])
    tmp = P.sbuf("tmpm", [128, 512])
    wbr_v = wbr_d.rearrange("(c p) n -> p c n", p=128)
    wg_v = wg_d.rearrange("(c p) n -> p c n", p=128)
    cnt = 0
    for j in range(16):
        js = slice(j * 128, (j + 1) * 128)
        P.dma("pool", [(wbr[:, k, :], wbr_v[:, k, js]) for k in range(24)], wbr, writes=[wbr])
        for b in range(4):
            w = wg[cnt % 2]
            g0 = b * D + j * 128
            P.dma("pool", [(w[:, k, :], wg_v[:, k, g0:g0 + 128]) for k in range(16)], w, writes=[w])
            k0, k1 = BR_CHUNKS[b]
            for half in range(2):
                hs = slice(half * 512, (half + 1) * 512)
                p1, p2, s_ = pb[cnt % 2], pg[cnt % 2], sg[cnt % 2]
                cnt += 1
                for k in range(16):
                    P.op("pe", lambda e, p2=p2, k=k, w=w, hs=hs: e.matmul(p2[:], w[:, k, :], xnT[:, k, hs],
                                                                         start=(k == 0), stop=(k == 15)),
                         reads=[w, xnT], writes=[p2])
                for k in range(k0, k1):
                    P.op("pe", lambda e, p1=p1, k=k, hs=hs, k0=k0, k1=k1: e.matmul(p1[:], wbr[:, k, :], Y[:, k, hs],
                                                                                  start=(k == k0), stop=(k == k1 - 1)),
                         reads=[wbr, Y], writes=[p1])
                P.op("act", lambda e, p2=p2, s_=s_: e.activation(out=s_[:], in_=p2[:], func=AF.Sigmoid),
                     reads=[p2], writes=[s_])
                if b == 0:
                    P.op("dve", lambda e, p1=p1, s_=s_, hs=hs: e.tensor_tensor(out=acc[:, hs], in0=p1[:], in1=s_[:],
                                                                              op=ALU.mult), reads=[p1, s_], writes=[acc])
                else:
                    P.op("dve", lambda e, p1=p1, s_=s_: e.tensor_tensor(out=tmp[:], in0=p1[:], in1=s_[:], op=ALU.mult),
                         reads=[p1, s_], writes=[tmp])
                    P.op("pool", lambda e, hs=hs: e.tensor_tensor(out=acc[:, hs], in0=acc[:, hs], in1=tmp[:],
                                                                  op=ALU.add), reads=[acc, tmp], writes=[acc])
        P.op("act", lambda e, j=j: e.copy(out=mT[:, j, :], in_=acc[:]), reads=[acc], writes=[mT])
    wo_view = Y[:, 0:8, :].rearrange("p a (b c) -> p (a b) c", c=512)
    wout_v = wout_d.rearrange("(c p) n -> p c n", p=128)
    xr = [P.sbuf(f"xr{i}", [128, 512]) for i in range(2)]
    ot = [P.sbuf(f"oo{i}", [128, 512]) for i in range(2)]
    cnt = 0
    for cb in range(4):
        cs = slice(cb * 512, (cb + 1) * 512)
        P.dma("pool", [(wo_view[:, k, :], wout_v[:, k, cs]) for k in range(16)], Y, writes=[Y])
        for i in range(TPC // 128):
            ps = pb[cnt % 2]
            r_ = xr[cnt % 2]
            o_ = ot[cnt % 2]
            cnt += 1
            ts = slice(i * 128, (i + 1) * 128)
            P.load("sp", r_, r_[:], x_d[ts, cs])
            for k in range(16):
                P.op("pe", lambda e, ps=ps, k=k, ts=ts: e.matmul(ps[:], mT[:, k, ts], wo_view[:, k, :],
                                                                start=(k == 0), stop=(k == 15)),
                     reads=[mT, Y], writes=[ps])
            P.op("dve", lambda e, ps=ps, r_=r_, o_=o_: e.tensor_tensor(out=o_[:], in0=ps[:], in1=r_[:], op=ALU.add),
                 reads=[ps, r_], writes=[o_])
            P.store("sp", o_, out_d[ts, cs], o_[:])
    return P


def run_C2(x, norm_g, w_in_l, yTs, w_br, w_o):
    P = build_C2()
    common = dict(gb=bcast_rows(norm_g), ident=np.eye(128, dtype=np.float32),
                  wg=np.ascontiguousarray(w_in_l[:, 8216:]), wbr=np.ascontiguousarray(w_br),
                  wout=np.ascontiguousarray(w_o))
    maps = [dict(common, x=np.ascontiguousarray(x[c * TPC:(c + 1) * TPC]), yT=yTs[c]) for c in range(NCORES)]
    res = run(P, maps)
    return np.concatenate([r["xnew"] for r in res], axis=0)


def build_F():
    P = Prog()
    x_d = P.dram_in("x", [TPC, D])
    gb_d = P.dram_in("gb", [128, D])
    out_d = P.dram_out("y", [TPC, D])
    gb = P.sbuf("gb", [128, D])
    P.load("sp", gb, gb[:], gb_d)
    xt = [P.sbuf(f"xt{i}", [128, D]) for i in range(2)]
    yt = [P.sbuf(f"yt{i}", [128, D]) for i in range(2)]
    junk = P.sbuf("junk", [128, D], BF16)
    ss = [P.sbuf(f"ss{i}", [128, 16]) for i in range(2)]
    for i in range(TPC // 128):
        b = i % 2
        ts = slice(i * 128, (i + 1) * 128)
        P.load("sp", xt[b], xt[b][:], x_d[ts, :])
        P.op("act", lambda e, b=b: e.activation(out=junk[:], in_=xt[b][:], func=AF.Square, accum_out=ss[b][:, 0:1]),
             reads=[xt[b]], writes=[junk, ss[b]])
        P.op("act", lambda e, b=b: e.activation(out=ss[b][:, 1:2], in_=ss[b][:, 0:1], func=AF.Sqrt, scale=1.0 / D,
                                                bias=EPS), reads=[ss[b]], writes=[ss[b]])
        P.op("dve", lambda e, b=b: e.reciprocal(out=ss[b][:, 1:2], in_=ss[b][:, 1:2]), reads=[ss[b]], writes=[ss[b]])
        P.op("dve", lambda e, b=b: e.scalar_tensor_tensor(out=yt[b][:], in0=xt[b][:], scalar=ss[b][:, 1:2], in1=gb[:],
                                                          op0=ALU.mult, op1=ALU.mult),
             reads=[xt[b], ss[b], gb], writes=[yt[b]])
        P.store("sp", yt[b], out_d[ts, :], yt[b][:])
    return P


def run_F(x, g):
    P = build_F()
    maps = [dict(x=np.ascontiguousarray(x[c * TPC:(c + 1) * TPC]), gb=bcast_rows(g)) for c in range(NCORES)]
    res = run(P, maps)
    return np.concatenate([r["y"] for r in res], axis=0)


def layer_forward(xs, L, inp):
    g = lambda k: np.asarray(inp[k][L], np.float32)
    w_in_l = g("w_in")
    hA = run_A(xs, g("norm_g"), w_in_l, g("attn_q_norm"), g("attn_k_norm"))
    yc = run_ATT(hA)
    yf, yb = run_S5(hA, g("ssm_a_re"), g("ssm_a_im"), g("ssm_log_step"), g("ssm_b_re"), g("ssm_b_im"),
                    g("ssm_c_re"), g("ssm_c_im"))
    of, ob = run_DN(hA, g("dn_conv"), g("dn_a_log"), g("dn_dt_bias"))
    yTs = run_C1(xs, g("norm_g"), w_in_l, np.asarray(inp["mem"], np.float32)[0], g("mem_norm_g"), g("w_mem_kv"),
                 hA, yf, yb, g("ssm_d"), g("ssm_w_glu"), g("ssm_b_glu"), of, ob, g("dn_norm_g"), yc)
    return run_C2(xs, g("norm_g"), w_in_l, yTs, g("w_branch"), g("w_out"))


def kernel(**inp):
    xs = np.asarray(inp["x"], np.float32)[0]
    for L in range(2):
        xs = layer_forward(xs, L, inp)
    out = run_F(xs, np.asarray(inp["final_norm_g"], np.float32))
    return out[None].astype(np.float32)
```

```python
from contextlib import ExitStack
import os
import numpy as np
import concourse.bass as bass
import concourse.mybir as mybir
from concourse.bass_utils import run_bass_kernel_spmd

F32 = mybir.dt.float32
BF16 = mybir.dt.bfloat16
I32 = mybir.dt.int32
AF = mybir.ActivationFunctionType
ALU = mybir.AluOpType
AX = mybir.AxisListType

NCORES = 8
D = 2048
SEQ = 8192
TPC = SEQ // NCORES
EPS = 1e-6
TWO_PI = float(2 * np.pi)


class Buf:
    def __init__(self, name, t=None):
        self.name = name
        self.t = t
        self.w = None
        self.r = []
        self.dsem = None
        self.dcnt = 0

    def __getitem__(self, k):
        return self.t[k]


class Prog:
    ENG = ("sp", "act", "dve", "pool", "pe")

    def __init__(self):
        self.nc = bass.Bass("TRN2", target_bir_lowering=False)
        self.ctx = ExitStack()
        self.streams = {e: [] for e in self.ENG}
        self.seq = {e: 0 for e in self.ENG}
        self.esem = {e: self.ctx.enter_context(self.nc.semaphore("es_" + e)) for e in self.ENG}
        self.store_bufs = []
        self.nid = 0

    def dram_in(self, name, shape, dt=F32):
        return self.nc.dram_tensor(name, list(shape), dt, kind="ExternalInput").ap()

    def dram_out(self, name, shape, dt=F32):
        return self.nc.dram_tensor(name, list(shape), dt, kind="ExternalOutput").ap()

    def sbuf(self, name, shape, dt=F32):
        t = self.ctx.enter_context(self.nc.sbuf_tensor("sb_" + name, list(shape), dt))
        esz = 2 if dt == BF16 else 4
        nbytes = int(np.prod(shape[1:])) * esz
        rem = (-nbytes) % 64
        if rem > 32:
            self.ctx.enter_context(self.nc.sbuf_tensor("pad_" + name, [shape[0], 8], F32))
        elif 0 < rem <= 32 and ((nbytes + 31) // 32 * 32) % 64 != 0:
            self.ctx.enter_context(self.nc.sbuf_tensor("pad_" + name, [shape[0], 8], F32))
        return Buf(name, t)

    def psum(self, name, shape, dt=F32):
        t = self.ctx.enter_context(self.nc.psum_tensor("ps_" + name, list(shape), dt))
        return Buf(name, t)

    def _deps(self, reads, writes):
        toks = []
        for b in reads:
            if b.w is not None:
                toks.append((b.w[0], b.w[1], "raw:" + str(b.w[2])))
        for b in writes:
            if b.w is not None:
                toks.append(b.w)
            toks.extend(b.r)
        return toks

    def op(self, eng, fn, reads=(), writes=()):
        toks = self._deps(reads, writes)
        self.seq[eng] += 1
        tok = (self.esem[eng], self.seq[eng], eng)
        self.streams[eng].append((toks, fn, (self.esem[eng], 1)))
        for b in reads:
            b.r.append(tok)
        for b in writes:
            b.w = tok
            b.r = []

    def dma(self, eng, pairs, owner, reads=(), writes=()):
        if owner.dsem is None:
            self.nid += 1
            owner.dsem = self.ctx.enter_context(self.nc.semaphore("ds%d" % self.nid))
        toks = self._deps(reads, writes)
        owner.dcnt += len(pairs)
        tok = (owner.dsem, 16 * owner.dcnt, "dma")

        def fn(e, pairs=pairs):
            return [e.dma_start(out=o, in_=i) for (o, i) in pairs]

        self.streams[eng].append((toks, fn, (owner.dsem, 16)))
        for b in reads:
            b.r.append(tok)
        for b in writes:
            b.w = tok
            b.r = []

    def load(self, eng, buf, out_ap, in_ap):
        self.dma(eng, [(out_ap, in_ap)], buf, writes=[buf])

    def store(self, eng, buf, out_ap, in_ap):
        if buf not in self.store_bufs:
            self.store_bufs.append(buf)
        self.dma(eng, [(out_ap, in_ap)], buf, reads=[buf])

    def finish(self):
        final = [(b.dsem, 16 * b.dcnt, "dma") for b in self.store_bufs]
        self.streams["sp"].append((final, None, None))
        streams = self.streams

        def emit(name, e):
            waited = {}
            for toks, fn, inc in streams[name]:
                for (sem, val, teng) in toks:
                    if teng == name or (teng == "raw:" + name and name == "pe"):
                        continue
                    k = id(sem)
                    if waited.get(k, 0) >= val:
                        continue
                    e.wait_ge(sem, val)
                    waited[k] = val
                if fn is None:
                    continue
                ins = fn(e)
                if isinstance(ins, list):
                    for i_ in ins:
                        i_.then_inc(inc[0], inc[1])
                else:
                    ins.then_inc(inc[0], inc[1])

        with self.nc.Block() as block:
            @block.sync
            def _(e):
                emit("sp", e)

            @block.scalar
            def _(e):
                emit("act", e)

            @block.vector
            def _(e):
                emit("dve", e)

            @block.gpsimd
            def _(e):
                emit("pool", e)

            @block.tensor
            def _(e):
                emit("pe", e)
        self.ctx.close()
        return self.nc


def run(prog, in_maps):
    nc = prog.finish()
    n = int(os.environ.get("DBG_CORES", NCORES))
    if os.environ.get("DBG_TRACE"):
        res = run_bass_kernel_spmd(nc, in_maps[:n], core_ids=list(range(n)), trace=True)
        print("DBG_TRACE exec_time_ns", res.exec_time_ns, flush=True)
    else:
        res = run_bass_kernel_spmd(nc, in_maps[:n], core_ids=list(range(n)))
    out = list(res.results)
    while len(out) < NCORES:
        out.append(out[0])
    return out


def emit_norm_T(P, x_dram, gb, ident, xnT, ntiles, tag, single=False, bufs=None):
    if bufs is None:
        nb = 1 if single else 2
        bufs = dict(xt=[P.sbuf(f"{tag}_xt{i}", [128, D]) for i in range(nb)],
                    junk=P.sbuf(f"{tag}_junk", [128, D], BF16),
                    xn=[P.sbuf(f"{tag}_xn{i}", [128, D]) for i in range(nb)],
                    ss=[P.sbuf(f"{tag}_ss{i}", [128, 16]) for i in range(nb)],
                    tp=[P.psum(f"{tag}_tp{i}", [128, 512]) for i in range(2)])
    xt, junk, xn, ss, tp = bufs["xt"], bufs["junk"], bufs["xn"], bufs["ss"], bufs["tp"]
    nb = len(xt)
    tcount = 0
    for i in range(ntiles):
        b = i % nb
        P.load("sp", xt[b], xt[b][:], x_dram[i * 128:(i + 1) * 128, :])
        P.op("dve", lambda e, b=b: e.memset(ss[b][:], 0.0), writes=[ss[b]])
        P.op("act", lambda e, b=b: e.activation(out=junk[:], in_=xt[b][:], func=AF.Square,
                                                accum_out=ss[b][:, 0:1]),
             reads=[xt[b]], writes=[junk, ss[b]])
        P.op("act", lambda e, b=b: e.activation(out=ss[b][:, 1:2], in_=ss[b][:, 0:1], func=AF.Sqrt,
                                                scale=1.0 / D, bias=EPS),
             reads=[ss[b]], writes=[ss[b]])
        P.op("dve", lambda e, b=b: e.reciprocal(out=ss[b][:, 1:2], in_=ss[b][:, 1:2]),
             reads=[ss[b]], writes=[ss[b]])
        P.op("dve", lambda e, b=b: e.scalar_tensor_tensor(out=xn[b][:], in0=xt[b][:], scalar=ss[b][:, 1:2],
                                                          in1=gb[:], op0=ALU.mult, op1=ALU.mult),
             reads=[xt[b], ss[b], gb], writes=[xn[b]])
        for kk in range(4):
            pb = tp[tcount % 2]
            tcount += 1
            for j in range(4):
                k = kk * 4 + j
                P.op("pe", lambda e, b=b, k=k, j=j, pb=pb: e.transpose(pb[:, j * 128:(j + 1) * 128],
                                                                      xn[b][:, k * 128:(k + 1) * 128], ident[:]),
                     reads=[xn[b], ident], writes=[pb])
            eng = "act" if kk % 2 == 0 else "dve"
            if eng == "act":
                P.op("act", lambda e, kk=kk, i=i, pb=pb: e.copy(
                    out=xnT[:, kk * 4:(kk + 1) * 4, i * 128:(i + 1) * 128],
                    in_=pb[:].rearrange("p (a b) -> p a b", a=4)), reads=[pb], writes=[xnT])
            else:
                P.op("dve", lambda e, kk=kk, i=i, pb=pb: e.tensor_copy(
                    out=xnT[:, kk * 4:(kk + 1) * 4, i * 128:(i + 1) * 128],
                    in_=pb[:].rearrange("p (a b) -> p a b", a=4)), reads=[pb], writes=[xnT])

    return bufs


NA = 4632
NA_MAIN = 4608


def build_A():
    P = Prog()
    x_d = P.dram_in("x", [TPC, D])
    gb_d = P.dram_in("gb", [128, D])
    w_d = P.dram_in("wA", [D, NA])
    id_d = P.dram_in("ident", [128, 128])
    pos_d = P.dram_in("pos", [128, TPC // 128, 64])
    frq_d = P.dram_in("frq", [128, 64])
    qg_d = P.dram_in("qg", [128, 128])
    kg_d = P.dram_in("kg", [128, 128])
    out_d = P.dram_out("hA", [TPC, NA])
    NT = TPC // 128

    ident = P.sbuf("ident", [128, 128])
    gb = P.sbuf("gb", [128, D])
    qg = P.sbuf("qg", [128, 128])
    kg = P.sbuf("kg", [128, 128])
    pos = P.sbuf("pos", [128, NT, 64])
    frq = P.sbuf("frq", [128, 64])
    P.load("sp", ident, ident[:], id_d)
    P.load("sp", gb, gb[:], gb_d)
    P.load("sp", qg, qg[:], qg_d)
    P.load("sp", kg, kg[:], kg_d)
    P.load("sp", pos, pos[:], pos_d)
    P.load("sp", frq, frq[:], frq_d)

    ang = P.sbuf("ang", [128, NT, 64])
    tmpf = P.sbuf("tmpf", [128, NT, 64])
    tmpi = P.sbuf("tmpi", [128, NT, 64], I32)
    cosT = P.sbuf("cosT", [128, NT, 64])
    sinT = P.sbuf("sinT", [128, NT, 64])
    P.op("dve", lambda e: e.tensor_tensor(out=ang[:], in0=pos[:], in1=frq[:].unsqueeze(1).to_broadcast([128, NT, 64]),
                                          op=ALU.mult), reads=[pos, frq], writes=[ang])
    for (dst, shift) in ((sinT, 0.0), (cosT, float(np.pi / 2))):
        if shift != 0.0:
            P.op("dve", lambda e, shift=shift: e.tensor_scalar(out=ang[:], in0=ang[:], scalar1=shift, scalar2=None,
                                                                op0=ALU.add), reads=[ang], writes=[ang])
        P.op("dve", lambda e: e.tensor_scalar(out=tmpf[:], in0=ang[:], scalar1=1.0 / TWO_PI, scalar2=None,
                                              op0=ALU.mult), reads=[ang], writes=[tmpf])
        P.op("dve", lambda e: e.tensor_copy(out=tmpi[:], in_=tmpf[:]), reads=[tmpf], writes=[tmpi])
        P.op("dve", lambda e: e.tensor_copy(out=tmpf[:], in_=tmpi[:]), reads=[tmpi], writes=[tmpf])
        P.op("dve", lambda e: e.scalar_tensor_tensor(out=tmpf[:], in0=tmpf[:], scalar=-TWO_PI, in1=ang[:],
                                                     op0=ALU.mult, op1=ALU.add), reads=[tmpf, ang], writes=[tmpf])
        P.op("act", lambda e, dst=dst: e.activation(out=dst[:], in_=tmpf[:], func=AF.Sin),
             reads=[tmpf], writes=[dst])

    xnT = P.sbuf("xnT", [128, 16, TPC], BF16)
    emit_norm_T(P, x_d, gb, ident, xnT, NT, "nA")

    wblk = [P.sbuf(f"wblk{i}", [128, 16, 512], BF16) for i in range(2)]
    pp = [P.psum(f"pp{i}", [128, 512]) for i in range(2)]
    ot = [P.sbuf(f"ot{i}", [128, 512]) for i in range(3)]
    ss4 = P.sbuf("ss4", [128, 8])
    junk2 = P.sbuf("junk2", [128, 128])
    t1 = P.sbuf("rt1", [128, 4, 64])
    t2 = P.sbuf("rt2", [128, 4, 64])
    qn = P.sbuf("qn", [128, 512])
    w_v = w_d.rearrange("(c p) n -> p c n", p=128)
    nblk = 10
    cnt = 0
    for cb in range(nblk):
        wb = wblk[cb % 2]
        c0 = cb * 512
        ncol = 512 if cb < 9 else NA - NA_MAIN
        P.dma("pool", [(wb[:, k, 0:ncol], w_v[:, k, c0:c0 + ncol]) for k in range(16)], wb, writes=[wb])
        for i in range(NT):
            ps = pp[cnt % 2]
            o = ot[cnt % 3]
            cnt += 1
            for k in range(16):
                P.op("pe", lambda e, ps=ps, k=k, i=i, wb=wb, ncol=ncol: e.matmul(
                    ps[:, 0:ncol], xnT[:, k, i * 128:(i + 1) * 128], wb[:, k, 0:ncol],
                    start=(k == 0), stop=(k == 15)), reads=[xnT, wb], writes=[ps])
            if cb in (6, 7) or cb == 8:
                nh = 4 if cb in (6, 7) else 2
                g = qg if cb in (6, 7) else kg
                sc = (128.0 ** -0.5) if cb in (6, 7) else 1.0
                P.op("dve", lambda e: e.memset(ss4[:], 0.0), writes=[ss4])
                for h in range(nh):
                    P.op("act", lambda e, ps=ps, h=h: e.activation(out=junk2[:], in_=ps[:, h * 128:(h + 1) * 128],
                                                                   func=AF.Square, accum_out=ss4[:, h:h + 1]),
                         reads=[ps], writes=[junk2, ss4])
                P.op("act", lambda e, nh=nh: e.activation(out=ss4[:, 4:4 + nh], in_=ss4[:, 0:nh], func=AF.Sqrt,
                                                          scale=1.0 / 128, bias=EPS), reads=[ss4], writes=[ss4])
                P.op("dve", lambda e, nh=nh: e.reciprocal(out=ss4[:, 4:4 + nh], in_=ss4[:, 4:4 + nh]),
                     reads=[ss4], writes=[ss4])
                if sc != 1.0:
                    P.op("dve", lambda e, nh=nh, sc=sc: e.tensor_scalar(out=ss4[:, 4:4 + nh], in0=ss4[:, 4:4 + nh],
                                                                        scalar1=sc, scalar2=None, op0=ALU.mult),
                         reads=[ss4], writes=[ss4])
                for h in range(nh):
                    P.op("dve", lambda e, ps=ps, h=h, g=g: e.scalar_tensor_tensor(
                        out=qn[:, h * 128:(h + 1) * 128], in0=ps[:, h * 128:(h + 1) * 128],
                        scalar=ss4[:, 4 + h:5 + h], in1=g[:], op0=ALU.mult, op1=ALU.mult),
                         reads=[ps, ss4, g], writes=[qn])
                if nh < 4:
                    P.op("act", lambda e, ps=ps, o=o: e.copy(out=o[:, 256:512], in_=ps[:, 256:512]),
                         reads=[ps], writes=[o])
                W = nh * 128
                qv = qn[:, 0:W].rearrange("p (h i two) -> p h i two", h=nh, two=2)
                ov = o[:, 0:W].rearrange("p (h i two) -> p h i two", h=nh, two=2)
                x0 = qv[:, :, :, 0]
                x1 = qv[:, :, :, 1]
                cb_ = cosT[:, i, :].unsqueeze(1).to_broadcast([128, nh, 64])
                sb_ = sinT[:, i, :].unsqueeze(1).to_broadcast([128, nh, 64])
                a1 = t1[:, 0:nh, :]
                a2 = t2[:, 0:nh, :]
                P.op("dve", lambda e, x0=x0, cb_=cb_, a1=a1: e.tensor_tensor(out=a1, in0=x0, in1=cb_, op=ALU.mult),
                     reads=[qn, cosT], writes=[t1])
                P.op("pool", lambda e, x1=x1, sb_=sb_, a2=a2: e.tensor_tensor(out=a2, in0=x1, in1=sb_, op=ALU.mult),
                     reads=[qn, sinT], writes=[t2])
                P.op("dve", lambda e, ov=ov, a1=a1, a2=a2: e.tensor_tensor(out=ov[:, :, :, 0], in0=a1, in1=a2,
                                                                          op=ALU.subtract),
                     reads=[t1, t2], writes=[o])
                P.op("dve", lambda e, x0=x0, sb_=sb_, a1=a1: e.tensor_tensor(out=a1, in0=x0, in1=sb_, op=ALU.mult),
                     reads=[qn, sinT], writes=[t1])
                P.op("pool", lambda e, x1=x1, cb_=cb_, a2=a2: e.tensor_tensor(out=a2, in0=x1, in1=cb_, op=ALU.mult),
                     reads=[qn, cosT], writes=[t2])
                P.op("dve", lambda e, ov=ov, a1=a1, a2=a2: e.tensor_tensor(out=ov[:, :, :, 1], in0=a1, in1=a2,
                                                                          op=ALU.add),
                     reads=[t1, t2], writes=[o])
            else:
                if cnt % 2 == 0:
                    P.op("act", lambda e, ps=ps, o=o, ncol=ncol: e.copy(out=o[:, 0:ncol], in_=ps[:, 0:ncol]),
                         reads=[ps], writes=[o])
                else:
                    P.op("dve", lambda e, ps=ps, o=o, ncol=ncol: e.tensor_copy(out=o[:, 0:ncol], in_=ps[:, 0:ncol]),
                         reads=[ps], writes=[o])
            P.store("sp", o, out_d[i * 128:(i + 1) * 128, c0:c0 + ncol], o[:, 0:ncol])
    return P


COLS_A = np.concatenate([
    np.arange(0, 768),
    np.arange(1536, 3840),
    np.arange(4632, 6168),
    np.arange(3840, 3864),
])


def rope_consts():
    t = np.arange(SEQ)
    row = (t // 64).astype(np.float32)
    col = (t % 64).astype(np.float32)
    pos = np.concatenate([np.repeat(row[:, None], 32, 1), np.repeat(col[:, None], 32, 1)], axis=1)
    freqs = (10000.0 ** (-np.arange(0, 64, 2, dtype=np.float32) / 64)).astype(np.float32)
    frq = np.concatenate([freqs, freqs])[None, :].repeat(128, 0).astype(np.float32)
    return pos.astype(np.float32), frq


def bcast_rows(v, n=128):
    return np.ascontiguousarray(np.broadcast_to(np.asarray(v, np.float32)[None, :], (n, v.shape[0])))


def run_A(x, norm_g, w_in_l, qn_g, kn_g):
    P = build_A()
    pos, frq = rope_consts()
    wA = np.ascontiguousarray(w_in_l[:, COLS_A])
    common = dict(gb=bcast_rows(norm_g), wA=wA, ident=np.eye(128, dtype=np.float32), frq=frq,
                  qg=bcast_rows(qn_g), kg=bcast_rows(kn_g))
    maps = []
    for c in range(NCORES):
        pc = pos[c * TPC:(c + 1) * TPC].reshape(TPC // 128, 128, 64).transpose(1, 0, 2)
        maps.append(dict(common, x=np.ascontiguousarray(x[c * TPC:(c + 1) * TPC]), pos=np.ascontiguousarray(pc)))
    res = run(P, maps)
    return np.concatenate([r["hA"] for r in res], axis=0)


def build_ATT():
    P = Prog()
    qT_d = P.dram_in("qT", [128, SEQ])
    kT_d = P.dram_in("kT", [128, SEQ])
    v_d = P.dram_in("v", [128, SEQ // 128, 128])
    out_d = P.dram_out("oT", [128, SEQ])
    qT = P.sbuf("qT", [128, SEQ], BF16)
    kT = P.sbuf("kT", [128, SEQ], BF16)
    v = P.sbuf("v", [128, SEQ // 128, 128], BF16)
    ones = P.sbuf("ones", [128, 128], BF16)
    P.op("dve", lambda e: e.memset(ones[:], 1.0), writes=[ones])
    for j in range(4):
        sl = slice(j * 2048, (j + 1) * 2048)
        P.dma("pool", [(kT[:, sl], kT_d[:, sl])], kT, writes=[kT])
        P.dma("pool", [(qT[:, sl], qT_d[:, sl])], qT, writes=[qT])
        P.dma("pool", [(v[:, j * 16:(j + 1) * 16, :], v_d[:, j * 16:(j + 1) * 16, :])], v, writes=[v])
    ps_s = [P.psum(f"s{i}", [128, 512]) for i in range(2)]
    ps_o = [P.psum(f"o{i}", [128, 512]) for i in range(2)]
    ps_d = [P.psum(f"d{i}", [128, 512]) for i in range(2)]
    pt = [P.sbuf(f"pt{i}", [128, 512], BF16) for i in range(3)]
    rd = P.sbuf("rd", [128, 512])
    ot = [P.sbuf(f"ot{i}", [128, 512]) for i in range(2)]
    NKT = SEQ // 128
    NQB = SEQ // 512
    ps_s = ps_s + [P.psum("s2", [128, 512])]
    steps = [(qb, kt) for qb in range(NQB) for kt in range(NKT)]

    def emit_S(idx):
        qb, kt = steps[idx]
        s_ = ps_s[idx % 3]
        P.op("pe", lambda e, s_=s_, kt=kt, qb=qb: e.matmul(s_[:], kT[:, kt * 128:(kt + 1) * 128],
                                                        qT[:, qb * 512:(qb + 1) * 512], start=True, stop=True),
             reads=[kT, qT], writes=[s_])

    emit_S(0)
    emit_S(1)
    for idx, (qb, kt) in enumerate(steps):
        po = ps_o[qb % 2]
        pd = ps_d[qb % 2]
        s_ = ps_s[idx % 3]
        p = pt[idx % 3]
        P.op("act", lambda e, s_=s_, p=p: e.activation(out=p[:], in_=s_[:], func=AF.Exp), reads=[s_], writes=[p])
        if idx + 2 < len(steps):
            emit_S(idx + 2)
        P.op("pe", lambda e, po=po, p=p, kt=kt: e.matmul(po[:], v[:, kt, :], p[:], start=(kt == 0),
                                                      stop=(kt == NKT - 1)), reads=[v, p], writes=[po])
        P.op("pe", lambda e, pd=pd, p=p, kt=kt: e.matmul(pd[:], ones[:], p[:], start=(kt == 0),
                                                      stop=(kt == NKT - 1)), reads=[ones, p], writes=[pd])
        if kt == NKT - 1:
            o = ot[qb % 2]
            P.op("dve", lambda e, pd=pd: e.reciprocal(out=rd[:], in_=pd[:]), reads=[pd], writes=[rd])
            P.op("dve", lambda e, po=po, o=o: e.tensor_tensor(out=o[:], in0=po[:], in1=rd[:], op=ALU.mult),
                 reads=[po, rd], writes=[o])
            P.store("sp", o, out_d[:, qb * 512:(qb + 1) * 512], o[:])
    return P


def run_ATT(hA):
    P = build_ATT()
    q = hA[:, 3072:4096]
    k = hA[:, 4096:4352]
    vv = hA[:, 4352:4608]
    maps = []
    for c in range(NCORES):
        kv = c // 4
        maps.append(dict(
            qT=np.ascontiguousarray(q[:, c * 128:(c + 1) * 128].T),
            kT=np.ascontiguousarray(k[:, kv * 128:(kv + 1) * 128].T),
            v=np.ascontiguousarray(vv[:, kv * 128:(kv + 1) * 128].reshape(SEQ // 128, 128, 128).transpose(1, 0, 2)),
        ))
    res = run(P, maps)
    return np.concatenate([r["oT"].T for r in res], axis=1)


S5C = 512


def _range_reduce_sin(P, dst, src, tmpf, tmpi, shift):
    P.op("dve", lambda e: e.tensor_scalar(out=tmpf[:], in0=src[:], scalar1=shift, scalar2=1.0 / TWO_PI,
                                          op0=ALU.add, op1=ALU.mult), reads=[src], writes=[tmpf])
    P.op("dve", lambda e: e.tensor_copy(out=tmpi[:], in_=tmpf[:]), reads=[tmpf], writes=[tmpi])
    P.op("dve", lambda e: e.tensor_copy(out=tmpf[:], in_=tmpi[:]), reads=[tmpi], writes=[tmpf])
    P.op("dve", lambda e: e.scalar_tensor_tensor(out=tmpf[:], in0=tmpf[:], scalar=-TWO_PI, in1=src[:],
                                                 op0=ALU.mult, op1=ALU.add), reads=[tmpf, src], writes=[tmpf])
    if shift != 0.0:
        P.op("dve", lambda e: e.tensor_scalar(out=tmpf[:], in0=tmpf[:], scalar1=shift, scalar2=None, op0=ALU.add),
             reads=[tmpf], writes=[tmpf])
    P.op("act", lambda e: e.activation(out=dst[:], in_=tmpf[:], func=AF.Sin), reads=[tmpf], writes=[dst])


def build_S5():
    P = Prog()
    NU = 6
    NCH = SEQ // S5C
    uT_d = P.dram_in("uT", [NU, 32, SEQ])
    prm_d = P.dram_in("prm", [NU, 128, 3])
    b_d = P.dram_in("bmat", [NU, 128, 64])
    c_d = P.dram_in("cmat", [NU, 128, 64])
    jidx_d = P.dram_in("jidx", [128, S5C + 1])
    id_d = P.dram_in("ident", [128, 128])
    out_d = P.dram_out("yT", [NU, 32, SEQ])

    ident = P.sbuf("ident", [128, 128])
    jidx = P.sbuf("jidx", [128, S5C + 1])
    P.load("sp", ident, ident[:], id_d)
    P.load("sp", jidx, jidx[:], jidx_d)
    prm = P.sbuf("prm", [128, 3])
    bm = P.sbuf("bm", [128, 64])
    cm = P.sbuf("cm", [128, 64])
    sc = P.sbuf("sc", [128, 16])
    s1f = P.sbuf("s1f", [128, 1])
    s1i = P.sbuf("s1i", [128, 1], I32)
    th = P.sbuf("th", [128, 1])
    ph = P.sbuf("ph", [128, S5C + 1])
    tmpf = P.sbuf("tmpf", [128, S5C + 1])
    tmpi = P.sbuf("tmpi", [128, S5C + 1], I32)
    Pc = P.sbuf("Pc", [128, S5C + 1])
    Ps = P.sbuf("Ps", [128, S5C + 1])
    rt = P.sbuf("rt", [128, S5C])
    bb = P.sbuf("bb", [128, 64])
    bt1 = P.sbuf("bt1", [128, 32])
    BT = P.sbuf("BT", [32, 256], BF16)
    CT = P.sbuf("CT", [128, 96], BF16)
    pst = P.psum("pst", [32, 256])
    ps_re = [P.psum(f"psre{i}", [128, S5C]) for i in range(2)]
    ps_im = [P.psum(f"psim{i}", [128, S5C]) for i in range(2)]
    ps_y = [P.psum(f"psy{i}", [32, S5C]) for i in range(2)]
    ut = [P.sbuf(f"ut{i}", [32, S5C], BF16) for i in range(2)]
    m = [P.sbuf(f"m{i}", [128, S5C]) for i in range(4)]
    cre = P.sbuf("cre_", [128, S5C])
    cim = P.sbuf("cim_", [128, S5C])
    zre = [P.sbuf(f"zre{i}", [128, S5C]) for i in range(2)]
    zim = [P.sbuf(f"zim{i}", [128, S5C]) for i in range(2)]
    nn = [P.sbuf(f"nn{i}", [128, S5C], BF16) for i in range(4)]
    init = P.sbuf("init", [128, 4])
    yt = [P.sbuf(f"yt{i}", [32, S5C]) for i in range(2)]

    def col(t, j):
        return t[:, j:j + 1]

    for u in range(NU):
        P.load("sp", prm, prm[:], prm_d[u])
        P.load("sp", bm, bm[:], b_d[u])
        P.load("sp", cm, cm[:], c_d[u])
        P.op("act", lambda e: e.activation(out=col(sc, 0), in_=col(prm, 2), func=AF.Exp), reads=[prm], writes=[sc])
        P.op("dve", lambda e: e.tensor_tensor(out=col(sc, 10), in0=col(prm, 0), in1=col(sc, 0), op=ALU.mult),
             reads=[prm, sc], writes=[sc])
        P.op("act", lambda e: e.activation(out=col(sc, 1), in_=col(sc, 10), func=AF.Exp), reads=[sc], writes=[sc])
        P.op("dve", lambda e: e.tensor_tensor(out=col(sc, 2), in0=col(prm, 1), in1=col(sc, 0), op=ALU.mult),
             reads=[prm, sc], writes=[sc])
        P.op("dve", lambda e: e.tensor_scalar(out=s1f[:], in0=col(sc, 2), scalar1=1.0 / TWO_PI, scalar2=None,
                                              op0=ALU.mult), reads=[sc], writes=[s1f])
        P.op("dve", lambda e: e.tensor_copy(out=s1i[:], in_=s1f[:]), reads=[s1f], writes=[s1i])
        P.op("dve", lambda e: e.tensor_copy(out=s1f[:], in_=s1i[:]), reads=[s1i], writes=[s1f])
        P.op("dve", lambda e: e.scalar_tensor_tensor(out=th[:], in0=s1f[:], scalar=-TWO_PI, in1=col(sc, 2),
                                                     op0=ALU.mult, op1=ALU.add), reads=[s1f, sc], writes=[th])
        P.op("dve", lambda e: e.tensor_scalar(out=ph[:], in0=jidx[:], scalar1=th[:, 0:1], scalar2=None,
                                              op0=ALU.mult), reads=[jidx, th], writes=[ph])
        _range_reduce_sin(P, Ps, ph, tmpf, tmpi, 0.0)
        _range_reduce_sin(P, Pc, ph, tmpf, tmpi, float(np.pi / 2))
        P.op("dve", lambda e: e.tensor_tensor(out=col(sc, 5), in0=col(sc, 1), in1=col(Pc, 1), op=ALU.mult),
             reads=[sc, Pc], writes=[sc])
        P.op("dve", lambda e: e.tensor_scalar(out=col(sc, 5), in0=col(sc, 5), scalar1=-1.0, scalar2=None,
                                              op0=ALU.add), reads=[sc], writes=[sc])
        P.op("dve", lambda e: e.tensor_tensor(out=col(sc, 6), in0=col(sc, 1), in1=col(Ps, 1), op=ALU.mult),
             reads=[sc, Ps], writes=[sc])
        P.op("dve", lambda e: e.tensor_tensor(out=col(sc, 7), in0=col(prm, 0), in1=col(prm, 0), op=ALU.mult),
             reads=[prm], writes=[sc])
        P.op("dve", lambda e: e.scalar_tensor_tensor(out=col(sc, 7), in0=col(prm, 1), scalar=col(prm, 1),
                                                     in1=col(sc, 7), op0=ALU.mult, op1=ALU.add),
             reads=[prm, sc], writes=[sc])
        P.op("dve", lambda e: e.reciprocal(out=col(sc, 7), in_=col(sc, 7)), reads=[sc], writes=[sc])
        P.op("dve", lambda e: e.tensor_tensor(out=col(sc, 10), in0=col(sc, 5), in1=col(prm, 0), op=ALU.mult),
             reads=[sc, prm], writes=[sc])
        P.op("dve", lambda e: e.scalar_tensor_tensor(out=col(sc, 10), in0=col(sc, 6), scalar=col(prm, 1),
                                                     in1=col(sc, 10), op0=ALU.mult, op1=ALU.add),
             reads=[sc, prm], writes=[sc])
        P.op("dve", lambda e: e.tensor_tensor(out=col(sc, 8), in0=col(sc, 10), in1=col(sc, 7), op=ALU.mult),
             reads=[sc], writes=[sc])
        P.op("dve", lambda e: e.tensor_tensor(out=col(sc, 11), in0=col(sc, 5), in1=col(prm, 1), op=ALU.mult),
             reads=[sc, prm], writes=[sc])
        P.op("dve", lambda e: e.scalar_tensor_tensor(out=col(sc, 11), in0=col(sc, 6), scalar=col(prm, 0),
                                                     in1=col(sc, 11), op0=ALU.mult, op1=ALU.subtract),
             reads=[sc, prm], writes=[sc])
        P.op("dve", lambda e: e.tensor_tensor(out=col(sc, 9), in0=col(sc, 11), in1=col(sc, 7), op=ALU.mult),
             reads=[sc], writes=[sc])
        P.op("dve", lambda e: e.tensor_scalar(out=bt1[:], in0=bm[:, 32:64], scalar1=col(sc, 9), scalar2=None,
                                              op0=ALU.mult), reads=[bm, sc], writes=[bt1])
        P.op("dve", lambda e: e.scalar_tensor_tensor(out=bb[:, 0:32], in0=bm[:, 0:32], scalar=col(sc, 8),
                                                     in1=bt1[:], op0=ALU.mult, op1=ALU.subtract),
             reads=[bm, sc, bt1], writes=[bb])
        P.op("dve", lambda e: e.tensor_scalar(out=bt1[:], in0=bm[:, 0:32], scalar1=col(sc, 9), scalar2=None,
                                              op0=ALU.mult), reads=[bm, sc, bb], writes=[bt1])
        P.op("dve", lambda e: e.scalar_tensor_tensor(out=bb[:, 32:64], in0=bm[:, 32:64], scalar=col(sc, 8),
                                                     in1=bt1[:], op0=ALU.mult, op1=ALU.add),
             reads=[bm, sc, bt1], writes=[bb])
        P.op("pe", lambda e: e.transpose(pst[:, 0:128], bb[:, 0:32], ident[:]), reads=[bb, ident], writes=[pst])
        P.op("pe", lambda e: e.transpose(pst[:, 128:256], bb[:, 32:64], ident[:]), reads=[bb, ident], writes=[pst])
        P.op("dve", lambda e: e.tensor_copy(out=BT[:], in_=pst[:]), reads=[pst], writes=[BT])
        P.op("dve", lambda e: e.tensor_copy(out=CT[:, 0:32], in_=cm[:, 0:32]), reads=[cm], writes=[CT])
        P.op("dve", lambda e: e.tensor_scalar(out=CT[:, 32:96], in0=cm[:, 0:64], scalar1=-1.0, scalar2=None,
                                              op0=ALU.mult), reads=[cm], writes=[CT])
        P.op("dve", lambda e: e.tensor_scalar(out=rt[:], in0=jidx[:, 0:S5C], scalar1=0.0, scalar2=col(sc, 1),
                                              op0=ALU.mult, op1=ALU.add), reads=[jidx, sc], writes=[rt])
        for ch in range(NCH):
            b = ch % 2
            tsl = slice(ch * S5C, (ch + 1) * S5C)
            P.dma("pool", [(ut[b][:], uT_d[u, :, tsl])], ut[b], writes=[ut[b]])
            pr, pi_ = ps_re[b], ps_im[b]
            P.op("pe", lambda e, pr=pr, b=b: e.matmul(pr[:], BT[:, 0:128], ut[b][:], start=True, stop=True),
                 reads=[BT, ut[b]], writes=[pr])
            P.op("pe", lambda e, pi_=pi_, b=b: e.matmul(pi_[:], BT[:, 128:256], ut[b][:], start=True, stop=True),
                 reads=[BT, ut[b]], writes=[pi_])
            PcS, PsS = Pc[:, 0:S5C], Ps[:, 0:S5C]
            P.op("dve", lambda e, pr=pr: e.tensor_tensor(out=m[0][:], in0=pr[:], in1=PcS, op=ALU.mult),
                 reads=[pr, Pc], writes=[m[0]])
            P.op("dve", lambda e, pi_=pi_: e.tensor_tensor(out=m[1][:], in0=pi_[:], in1=PsS, op=ALU.mult),
                 reads=[pi_, Ps], writes=[m[1]])
            P.op("pool", lambda e: e.tensor_tensor(out=cre[:], in0=m[0][:], in1=m[1][:], op=ALU.add),
                 reads=[m[0], m[1]], writes=[cre])
            P.op("dve", lambda e, pi_=pi_: e.tensor_tensor(out=m[2][:], in0=pi_[:], in1=PcS, op=ALU.mult),
                 reads=[pi_, Pc], writes=[m[2]])
            P.op("dve", lambda e, pr=pr: e.tensor_tensor(out=m[3][:], in0=pr[:], in1=PsS, op=ALU.mult),
                 reads=[pr, Ps], writes=[m[3]])
            P.op("pool", lambda e: e.tensor_tensor(out=cim[:], in0=m[2][:], in1=m[3][:], op=ALU.subtract),
                 reads=[m[2], m[3]], writes=[cim])
            if ch == 0:
                P.op("dve", lambda e: e.memset(init[:], 0.0), writes=[init])
            else:
                pzr, pzi = zre[1 - b], zim[1 - b]
                L = S5C - 1
                P.op("dve", lambda e, pzi=pzi: e.tensor_tensor(out=col(init, 2), in0=col(pzi, L), in1=col(Ps, S5C),
                                                              op=ALU.mult), reads=[pzi, Ps], writes=[init])
                P.op("dve", lambda e, pzr=pzr: e.scalar_tensor_tensor(out=col(init, 0), in0=col(pzr, L),
                                                                     scalar=col(Pc, S5C), in1=col(init, 2),
                                                                     op0=ALU.mult, op1=ALU.subtract),
                     reads=[pzr, Pc, init], writes=[init])
                P.op("dve", lambda e, pzr=pzr: e.tensor_tensor(out=col(init, 3), in0=col(pzr, L), in1=col(Ps, S5C),
                                                              op=ALU.mult), reads=[pzr, Ps], writes=[init])
                P.op("dve", lambda e, pzi=pzi: e.scalar_tensor_tensor(out=col(init, 1), in0=col(pzi, L),
                                                                     scalar=col(Pc, S5C), in1=col(init, 3),
                                                                     op0=ALU.mult, op1=ALU.add),
                     reads=[pzi, Pc, init], writes=[init])
            zr, zi = zre[b], zim[b]
            P.op("dve", lambda e, zr=zr: e.tensor_tensor_scan(out=zr[:], data0=rt[:], data1=cre[:],
                                                             initial=col(init, 0), op0=ALU.mult, op1=ALU.add),
                 reads=[rt, cre, init], writes=[zr])
            P.op("dve", lambda e, zi=zi: e.tensor_tensor_scan(out=zi[:], data0=rt[:], data1=cim[:],
                                                             initial=col(init, 1), op0=ALU.mult, op1=ALU.add),
                 reads=[rt, cim, init], writes=[zi])
            P.op("pool", lambda e, zr=zr: e.tensor_tensor(out=nn[0][:], in0=zr[:], in1=PcS, op=ALU.mult),
                 reads=[zr, Pc], writes=[nn[0]])
            P.op("pool", lambda e, zi=zi: e.tensor_tensor(out=nn[1][:], in0=zi[:], in1=PsS, op=ALU.mult),
                 reads=[zi, Ps], writes=[nn[1]])
            P.op("pool", lambda e, zi=zi: e.tensor_tensor(out=nn[2][:], in0=zi[:], in1=PcS, op=ALU.mult),
                 reads=[zi, Pc], writes=[nn[2]])
            P.op("dve", lambda e, zr=zr: e.tensor_tensor(out=nn[3][:], in0=zr[:], in1=PsS, op=ALU.mult),
                 reads=[zr, Ps], writes=[nn[3]])
            py = ps_y[b]
            lts = [CT[:, 0:32], CT[:, 32:64], CT[:, 64:96], CT[:, 64:96]]
            for q in range(4):
                P.op("pe", lambda e, py=py, q=q, lt=lts[q]: e.matmul(py[:], lt, nn[q][:], start=(q == 0),
                                                                    stop=(q == 3)), reads=[CT, nn[q]], writes=[py])
            y = yt[b]
            P.op("act", lambda e, py=py, y=y: e.copy(out=y[:], in_=py[:]), reads=[py], writes=[y])
            P.store("sp", y, out_d[u, :, tsl], y[:])
    return P


def run_S5(hA, a_re, a_im, log_step, b_re, b_im, c_re, c_im):
    P = build_S5()
    u = hA[:, 0:768]
    uT = np.ascontiguousarray(u.T)
    uTr = np.ascontiguousarray(uT[:, ::-1])
    jidx = bcast_rows(np.arange(S5C + 1, dtype=np.float32))
    maps = []
    for c in range(NCORES):
        uTc = np.zeros((6, 32, SEQ), np.float32)
        prm = np.zeros((6, 128, 3), np.float32)
        bmat = np.zeros((6, 128, 64), np.float32)
        cmat = np.zeros((6, 128, 64), np.float32)
        for pq in range(3):
            for d in range(2):
                un = pq * 2 + d
                for gg in range(2):
                    g = c * 6 + pq * 2 + gg
                    src = uTr if d == 1 else uT
                    uTc[un, gg * 16:(gg + 1) * 16] = src[g * 16:(g + 1) * 16]
                    rs = slice(gg * 64, (gg + 1) * 64)
                    prm[un, rs, 0] = a_re[d, g]
                    prm[un, rs, 1] = a_im[d, g]
                    prm[un, rs, 2] = log_step[d, g]
                    bmat[un, rs, gg * 16:(gg + 1) * 16] = b_re[d, g]
                    bmat[un, rs, 32 + gg * 16:32 + (gg + 1) * 16] = b_im[d, g]
                    cmat[un, rs, gg * 16:(gg + 1) * 16] = c_re[d, g].T
                    cmat[un, rs, 32 + gg * 16:32 + (gg + 1) * 16] = c_im[d, g].T
        maps.append(dict(uT=uTc, prm=prm, bmat=bmat, cmat=cmat, jidx=jidx, ident=np.eye(128, dtype=np.float32)))
    res = run(P, maps)
    yf = np.zeros((SEQ, 768), np.float32)
    yb = np.zeros((SEQ, 768), np.float32)
    for c in range(NCORES):
        yT = res[c]["yT"]
        for pq in range(3):
            cs = slice((c * 6 + pq * 2) * 16, (c * 6 + pq * 2 + 2) * 16)
            yf[:, cs] = yT[pq * 2].T
            yb[:, cs] = yT[pq * 2 + 1].T[::-1]
    return yf, yb


DNC = 128
NDC = SEQ // DNC


def build_DN():
    P = Prog()
    NU = 2
    xin_d = P.dram_in("xin", [NU, 3, 128, SEQ + 4])
    cw_d = P.dram_in("cw", [NU, 128, 15])
    ab_d = P.dram_in("ab", [NU, 128, 2, NDC])
    hp_d = P.dram_in("hp", [NU, 128, 2])
    id_d = P.dram_in("ident", [128, 128])
    tri_d = P.dram_in("triu", [128, 128])
    mb_d = P.dram_in("maskb", [128, 128])
    m0_d = P.dram_in("msk0", [128, 128])
    mT_d = P.dram_in("mskT", [128, 6, 128])
    out_d = P.dram_out("o", [NU, SEQ, 128])

    ident = P.sbuf("ident", [128, 128])
    identb = P.sbuf("identb", [128, 128], BF16)
    triu = P.sbuf("triu", [128, 128])
    maskb = P.sbuf("maskb", [128, 128])
    onesf = P.sbuf("onesf", [128, 128])
    P.load("sp", ident, ident[:], id_d)
    P.load("sp", triu, triu[:], tri_d)
    P.load("sp", maskb, maskb[:], mb_d)
    onesb = P.sbuf("onesb", [128, 128], BF16)
    triub = P.sbuf("triub", [128, 128], BF16)
    P.op("dve", lambda e: e.memset(onesf[:], 1.0), writes=[onesf])
    P.op("dve", lambda e: e.memset(onesb[:], 1.0), writes=[onesb])
    P.op("dve", lambda e: e.tensor_copy(out=triub[:], in_=triu[:]), reads=[triu], writes=[triub])
    P.op("dve", lambda e: e.tensor_copy(out=identb[:], in_=ident[:]), reads=[ident], writes=[identb])

    qT = P.sbuf("qT", [128, SEQ], BF16)
    kT = P.sbuf("kT", [128, SEQ], BF16)
    vT = P.sbuf("vT", [128, SEQ], BF16)
    cw = P.sbuf("cw", [128, 15])
    ab = P.sbuf("ab", [128, 2, NDC])
    hp = P.sbuf("hp", [128, 16])
    PW = 2048
    xp = [P.sbuf(f"xp{i}", [128, PW + 4]) for i in range(2)]
    acc = P.sbuf("acc", [128, PW])
    sq = P.sbuf("sq", [128, PW], BF16)
    ghl = P.sbuf("ghl", [128, 2, NDC], BF16)
    gtmp = P.sbuf("gtmp", [128, NDC])
    dgh = P.sbuf("dgh", [128, 128], BF16)
    dgl = P.sbuf("dgl", [128, 128], BF16)
    dgt = P.sbuf("dgt", [128, 128])
    rn = P.sbuf("rn", [128, 512])
    pss = [P.psum(f"pss{i}", [128, 512]) for i in range(2)]

    tb = {n: P.sbuf("tb_" + n, [128, NDC]) for n in ("g", "beta", "gc", "egc", "negegc", "egl", "ekd", "tmp")}
    pt64 = pss[0]

    ptr = P.psum("ptr", [128, 256], BF16)
    KV = P.sbuf("KV", [128, 256], BF16)
    dg = P.sbuf("dg", [128, 128])
    kTc = P.sbuf("kTc", [128, 128], BF16)
    qTc = P.sbuf("qTc", [128, 128], BF16)
    Winvw = P.sbuf("Winvw", [128, 128], BF16)
    pg = P.psum("pg", [128, 128])
    xe = P.sbuf("xe", [128, 128])
    E = P.sbuf("E", [128, 128])
    Es = P.sbuf("Es", [128, 128])
    pkk = P.psum("pkk", [128, 256])
    AT = P.sbuf("AT", [128, 128], BF16)
    X = [P.sbuf(f"X{i}", [128, 128], BF16) for i in range(2)]
    XT = [P.sbuf(f"XT{i}", [128, 128], BF16) for i in range(2)]
    W = [P.sbuf(f"W{i}", [128, 128]) for i in range(2)]
    Wb = [P.sbuf(f"Wb{i}", [128, 128], BF16) for i in range(2)]
    Xw = [P.sbuf(f"Xw{i}", [128, 128], BF16) for i in range(2)]
    XTw = [P.sbuf(f"XTw{i}", [128, 128], BF16) for i in range(2)]
    x0f = P.sbuf("x0f", [128, 128])
    UT = P.sbuf("UT", [128, 128])
    G32 = P.sbuf("G32", [128, 128])
    Gm = P.sbuf("Gm", [128, 128], BF16)
    Gw = P.sbuf("Gw", [128, 128], BF16)
    GTw = P.sbuf("GTw", [128, 128], BF16)
    Ysb = P.sbuf("Ysb", [128, 128], BF16)
    CT = [P.sbuf(f"CTl{i}", [128, 128], BF16) for i in range(6)]
    msk0 = P.sbuf("msk0", [128, 128])
    mskT = P.sbuf("mskT", [128, 6, 128])
    P.load("sp", msk0, msk0[:], m0_d)
    P.load("sp", mskT, mskT[:], mT_d)
    pX = P.psum("pX", [128, 256])
    pW = P.psum("pW", [128, 128])
    S = P.sbuf("S", [128, 128])
    Sb = P.sbuf("Sb", [128, 128], BF16)
    pks = P.psum("pks", [128, 256])
    Rp = P.sbuf("Rp", [128, 128], BF16)
    vnew = P.sbuf("vnew", [128, 128], BF16)
    oq = P.sbuf("oq", [128, 128])
    ot = [P.sbuf(f"ot{i}", [128, 128]) for i in range(2)]
    Kd = P.sbuf("Kd", [128, 128], BF16)
    pgs = pss[0]
    TT = []
    for i in range(2):
        T = {n: P.sbuf(f"T{i}_{n}", [128, 128]) for n in ("dg", "dgt", "xe", "E", "Es", "x0f", "UT", "G32")}
        T.update({n: P.sbuf(f"T{i}_{n}", [128, 128], BF16) for n in ("dgh", "dgl", "kTc", "qTc", "Gm", "GTw", "Ysb")})
        T["CT"] = [P.sbuf(f"T{i}_CT{l}", [128, 128], BF16) for l in range(6)]
        TT.append(T)
    ORing = [(P.sbuf(f"O{i}_KV", [128, 256], BF16), P.sbuf(f"O{i}_AT", [128, 128], BF16),
              P.sbuf(f"O{i}_Gw", [128, 128], BF16)) for i in range(4)]

    def col(t, j):
        return t[:, j:j + 1]

    for u in range(NU):
        P.load("sp", cw, cw[:], cw_d[u])
        P.load("sp", ab, ab[:], ab_d[u])
        P.load("sp", hp, hp[:, 0:2], hp_d[u])
        cnt = 0
        for ti, dst in enumerate((qT, kT, vT)):
            for pc in range(SEQ // PW):
                x_ = xp[cnt % 2]
                cnt += 1
                P.load("sp", x_, x_[:], xin_d[u, ti, :, pc * PW:pc * PW + PW + 4])
                P.op("act", lambda e, x_=x_, ti=ti: e.activation(out=acc[:], in_=x_[:, 0:PW], func=AF.Copy,
                                                                scale=col(cw, ti * 5)), reads=[x_, cw], writes=[acc])
                for k in range(1, 5):
                    P.op("dve", lambda e, x_=x_, ti=ti, k=k: e.scalar_tensor_tensor(
                        out=acc[:], in0=x_[:, k:k + PW], scalar=col(cw, ti * 5 + k), in1=acc[:],
                        op0=ALU.mult, op1=ALU.add), reads=[x_, cw, acc], writes=[acc])
                dsl = slice(pc * PW, (pc + 1) * PW)
                if ti == 2:
                    P.op("act", lambda e, dsl=dsl: e.activation(out=vT[:, dsl], in_=acc[:], func=AF.Silu),
                         reads=[acc], writes=[vT])
                    continue
                P.op("act", lambda e: e.activation(out=acc[:], in_=acc[:], func=AF.Silu), reads=[acc], writes=[acc])
                P.op("pool", lambda e: e.tensor_tensor(out=sq[:], in0=acc[:], in1=acc[:], op=ALU.mult),
                     reads=[acc], writes=[sq])
                for j in range(PW // 512):
                    ps = pss[j % 2]
                    js = slice(j * 512, (j + 1) * 512)
                    P.op("pe", lambda e, ps=ps, js=js: e.matmul(ps[:], onesb[:], sq[:, js], start=True, stop=True),
                         reads=[onesb, sq], writes=[ps])
                    P.op("act", lambda e, ps=ps: e.activation(out=rn[:], in_=ps[:], func=AF.Sqrt, bias=EPS),
                         reads=[ps], writes=[rn])
                    P.op("dve", lambda e: e.reciprocal(out=rn[:], in_=rn[:]), reads=[rn], writes=[rn])
                    scl = (128.0 ** -0.5) if ti == 0 else 1.0
                    P.op("dve", lambda e, js=js, dst=dst, pc=pc, j=j, scl=scl: e.scalar_tensor_tensor(
                        out=dst[:, pc * PW + j * 512:pc * PW + (j + 1) * 512], in0=acc[:, js], scalar=scl, in1=rn[:],
                        op0=ALU.mult, op1=ALU.mult), reads=[acc, rn], writes=[dst])
        P.op("act", lambda e: e.activation(out=tb["tmp"][:], in_=ab[:, 0, :], func=AF.Exp, bias=col(hp, 1)),
             reads=[ab, hp], writes=[tb["tmp"]])
        P.op("act", lambda e: e.activation(out=tb["tmp"][:], in_=tb["tmp"][:], func=AF.Ln, bias=1.0),
             reads=[tb["tmp"]], writes=[tb["tmp"]])
        P.op("act", lambda e: e.activation(out=col(hp, 2), in_=col(hp, 0), func=AF.Exp), reads=[hp], writes=[hp])
        P.op("dve", lambda e: e.tensor_scalar(out=col(hp, 3), in0=col(hp, 2), scalar1=-1.0, scalar2=None,
                                              op0=ALU.mult), reads=[hp], writes=[hp])
        P.op("dve", lambda e: e.tensor_scalar(out=tb["g"][:], in0=tb["tmp"][:], scalar1=col(hp, 3), scalar2=None,
                                              op0=ALU.mult), reads=[tb["tmp"], hp], writes=[tb["g"]])
        P.op("act", lambda e: e.activation(out=tb["beta"][:], in_=ab[:, 1, :], func=AF.Sigmoid),
             reads=[ab], writes=[tb["beta"]])
        P.op("dve", lambda e: e.tensor_copy(out=ghl[:, 0, :], in_=tb["g"][:]), reads=[tb["g"]], writes=[ghl])
        P.op("dve", lambda e: e.tensor_copy(out=gtmp[:], in_=ghl[:, 0, :]), reads=[ghl], writes=[gtmp])
        P.op("dve", lambda e: e.tensor_tensor(out=ghl[:, 1, :], in0=tb["g"][:], in1=gtmp[:], op=ALU.subtract),
             reads=[tb["g"], gtmp], writes=[ghl])
        for hl in range(2):
            P.op("pe", lambda e, hl=hl: e.matmul(pt64[:, 0:NDC], triub[:], ghl[:, hl, :], start=(hl == 0),
                                                 stop=(hl == 1)), reads=[triub, ghl], writes=[pt64])
        for hl in range(2):
            P.op("pe", lambda e, hl=hl: e.matmul(pt64[:, NDC:2 * NDC], onesb[:], ghl[:, hl, :], start=(hl == 0),
                                                 stop=(hl == 1)), reads=[onesb, ghl], writes=[pt64])
        P.op("dve", lambda e: e.tensor_copy(out=tb["gc"][:], in_=pt64[:, 0:NDC]), reads=[pt64], writes=[tb["gc"]])
        P.op("dve", lambda e: e.tensor_copy(out=gtmp[:], in_=pt64[:, NDC:2 * NDC]), reads=[pt64], writes=[gtmp])
        P.op("act", lambda e: e.activation(out=tb["egc"][:], in_=tb["gc"][:], func=AF.Exp),
             reads=[tb["gc"]], writes=[tb["egc"]])
        P.op("dve", lambda e: e.tensor_scalar(out=tb["negegc"][:], in0=tb["egc"][:], scalar1=-1.0, scalar2=None,
                                              op0=ALU.mult), reads=[tb["egc"]], writes=[tb["negegc"]])
        P.op("act", lambda e: e.activation(out=tb["egl"][:], in_=gtmp[:], func=AF.Exp),
             reads=[gtmp], writes=[tb["egl"]])
        P.op("dve", lambda e: e.tensor_tensor(out=tb["tmp"][:], in0=gtmp[:], in1=tb["gc"][:],
                                              op=ALU.subtract), reads=[gtmp, tb["gc"]], writes=[tb["tmp"]])
        P.op("act", lambda e: e.activation(out=tb["ekd"][:], in_=tb["tmp"][:], func=AF.Exp),
             reads=[tb["tmp"]], writes=[tb["ekd"]])
        P.op("dve", lambda e: e.memset(S[:], 0.0), writes=[S])
        P.op("dve", lambda e: e.memset(Sb[:], 0.0), writes=[Sb])
        def pre(c, T, O):
            csl = slice(c * DNC, (c + 1) * DNC)
            gcc, bec = col(tb["gc"], c), col(tb["beta"], c)
            KV_, AT_, Gw_ = O
            P.op("pe", lambda e: e.transpose(ptr[:, 0:128], kT[:, csl], identb[:]), reads=[kT, identb], writes=[ptr])
            P.op("pe", lambda e: e.transpose(ptr[:, 128:256], vT[:, csl], identb[:]), reads=[vT, identb], writes=[ptr])
            P.op("act", lambda e: e.copy(out=KV_[:], in_=ptr[:]), reads=[ptr], writes=[KV_])
            yield
            P.op("dve", lambda e: e.tensor_scalar(out=T["dg"][:], in0=ident[:], scalar1=gcc, scalar2=None,
                                                  op0=ALU.mult), reads=[ident, tb["gc"]], writes=[T["dg"]])
            yield
            P.op("dve", lambda e: e.tensor_copy(out=T["dgh"][:], in_=T["dg"][:]), reads=[T["dg"]], writes=[T["dgh"]])
            yield
            P.op("pool", lambda e: e.tensor_copy(out=T["dgt"][:], in_=T["dgh"][:]), reads=[T["dgh"]], writes=[T["dgt"]])
            yield
            P.op("pool", lambda e: e.tensor_tensor(out=T["dgl"][:], in0=T["dg"][:], in1=T["dgt"][:], op=ALU.subtract),
                 reads=[T["dg"], T["dgt"]], writes=[T["dgl"]])
            yield
            P.op("pe", lambda e: e.matmul(pg[:], onesb[:], T["dgh"][:], start=True, stop=False),
                 reads=[onesb, T["dgh"]], writes=[pg])
            P.op("pe", lambda e: e.matmul(pg[:], onesb[:], T["dgl"][:], start=False, stop=True),
                 reads=[onesb, T["dgl"]], writes=[pg])
            P.op("dve", lambda e: e.tensor_scalar(out=T["xe"][:], in0=pg[:], scalar1=gcc, scalar2=0.0,
                                                  op0=ALU.subtract, op1=ALU.min), reads=[pg, tb["gc"]], writes=[T["xe"]])
            yield
            P.op("pool", lambda e: e.tensor_tensor(out=T["xe"][:], in0=T["xe"][:], in1=maskb[:], op=ALU.add),
                 reads=[T["xe"], maskb], writes=[T["xe"]])
            yield
            P.op("act", lambda e: e.activation(out=T["E"][:], in_=T["xe"][:], func=AF.Exp), reads=[T["xe"]], writes=[T["E"]])
            yield
            P.op("pool", lambda e: e.tensor_copy(out=T["kTc"][:], in_=kT[:, csl]), reads=[kT], writes=[T["kTc"]])
            yield
            P.op("pool", lambda e: e.tensor_copy(out=T["qTc"][:], in_=qT[:, csl]), reads=[qT], writes=[T["qTc"]])
            yield
            P.op("pe", lambda e: e.matmul(pkk[:, 0:128], kT[:, csl], T["kTc"][:], start=True, stop=True),
                 reads=[kT, T["kTc"]], writes=[pkk])
            P.op("pe", lambda e: e.matmul(pkk[:, 128:256], kT[:, csl], T["qTc"][:], start=True, stop=True),
                 reads=[kT, T["qTc"]], writes=[pkk])
            P.op("pool", lambda e: e.tensor_tensor(out=T["Es"][:], in0=T["E"][:], in1=ident[:], op=ALU.subtract),
                 reads=[T["E"], ident], writes=[T["Es"]])
            P.op("dve", lambda e: e.scalar_tensor_tensor(out=T["x0f"][:], in0=pkk[:, 0:128], scalar=bec,
                                                         in1=T["Es"][:], op0=ALU.mult, op1=ALU.mult),
                 reads=[pkk, tb["beta"], T["Es"]], writes=[T["x0f"]])
            P.op("dve", lambda e: e.tensor_tensor(out=AT_[:], in0=pkk[:, 128:256], in1=T["E"][:], op=ALU.mult),
                 reads=[pkk, T["E"]], writes=[AT_])
            yield
            P.op("pe", lambda e: e.transpose(pss[1][:, 0:128], T["x0f"][:], ident[:]), reads=[T["x0f"], ident],
                 writes=[pss[1]])
            P.op("dve", lambda e: e.tensor_copy(out=T["UT"][:], in_=pss[1][:, 0:128]), reads=[pss[1]], writes=[T["UT"]])
            yield
            for lv in range(1, 7):
                eng = "pool" if lv % 2 else "dve"
                P.op(eng, lambda e, lv=lv: e.tensor_tensor(out=T["CT"][lv - 1][:], in0=T["UT"][:],
                                                           in1=mskT[:, lv - 1, :], op=ALU.mult),
                     reads=[T["UT"], mskT], writes=[T["CT"][lv - 1]])
                yield
            P.op("dve", lambda e: e.tensor_tensor(out=T["G32"][:], in0=T["x0f"][:], in1=msk0[:], op=ALU.mult),
                 reads=[T["x0f"], msk0], writes=[T["G32"]])
            yield
            P.op("dve", lambda e: e.tensor_tensor(out=T["G32"][:], in0=ident[:], in1=T["G32"][:], op=ALU.subtract),
                 reads=[ident, T["G32"]], writes=[T["G32"]])
            yield
            P.op("act", lambda e: e.copy(out=T["Gm"][:], in_=T["G32"][:]), reads=[T["G32"]], writes=[T["Gm"]])
            yield
            P.op("pool", lambda e: e.tensor_copy(out=Gw_[:], in_=T["G32"][:]), reads=[T["G32"]], writes=[Gw_])
            yield
            for lv in range(1, 7):
                P.op("pe", lambda e, lv=lv: e.matmul(pX[:, 0:128], T["CT"][lv - 1][:], T["Gm"][:], start=True, stop=True),
                     reads=[T["CT"][lv - 1], T["Gm"]], writes=[pX])
                P.op("pe", lambda e: e.transpose(ptr[:, 0:128], Gw_[:], identb[:]), reads=[Gw_, identb], writes=[ptr])
                P.op("act", lambda e: e.copy(out=T["Ysb"][:], in_=pX[:, 0:128]), reads=[pX], writes=[T["Ysb"]])
                P.op("act", lambda e: e.copy(out=T["GTw"][:], in_=ptr[:, 0:128]), reads=[ptr], writes=[T["GTw"]])
                yield
                P.op("pe", lambda e: e.matmul(pW[:], T["GTw"][:], T["Ysb"][:], start=True, stop=True),
                     reads=[T["GTw"], T["Ysb"]], writes=[pW])
                P.op("dve", lambda e: e.tensor_tensor(out=T["G32"][:], in0=T["G32"][:], in1=pW[:], op=ALU.subtract),
                     reads=[T["G32"], pW], writes=[T["G32"]])
                yield
                if lv < 6:
                    P.op("act", lambda e: e.copy(out=T["Gm"][:], in_=T["G32"][:]), reads=[T["G32"]], writes=[T["Gm"]])
                    yield
                P.op("pool", lambda e: e.tensor_copy(out=Gw_[:], in_=T["G32"][:]), reads=[T["G32"]], writes=[Gw_])
                yield

        def seq(c, O):
            csl = slice(c * DNC, (c + 1) * DNC)
            bec = col(tb["beta"], c)
            KV_, AT_, Gw_ = O
            P.op("pe", lambda e: e.matmul(pks[:, 0:128], kT[:, csl], Sb[:], start=True, stop=True),
                 reads=[kT, Sb], writes=[pks])
            P.op("pe", lambda e: e.matmul(pks[:, 128:256], qT[:, csl], Sb[:], start=True, stop=True),
                 reads=[qT, Sb], writes=[pks])
            yield
            P.op("dve", lambda e: e.scalar_tensor_tensor(out=Rp[:], in0=pks[:, 0:128], scalar=col(tb["negegc"], c),
                                                         in1=KV_[:, 128:256], op0=ALU.mult, op1=ALU.add),
                 reads=[pks, tb["negegc"], KV_], writes=[Rp])
            yield
            P.op("pe", lambda e: e.matmul(pgs[:, 0:128], Gw_[:], Rp[:], start=True, stop=True), reads=[Gw_, Rp], writes=[pgs])
            yield
            P.op("dve", lambda e: e.tensor_scalar(out=vnew[:], in0=pgs[:, 0:128], scalar1=bec, scalar2=None, op0=ALU.mult),
                 reads=[pgs, tb["beta"]], writes=[vnew])
            yield
            P.op("pe", lambda e: e.matmul(pgs[:, 0:128], AT_[:], vnew[:], start=True, stop=True), reads=[AT_, vnew], writes=[pgs])
            yield
            P.op("dve", lambda e: e.tensor_scalar(out=oq[:], in0=pks[:, 128:256], scalar1=col(tb["egc"], c),
                                                  scalar2=None, op0=ALU.mult), reads=[pks, tb["egc"]], writes=[oq])
            yield
            o = ot[c % 2]
            P.op("dve", lambda e: e.tensor_tensor(out=o[:], in0=pgs[:, 0:128], in1=oq[:], op=ALU.add),
                 reads=[pgs, oq], writes=[o])
            P.store("sp", o, out_d[u, csl, :], o[:])
            yield
            P.op("pool", lambda e: e.tensor_scalar(out=Kd[:], in0=KV_[:, 0:128], scalar1=col(tb["ekd"], c),
                                                   scalar2=None, op0=ALU.mult), reads=[KV_, tb["ekd"]], writes=[Kd])
            yield
            P.op("pe", lambda e: e.matmul(pks[:, 0:128], Kd[:], vnew[:], start=True, stop=True),
                 reads=[Kd, vnew], writes=[pks])
            yield
            P.op("dve", lambda e: e.scalar_tensor_tensor(out=S[:], in0=S[:], scalar=col(tb["egl"], c),
                                                         in1=pks[:, 0:128], op0=ALU.mult, op1=ALU.add),
                 reads=[S, tb["egl"], pks], writes=[S])
            yield
            P.op("act", lambda e: e.copy(out=Sb[:], in_=S[:]), reads=[S], writes=[Sb])
            yield

        def seq_pair(c0):
            for cc in (c0, c0 + 1):
                yield from seq(cc, ORing[cc % 4])

        def round_robin(gens):
            gens = list(gens)
            while gens:
                for g_ in list(gens):
                    try:
                        next(g_)
                    except StopIteration:
                        gens.remove(g_)

        nch = int(os.environ.get('DN_NCH', NDC))
        for p in range(nch // 2):
            gens = [pre(2 * p, TT[0], ORing[(2 * p) % 4]), pre(2 * p + 1, TT[1], ORing[(2 * p + 1) % 4])]
            if p >= 1:
                gens.append(seq_pair(2 * p - 2))
            round_robin(gens)
        if nch >= 2:
            round_robin([seq_pair(nch - 2)])
    return P


DN_UNITS = [(h, d) for h in range(6) for d in range(2)]


def run_DN(hA, conv_w, a_log, dt_bias):
    P = build_DN()
    qkv = hA[:, 768:3072]
    da = hA[:, 4608:4620]
    db = hA[:, 4620:4632]
    ii = np.arange(128)
    triu = (ii[:, None] <= ii[None, :]).astype(np.float32)
    maskb = np.where(ii[None, :] >= ii[:, None], 0.0, -30000.0).astype(np.float32)
    msk0 = np.zeros((128, 128), np.float32)
    mskT = np.zeros((128, 6, 128), np.float32)
    for lv in range(7):
        b = 1 << lv
        jj, i2 = np.meshgrid(ii, ii, indexing="ij")
        m = ((jj // (2 * b) == i2 // (2 * b)) & (jj % (2 * b) < b) & (i2 % (2 * b) >= b)).astype(np.float32)
        if lv == 0:
            msk0 = m
        else:
            mskT[:, lv - 1, :] = m.T
    units = DN_UNITS + DN_UNITS[:4]
    maps = []
    for c in range(NCORES):
        xin = np.zeros((2, 3, 128, SEQ + 4), np.float32)
        cw = np.zeros((2, 128, 15), np.float32)
        ab = np.zeros((2, 128, 2, NDC), np.float32)
        hp = np.zeros((2, 128, 2), np.float32)
        for s in range(2):
            h, d = units[c * 2 + s]
            for ti in range(3):
                cs = slice(ti * 768 + h * 128, ti * 768 + (h + 1) * 128)
                xt = qkv[:, cs].T
                w = conv_w[cs]
                if d == 1:
                    xt = xt[:, ::-1]
                    w = w[:, ::-1]
                xin[s, ti, :, 2:2 + SEQ] = xt
                cw[s, :, ti * 5:(ti + 1) * 5] = w
            av = da[:, d * 6 + h]
            bv = db[:, d * 6 + h]
            if d == 1:
                av = av[::-1]
                bv = bv[::-1]
            ab[s, :, 0, :] = av.reshape(NDC, 128).T
            ab[s, :, 1, :] = bv.reshape(NDC, 128).T
            hp[s, :, 0] = a_log[d, h]
            hp[s, :, 1] = dt_bias[d, h]
        maps.append(dict(xin=xin, cw=cw, ab=ab, hp=hp, ident=np.eye(128, dtype=np.float32), triu=triu, maskb=maskb,
                         msk0=msk0, mskT=mskT))
    res = run(P, maps)
    of = np.zeros((SEQ, 768), np.float32)
    ob = np.zeros((SEQ, 768), np.float32)
    for idx, (h, d) in enumerate(DN_UNITS):
        o = res[idx // 2]["o"][idx % 2]
        if d == 0:
            of[:, h * 128:(h + 1) * 128] = o
        else:
            ob[:, h * 128:(h + 1) * 128] = o[::-1]
    return of, ob


COLS_Z = np.concatenate([np.arange(768, 1536), np.arange(3864, 4632), np.arange(6168, 7192),
                         np.arange(7704, 8216), np.arange(7192, 7704)])
NZ = 3584


def build_C1():
    P = Prog()
    x_d = P.dram_in("x", [TPC, D])
    gb_d = P.dram_in("gb", [128, D])
    id_d = P.dram_in("ident", [128, 128])
    wz_d = P.dram_in("wz", [D, NZ])
    mem_d = P.dram_in("mem", [256, D])
    mgb_d = P.dram_in("mgb", [128, D])
    wkv_d = P.dram_in("wkv", [D, 1024])
    s5_d = P.dram_in("s5", [3, 768, TPC])
    sd_d = P.dram_in("sd", [128, 16])
    wglu_d = P.dram_in("wglu", [768, 768])
    dn_d = P.dram_in("dn", [2, 768, TPC])
    oc_d = P.dram_in("oc", [1024, TPC])
    y_d = P.dram_out("yT", [24, 128, TPC], BF16)

    ident = P.sbuf("ident", [128, 128])
    gb = P.sbuf("gb", [128, D])
    sd = P.sbuf("sd", [128, 16])
    onesb = P.sbuf("onesb", [128, 128], BF16)
    P.load("sp", ident, ident[:], id_d)
    P.load("sp", gb, gb[:], gb_d)
    P.load("sp", sd, sd[:], sd_d)
    P.op("dve", lambda e: e.memset(onesb[:], 1.0), writes=[onesb])
    xnT = P.sbuf("xnT", [128, 16, TPC], BF16)
    nb = emit_norm_T(P, x_d, gb, ident, xnT, TPC // 128, "nC", single=True)
    memnT = P.sbuf("memnT", [128, 16, 256], BF16)
    P.load("sp", gb, gb[:], mgb_d)
    emit_norm_T(P, mem_d, gb, ident, memnT, 2, "nC", bufs=nb)

    pp = [P.psum(f"pp{i}", [128, 512]) for i in range(2)]
    pa = [P.psum(f"pa{i}", [128, 512]) for i in range(2)]
    po = P.psum("po", [128, 512])
    pd = P.psum("pd", [128, 512])

    wk = P.sbuf("wk", [128, 16, 512], BF16)
    KmT = P.sbuf("KmT", [128, 4, 256], BF16)
    Vm = P.sbuf("Vm", [128, 2, 512], BF16)
    wkv_v = wkv_d.rearrange("(c p) n -> p c n", p=128)
    P.dma("pool", [(wk[:, k, :], wkv_v[:, k, 0:512]) for k in range(16)], wk, writes=[wk])
    for h in range(4):
        ps = pp[h % 2]
        for k in range(16):
            P.op("pe", lambda e, ps=ps, k=k, h=h: e.matmul(ps[:, 0:256], wk[:, k, h * 128:(h + 1) * 128],
                                                          memnT[:, k, :], start=(k == 0), stop=(k == 15)),
                 reads=[wk, memnT], writes=[ps])
        P.op("act", lambda e, ps=ps, h=h: e.copy(out=KmT[:, h, :], in_=ps[:, 0:256]), reads=[ps], writes=[KmT])
    P.dma("pool", [(wk[:, k, :], wkv_v[:, k, 512:1024]) for k in range(16)], wk, writes=[wk])
    for mt in range(2):
        ps = pp[mt % 2]
        for k in range(16):
            P.op("pe", lambda e, ps=ps, k=k, mt=mt: e.matmul(ps[:], memnT[:, k, mt * 128:(mt + 1) * 128],
                                                            wk[:, k, :], start=(k == 0), stop=(k == 15)),
                 reads=[wk, memnT], writes=[ps])
        P.op("act", lambda e, ps=ps, mt=mt: e.copy(out=Vm[:, mt, :], in_=ps[:]), reads=[ps], writes=[Vm])

    wj = [P.sbuf(f"wj{i}", [128, 16, 128], BF16) for i in range(2)]
    sz = [P.sbuf(f"sz{i}", [128, TPC], BF16) for i in range(2)]
    wz_v = wz_d.rearrange("(c p) n -> p c n", p=128)
    cnt = {"w": 0, "p": 0}

    def projT(col0, dst_ap_fn, func, dst_buf):
        w = wj[cnt["w"] % 2]
        cnt["w"] += 1
        P.dma("pool", [(w[:, k, :], wz_v[:, k, col0:col0 + 128]) for k in range(16)], w, writes=[w])
        for half in range(2):
            ps = pp[cnt["p"] % 2]
            cnt["p"] += 1
            hs = slice(half * 512, (half + 1) * 512)
            for k in range(16):
                P.op("pe", lambda e, ps=ps, k=k, w=w, hs=hs: e.matmul(ps[:], w[:, k, :], xnT[:, k, hs],
                                                                     start=(k == 0), stop=(k == 15)),
                     reads=[w, xnT], writes=[ps])
            P.op("act", lambda e, ps=ps, hs=hs: e.activation(out=dst_ap_fn(hs), in_=ps[:], func=func),
                 reads=[ps], writes=[dst_buf])

    def proj_silu(col0):
        z = sz[cnt["w"] % 2]
        projT(col0, lambda hs, z=z: z[:, hs], AF.Silu, z)
        return z

    f1 = [P.sbuf(f"f1_{i}", [128, TPC]) for i in range(2)]
    f2 = [P.sbuf(f"f2_{i}", [128, TPC]) for i in range(2)]
    f3 = P.sbuf("f3", [128, TPC])
    f4 = P.sbuf("f4", [128, TPC])
    yo = [P.sbuf(f"yo{i}", [128, TPC], BF16) for i in range(2)]
    sqb = P.sbuf("sqb", [128, TPC], BF16)
    rn = P.sbuf("rn", [128, 512])
    sg = P.sbuf("sg", [128, 512])
    ycnt = {"n": 0}

    def next_yo():
        y = yo[ycnt["n"] % 2]
        ycnt["n"] += 1
        return y

    GY = P.sbuf("GY", [128, 6, TPC], BF16)
    wglu = P.sbuf("wglu", [128, 6, 768], BF16)
    wglu_v = wglu_d.rearrange("(c p) n -> p c n", p=128)
    P.dma("pool", [(wglu[:, k, :], wglu_v[:, k, :]) for k in range(6)], wglu, writes=[wglu])
    for j in range(6):
        a, b_ = f1[j % 2], f2[j % 2]
        rs = slice(j * 128, (j + 1) * 128)
        P.load("sp", a, a[:], s5_d[0, rs, :])
        P.load("sp", b_, b_[:], s5_d[1, rs, :])
        P.load("sp", f3, f3[:], s5_d[2, rs, :])
        P.op("pool", lambda e, a=a, b_=b_: e.tensor_tensor(out=a[:], in0=a[:], in1=b_[:], op=ALU.add),
             reads=[a, b_], writes=[a])
        P.op("dve", lambda e, a=a, j=j: e.scalar_tensor_tensor(out=a[:], in0=f3[:], scalar=sd[:, j:j + 1], in1=a[:],
                                                              op0=ALU.mult, op1=ALU.add), reads=[f3, sd, a], writes=[a])
        P.op("pool", lambda e, a=a, b_=b_: e.tensor_tensor(out=b_[:], in0=a[:], in1=a[:], op=ALU.mult),
             reads=[a], writes=[b_])
        P.op("dve", lambda e, b_=b_: e.tensor_scalar(out=b_[:], in0=b_[:], scalar1=0.044715, scalar2=1.0,
                                                     op0=ALU.mult, op1=ALU.add), reads=[b_], writes=[b_])
        P.op("pool", lambda e, a=a, b_=b_: e.tensor_tensor(out=b_[:], in0=b_[:], in1=a[:], op=ALU.mult),
             reads=[a, b_], writes=[b_])
        P.op("act", lambda e, b_=b_: e.activation(out=f4[:], in_=b_[:], func=AF.Sigmoid, scale=1.5957691216057308),
             reads=[b_], writes=[f4])
        P.op("dve", lambda e, a=a, j=j: e.tensor_tensor(out=GY[:, j, :], in0=a[:], in1=f4[:], op=ALU.mult),
             reads=[a, f4], writes=[GY])
    for j in range(6):
        z = proj_silu(0 + j * 128)
        y = next_yo()
        for half in range(2):
            hs = slice(half * 512, (half + 1) * 512)
            ps = pa[half]
            for k in range(6):
                P.op("pe", lambda e, ps=ps, k=k, j=j, hs=hs: e.matmul(ps[:], wglu[:, k, j * 128:(j + 1) * 128],
                                                                     GY[:, k, hs], start=(k == 0), stop=(k == 5)),
                     reads=[wglu, GY], writes=[ps])
            P.op("act", lambda e, ps=ps, j=j: e.activation(out=sg[:], in_=ps[:], func=AF.Sigmoid,
                                                           bias=sd[:, 6 + j:7 + j]), reads=[ps, sd], writes=[sg])
            P.op("dve", lambda e, j=j, hs=hs: e.tensor_tensor(out=sg[:], in0=sg[:], in1=GY[:, j, hs], op=ALU.mult),
                 reads=[sg, GY], writes=[sg])
            P.op("dve", lambda e, y=y, z=z, hs=hs: e.tensor_tensor(out=y[:, hs], in0=sg[:], in1=z[:, hs], op=ALU.mult),
                 reads=[sg, z], writes=[y])
        P.store("sp", y, y_d[j], y[:])
    for h in range(6):
        a, b_ = f1[h % 2], f2[h % 2]
        rs = slice(h * 128, (h + 1) * 128)
        P.load("sp", a, a[:], dn_d[0, rs, :])
        P.load("sp", b_, b_[:], dn_d[1, rs, :])
        P.op("pool", lambda e, a=a, b_=b_: e.tensor_tensor(out=a[:], in0=a[:], in1=b_[:], op=ALU.add),
             reads=[a, b_], writes=[a])
        P.op("pool", lambda e, a=a: e.tensor_tensor(out=sqb[:], in0=a[:], in1=a[:], op=ALU.mult),
             reads=[a], writes=[sqb])
        z = proj_silu(768 + h * 128)
        y = next_yo()
        for half in range(2):
            hs = slice(half * 512, (half + 1) * 512)
            ps = pa[half]
            P.op("pe", lambda e, ps=ps, hs=hs: e.matmul(ps[:], onesb[:], sqb[:, hs], start=True, stop=True),
                 reads=[onesb, sqb], writes=[ps])
            P.op("act", lambda e, ps=ps: e.activation(out=rn[:], in_=ps[:], func=AF.Sqrt, scale=1.0 / 128, bias=EPS),
                 reads=[ps], writes=[rn])
            P.op("dve", lambda e: e.reciprocal(out=rn[:], in_=rn[:]), reads=[rn], writes=[rn])
            P.op("dve", lambda e, a=a, hs=hs: e.scalar_tensor_tensor(out=rn[:], in0=a[:, hs], scalar=sd[:, 12:13],
                                                                    in1=rn[:], op0=ALU.mult, op1=ALU.mult),
                 reads=[a, sd, rn], writes=[rn])
            P.op("dve", lambda e, y=y, z=z, hs=hs: e.tensor_tensor(out=y[:, hs], in0=rn[:], in1=z[:, hs], op=ALU.mult),
                 reads=[rn, z], writes=[y])
        P.store("sp", y, y_d[6 + h], y[:])
    for c in range(8):
        a = f1[c % 2]
        P.load("sp", a, a[:], oc_d[c * 128:(c + 1) * 128, :])
        z = proj_silu(1536 + c * 128)
        y = next_yo()
        P.op("dve", lambda e, a=a, y=y, z=z: e.tensor_tensor(out=y[:], in0=a[:], in1=z[:], op=ALU.mult),
             reads=[a, z], writes=[y])
        P.store("sp", y, y_d[12 + c], y[:])
    QmT = P.sbuf("QmT", [128, TPC], BF16)
    pm = [P.sbuf(f"pm{i}", [128, 512], BF16) for i in range(2)]
    pc = 0
    for h in range(4):
        projT(3072 + h * 128, lambda hs: QmT[:, hs], AF.Copy, QmT)
        z = proj_silu(2560 + h * 128)
        y = next_yo()
        for half in range(2):
            hs = slice(half * 512, (half + 1) * 512)
            for mt in range(2):
                ps = pa[mt]
                p_ = pm[pc % 2]
                pc += 1
                P.op("pe", lambda e, ps=ps, h=h, mt=mt, hs=hs: e.matmul(ps[:], KmT[:, h, mt * 128:(mt + 1) * 128],
                                                                       QmT[:, hs], start=True, stop=True),
                     reads=[KmT, QmT], writes=[ps])
                P.op("act", lambda e, ps=ps, p_=p_: e.activation(out=p_[:], in_=ps[:], func=AF.Exp, scale=128.0 ** -0.5),
                     reads=[ps], writes=[p_])
                P.op("pe", lambda e, p_=p_, h=h, mt=mt: e.matmul(po[:], Vm[:, mt, h * 128:(h + 1) * 128], p_[:],
                                                                start=(mt == 0), stop=(mt == 1)),
                     reads=[Vm, p_], writes=[po])
                P.op("pe", lambda e, p_=p_, mt=mt: e.matmul(pd[:], onesb[:], p_[:], start=(mt == 0), stop=(mt == 1)),
                     reads=[onesb, p_], writes=[pd])
            P.op("dve", lambda e: e.reciprocal(out=rn[:], in_=pd[:]), reads=[pd], writes=[rn])
            P.op("dve", lambda e: e.tensor_tensor(out=rn[:], in0=po[:], in1=rn[:], op=ALU.mult),
                 reads=[po, rn], writes=[rn])
            P.op("dve", lambda e, y=y, z=z, hs=hs: e.tensor_tensor(out=y[:, hs], in0=rn[:], in1=z[:, hs], op=ALU.mult),
                 reads=[rn, z], writes=[y])
        P.store("sp", y, y_d[20 + h], y[:])
    return P


def run_C1(x, norm_g, w_in_l, mem, mem_g, w_kv, hA, yf, yb, ssm_d, w_glu, b_glu, of, ob, dn_g, yc):
    P = build_C1()
    sd = np.zeros((128, 16), np.float32)
    sd[:, 0:6] = np.asarray(ssm_d, np.float32).reshape(6, 128).T
    sd[:, 6:12] = np.asarray(b_glu, np.float32).reshape(6, 128).T
    sd[:, 12] = np.asarray(dn_g, np.float32)
    common = dict(gb=bcast_rows(norm_g), ident=np.eye(128, dtype=np.float32),
                  wz=np.ascontiguousarray(w_in_l[:, COLS_Z]), mem=np.ascontiguousarray(mem),
                  mgb=bcast_rows(mem_g), wkv=np.ascontiguousarray(w_kv), sd=sd,
                  wglu=np.ascontiguousarray(w_glu))
    maps = []
    for c in range(NCORES):
        ts = slice(c * TPC, (c + 1) * TPC)
        s5 = np.stack([yf[ts].T, yb[ts].T, hA[ts, 0:768].T]).astype(np.float32)
        dn = np.stack([of[ts].T, ob[ts].T]).astype(np.float32)
        maps.append(dict(common, x=np.ascontiguousarray(x[ts]), s5=np.ascontiguousarray(s5),
                         dn=np.ascontiguousarray(dn), oc=np.ascontiguousarray(yc[ts].T)))
    res = run(P, maps)
    return [r["yT"] for r in res]


BR_CHUNKS = [(0, 6), (6, 12), (12, 20), (20, 24)]


def build_C2():
    P = Prog()
    x_d = P.dram_in("x", [TPC, D])
    gb_d = P.dram_in("gb", [128, D])
    id_d = P.dram_in("ident", [128, 128])
    wg_d = P.dram_in("wg", [D, 4 * D])
    y_d = P.dram_in("yT", [24, 128, TPC], BF16)
    wbr_d = P.dram_in("wbr", [3072, D])
    wout_d = P.dram_in("wout", [D, D])
    out_d = P.dram_out("xnew", [TPC, D])

    ident = P.sbuf("ident", [128, 128])
    gb = P.sbuf("gb", [128, D])
    P.load("sp", ident, ident[:], id_d)
    P.load("sp", gb, gb[:], gb_d)
    xnT = P.sbuf("xnT", [128, 16, TPC], BF16)
    emit_norm_T(P, x_d, gb, ident, xnT, TPC // 128, "nD", single=True)
    Y = P.sbuf("Y", [128, 24, TPC], BF16)
    for k in range(24):
        P.dma("sp", [(Y[:, k, :], y_d[k])], Y, writes=[Y])
    mT = P.sbuf("mT", [128, 16, TPC], BF16)
    wbr = P.sbuf("wbr", [128, 24, 128], BF16)
    wg = [P.sbuf(f"wg{i}", [128, 16, 128], BF16) for i in range(2)]
    pb = [P.psum(f"pb{i}", [128, 512]) for i in range(2)]
    pg = [P.psum(f"pg{i}", [128, 512]) for i in range(2)]
    sg = [P.sbuf(f"sg{i}", [128, 512]) for i in range(2)]
    acc = P.sbuf("acc", [128, TPC])
    tmp = P.sbuf("tmpm", [128, 512])
    wbr_v = wbr_d.rearrange("(c p) n -> p c n", p=128)
    wg_v = wg_d.rearrange("(c p) n -> p c n", p=128)
    cnt = 0
    for j in range(16):
        js = slice(j * 128, (j + 1) * 128)
        P.dma("pool", [(wbr[:, k, :], wbr_v[:, k, js]) for k in range(24)], wbr, writes=[wbr])
        for b in range(4):
            w = wg[cnt % 2]
            g0 = b * D + j * 128
            P.dma("pool", [(w[:, k, :], wg_v[:, k, g0:g0 + 128]) for k in range(16)], w, writes=[w])
            k0, k1 = BR_CHUNKS[b]
            for half in range(2):
                hs = slice(half * 512, (half + 1) * 512)
                p1, p2, s_ = pb[cnt % 2], pg[cnt % 2], sg[cnt % 2]
                cnt += 1
                for k in range(16):
                    P.op("pe", lambda e, p2=p2, k=k, w=w, hs=hs: e.matmul(p2[:], w[:, k, :], xnT[:, k, hs],
                                                                         start=(k == 0), stop=(k == 15)),
                         reads=[w, xnT], writes=[p2])
                for k in range(k0, k1):
                    P.op("pe", lambda e, p1=p1, k=k, hs=hs, k0=k0, k1=k1: e.matmul(p1[:], wbr[:, k, :], Y[:, k, hs],
                                                                                  start=(k == k0), stop=(k == k1 - 1)),
                         reads=[wbr, Y], writes=[p1])
                P.op("act", lambda e, p2=p2, s_=s_: e.activation(out=s_[:], in_=p2[:], func=AF.Sigmoid),
                     reads=[p2], writes=[s_])
                if b == 0:
                    P.op("dve", lambda e, p1=p1, s_=s_, hs=hs: e.tensor_tensor(out=acc[:, hs], in0=p1[:], in1=s_[:],
                                                                              op=ALU.mult), reads=[p1, s_], writes=[acc])
                else:
                    P.op("dve", lambda e, p1=p1, s_=s_: e.tensor_tensor(out=tmp[:], in0=p1[:], in1=s_[:], op=ALU.mult),
                         reads=[p1, s_], writes=[tmp])
                    P.op("pool", lambda e, hs=hs: e.tensor_tensor(out=acc[:, hs], in0=acc[:, hs], in1=tmp[:],
                                                                  op=ALU.add), reads=[acc, tmp], writes=[acc])
        P.op("act", lambda e, j=j: e.copy(out=mT[:, j, :], in_=acc[:]), reads=[acc], writes=[mT])
    wo_view = Y[:, 0:8, :].rearrange("p a (b c) -> p (a b) c", c=512)
    wout_v = wout_d.rearrange("(c p) n -> p c n", p=128)
    xr = [P.sbuf(f"xr{i}", [128, 512]) for i in range(2)]
    ot = [P.sbuf(f"oo{i}", [128, 512]) for i in range(2)]
    cnt = 0
    for cb in range(4):
        cs = slice(cb * 512, (cb + 1) * 512)
        P.dma("pool", [(wo_view[:, k, :], wout_v[:, k, cs]) for k in range(16)], Y, writes=[Y])
        for i in range(TPC // 128):
            ps = pb[cnt % 2]
            r_ = xr[cnt % 2]
            o_ = ot[cnt % 2]
            cnt += 1
            ts = slice(i * 128, (i + 1) * 128)
            P.load("sp", r_, r_[:], x_d[ts, cs])
            for k in range(16):
                P.op("pe", lambda e, ps=ps, k=k, ts=ts: e.matmul(ps[:], mT[:, k, ts], wo_view[:, k, :],
                                                                start=(k == 0), stop=(k == 15)),
                     reads=[mT, Y], writes=[ps])
            P.op("dve", lambda e, ps=ps, r_=r_, o_=o_: e.tensor_tensor(out=o_[:], in0=ps[:], in1=r_[:], op=ALU.add),
                 reads=[ps, r_], writes=[o_])
            P.store("sp", o_, out_d[ts, cs], o_[:])
    return P


def run_C2(x, norm_g, w_in_l, yTs, w_br, w_o):
    P = build_C2()
    common = dict(gb=bcast_rows(norm_g), ident=np.eye(128, dtype=np.float32),
                  wg=np.ascontiguousarray(w_in_l[:, 8216:]), wbr=np.ascontiguousarray(w_br),
                  wout=np.ascontiguousarray(w_o))
    maps = [dict(common, x=np.ascontiguousarray(x[c * TPC:(c + 1) * TPC]), yT=yTs[c]) for c in range(NCORES)]
    res = run(P, maps)
    return np.concatenate([r["xnew"] for r in res], axis=0)


def build_F():
    P = Prog()
    x_d = P.dram_in("x", [TPC, D])
    gb_d = P.dram_in("gb", [128, D])
    out_d = P.dram_out("y", [TPC, D])
    gb = P.sbuf("gb", [128, D])
    P.load("sp", gb, gb[:], gb_d)
    xt = [P.sbuf(f"xt{i}", [128, D]) for i in range(2)]
    yt = [P.sbuf(f"yt{i}", [128, D]) for i in range(2)]
    junk = P.sbuf("junk", [128, D], BF16)
    ss = [P.sbuf(f"ss{i}", [128, 16]) for i in range(2)]
    for i in range(TPC // 128):
        b = i % 2
        ts = slice(i * 128, (i + 1) * 128)
        P.load("sp", xt[b], xt[b][:], x_d[ts, :])
        P.op("act", lambda e, b=b: e.activation(out=junk[:], in_=xt[b][:], func=AF.Square, accum_out=ss[b][:, 0:1]),
             reads=[xt[b]], writes=[junk, ss[b]])
        P.op("act", lambda e, b=b: e.activation(out=ss[b][:, 1:2], in_=ss[b][:, 0:1], func=AF.Sqrt, scale=1.0 / D,
                                                bias=EPS), reads=[ss[b]], writes=[ss[b]])
        P.op("dve", lambda e, b=b: e.reciprocal(out=ss[b][:, 1:2], in_=ss[b][:, 1:2]), reads=[ss[b]], writes=[ss[b]])
        P.op("dve", lambda e, b=b: e.scalar_tensor_tensor(out=yt[b][:], in0=xt[b][:], scalar=ss[b][:, 1:2], in1=gb[:],
                                                          op0=ALU.mult, op1=ALU.mult),
             reads=[xt[b], ss[b], gb], writes=[yt[b]])
        P.store("sp", yt[b], out_d[ts, :], yt[b][:])
    return P


def run_F(x, g):
    P = build_F()
    maps = [dict(x=np.ascontiguousarray(x[c * TPC:(c + 1) * TPC]), gb=bcast_rows(g)) for c in range(NCORES)]
    res = run(P, maps)
    return np.concatenate([r["y"] for r in res], axis=0)


def layer_forward(xs, L, inp):
    g = lambda k: np.asarray(inp[k][L], np.float32)
    w_in_l = g("w_in")
    hA = run_A(xs, g("norm_g"), w_in_l, g("attn_q_norm"), g("attn_k_norm"))
    yc = run_ATT(hA)
    yf, yb = run_S5(hA, g("ssm_a_re"), g("ssm_a_im"), g("ssm_log_step"), g("ssm_b_re"), g("ssm_b_im"),
                    g("ssm_c_re"), g("ssm_c_im"))
    of, ob = run_DN(hA, g("dn_conv"), g("dn_a_log"), g("dn_dt_bias"))
    yTs = run_C1(xs, g("norm_g"), w_in_l, np.asarray(inp["mem"], np.float32)[0], g("mem_norm_g"), g("w_mem_kv"),
                 hA, yf, yb, g("ssm_d"), g("ssm_w_glu"), g("ssm_b_glu"), of, ob, g("dn_norm_g"), yc)
    return run_C2(xs, g("norm_g"), w_in_l, yTs, g("w_branch"), g("w_out"))


def kernel(**inp):
    xs = np.asarray(inp["x"], np.float32)[0]
    for L in range(2):
        xs = layer_forward(xs, L, inp)
    out = run_F(xs, np.asarray(inp["final_norm_g"], np.float32))
    return out[None].astype(np.float32)
```

```python
from contextlib import ExitStack
import os
import numpy as np
import concourse.bass as bass
import concourse.mybir as mybir
from concourse.bass_utils import run_bass_kernel_spmd

F32 = mybir.dt.float32
BF16 = mybir.dt.bfloat16
I32 = mybir.dt.int32
AF = mybir.ActivationFunctionType
ALU = mybir.AluOpType
AX = mybir.AxisListType

NCORES = 8
D = 2048
SEQ = 8192
TPC = SEQ // NCORES
EPS = 1e-6
TWO_PI = float(2 * np.pi)


class Buf:
    def __init__(self, name, t=None):
        self.name = name
        self.t = t
        self.w = None
        self.r = []
        self.dsem = None
        self.dcnt = 0

    def __getitem__(self, k):
        return self.t[k]


class Prog:
    ENG = ("sp", "act", "dve", "pool", "pe")

    def __init__(self):
        self.nc = bass.Bass("TRN2", target_bir_lowering=False)
        self.ctx = ExitStack()
        self.streams = {e: [] for e in self.ENG}
        self.seq = {e: 0 for e in self.ENG}
        self.esem = {e: self.ctx.enter_context(self.nc.semaphore("es_" + e)) for e in self.ENG}
        self.store_bufs = []
        self.nid = 0

    def dram_in(self, name, shape, dt=F32):
        return self.nc.dram_tensor(name, list(shape), dt, kind="ExternalInput").ap()

    def dram_out(self, name, shape, dt=F32):
        return self.nc.dram_tensor(name, list(shape), dt, kind="ExternalOutput").ap()

    def sbuf(self, name, shape, dt=F32):
        t = self.ctx.enter_context(self.nc.sbuf_tensor("sb_" + name, list(shape), dt))
        esz = 2 if dt == BF16 else 4
        nbytes = int(np.prod(shape[1:])) * esz
        rem = (-nbytes) % 64
        if rem > 32:
            self.ctx.enter_context(self.nc.sbuf_tensor("pad_" + name, [shape[0], 8], F32))
        elif 0 < rem <= 32 and ((nbytes + 31) // 32 * 32) % 64 != 0:
            self.ctx.enter_context(self.nc.sbuf_tensor("pad_" + name, [shape[0], 8], F32))
        return Buf(name, t)

    def psum(self, name, shape, dt=F32):
        t = self.ctx.enter_context(self.nc.psum_tensor("ps_" + name, list(shape), dt))
        return Buf(name, t)

    def _deps(self, reads, writes):
        toks = []
        for b in reads:
            if b.w is not None:
                toks.append((b.w[0], b.w[1], "raw:" + str(b.w[2])))
        for b in writes:
            if b.w is not None:
                toks.append(b.w)
            toks.extend(b.r)
        return toks

    def op(self, eng, fn, reads=(), writes=()):
        toks = self._deps(reads, writes)
        self.seq[eng] += 1
        tok = (self.esem[eng], self.seq[eng], eng)
        self.streams[eng].append((toks, fn, (self.esem[eng], 1)))
        for b in reads:
            b.r.append(tok)
        for b in writes:
            b.w = tok
            b.r = []

    def dma(self, eng, pairs, owner, reads=(), writes=()):
        if owner.dsem is None:
            self.nid += 1
            owner.dsem = self.ctx.enter_context(self.nc.semaphore("ds%d" % self.nid))
        toks = self._deps(reads, writes)
        owner.dcnt += len(pairs)
        tok = (owner.dsem, 16 * owner.dcnt, "dma")

        def fn(e, pairs=pairs):
            return [e.dma_start(out=o, in_=i) for (o, i) in pairs]

        self.streams[eng].append((toks, fn, (owner.dsem, 16)))
        for b in reads:
            b.r.append(tok)
        for b in writes:
            b.w = tok
            b.r = []

    def load(self, eng, buf, out_ap, in_ap):
        self.dma(eng, [(out_ap, in_ap)], buf, writes=[buf])

    def store(self, eng, buf, out_ap, in_ap):
        if buf not in self.store_bufs:
            self.store_bufs.append(buf)
        self.dma(eng, [(out_ap, in_ap)], buf, reads=[buf])

    def finish(self):
        final = [(b.dsem, 16 * b.dcnt, "dma") for b in self.store_bufs]
        self.streams["sp"].append((final, None, None))
        streams = self.streams

        def emit(name, e):
            waited = {}
            for toks, fn, inc in streams[name]:
                for (sem, val, teng) in toks:
                    if teng == name or (teng == "raw:" + name and name == "pe"):
                        continue
                    k = id(sem)
                    if waited.get(k, 0) >= val:
                        continue
                    e.wait_ge(sem, val)
                    waited[k] = val
                if fn is None:
                    continue
                ins = fn(e)
                if isinstance(ins, list):
                    for i_ in ins:
                        i_.then_inc(inc[0], inc[1])
                else:
                    ins.then_inc(inc[0], inc[1])

        with self.nc.Block() as block:
            @block.sync
            def _(e):
                emit("sp", e)

            @block.scalar
            def _(e):
                emit("act", e)

            @block.vector
            def _(e):
                emit("dve", e)

            @block.gpsimd
            def _(e):
                emit("pool", e)

            @block.tensor
            def _(e):
                emit("pe", e)
        self.ctx.close()
        return self.nc


def run(prog, in_maps):
    nc = prog.finish()
    n = int(os.environ.get("DBG_CORES", NCORES))
    if os.environ.get("DBG_TRACE"):
        res = run_bass_kernel_spmd(nc, in_maps[:n], core_ids=list(range(n)), trace=True)
        print("DBG_TRACE exec_time_ns", res.exec_time_ns, flush=True)
    else:
        res = run_bass_kernel_spmd(nc, in_maps[:n], core_ids=list(range(n)))
    out = list(res.results)
    while len(out) < NCORES:
        out.append(out[0])
    return out


def emit_norm_T(P, x_dram, gb, ident, xnT, ntiles, tag, single=False, bufs=None):
    if bufs is None:
        nb = 1 if single else 2
        bufs = dict(xt=[P.sbuf(f"{tag}_xt{i}", [128, D]) for i in range(nb)],
                    junk=P.sbuf(f"{tag}_junk", [128, D], BF16),
                    xn=[P.sbuf(f"{tag}_xn{i}", [128, D]) for i in range(nb)],
                    ss=[P.sbuf(f"{tag}_ss{i}", [128, 16]) for i in range(nb)],
                    tp=[P.psum(f"{tag}_tp{i}", [128, 512]) for i in range(2)])
    xt, junk, xn, ss, tp = bufs["xt"], bufs["junk"], bufs["xn"], bufs["ss"], bufs["tp"]
    nb = len(xt)
    tcount = 0
    for i in range(ntiles):
        b = i % nb
        P.load("sp", xt[b], xt[b][:], x_dram[i * 128:(i + 1) * 128, :])
        P.op("dve", lambda e, b=b: e.memset(ss[b][:], 0.0), writes=[ss[b]])
        P.op("act", lambda e, b=b: e.activation(out=junk[:], in_=xt[b][:], func=AF.Square,
                                                accum_out=ss[b][:, 0:1]),
             reads=[xt[b]], writes=[junk, ss[b]])
        P.op("act", lambda e, b=b: e.activation(out=ss[b][:, 1:2], in_=ss[b][:, 0:1], func=AF.Sqrt,
                                                scale=1.0 / D, bias=EPS),
             reads=[ss[b]], writes=[ss[b]])
        P.op("dve", lambda e, b=b: e.reciprocal(out=ss[b][:, 1:2], in_=ss[b][:, 1:2]),
             reads=[ss[b]], writes=[ss[b]])
        P.op("dve", lambda e, b=b: e.scalar_tensor_tensor(out=xn[b][:], in0=xt[b][:], scalar=ss[b][:, 1:2],
                                                          in1=gb[:], op0=ALU.mult, op1=ALU.mult),
             reads=[xt[b], ss[b], gb], writes=[xn[b]])
        for kk in range(4):
            pb = tp[tcount % 2]
            tcount += 1
            for j in range(4):
                k = kk * 4 + j
                P.op("pe", lambda e, b=b, k=k, j=j, pb=pb: e.transpose(pb[:, j * 128:(j + 1) * 128],
                                                                      xn[b][:, k * 128:(k + 1) * 128], ident[:]),
                     reads=[xn[b], ident], writes=[pb])
            eng = "act" if kk % 2 == 0 else "dve"
            if eng == "act":
                P.op("act", lambda e, kk=kk, i=i, pb=pb: e.copy(
                    out=xnT[:, kk * 4:(kk + 1) * 4, i * 128:(i + 1) * 128],
                    in_=pb[:].rearrange("p (a b) -> p a b", a=4)), reads=[pb], writes=[xnT])
            else:
                P.op("dve", lambda e, kk=kk, i=i, pb=pb: e.tensor_copy(
                    out=xnT[:, kk * 4:(kk + 1) * 4, i * 128:(i + 1) * 128],
                    in_=pb[:].rearrange("p (a b) -> p a b", a=4)), reads=[pb], writes=[xnT])

    return bufs


NA = 4632
NA_MAIN = 4608


def build_A():
    P = Prog()
    x_d = P.dram_in("x", [TPC, D])
    gb_d = P.dram_in("gb", [128, D])
    w_d = P.dram_in("wA", [D, NA])
    id_d = P.dram_in("ident", [128, 128])
    pos_d = P.dram_in("pos", [128, TPC // 128, 64])
    frq_d = P.dram_in("frq", [128, 64])
    qg_d = P.dram_in("qg", [128, 128])
    kg_d = P.dram_in("kg", [128, 128])
    out_d = P.dram_out("hA", [TPC, NA])
    NT = TPC // 128

    ident = P.sbuf("ident", [128, 128])
    gb = P.sbuf("gb", [128, D])
    qg = P.sbuf("qg", [128, 128])
    kg = P.sbuf("kg", [128, 128])
    pos = P.sbuf("pos", [128, NT, 64])
    frq = P.sbuf("frq", [128, 64])
    P.load("sp", ident, ident[:], id_d)
    P.load("sp", gb, gb[:], gb_d)
    P.load("sp", qg, qg[:], qg_d)
    P.load("sp", kg, kg[:], kg_d)
    P.load("sp", pos, pos[:], pos_d)
    P.load("sp", frq, frq[:], frq_d)

    ang = P.sbuf("ang", [128, NT, 64])
    tmpf = P.sbuf("tmpf", [128, NT, 64])
    tmpi = P.sbuf("tmpi", [128, NT, 64], I32)
    cosT = P.sbuf("cosT", [128, NT, 64])
    sinT = P.sbuf("sinT", [128, NT, 64])
    P.op("dve", lambda e: e.tensor_tensor(out=ang[:], in0=pos[:], in1=frq[:].unsqueeze(1).to_broadcast([128, NT, 64]),
                                          op=ALU.mult), reads=[pos, frq], writes=[ang])
    for (dst, shift) in ((sinT, 0.0), (cosT, float(np.pi / 2))):
        if shift != 0.0:
            P.op("dve", lambda e, shift=shift: e.tensor_scalar(out=ang[:], in0=ang[:], scalar1=shift, scalar2=None,
                                                                op0=ALU.add), reads=[ang], writes=[ang])
        P.op("dve", lambda e: e.tensor_scalar(out=tmpf[:], in0=ang[:], scalar1=1.0 / TWO_PI, scalar2=None,
                                              op0=ALU.mult), reads=[ang], writes=[tmpf])
        P.op("dve", lambda e: e.tensor_copy(out=tmpi[:], in_=tmpf[:]), reads=[tmpf], writes=[tmpi])
        P.op("dve", lambda e: e.tensor_copy(out=tmpf[:], in_=tmpi[:]), reads=[tmpi], writes=[tmpf])
        P.op("dve", lambda e: e.scalar_tensor_tensor(out=tmpf[:], in0=tmpf[:], scalar=-TWO_PI, in1=ang[:],
                                                     op0=ALU.mult, op1=ALU.add), reads=[tmpf, ang], writes=[tmpf])
        P.op("act", lambda e, dst=dst: e.activation(out=dst[:], in_=tmpf[:], func=AF.Sin),
             reads=[tmpf], writes=[dst])

    xnT = P.sbuf("xnT", [128, 16, TPC], BF16)
    emit_norm_T(P, x_d, gb, ident, xnT, NT, "nA")

    wblk = [P.sbuf(f"wblk{i}", [128, 16, 512], BF16) for i in range(2)]
    pp = [P.psum(f"pp{i}", [128, 512]) for i in range(2)]
    ot = [P.sbuf(f"ot{i}", [128, 512]) for i in range(3)]
    ss4 = P.sbuf("ss4", [128, 8])
    junk2 = P.sbuf("junk2", [128, 128])
    t1 = P.sbuf("rt1", [128, 4, 64])
    t2 = P.sbuf("rt2", [128, 4, 64])
    qn = P.sbuf("qn", [128, 512])
    w_v = w_d.rearrange("(c p) n -> p c n", p=128)
    nblk = 10
    cnt = 0
    for cb in range(nblk):
        wb = wblk[cb % 2]
        c0 = cb * 512
        ncol = 512 if cb < 9 else NA - NA_MAIN
        P.dma("pool", [(wb[:, k, 0:ncol], w_v[:, k, c0:c0 + ncol]) for k in range(16)], wb, writes=[wb])
        for i in range(NT):
            ps = pp[cnt % 2]
            o = ot[cnt % 3]
            cnt += 1
            for k in range(16):
                P.op("pe", lambda e, ps=ps, k=k, i=i, wb=wb, ncol=ncol: e.matmul(
                    ps[:, 0:ncol], xnT[:, k, i * 128:(i + 1) * 128], wb[:, k, 0:ncol],
                    start=(k == 0), stop=(k == 15)), reads=[xnT, wb], writes=[ps])
            if cb in (6, 7) or cb == 8:
                nh = 4 if cb in (6, 7) else 2
                g = qg if cb in (6, 7) else kg
                sc = (128.0 ** -0.5) if cb in (6, 7) else 1.0
                P.op("dve", lambda e: e.memset(ss4[:], 0.0), writes=[ss4])
                for h in range(nh):
                    P.op("act", lambda e, ps=ps, h=h: e.activation(out=junk2[:], in_=ps[:, h * 128:(h + 1) * 128],
                                                                   func=AF.Square, accum_out=ss4[:, h:h + 1]),
                         reads=[ps], writes=[junk2, ss4])
                P.op("act", lambda e, nh=nh: e.activation(out=ss4[:, 4:4 + nh], in_=ss4[:, 0:nh], func=AF.Sqrt,
                                                          scale=1.0 / 128, bias=EPS), reads=[ss4], writes=[ss4])
                P.op("dve", lambda e, nh=nh: e.reciprocal(out=ss4[:, 4:4 + nh], in_=ss4[:, 4:4 + nh]),
                     reads=[ss4], writes=[ss4])
                if sc != 1.0:
                    P.op("dve", lambda e, nh=nh, sc=sc: e.tensor_scalar(out=ss4[:, 4:4 + nh], in0=ss4[:, 4:4 + nh],
                                                                        scalar1=sc, scalar2=None, op0=ALU.mult),
                         reads=[ss4], writes=[ss4])
                for h in range(nh):
                    P.op("dve", lambda e, ps=ps, h=h, g=g: e.scalar_tensor_tensor(
                        out=qn[:, h * 128:(h + 1) * 128], in0=ps[:, h * 128:(h + 1) * 128],
                        scalar=ss4[:, 4 + h:5 + h], in1=g[:], op0=ALU.mult, op1=ALU.mult),
                         reads=[ps, ss4, g], writes=[qn])
                if nh < 4:
                    P.op("act", lambda e, ps=ps, o=o: e.copy(out=o[:, 256:512], in_=ps[:, 256:512]),
                         reads=[ps], writes=[o])
                W = nh * 128
                qv = qn[:, 0:W].rearrange("p (h i two) -> p h i two", h=nh, two=2)
                ov = o[:, 0:W].rearrange("p (h i two) -> p h i two", h=nh, two=2)
                x0 = qv[:, :, :, 0]
                x1 = qv[:, :, :, 1]
                cb_ = cosT[:, i, :].unsqueeze(1).to_broadcast([128, nh, 64])
                sb_ = sinT[:, i, :].unsqueeze(1).to_broadcast([128, nh, 64])
                a1 = t1[:, 0:nh, :]
                a2 = t2[:, 0:nh, :]
                P.op("dve", lambda e, x0=x0, cb_=cb_, a1=a1: e.tensor_tensor(out=a1, in0=x0, in1=cb_, op=ALU.mult),
                     reads=[qn, cosT], writes=[t1])
                P.op("pool", lambda e, x1=x1, sb_=sb_, a2=a2: e.tensor_tensor(out=a2, in0=x1, in1=sb_, op=ALU.mult),
                     reads=[qn, sinT], writes=[t2])
                P.op("dve", lambda e, ov=ov, a1=a1, a2=a2: e.tensor_tensor(out=ov[:, :, :, 0], in0=a1, in1=a2,
                                                                          op=ALU.subtract),
                     reads=[t1, t2], writes=[o])
                P.op("dve", lambda e, x0=x0, sb_=sb_, a1=a1: e.tensor_tensor(out=a1, in0=x0, in1=sb_, op=ALU.mult),
                     reads=[qn, sinT], writes=[t1])
                P.op("pool", lambda e, x1=x1, cb_=cb_, a2=a2: e.tensor_tensor(out=a2, in0=x1, in1=cb_, op=ALU.mult),
                     reads=[qn, cosT], writes=[t2])
                P.op("dve", lambda e, ov=ov, a1=a1, a2=a2: e.tensor_tensor(out=ov[:, :, :, 1], in0=a1, in1=a2,
                                                                          op=ALU.add),
                     reads=[t1, t2], writes=[o])
            else:
                if cnt % 2 == 0:
                    P.op("act", lambda e, ps=ps, o=o, ncol=ncol: e.copy(out=o[:, 0:ncol], in_=ps[:, 0:ncol]),
                         reads=[ps], writes=[o])
                else:
                    P.op("dve", lambda e, ps=ps, o=o, ncol=ncol: e.tensor_copy(out=o[:, 0:ncol], in_=ps[:, 0:ncol]),
                         reads=[ps], writes=[o])
            P.store("sp", o, out_d[i * 128:(i + 1) * 128, c0:c0 + ncol], o[:, 0:ncol])
    return P


COLS_A = np.concatenate([
    np.arange(0, 768),
    np.arange(1536, 3840),
    np.arange(4632, 6168),
    np.arange(3840, 3864),
])


def rope_consts():
    t = np.arange(SEQ)
    row = (t // 64).astype(np.float32)
    col = (t % 64).astype(np.float32)
    pos = np.concatenate([np.repeat(row[:, None], 32, 1), np.repeat(col[:, None], 32, 1)], axis=1)
    freqs = (10000.0 ** (-np.arange(0, 64, 2, dtype=np.float32) / 64)).astype(np.float32)
    frq = np.concatenate([freqs, freqs])[None, :].repeat(128, 0).astype(np.float32)
    return pos.astype(np.float32), frq


def bcast_rows(v, n=128):
    return np.ascontiguousarray(np.broadcast_to(np.asarray(v, np.float32)[None, :], (n, v.shape[0])))


def run_A(x, norm_g, w_in_l, qn_g, kn_g):
    P = build_A()
    pos, frq = rope_consts()
    wA = np.ascontiguousarray(w_in_l[:, COLS_A])
    common = dict(gb=bcast_rows(norm_g), wA=wA, ident=np.eye(128, dtype=np.float32), frq=frq,
                  qg=bcast_rows(qn_g), kg=bcast_rows(kn_g))
    maps = []
    for c in range(NCORES):
        pc = pos[c * TPC:(c + 1) * TPC].reshape(TPC // 128, 128, 64).transpose(1, 0, 2)
        maps.append(dict(common, x=np.ascontiguousarray(x[c * TPC:(c + 1) * TPC]), pos=np.ascontiguousarray(pc)))
    res = run(P, maps)
    return np.concatenate([r["hA"] for r in res], axis=0)


def build_ATT():
    P = Prog()
    qT_d = P.dram_in("qT", [128, SEQ])
    kT_d = P.dram_in("kT", [128, SEQ])
    v_d = P.dram_in("v", [128, SEQ // 128, 128])
    out_d = P.dram_out("oT", [128, SEQ])
    qT = P.sbuf("qT", [128, SEQ], BF16)
    kT = P.sbuf("kT", [128, SEQ], BF16)
    v = P.sbuf("v", [128, SEQ // 128, 128], BF16)
    ones = P.sbuf("ones", [128, 128], BF16)
    P.op("dve", lambda e: e.memset(ones[:], 1.0), writes=[ones])
    for j in range(4):
        sl = slice(j * 2048, (j + 1) * 2048)
        P.dma("pool", [(kT[:, sl], kT_d[:, sl])], kT, writes=[kT])
        P.dma("pool", [(qT[:, sl], qT_d[:, sl])], qT, writes=[qT])
        P.dma("pool", [(v[:, j * 16:(j + 1) * 16, :], v_d[:, j * 16:(j + 1) * 16, :])], v, writes=[v])
    ps_s = [P.psum(f"s{i}", [128, 512]) for i in range(2)]
    ps_o = [P.psum(f"o{i}", [128, 512]) for i in range(2)]
    ps_d = [P.psum(f"d{i}", [128, 512]) for i in range(2)]
    pt = [P.sbuf(f"pt{i}", [128, 512], BF16) for i in range(3)]
    rd = P.sbuf("rd", [128, 512])
    ot = [P.sbuf(f"ot{i}", [128, 512]) for i in range(2)]
    NKT = SEQ // 128
    NQB = SEQ // 512
    ps_s = ps_s + [P.psum("s2", [128, 512])]
    steps = [(qb, kt) for qb in range(NQB) for kt in range(NKT)]

    def emit_S(idx):
        qb, kt = steps[idx]
        s_ = ps_s[idx % 3]
        P.op("pe", lambda e, s_=s_, kt=kt, qb=qb: e.matmul(s_[:], kT[:, kt * 128:(kt + 1) * 128],
                                                        qT[:, qb * 512:(qb + 1) * 512], start=True, stop=True),
             reads=[kT, qT], writes=[s_])

    emit_S(0)
    emit_S(1)
    for idx, (qb, kt) in enumerate(steps):
        po = ps_o[qb % 2]
        pd = ps_d[qb % 2]
        s_ = ps_s[idx % 3]
        p = pt[idx % 3]
        P.op("act", lambda e, s_=s_, p=p: e.activation(out=p[:], in_=s_[:], func=AF.Exp), reads=[s_], writes=[p])
        if idx + 2 < len(steps):
            emit_S(idx + 2)
        P.op("pe", lambda e, po=po, p=p, kt=kt: e.matmul(po[:], v[:, kt, :], p[:], start=(kt == 0),
                                                      stop=(kt == NKT - 1)), reads=[v, p], writes=[po])
        P.op("pe", lambda e, pd=pd, p=p, kt=kt: e.matmul(pd[:], ones[:], p[:], start=(kt == 0),
                                                      stop=(kt == NKT - 1)), reads=[ones, p], writes=[pd])
        if kt == NKT - 1:
            o = ot[qb % 2]
            P.op("dve", lambda e, pd=pd: e.reciprocal(out=rd[:], in_=pd[:]), reads=[pd], writes=[rd])
            P.op("dve", lambda e, po=po, o=o: e.tensor_tensor(out=o[:], in0=po[:], in1=rd[:], op=ALU.mult),
                 reads=[po, rd], writes=[o])
            P.store("sp", o, out_d[:, qb * 512:(qb + 1) * 512], o[:])
    return P


def run_ATT(hA):
    P = build_ATT()
    q = hA[:, 3072:4096]
    k = hA[:, 4096:4352]
    vv = hA[:, 4352:4608]
    maps = []
    for c in range(NCORES):
        kv = c // 4
        maps.append(dict(
            qT=np.ascontiguousarray(q[:, c * 128:(c + 1) * 128].T),
            kT=np.ascontiguousarray(k[:, kv * 128:(kv + 1) * 128].T),
            v=np.ascontiguousarray(vv[:, kv * 128:(kv + 1) * 128].reshape(SEQ // 128, 128, 128).transpose(1, 0, 2)),
        ))
    res = run(P, maps)
    return np.concatenate([r["oT"].T for r in res], axis=1)


S5C = 512


def _range_reduce_sin(P, dst, src, tmpf, tmpi, shift):
    P.op("dve", lambda e: e.tensor_scalar(out=tmpf[:], in0=src[:], scalar1=shift, scalar2=1.0 / TWO_PI,
                                          op0=ALU.add, op1=ALU.mult), reads=[src], writes=[tmpf])
    P.op("dve", lambda e: e.tensor_copy(out=tmpi[:], in_=tmpf[:]), reads=[tmpf], writes=[tmpi])
    P.op("dve", lambda e: e.tensor_copy(out=tmpf[:], in_=tmpi[:]), reads=[tmpi], writes=[tmpf])
    P.op("dve", lambda e: e.scalar_tensor_tensor(out=tmpf[:], in0=tmpf[:], scalar=-TWO_PI, in1=src[:],
                                                 op0=ALU.mult, op1=ALU.add), reads=[tmpf, src], writes=[tmpf])
    if shift != 0.0:
        P.op("dve", lambda e: e.tensor_scalar(out=tmpf[:], in0=tmpf[:], scalar1=shift, scalar2=None, op0=ALU.add),
             reads=[tmpf], writes=[tmpf])
    P.op("act", lambda e: e.activation(out=dst[:], in_=tmpf[:], func=AF.Sin), reads=[tmpf], writes=[dst])


def build_S5():
    P = Prog()
    NU = 6
    NCH = SEQ // S5C
    uT_d = P.dram_in("uT", [NU, 32, SEQ])
    prm_d = P.dram_in("prm", [NU, 128, 3])
    b_d = P.dram_in("bmat", [NU, 128, 64])
    c_d = P.dram_in("cmat", [NU, 128, 64])
    jidx_d = P.dram_in("jidx", [128, S5C + 1])
    id_d = P.dram_in("ident", [128, 128])
    out_d = P.dram_out("yT", [NU, 32, SEQ])

    ident = P.sbuf("ident", [128, 128])
    jidx = P.sbuf("jidx", [128, S5C + 1])
    P.load("sp", ident, ident[:], id_d)
    P.load("sp", jidx, jidx[:], jidx_d)
    prm = P.sbuf("prm", [128, 3])
    bm = P.sbuf("bm", [128, 64])
    cm = P.sbuf("cm", [128, 64])
    sc = P.sbuf("sc", [128, 16])
    s1f = P.sbuf("s1f", [128, 1])
    s1i = P.sbuf("s1i", [128, 1], I32)
    th = P.sbuf("th", [128, 1])
    ph = P.sbuf("ph", [128, S5C + 1])
    tmpf = P.sbuf("tmpf", [128, S5C + 1])
    tmpi = P.sbuf("tmpi", [128, S5C + 1], I32)
    Pc = P.sbuf("Pc", [128, S5C + 1])
    Ps = P.sbuf("Ps", [128, S5C + 1])
    rt = P.sbuf("rt", [128, S5C])
    bb = P.sbuf("bb", [128, 64])
    bt1 = P.sbuf("bt1", [128, 32])
    BT = P.sbuf("BT", [32, 256], BF16)
    CT = P.sbuf("CT", [128, 96], BF16)
    pst = P.psum("pst", [32, 256])
    ps_re = [P.psum(f"psre{i}", [128, S5C]) for i in range(2)]
    ps_im = [P.psum(f"psim{i}", [128, S5C]) for i in range(2)]
    ps_y = [P.psum(f"psy{i}", [32, S5C]) for i in range(2)]
    ut = [P.sbuf(f"ut{i}", [32, S5C], BF16) for i in range(2)]
    m = [P.sbuf(f"m{i}", [128, S5C]) for i in range(4)]
    cre = P.sbuf("cre_", [128, S5C])
    cim = P.sbuf("cim_", [128, S5C])
    zre = [P.sbuf(f"zre{i}", [128, S5C]) for i in range(2)]
    zim = [P.sbuf(f"zim{i}", [128, S5C]) for i in range(2)]
    nn = [P.sbuf(f"nn{i}", [128, S5C], BF16) for i in range(4)]
    init = P.sbuf("init", [128, 4])
    yt = [P.sbuf(f"yt{i}", [32, S5C]) for i in range(2)]

    def col(t, j):
        return t[:, j:j + 1]

    for u in range(NU):
        P.load("sp", prm, prm[:], prm_d[u])
        P.load("sp", bm, bm[:], b_d[u])
        P.load("sp", cm, cm[:], c_d[u])
        P.op("act", lambda e: e.activation(out=col(sc, 0), in_=col(prm, 2), func=AF.Exp), reads=[prm], writes=[sc])
        P.op("dve", lambda e: e.tensor_tensor(out=col(sc, 10), in0=col(prm, 0), in1=col(sc, 0), op=ALU.mult),
             reads=[prm, sc], writes=[sc])
        P.op("act", lambda e: e.activation(out=col(sc, 1), in_=col(sc, 10), func=AF.Exp), reads=[sc], writes=[sc])
        P.op("dve", lambda e: e.tensor_tensor(out=col(sc, 2), in0=col(prm, 1), in1=col(sc, 0), op=ALU.mult),
             reads=[prm, sc], writes=[sc])
        P.op("dve", lambda e: e.tensor_scalar(out=s1f[:], in0=col(sc, 2), scalar1=1.0 / TWO_PI, scalar2=None,
                                              op0=ALU.mult), reads=[sc], writes=[s1f])
        P.op("dve", lambda e: e.tensor_copy(out=s1i[:], in_=s1f[:]), reads=[s1f], writes=[s1i])
        P.op("dve", lambda e: e.tensor_copy(out=s1f[:], in_=s1i[:]), reads=[s1i], writes=[s1f])
        P.op("dve", lambda e: e.scalar_tensor_tensor(out=th[:], in0=s1f[:], scalar=-TWO_PI, in1=col(sc, 2),
                                                     op0=ALU.mult, op1=ALU.add), reads=[s1f, sc], writes=[th])
        P.op("dve", lambda e: e.tensor_scalar(out=ph[:], in0=jidx[:], scalar1=th[:, 0:1], scalar2=None,
                                              op0=ALU.mult), reads=[jidx, th], writes=[ph])
        _range_reduce_sin(P, Ps, ph, tmpf, tmpi, 0.0)
        _range_reduce_sin(P, Pc, ph, tmpf, tmpi, float(np.pi / 2))
        P.op("dve", lambda e: e.tensor_tensor(out=col(sc, 5), in0=col(sc, 1), in1=col(Pc, 1), op=ALU.mult),
             reads=[sc, Pc], writes=[sc])
        P.op("dve", lambda e: e.tensor_scalar(out=col(sc, 5), in0=col(sc, 5), scalar1=-1.0, scalar2=None,
                                              op0=ALU.add), reads=[sc], writes=[sc])
        P.op("dve", lambda e: e.tensor_tensor(out=col(sc, 6), in0=col(sc, 1), in1=col(Ps, 1), op=ALU.mult),
             reads=[sc, Ps], writes=[sc])
        P.op("dve", lambda e: e.tensor_tensor(out=col(sc, 7), in0=col(prm, 0), in1=col(prm, 0), op=ALU.mult),
             reads=[prm], writes=[sc])
        P.op("dve", lambda e: e.scalar_tensor_tensor(out=col(sc, 7), in0=col(prm, 1), scalar=col(prm, 1),
                                                     in1=col(sc, 7), op0=ALU.mult, op1=ALU.add),
             reads=[prm, sc], writes=[sc])
        P.op("dve", lambda e: e.reciprocal(out=col(sc, 7), in_=col(sc, 7)), reads=[sc], writes=[sc])
        P.op("dve", lambda e: e.tensor_tensor(out=col(sc, 10), in0=col(sc, 5), in1=col(prm, 0), op=ALU.mult),
             reads=[sc, prm], writes=[sc])
        P.op("dve", lambda e: e.scalar_tensor_tensor(out=col(sc, 10), in0=col(sc, 6), scalar=col(prm, 1),
                                                     in1=col(sc, 10), op0=ALU.mult, op1=ALU.add),
             reads=[sc, prm], writes=[sc])
        P.op("dve", lambda e: e.tensor_tensor(out=col(sc, 8), in0=col(sc, 10), in1=col(sc, 7), op=ALU.mult),
             reads=[sc], writes=[sc])
        P.op("dve", lambda e: e.tensor_tensor(out=col(sc, 11), in0=col(sc, 5), in1=col(prm, 1), op=ALU.mult),
             reads=[sc, prm], writes=[sc])
        P.op("dve", lambda e: e.scalar_tensor_tensor(out=col(sc, 11), in0=col(sc, 6), scalar=col(prm, 0),
                                                     in1=col(sc, 11), op0=ALU.mult, op1=ALU.subtract),
             reads=[sc, prm], writes=[sc])
        P.op("dve", lambda e: e.tensor_tensor(out=col(sc, 9), in0=col(sc, 11), in1=col(sc, 7), op=ALU.mult),
             reads=[sc], writes=[sc])
        P.op("dve", lambda e: e.tensor_scalar(out=bt1[:], in0=bm[:, 32:64], scalar1=col(sc, 9), scalar2=None,
                                              op0=ALU.mult), reads=[bm, sc], writes=[bt1])
        P.op("dve", lambda e: e.scalar_tensor_tensor(out=bb[:, 0:32], in0=bm[:, 0:32], scalar=col(sc, 8),
                                                     in1=bt1[:], op0=ALU.mult, op1=ALU.subtract),
             reads=[bm, sc, bt1], writes=[bb])
        P.op("dve", lambda e: e.tensor_scalar(out=bt1[:], in0=bm[:, 0:32], scalar1=col(sc, 9), scalar2=None,
                                              op0=ALU.mult), reads=[bm, sc, bb], writes=[bt1])
        P.op("dve", lambda e: e.scalar_tensor_tensor(out=bb[:, 32:64], in0=bm[:, 32:64], scalar=col(sc, 8),
                                                     in1=bt1[:], op0=ALU.mult, op1=ALU.add),
             reads=[bm, sc, bt1], writes=[bb])
        P.op("pe", lambda e: e.transpose(pst[:, 0:128], bb[:, 0:32], ident[:]), reads=[bb, ident], writes=[pst])
        P.op("pe", lambda e: e.transpose(pst[:, 128:256], bb[:, 32:64], ident[:]), reads=[bb, ident], writes=[pst])
        P.op("dve", lambda e: e.tensor_copy(out=BT[:], in_=pst[:]), reads=[pst], writes=[BT])
        P.op("dve", lambda e: e.tensor_copy(out=CT[:, 0:32], in_=cm[:, 0:32]), reads=[cm], writes=[CT])
        P.op("dve", lambda e: e.tensor_scalar(out=CT[:, 32:96], in0=cm[:, 0:64], scalar1=-1.0, scalar2=None,
                                              op0=ALU.mult), reads=[cm], writes=[CT])
        P.op("dve", lambda e: e.tensor_scalar(out=rt[:], in0=jidx[:, 0:S5C], scalar1=0.0, scalar2=col(sc, 1),
                                              op0=ALU.mult, op1=ALU.add), reads=[jidx, sc], writes=[rt])
        for ch in range(NCH):
            b = ch % 2
            tsl = slice(ch * S5C, (ch + 1) * S5C)
            P.dma("pool", [(ut[b][:], uT_d[u, :, tsl])], ut[b], writes=[ut[b]])
            pr, pi_ = ps_re[b], ps_im[b]
            P.op("pe", lambda e, pr=pr, b=b: e.matmul(pr[:], BT[:, 0:128], ut[b][:], start=True, stop=True),
                 reads=[BT, ut[b]], writes=[pr])
            P.op("pe", lambda e, pi_=pi_, b=b: e.matmul(pi_[:], BT[:, 128:256], ut[b][:], start=True, stop=True),
                 reads=[BT, ut[b]], writes=[pi_])
            PcS, PsS = Pc[:, 0:S5C], Ps[:, 0:S5C]
            P.op("dve", lambda e, pr=pr: e.tensor_tensor(out=m[0][:], in0=pr[:], in1=PcS, op=ALU.mult),
                 reads=[pr, Pc], writes=[m[0]])
            P.op("dve", lambda e, pi_=pi_: e.tensor_tensor(out=m[1][:], in0=pi_[:], in1=PsS, op=ALU.mult),
                 reads=[pi_, Ps], writes=[m[1]])
            P.op("pool", lambda e: e.tensor_tensor(out=cre[:], in0=m[0][:], in1=m[1][:], op=ALU.add),
                 reads=[m[0], m[1]], writes=[cre])
            P.op("dve", lambda e, pi_=pi_: e.tensor_tensor(out=m[2][:], in0=pi_[:], in1=PcS, op=ALU.mult),
                 reads=[pi_, Pc], writes=[m[2]])
            P.op("dve", lambda e, pr=pr: e.tensor_tensor(out=m[3][:], in0=pr[:], in1=PsS, op=ALU.mult),
                 reads=[pr, Ps], writes=[m[3]])
            P.op("pool", lambda e: e.tensor_tensor(out=cim[:], in0=m[2][:], in1=m[3][:], op=ALU.subtract),
                 reads=[m[2], m[3]], writes=[cim])
            if ch == 0:
                P.op("dve", lambda e: e.memset(init[:], 0.0), writes=[init])
            else:
                pzr, pzi = zre[1 - b], zim[1 - b]
                L = S5C - 1
                P.op("dve", lambda e, pzi=pzi: e.tensor_tensor(out=col(init, 2), in0=col(pzi, L), in1=col(Ps, S5C),
                                                              op=ALU.mult), reads=[pzi, Ps], writes=[init])
                P.op("dve", lambda e, pzr=pzr: e.scalar_tensor_tensor(out=col(init, 0), in0=col(pzr, L),
                                                                     scalar=col(Pc, S5C), in1=col(init, 2),
                                                                     op0=ALU.mult, op1=ALU.subtract),
                     reads=[pzr, Pc, init], writes=[init])
                P.op("dve", lambda e, pzr=pzr: e.tensor_tensor(out=col(init, 3), in0=col(pzr, L), in1=col(Ps, S5C),
                                                              op=ALU.mult), reads=[pzr, Ps], writes=[init])
                P.op("dve", lambda e, pzi=pzi: e.scalar_tensor_tensor(out=col(init, 1), in0=col(pzi, L),
                                                                     scalar=col(Pc, S5C), in1=col(init, 3),
                                                                     op0=ALU.mult, op1=ALU.add),
                     reads=[pzi, Pc, init], writes=[init])
            zr, zi = zre[b], zim[b]
            P.op("dve", lambda e, zr=zr: e.tensor_tensor_scan(out=zr[:], data0=rt[:], data1=cre[:],
                                                             initial=col(init, 0), op0=ALU.mult, op1=ALU.add),
                 reads=[rt, cre, init], writes=[zr])
            P.op("dve", lambda e, zi=zi: e.tensor_tensor_scan(out=zi[:], data0=rt[:], data1=cim[:],
                                                             initial=col(init, 1), op0=ALU.mult, op1=ALU.add),
                 reads=[rt, cim, init], writes=[zi])
            P.op("pool", lambda e, zr=zr: e.tensor_tensor(out=nn[0][:], in0=zr[:], in1=PcS, op=ALU.mult),
                 reads=[zr, Pc], writes=[nn[0]])
            P.op("pool", lambda e, zi=zi: e.tensor_tensor(out=nn[1][:], in0=zi[:], in1=PsS, op=ALU.mult),
                 reads=[zi, Ps], writes=[nn[1]])
            P.op("pool", lambda e, zi=zi: e.tensor_tensor(out=nn[2][:], in0=zi[:], in1=PcS, op=ALU.mult),
                 reads=[zi, Pc], writes=[nn[2]])
            P.op("dve", lambda e, zr=zr: e.tensor_tensor(out=nn[3][:], in0=zr[:], in1=PsS, op=ALU.mult),
                 reads=[zr, Ps], writes=[nn[3]])
            py = ps_y[b]
            lts = [CT[:, 0:32], CT[:, 32:64], CT[:, 64:96], CT[:, 64:96]]
            for q in range(4):
                P.op("pe", lambda e, py=py, q=q, lt=lts[q]: e.matmul(py[:], lt, nn[q][:], start=(q == 0),
                                                                    stop=(q == 3)), reads=[CT, nn[q]], writes=[py])
            y = yt[b]
            P.op("act", lambda e, py=py, y=y: e.copy(out=y[:], in_=py[:]), reads=[py], writes=[y])
            P.store("sp", y, out_d[u, :, tsl], y[:])
    return P


def run_S5(hA, a_re, a_im, log_step, b_re, b_im, c_re, c_im):
    P = build_S5()
    u = hA[:, 0:768]
    uT = np.ascontiguousarray(u.T)
    uTr = np.ascontiguousarray(uT[:, ::-1])
    jidx = bcast_rows(np.arange(S5C + 1, dtype=np.float32))
    maps = []
    for c in range(NCORES):
        uTc = np.zeros((6, 32, SEQ), np.float32)
        prm = np.zeros((6, 128, 3), np.float32)
        bmat = np.zeros((6, 128, 64), np.float32)
        cmat = np.zeros((6, 128, 64), np.float32)
        for pq in range(3):
            for d in range(2):
                un = pq * 2 + d
                for gg in range(2):
                    g = c * 6 + pq * 2 + gg
                    src = uTr if d == 1 else uT
                    uTc[un, gg * 16:(gg + 1) * 16] = src[g * 16:(g + 1) * 16]
                    rs = slice(gg * 64, (gg + 1) * 64)
                    prm[un, rs, 0] = a_re[d, g]
                    prm[un, rs, 1] = a_im[d, g]
                    prm[un, rs, 2] = log_step[d, g]
                    bmat[un, rs, gg * 16:(gg + 1) * 16] = b_re[d, g]
                    bmat[un, rs, 32 + gg * 16:32 + (gg + 1) * 16] = b_im[d, g]
                    cmat[un, rs, gg * 16:(gg + 1) * 16] = c_re[d, g].T
                    cmat[un, rs, 32 + gg * 16:32 + (gg + 1) * 16] = c_im[d, g].T
        maps.append(dict(uT=uTc, prm=prm, bmat=bmat, cmat=cmat, jidx=jidx, ident=np.eye(128, dtype=np.float32)))
    res = run(P, maps)
    yf = np.zeros((SEQ, 768), np.float32)
    yb = np.zeros((SEQ, 768), np.float32)
    for c in range(NCORES):
        yT = res[c]["yT"]
        for pq in range(3):
            cs = slice((c * 6 + pq * 2) * 16, (c * 6 + pq * 2 + 2) * 16)
            yf[:, cs] = yT[pq * 2].T
            yb[:, cs] = yT[pq * 2 + 1].T[::-1]
    return yf, yb


DNC = 128
NDC = SEQ // DNC


def build_DN():
    P = Prog()
    NU = 2
    xin_d = P.dram_in("xin", [NU, 3, 128, SEQ + 4])
    cw_d = P.dram_in("cw", [NU, 128, 15])
    ab_d = P.dram_in("ab", [NU, 128, 2, NDC])
    hp_d = P.dram_in("hp", [NU, 128, 2])
    id_d = P.dram_in("ident", [128, 128])
    tri_d = P.dram_in("triu", [128, 128])
    mb_d = P.dram_in("maskb", [128, 128])
    m0_d = P.dram_in("msk0", [128, 128])
    mT_d = P.dram_in("mskT", [128, 6, 128])
    out_d = P.dram_out("o", [NU, SEQ, 128])

    ident = P.sbuf("ident", [128, 128])
    identb = P.sbuf("identb", [128, 128], BF16)
    triu = P.sbuf("triu", [128, 128])
    maskb = P.sbuf("maskb", [128, 128])
    onesf = P.sbuf("onesf", [128, 128])
    P.load("sp", ident, ident[:], id_d)
    P.load("sp", triu, triu[:], tri_d)
    P.load("sp", maskb, maskb[:], mb_d)
    onesb = P.sbuf("onesb", [128, 128], BF16)
    triub = P.sbuf("triub", [128, 128], BF16)
    P.op("dve", lambda e: e.memset(onesf[:], 1.0), writes=[onesf])
    P.op("dve", lambda e: e.memset(onesb[:], 1.0), writes=[onesb])
    P.op("dve", lambda e: e.tensor_copy(out=triub[:], in_=triu[:]), reads=[triu], writes=[triub])
    P.op("dve", lambda e: e.tensor_copy(out=identb[:], in_=ident[:]), reads=[ident], writes=[identb])

    qT = P.sbuf("qT", [128, SEQ], BF16)
    kT = P.sbuf("kT", [128, SEQ], BF16)
    vT = P.sbuf("vT", [128, SEQ], BF16)
    cw = P.sbuf("cw", [128, 15])
    ab = P.sbuf("ab", [128, 2, NDC])
    hp = P.sbuf("hp", [128, 16])
    PW = 2048
    xp = [P.sbuf(f"xp{i}", [128, PW + 4]) for i in range(2)]
    acc = P.sbuf("acc", [128, PW])
    sq = P.sbuf("sq", [128, PW], BF16)
    ghl = P.sbuf("ghl", [128, 2, NDC], BF16)
    gtmp = P.sbuf("gtmp", [128, NDC])
    dgh = P.sbuf("dgh", [128, 128], BF16)
    dgl = P.sbuf("dgl", [128, 128], BF16)
    dgt = P.sbuf("dgt", [128, 128])
    rn = P.sbuf("rn", [128, 512])
    pss = [P.psum(f"pss{i}", [128, 512]) for i in range(2)]

    tb = {n: P.sbuf("tb_" + n, [128, NDC]) for n in ("g", "beta", "gc", "egc", "negegc", "egl", "ekd", "tmp")}
    pt64 = pss[0]

    ptr = P.psum("ptr", [128, 256], BF16)
    KV = P.sbuf("KV", [128, 256], BF16)
    dg = P.sbuf("dg", [128, 128])
    kTc = P.sbuf("kTc", [128, 128], BF16)
    qTc = P.sbuf("qTc", [128, 128], BF16)
    Winvw = P.sbuf("Winvw", [128, 128], BF16)
    pg = P.psum("pg", [128, 128])
    xe = P.sbuf("xe", [128, 128])
    E = P.sbuf("E", [128, 128])
    Es = P.sbuf("Es", [128, 128])
    pkk = P.psum("pkk", [128, 256])
    AT = P.sbuf("AT", [128, 128], BF16)
    X = [P.sbuf(f"X{i}", [128, 128], BF16) for i in range(2)]
    XT = [P.sbuf(f"XT{i}", [128, 128], BF16) for i in range(2)]
    W = [P.sbuf(f"W{i}", [128, 128]) for i in range(2)]
    Wb = [P.sbuf(f"Wb{i}", [128, 128], BF16) for i in range(2)]
    Xw = [P.sbuf(f"Xw{i}", [128, 128], BF16) for i in range(2)]
    XTw = [P.sbuf(f"XTw{i}", [128, 128], BF16) for i in range(2)]
    x0f = P.sbuf("x0f", [128, 128])
    UT = P.sbuf("UT", [128, 128])
    G32 = P.sbuf("G32", [128, 128])
    Gm = P.sbuf("Gm", [128, 128], BF16)
    Gw = P.sbuf("Gw", [128, 128], BF16)
    GTw = P.sbuf("GTw", [128, 128], BF16)
    Ysb = P.sbuf("Ysb", [128, 128], BF16)
    CT = [P.sbuf(f"CTl{i}", [128, 128], BF16) for i in range(6)]
    msk0 = P.sbuf("msk0", [128, 128])
    mskT = P.sbuf("mskT", [128, 6, 128])
    P.load("sp", msk0, msk0[:], m0_d)
    P.load("sp", mskT, mskT[:], mT_d)
    pX = P.psum("pX", [128, 256])
    pW = P.psum("pW", [128, 128])
    S = P.sbuf("S", [128, 128])
    Sb = P.sbuf("Sb", [128, 128], BF16)
    pks = P.psum("pks", [128, 256])
    Rp = P.sbuf("Rp", [128, 128], BF16)
    vnew = P.sbuf("vnew", [128, 128], BF16)
    oq = P.sbuf("oq", [128, 128])
    ot = [P.sbuf(f"ot{i}", [128, 128]) for i in range(2)]
    Kd = P.sbuf("Kd", [128, 128], BF16)
    pgs = pss[0]
    TT = []
    for i in range(2):
        T = {n: P.sbuf(f"T{i}_{n}", [128, 128]) for n in ("dg", "dgt", "xe", "E", "Es", "x0f", "UT", "G32")}
        T.update({n: P.sbuf(f"T{i}_{n}", [128, 128], BF16) for n in ("dgh", "dgl", "kTc", "qTc", "Gm", "GTw", "Ysb")})
        T["CT"] = [P.sbuf(f"T{i}_CT{l}", [128, 128], BF16) for l in range(6)]
        TT.append(T)
    ORing = [(P.sbuf(f"O{i}_KV", [128, 256], BF16), P.sbuf(f"O{i}_AT", [128, 128], BF16),
              P.sbuf(f"O{i}_Gw", [128, 128], BF16)) for i in range(4)]

    def col(t, j):
        return t[:, j:j + 1]

    for u in range(NU):
        P.load("sp", cw, cw[:], cw_d[u])
        P.load("sp", ab, ab[:], ab_d[u])
        P.load("sp", hp, hp[:, 0:2], hp_d[u])
        cnt = 0
        for ti, dst in enumerate((qT, kT, vT)):
            for pc in range(SEQ // PW):
                x_ = xp[cnt % 2]
                cnt += 1
                P.load("sp", x_, x_[:], xin_d[u, ti, :, pc * PW:pc * PW + PW + 4])
                P.op("act", lambda e, x_=x_, ti=ti: e.activation(out=acc[:], in_=x_[:, 0:PW], func=AF.Copy,
                                                                scale=col(cw, ti * 5)), reads=[x_, cw], writes=[acc])
                for k in range(1, 5):
                    P.op("dve", lambda e, x_=x_, ti=ti, k=k: e.scalar_tensor_tensor(
                        out=acc[:], in0=x_[:, k:k + PW], scalar=col(cw, ti * 5 + k), in1=acc[:],
                        op0=ALU.mult, op1=ALU.add), reads=[x_, cw, acc], writes=[acc])
                dsl = slice(pc * PW, (pc + 1) * PW)
                if ti == 2:
                    P.op("act", lambda e, dsl=dsl: e.activation(out=vT[:, dsl], in_=acc[:], func=AF.Silu),
                         reads=[acc], writes=[vT])
                    continue
                P.op("act", lambda e: e.activation(out=acc[:], in_=acc[:], func=AF.Silu), reads=[acc], writes=[acc])
                P.op("pool", lambda e: e.tensor_tensor(out=sq[:], in0=acc[:], in1=acc[:], op=ALU.mult),
                     reads=[acc], writes=[sq])
                for j in range(PW // 512):
                    ps = pss[j % 2]
                    js = slice(j * 512, (j + 1) * 512)
                    P.op("pe", lambda e, ps=ps, js=js: e.matmul(ps[:], onesb[:], sq[:, js], start=True, stop=True),
                         reads=[onesb, sq], writes=[ps])
                    P.op("act", lambda e, ps=ps: e.activation(out=rn[:], in_=ps[:], func=AF.Sqrt, bias=EPS),
                         reads=[ps], writes=[rn])
                    P.op("dve", lambda e: e.reciprocal(out=rn[:], in_=rn[:]), reads=[rn], writes=[rn])
                    scl = (128.0 ** -0.5) if ti == 0 else 1.0
                    P.op("dve", lambda e, js=js, dst=dst, pc=pc, j=j, scl=scl: e.scalar_tensor_tensor(
                        out=dst[:, pc * PW + j * 512:pc * PW + (j + 1) * 512], in0=acc[:, js], scalar=scl, in1=rn[:],
                        op0=ALU.mult, op1=ALU.mult), reads=[acc, rn], writes=[dst])
        P.op("act", lambda e: e.activation(out=tb["tmp"][:], in_=ab[:, 0, :], func=AF.Exp, bias=col(hp, 1)),
             reads=[ab, hp], writes=[tb["tmp"]])
        P.op("act", lambda e: e.activation(out=tb["tmp"][:], in_=tb["tmp"][:], func=AF.Ln, bias=1.0),
             reads=[tb["tmp"]], writes=[tb["tmp"]])
        P.op("act", lambda e: e.activation(out=col(hp, 2), in_=col(hp, 0), func=AF.Exp), reads=[hp], writes=[hp])
        P.op("dve", lambda e: e.tensor_scalar(out=col(hp, 3), in0=col(hp, 2), scalar1=-1.0, scalar2=None,
                                              op0=ALU.mult), reads=[hp], writes=[hp])
        P.op("dve", lambda e: e.tensor_scalar(out=tb["g"][:], in0=tb["tmp"][:], scalar1=col(hp, 3), scalar2=None,
                                              op0=ALU.mult), reads=[tb["tmp"], hp], writes=[tb["g"]])
        P.op("act", lambda e: e.activation(out=tb["beta"][:], in_=ab[:, 1, :], func=AF.Sigmoid),
             reads=[ab], writes=[tb["beta"]])
        P.op("dve", lambda e: e.tensor_copy(out=ghl[:, 0, :], in_=tb["g"][:]), reads=[tb["g"]], writes=[ghl])
        P.op("dve", lambda e: e.tensor_copy(out=gtmp[:], in_=ghl[:, 0, :]), reads=[ghl], writes=[gtmp])
        P.op("dve", lambda e: e.tensor_tensor(out=ghl[:, 1, :], in0=tb["g"][:], in1=gtmp[:], op=ALU.subtract),
             reads=[tb["g"], gtmp], writes=[ghl])
        for hl in range(2):
            P.op("pe", lambda e, hl=hl: e.matmul(pt64[:, 0:NDC], triub[:], ghl[:, hl, :], start=(hl == 0),
                                                 stop=(hl == 1)), reads=[triub, ghl], writes=[pt64])
        for hl in range(2):
            P.op("pe", lambda e, hl=hl: e.matmul(pt64[:, NDC:2 * NDC], onesb[:], ghl[:, hl, :], start=(hl == 0),
                                                 stop=(hl == 1)), reads=[onesb, ghl], writes=[pt64])
        P.op("dve", lambda e: e.tensor_copy(out=tb["gc"][:], in_=pt64[:, 0:NDC]), reads=[pt64], writes=[tb["gc"]])
        P.op("dve", lambda e: e.tensor_copy(out=gtmp[:], in_=pt64[:, NDC:2 * NDC]), reads=[pt64], writes=[gtmp])
        P.op("act", lambda e: e.activation(out=tb["egc"][:], in_=tb["gc"][:], func=AF.Exp),
             reads=[tb["gc"]], writes=[tb["egc"]])
        P.op("dve", lambda e: e.tensor_scalar(out=tb["negegc"][:], in0=tb["egc"][:], scalar1=-1.0, scalar2=None,
                                              op0=ALU.mult), reads=[tb["egc"]], writes=[tb["negegc"]])
        P.op("act", lambda e: e.activation(out=tb["egl"][:], in_=gtmp[:], func=AF.Exp),
             reads=[gtmp], writes=[tb["egl"]])
        P.op("dve", lambda e: e.tensor_tensor(out=tb["tmp"][:], in0=gtmp[:], in1=tb["gc"][:],
                                              op=ALU.subtract), reads=[gtmp, tb["gc"]], writes=[tb["tmp"]])
        P.op("act", lambda e: e.activation(out=tb["ekd"][:], in_=tb["tmp"][:], func=AF.Exp),
             reads=[tb["tmp"]], writes=[tb["ekd"]])
        P.op("dve", lambda e: e.memset(S[:], 0.0), writes=[S])
        P.op("dve", lambda e: e.memset(Sb[:], 0.0), writes=[Sb])
        def pre(c, T, O):
            csl = slice(c * DNC, (c + 1) * DNC)
            gcc, bec = col(tb["gc"], c), col(tb["beta"], c)
            KV_, AT_, Gw_ = O
            P.op("pe", lambda e: e.transpose(ptr[:, 0:128], kT[:, csl], identb[:]), reads=[kT, identb], writes=[ptr])
            P.op("pe", lambda e: e.transpose(ptr[:, 128:256], vT[:, csl], identb[:]), reads=[vT, identb], writes=[ptr])
            P.op("act", lambda e: e.copy(out=KV_[:], in_=ptr[:]), reads=[ptr], writes=[KV_])
            yield
            P.op("dve", lambda e: e.tensor_scalar(out=T["dg"][:], in0=ident[:], scalar1=gcc, scalar2=None,
                                                  op0=ALU.mult), reads=[ident, tb["gc"]], writes=[T["dg"]])
            yield
            P.op("dve", lambda e: e.tensor_copy(out=T["dgh"][:], in_=T["dg"][:]), reads=[T["dg"]], writes=[T["dgh"]])
            yield
            P.op("pool", lambda e: e.tensor_copy(out=T["dgt"][:], in_=T["dgh"][:]), reads=[T["dgh"]], writes=[T["dgt"]])
            yield
            P.op("pool", lambda e: e.tensor_tensor(out=T["dgl"][:], in0=T["dg"][:], in1=T["dgt"][:], op=ALU.subtract),
                 reads=[T["dg"], T["dgt"]], writes=[T["dgl"]])
            yield
            P.op("pe", lambda e: e.matmul(pg[:], onesb[:], T["dgh"][:], start=True, stop=False),
                 reads=[onesb, T["dgh"]], writes=[pg])
            P.op("pe", lambda e: e.matmul(pg[:], onesb[:], T["dgl"][:], start=False, stop=True),
                 reads=[onesb, T["dgl"]], writes=[pg])
            P.op("dve", lambda e: e.tensor_scalar(out=T["xe"][:], in0=pg[:], scalar1=gcc, scalar2=0.0,
                                                  op0=ALU.subtract, op1=ALU.min), reads=[pg, tb["gc"]], writes=[T["xe"]])
            yield
            P.op("pool", lambda e: e.tensor_tensor(out=T["xe"][:], in0=T["xe"][:], in1=maskb[:], op=ALU.add),
                 reads=[T["xe"], maskb], writes=[T["xe"]])
            yield
            P.op("act", lambda e: e.activation(out=T["E"][:], in_=T["xe"][:], func=AF.Exp), reads=[T["xe"]], writes=[T["E"]])
            yield
            P.op("pool", lambda e: e.tensor_copy(out=T["kTc"][:], in_=kT[:, csl]), reads=[kT], writes=[T["kTc"]])
            yield
            P.op("pool", lambda e: e.tensor_copy(out=T["qTc"][:], in_=qT[:, csl]), reads=[qT], writes=[T["qTc"]])
            yield
            P.op("pe", lambda e: e.matmul(pkk[:, 0:128], kT[:, csl], T["kTc"][:], start=True, stop=True),
                 reads=[kT, T["kTc"]], writes=[pkk])
            P.op("pe", lambda e: e.matmul(pkk[:, 128:256], kT[:, csl], T["qTc"][:], start=True, stop=True),
                 reads=[kT, T["qTc"]], writes=[pkk])
            P.op("pool", lambda e: e.tensor_tensor(out=T["Es"][:], in0=T["E"][:], in1=ident[:], op=ALU.subtract),
                 reads=[T["E"], ident], writes=[T["Es"]])
            P.op("dve", lambda e: e.scalar_tensor_tensor(out=T["x0f"][:], in0=pkk[:, 0:128], scalar=bec,
                                                         in1=T["Es"][:], op0=ALU.mult, op1=ALU.mult),
                 reads=[pkk, tb["beta"], T["Es"]], writes=[T["x0f"]])
            P.op("dve", lambda e: e.tensor_tensor(out=AT_[:], in0=pkk[:, 128:256], in1=T["E"][:], op=ALU.mult),
                 reads=[pkk, T["E"]], writes=[AT_])
            yield
            P.op("pe", lambda e: e.transpose(pss[1][:, 0:128], T["x0f"][:], ident[:]), reads=[T["x0f"], ident],
                 writes=[pss[1]])
            P.op("dve", lambda e: e.tensor_copy(out=T["UT"][:], in_=pss[1][:, 0:128]), reads=[pss[1]], writes=[T["UT"]])
            yield
            for lv in range(1, 7):
                eng = "pool" if lv % 2 else "dve"
                P.op(eng, lambda e, lv=lv: e.tensor_tensor(out=T["CT"][lv - 1][:], in0=T["UT"][:],
                                                           in1=mskT[:, lv - 1, :], op=ALU.mult),
                     reads=[T["UT"], mskT], writes=[T["CT"][lv - 1]])
                yield
            P.op("dve", lambda e: e.tensor_tensor(out=T["G32"][:], in0=T["x0f"][:], in1=msk0[:], op=ALU.mult),
                 reads=[T["x0f"], msk0], writes=[T["G32"]])
            yield
            P.op("dve", lambda e: e.tensor_tensor(out=T["G32"][:], in0=ident[:], in1=T["G32"][:], op=ALU.subtract),
                 reads=[ident, T["G32"]], writes=[T["G32"]])
            yield
            P.op("act", lambda e: e.copy(out=T["Gm"][:], in_=T["G32"][:]), reads=[T["G32"]], writes=[T["Gm"]])
            yield
            P.op("pool", lambda e: e.tensor_copy(out=Gw_[:], in_=T["G32"][:]), reads=[T["G32"]], writes=[Gw_])
            yield
            for lv in range(1, 7):
                P.op("pe", lambda e, lv=lv: e.matmul(pX[:, 0:128], T["CT"][lv - 1][:], T["Gm"][:], start=True, stop=True),
                     reads=[T["CT"][lv - 1], T["Gm"]], writes=[pX])
                P.op("pe", lambda e: e.transpose(ptr[:, 0:128], Gw_[:], identb[:]), reads=[Gw_, identb], writes=[ptr])
                P.op("act", lambda e: e.copy(out=T["Ysb"][:], in_=pX[:, 0:128]), reads=[pX], writes=[T["Ysb"]])
                P.op("act", lambda e: e.copy(out=T["GTw"][:], in_=ptr[:, 0:128]), reads=[ptr], writes=[T["GTw"]])
                yield
                P.op("pe", lambda e: e.matmul(pW[:], T["GTw"][:], T["Ysb"][:], start=True, stop=True),
                     reads=[T["GTw"], T["Ysb"]], writes=[pW])
                P.op("dve", lambda e: e.tensor_tensor(out=T["G32"][:], in0=T["G32"][:], in1=pW[:], op=ALU.subtract),
                     reads=[T["G32"], pW], writes=[T["G32"]])
                yield
                if lv < 6:
                    P.op("act", lambda e: e.copy(out=T["Gm"][:], in_=T["G32"][:]), reads=[T["G32"]], writes=[T["Gm"]])
                    yield
                P.op("pool", lambda e: e.tensor_copy(out=Gw_[:], in_=T["G32"][:]), reads=[T["G32"]], writes=[Gw_])
                yield

        def seq(c, O):
            csl = slice(c * DNC, (c + 1) * DNC)
            bec = col(tb["beta"], c)
            KV_, AT_, Gw_ = O
            P.op("pe", lambda e: e.matmul(pks[:, 0:128], kT[:, csl], Sb[:], start=True, stop=True),
                 reads=[kT, Sb], writes=[pks])
            P.op("pe", lambda e: e.matmul(pks[:, 128:256], qT[:, csl], Sb[:], start=True, stop=True),
                 reads=[qT, Sb], writes=[pks])
            yield
            P.op("dve", lambda e: e.scalar_tensor_tensor(out=Rp[:], in0=pks[:, 0:128], scalar=col(tb["negegc"], c),
                                                         in1=KV_[:, 128:256], op0=ALU.mult, op1=ALU.add),
                 reads=[pks, tb["negegc"], KV_], writes=[Rp])
            yield
            P.op("pe", lambda e: e.matmul(pgs[:, 0:128], Gw_[:], Rp[:], start=True, stop=True), reads=[Gw_, Rp], writes=[pgs])
            yield
            P.op("dve", lambda e: e.tensor_scalar(out=vnew[:], in0=pgs[:, 0:128], scalar1=bec, scalar2=None, op0=ALU.mult),
                 reads=[pgs, tb["beta"]], writes=[vnew])
            yield
            P.op("pe", lambda e: e.matmul(pgs[:, 0:128], AT_[:], vnew[:], start=True, stop=True), reads=[AT_, vnew], writes=[pgs])
            yield
            P.op("dve", lambda e: e.tensor_scalar(out=oq[:], in0=pks[:, 128:256], scalar1=col(tb["egc"], c),
                                                  scalar2=None, op0=ALU.mult), reads=[pks, tb["egc"]], writes=[oq])
            yield
            o = ot[c % 2]
            P.op("dve", lambda e: e.tensor_tensor(out=o[:], in0=pgs[:, 0:128], in1=oq[:], op=ALU.add),
                 reads=[pgs, oq], writes=[o])
            P.store("sp", o, out_d[u, csl, :], o[:])
            yield
            P.op("pool", lambda e: e.tensor_scalar(out=Kd[:], in0=KV_[:, 0:128], scalar1=col(tb["ekd"], c),
                                                   scalar2=None, op0=ALU.mult), reads=[KV_, tb["ekd"]], writes=[Kd])
            yield
            P.op("pe", lambda e: e.matmul(pks[:, 0:128], Kd[:], vnew[:], start=True, stop=True),
                 reads=[Kd, vnew], writes=[pks])
            yield
            P.op("dve", lambda e: e.scalar_tensor_tensor(out=S[:], in0=S[:], scalar=col(tb["egl"], c),
                                                         in1=pks[:, 0:128], op0=ALU.mult, op1=ALU.add),
                 reads=[S, tb["egl"], pks], writes=[S])
            yield
            P.op("act", lambda e: e.copy(out=Sb[:], in_=S[:]), reads=[S], writes=[Sb])
            yield

        def seq_pair(c0):
            for cc in (c0, c0 + 1):
                yield from seq(cc, ORing[cc % 4])

        def round_robin(gens):
            gens = list(gens)
            while gens:
                for g_ in list(gens):
                    try:
                        next(g_)
                    except StopIteration:
                        gens.remove(g_)

        nch = int(os.environ.get('DN_NCH', NDC))
        for p in range(nch // 2):
            gens = [pre(2 * p, TT[0], ORing[(2 * p) % 4]), pre(2 * p + 1, TT[1], ORing[(2 * p + 1) % 4])]
            if p >= 1:
                gens.append(seq_pair(2 * p - 2))
            round_robin(gens)
        if nch >= 2:
            round_robin([seq_pair(nch - 2)])
    return P


DN_UNITS = [(h, d) for h in range(6) for d in range(2)]


def run_DN(hA, conv_w, a_log, dt_bias):
    P = build_DN()
    qkv = hA[:, 768:3072]
    da = hA[:, 4608:4620]
    db = hA[:, 4620:4632]
    ii = np.arange(128)
    triu = (ii[:, None] <= ii[None, :]).astype(np.float32)
    maskb = np.where(ii[None, :] >= ii[:, None], 0.0, -30000.0).astype(np.float32)
    msk0 = np.zeros((128, 128), np.float32)
    mskT = np.zeros((128, 6, 128), np.float32)
    for lv in range(7):
        b = 1 << lv
        jj, i2 = np.meshgrid(ii, ii, indexing="ij")
        m = ((jj // (2 * b) == i2 // (2 * b)) & (jj % (2 * b) < b) & (i2 % (2 * b) >= b)).astype(np.float32)
        if lv == 0:
            msk0 = m
        else:
            mskT[:, lv - 1, :] = m.T
    units = DN_UNITS + DN_UNITS[:4]
    maps = []
    for c in range(NCORES):
        xin = np.zeros((2, 3, 128, SEQ + 4), np.float32)
        cw = np.zeros((2, 128, 15), np.float32)
        ab = np.zeros((2, 128, 2, NDC), np.float32)
        hp = np.zeros((2, 128, 2), np.float32)
        for s in range(2):
            h, d = units[c * 2 + s]
            for ti in range(3):
                cs = slice(ti * 768 + h * 128, ti * 768 + (h + 1) * 128)
                xt = qkv[:, cs].T
                w = conv_w[cs]
                if d == 1:
                    xt = xt[:, ::-1]
                    w = w[:, ::-1]
                xin[s, ti, :, 2:2 + SEQ] = xt
                cw[s, :, ti * 5:(ti + 1) * 5] = w
            av = da[:, d * 6 + h]
            bv = db[:, d * 6 + h]
            if d == 1:
                av = av[::-1]
                bv = bv[::-1]
            ab[s, :, 0, :] = av.reshape(NDC, 128).T
            ab[s, :, 1, :] = bv.reshape(NDC, 128).T
            hp[s, :, 0] = a_log[d, h]
            hp[s, :, 1] = dt_bias[d, h]
        maps.append(dict(xin=xin, cw=cw, ab=ab, hp=hp, ident=np.eye(128, dtype=np.float32), triu=triu, maskb=maskb,
                         msk0=msk0, mskT=mskT))
    res = run(P, maps)
    of = np.zeros((SEQ, 768), np.float32)
    ob = np.zeros((SEQ, 768), np.float32)
    for idx, (h, d) in enumerate(DN_UNITS):
        o = res[idx // 2]["o"][idx % 2]
        if d == 0:
            of[:, h * 128:(h + 1) * 128] = o
        else:
            ob[:, h * 128:(h + 1) * 128] = o[::-1]
    return of, ob


COLS_Z = np.concatenate([np.arange(768, 1536), np.arange(3864, 4632), np.arange(6168, 7192),
                         np.arange(7704, 8216), np.arange(7192, 7704)])
NZ = 3584


def build_C1():
    P = Prog()
    x_d = P.dram_in("x", [TPC, D])
    gb_d = P.dram_in("gb", [128, D])
    id_d = P.dram_in("ident", [128, 128])
    wz_d = P.dram_in("wz", [D, NZ])
    mem_d = P.dram_in("mem", [256, D])
    mgb_d = P.dram_in("mgb", [128, D])
    wkv_d = P.dram_in("wkv", [D, 1024])
    s5_d = P.dram_in("s5", [3, 768, TPC])
    sd_d = P.dram_in("sd", [128, 16])
    wglu_d = P.dram_in("wglu", [768, 768])
    dn_d = P.dram_in("dn", [2, 768, TPC])
    oc_d = P.dram_in("oc", [1024, TPC])
    y_d = P.dram_out("yT", [24, 128, TPC], BF16)

    ident = P.sbuf("ident", [128, 128])
    gb = P.sbuf("gb", [128, D])
    sd = P.sbuf("sd", [128, 16])
    onesb = P.sbuf("onesb", [128, 128], BF16)
    P.load("sp", ident, ident[:], id_d)
    P.load("sp", gb, gb[:], gb_d)
    P.load("sp", sd, sd[:], sd_d)
    P.op("dve", lambda e: e.memset(onesb[:], 1.0), writes=[onesb])
    xnT = P.sbuf("xnT", [128, 16, TPC], BF16)
    nb = emit_norm_T(P, x_d, gb, ident, xnT, TPC // 128, "nC", single=True)
    memnT = P.sbuf("memnT", [128, 16, 256], BF16)
    P.load("sp", gb, gb[:], mgb_d)
    emit_norm_T(P, mem_d, gb, ident, memnT, 2, "nC", bufs=nb)

    pp = [P.psum(f"pp{i}", [128, 512]) for i in range(2)]
    pa = [P.psum(f"pa{i}", [128, 512]) for i in range(2)]
    po = P.psum("po", [128, 512])
    pd = P.psum("pd", [128, 512])

    wk = P.sbuf("wk", [128, 16, 512], BF16)
    KmT = P.sbuf("KmT", [128, 4, 256], BF16)
    Vm = P.sbuf("Vm", [128, 2, 512], BF16)
    wkv_v = wkv_d.rearrange("(c p) n -> p c n", p=128)
    P.dma("pool", [(wk[:, k, :], wkv_v[:, k, 0:512]) for k in range(16)], wk, writes=[wk])
    for h in range(4):
        ps = pp[h % 2]
        for k in range(16):
            P.op("pe", lambda e, ps=ps, k=k, h=h: e.matmul(ps[:, 0:256], wk[:, k, h * 128:(h + 1) * 128],
                                                          memnT[:, k, :], start=(k == 0), stop=(k == 15)),
                 reads=[wk, memnT], writes=[ps])
        P.op("act", lambda e, ps=ps, h=h: e.copy(out=KmT[:, h, :], in_=ps[:, 0:256]), reads=[ps], writes=[KmT])
    P.dma("pool", [(wk[:, k, :], wkv_v[:, k, 512:1024]) for k in range(16)], wk, writes=[wk])
    for mt in range(2):
        ps = pp[mt % 2]
        for k in range(16):
            P.op("pe", lambda e, ps=ps, k=k, mt=mt: e.matmul(ps[:], memnT[:, k, mt * 128:(mt + 1) * 128],
                                                            wk[:, k, :], start=(k == 0), stop=(k == 15)),
                 reads=[wk, memnT], writes=[ps])
        P.op("act", lambda e, ps=ps, mt=mt: e.copy(out=Vm[:, mt, :], in_=ps[:]), reads=[ps], writes=[Vm])

    wj = [P.sbuf(f"wj{i}", [128, 16, 128], BF16) for i in range(2)]
    sz = [P.sbuf(f"sz{i}", [128, TPC], BF16) for i in range(2)]
    wz_v = wz_d.rearrange("(c p) n -> p c n", p=128)
    cnt = {"w": 0, "p": 0}
    stg = [nb["xt"][0], nb["xn"][0]]

    def projT(col0, dst_ap_fn, func, dst_buf):
        w = wj[cnt["w"] % 2]
        cnt["w"] += 1
        P.dma("pool", [(w[:, k, :], wz_v[:, k, col0:col0 + 128]) for k in range(16)], w, writes=[w])
        for half in range(2):
            ps = pp[cnt["p"] % 2]
            cnt["p"] += 1
            hs = slice(half * 512, (half + 1) * 512)
            for k in range(16):
                P.op("pe", lambda e, ps=ps, k=k, w=w, hs=hs: e.matmul(ps[:], w[:, k, :], xnT[:, k, hs],
                                                                     start=(k == 0), stop=(k == 15)),
                     reads=[w, xnT], writes=[ps])
            P.op("act", lambda e, ps=ps, hs=hs: e.activation(out=dst_ap_fn(hs), in_=ps[:], func=func),
                 reads=[ps], writes=[dst_buf])

    def proj_silu(col0):
        z = sz[cnt["w"] % 2]
        projT(col0, lambda hs, z=z: z[:, hs], AF.Silu, z)
        return z

    f1 = [P.sbuf(f"f1_{i}", [128, TPC]) for i in range(2)]
    f2 = [P.sbuf(f"f2_{i}", [128, TPC]) for i in range(2)]
    f3 = P.sbuf("f3", [128, TPC])
    f4 = P.sbuf("f4", [128, TPC])
    yo = [P.sbuf(f"yo{i}", [128, TPC], BF16) for i in range(2)]
    sqb = P.sbuf("sqb", [128, TPC], BF16)
    rn = P.sbuf("rn", [128, 512])
    sg = P.sbuf("sg", [128, 512])
    ycnt = {"n": 0}

    def next_yo():
        y = yo[ycnt["n"] % 2]
        ycnt["n"] += 1
        return y

    GY = P.sbuf("GY", [128, 6, TPC], BF16)
    wglu = P.sbuf("wglu", [128, 6, 768], BF16)
    wglu_v = wglu_d.rearrange("(c p) n -> p c n", p=128)
    P.dma("pool", [(wglu[:, k, :], wglu_v[:, k, :]) for k in range(6)], wglu, writes=[wglu])
    for j in range(6):
        a, b_ = f1[j % 2], f2[j % 2]
        rs = slice(j * 128, (j + 1) * 128)
        P.load("sp", a, a[:], s5_d[0, rs, :])
        P.load("sp", b_, b_[:], s5_d[1, rs, :])
        P.load("sp", f3, f3[:], s5_d[2, rs, :])
        P.op("pool", lambda e, a=a, b_=b_: e.tensor_tensor(out=a[:], in0=a[:], in1=b_[:], op=ALU.add),
             reads=[a, b_], writes=[a])
        P.op("dve", lambda e, a=a, j=j: e.scalar_tensor_tensor(out=a[:], in0=f3[:], scalar=sd[:, j:j + 1], in1=a[:],
                                                              op0=ALU.mult, op1=ALU.add), reads=[f3, sd, a], writes=[a])
        P.op("pool", lambda e, a=a, b_=b_: e.tensor_tensor(out=b_[:], in0=a[:], in1=a[:], op=ALU.mult),
             reads=[a], writes=[b_])
        P.op("dve", lambda e, b_=b_: e.tensor_scalar(out=b_[:], in0=b_[:], scalar1=0.044715, scalar2=1.0,
                                                     op0=ALU.mult, op1=ALU.add), reads=[b_], writes=[b_])
        P.op("pool", lambda e, a=a, b_=b_: e.tensor_tensor(out=b_[:], in0=b_[:], in1=a[:], op=ALU.mult),
             reads=[a, b_], writes=[b_])
        P.op("act", lambda e, b_=b_: e.activation(out=f4[:], in_=b_[:], func=AF.Sigmoid, scale=1.5957691216057308),
             reads=[b_], writes=[f4])
        P.op("dve", lambda e, a=a, j=j: e.tensor_tensor(out=GY[:, j, :], in0=a[:], in1=f4[:], op=ALU.mult),
             reads=[a, f4], writes=[GY])
    for j in range(6):
        z = proj_silu(0 + j * 128)
        y = next_yo()
        for half in range(2):
            hs = slice(half * 512, (half + 1) * 512)
            ps = pa[half]
            for k in range(6):
                P.op("pe", lambda e, ps=ps, k=k, j=j, hs=hs: e.matmul(ps[:], wglu[:, k, j * 128:(j + 1) * 128],
                                                                     GY[:, k, hs], start=(k == 0), stop=(k == 5)),
                     reads=[wglu, GY], writes=[ps])
            P.op("act", lambda e, ps=ps, j=j: e.activation(out=sg[:], in_=ps[:], func=AF.Sigmoid,
                                                           bias=sd[:, 6 + j:7 + j]), reads=[ps, sd], writes=[sg])
            P.op("dve", lambda e, j=j, hs=hs: e.tensor_tensor(out=sg[:], in0=sg[:], in1=GY[:, j, hs], op=ALU.mult),
                 reads=[sg, GY], writes=[sg])
            P.op("dve", lambda e, y=y, z=z, hs=hs: e.tensor_tensor(out=y[:, hs], in0=sg[:], in1=z[:, hs], op=ALU.mult),
                 reads=[sg, z], writes=[y])
        P.store("sp", y, y_d[j], y[:])
    for h in range(6):
        a, b_ = f1[h % 2], f2[h % 2]
        rs = slice(h * 128, (h + 1) * 128)
        P.load("sp", a, a[:], dn_d[0, rs, :])
        P.load("sp", b_, b_[:], dn_d[1, rs, :])
        P.op("pool", lambda e, a=a, b_=b_: e.tensor_tensor(out=a[:], in0=a[:], in1=b_[:], op=ALU.add),
             reads=[a, b_], writes=[a])
        P.op("pool", lambda e, a=a: e.tensor_tensor(out=sqb[:], in0=a[:], in1=a[:], op=ALU.mult),
             reads=[a], writes=[sqb])
        z = proj_silu(768 + h * 128)
        y = next_yo()
        for half in range(2):
            hs = slice(half * 512, (half + 1) * 512)
            ps = pa[half]
            P.op("pe", lambda e, ps=ps, hs=hs: e.matmul(ps[:], onesb[:], sqb[:, hs], start=True, stop=True),
                 reads=[onesb, sqb], writes=[ps])
            P.op("act", lambda e, ps=ps: e.activation(out=rn[:], in_=ps[:], func=AF.Sqrt, scale=1.0 / 128, bias=EPS),
                 reads=[ps], writes=[rn])
            P.op("dve", lambda e: e.reciprocal(out=rn[:], in_=rn[:]), reads=[rn], writes=[rn])
            P.op("dve", lambda e, a=a, hs=hs: e.scalar_tensor_tensor(out=rn[:], in0=a[:, hs], scalar=sd[:, 12:13],
                                                                    in1=rn[:], op0=ALU.mult, op1=ALU.mult),
                 reads=[a, sd, rn], writes=[rn])
            P.op("dve", lambda e, y=y, z=z, hs=hs: e.tensor_tensor(out=y[:, hs], in0=rn[:], in1=z[:, hs], op=ALU.mult),
                 reads=[rn, z], writes=[y])
        P.store("sp", y, y_d[6 + h], y[:])
    for c in range(8):
        a = f1[c % 2]
        P.load("sp", a, a[:], oc_d[c * 128:(c + 1) * 128, :])
        z = proj_silu(1536 + c * 128)
        y = next_yo()
        P.op("dve", lambda e, a=a, y=y, z=z: e.tensor_tensor(out=y[:], in0=a[:], in1=z[:], op=ALU.mult),
             reads=[a, z], writes=[y])
        P.store("sp", y, y_d[12 + c], y[:])
    QmT = P.sbuf("QmT", [128, TPC], BF16)
    pm = [P.sbuf(f"pm{i}", [128, 512], BF16) for i in range(2)]
    pc = 0
    for h in range(4):
        projT(3072 + h * 128, lambda hs: QmT[:, hs], AF.Copy, QmT)
        z = proj_silu(2560 + h * 128)
        y = next_yo()
        for half in range(2):
            hs = slice(half * 512, (half + 1) * 512)
            for mt in range(2):
                ps = pa[mt]
                p_ = pm[pc % 2]
                pc += 1
                P.op("pe", lambda e, ps=ps, h=h, mt=mt, hs=hs: e.matmul(ps[:], KmT[:, h, mt * 128:(mt + 1) * 128],
                                                                       QmT[:, hs], start=True, stop=True),
                     reads=[KmT, QmT], writes=[ps])
                P.op("act", lambda e, ps=ps, p_=p_: e.activation(out=p_[:], in_=ps[:], func=AF.Exp, scale=128.0 ** -0.5),
                     reads=[ps], writes=[p_])
                P.op("pe", lambda e, p_=p_, h=h, mt=mt: e.matmul(po[:], Vm[:, mt, h * 128:(h + 1) * 128], p_[:],
                                                                start=(mt == 0), stop=(mt == 1)),
                     reads=[Vm, p_], writes=[po])
                P.op("pe", lambda e, p_=p_, mt=mt: e.matmul(pd[:], onesb[:], p_[:], start=(mt == 0), stop=(mt == 1)),
                     reads=[onesb, p_], writes=[pd])
            P.op("dve", lambda e: e.reciprocal(out=rn[:], in_=pd[:]), reads=[pd], writes=[rn])
            P.op("dve", lambda e: e.tensor_tensor(out=rn[:], in0=po[:], in1=rn[:], op=ALU.mult),
                 reads=[po, rn], writes=[rn])
            P.op("dve", lambda e, y=y, z=z, hs=hs: e.tensor_tensor(out=y[:, hs], in0=rn[:], in1=z[:, hs], op=ALU.mult),
                 reads=[rn, z], writes=[y])
        P.store("sp", y, y_d[20 + h], y[:])
    return P


def run_C1(x, norm_g, w_in_l, mem, mem_g, w_kv, hA, yf, yb, ssm_d, w_glu, b_glu, of, ob, dn_g, yc):
    P = build_C1()
    sd = np.zeros((128, 16), np.float32)
    sd[:, 0:6] = np.asarray(ssm_d, np.float32).reshape(6, 128).T
    sd[:, 6:12] = np.asarray(b_glu, np.float32).reshape(6, 128).T
    sd[:, 12] = np.asarray(dn_g, np.float32)
    common = dict(gb=bcast_rows(norm_g), ident=np.eye(128, dtype=np.float32),
                  wz=np.ascontiguousarray(w_in_l[:, COLS_Z]), mem=np.ascontiguousarray(mem),
                  mgb=bcast_rows(mem_g), wkv=np.ascontiguousarray(w_kv), sd=sd,
                  wglu=np.ascontiguousarray(w_glu))
    maps = []
    for c in range(NCORES):
        ts = slice(c * TPC, (c + 1) * TPC)
        s5 = np.stack([yf[ts].T, yb[ts].T, hA[ts, 0:768].T]).astype(np.float32)
        dn = np.stack([of[ts].T, ob[ts].T]).astype(np.float32)
        maps.append(dict(common, x=np.ascontiguousarray(x[ts]), s5=np.ascontiguousarray(s5),
                         dn=np.ascontiguousarray(dn), oc=np.ascontiguousarray(yc[ts].T)))
    res = run(P, maps)
    return [r["yT"] for r in res]


BR_CHUNKS = [(0, 6), (6, 12), (12, 20), (20, 24)]


def build_C2():
    P = Prog()
    x_d = P.dram_in("x", [TPC, D])
    gb_d = P.dram_in("gb", [128, D])
    id_d = P.dram_in("ident", [128, 128])
    wg_d = P.dram_in("wg", [D, 4 * D])
    y_d = P.dram_in("yT", [24, 128, TPC], BF16)
    wbr_d = P.dram_in("wbr", [3072, D])
    wout_d = P.dram_in("wout", [D, D])
    out_d = P.dram_out("xnew", [TPC, D])

    ident = P.sbuf("ident", [128, 128])
    gb = P.sbuf("gb", [128, D])
    P.load("sp", ident, ident[:], id_d)
    P.load("sp", gb, gb[:], gb_d)
    xnT = P.sbuf("xnT", [128, 16, TPC], BF16)
    nb = emit_norm_T(P, x_d, gb, ident, xnT, TPC // 128, "nD", single=True)
    stg = [nb["xt"][0], nb["xn"][0]]
    stg_v = [t_[:].rearrange("p (k c) -> p k c", c=128) for t_ in stg]
    gb_v = gb[:].rearrange("p (k c) -> p k c", c=128)
    Y = P.sbuf("Y", [128, 24, TPC], BF16)
    for k in range(24):
        P.dma("sp", [(Y[:, k, :], y_d[k])], Y, writes=[Y])
    mT = P.sbuf("mT", [128, 16, TPC], BF16)
    wbr = P.sbuf("wbr", [128, 24, 128], BF16)
    wg = [P.sbuf(f"wg{i}", [128, 16, 128], BF16) for i in range(2)]
    pb = [P.psum(f"pb{i}", [128, 512]) for i in range(2)]
    pg = [P.psum(f"pg{i}", [128, 512]) for i in range(2)]
    sg = [P.sbuf(f"sg{i}", [128, 512]) for i in range(2)]
    acc = P.sbuf("acc", [128, TPC])
    tmp = P.sbuf("tmpm", [128, 512])
    wbr_v = wbr_d.rearrange("(c p) n -> p c n", p=128)
    wg_v = wg_d.rearrange("(c p) n -> p c n", p=128)
    cnt = 0
    wcnt = 0
    for j in range(16):
        js = slice(j * 128, (j + 1) * 128)
        P.dma("sp", [(gb_v[:, k, :], wbr_v[:, k, js]) for k in range(16)], gb, writes=[gb])
        P.op("dve", lambda e: e.tensor_copy(out=wbr[:, 0:16, :], in_=gb_v), reads=[gb], writes=[wbr])
        P.dma("sp", [(gb_v[:, k, :], wbr_v[:, 16 + k, js]) for k in range(8)], gb, writes=[gb])
        P.op("dve", lambda e: e.tensor_copy(out=wbr[:, 16:24, :], in_=gb_v[:, 0:8, :]), reads=[gb], writes=[wbr])
        for b in range(4):
            w = wg[wcnt % 2]
            g0 = b * D + j * 128
            sb_, sv_ = stg[wcnt % 2], stg_v[wcnt % 2]
            wcnt += 1
            P.dma("sp", [(sv_[:, k, :], wg_v[:, k, g0:g0 + 128]) for k in range(16)], sb_, writes=[sb_])
            P.op("pool", lambda e, w=w, sv_=sv_: e.tensor_copy(out=w[:], in_=sv_), reads=[sb_], writes=[w])
            k0, k1 = BR_CHUNKS[b]
            for half in range(2):
                hs = slice(half * 512, (half + 1) * 512)
                p1, p2, s_ = pb[cnt % 2], pg[cnt % 2], sg[cnt % 2]
                cnt += 1
                for k in range(16):
                    P.op("pe", lambda e, p2=p2, k=k, w=w, hs=hs: e.matmul(p2[:], w[:, k, :], xnT[:, k, hs],
                                                                         start=(k == 0), stop=(k == 15)),
                         reads=[w, xnT], writes=[p2])
                for k in range(k0, k1):
                    P.op("pe", lambda e, p1=p1, k=k, hs=hs, k0=k0, k1=k1: e.matmul(p1[:], wbr[:, k, :], Y[:, k, hs],
                                                                                  start=(k == k0), stop=(k == k1 - 1)),
                         reads=[wbr, Y], writes=[p1])
                P.op("act", lambda e, p2=p2, s_=s_: e.activation(out=s_[:], in_=p2[:], func=AF.Sigmoid),
                     reads=[p2], writes=[s_])
                if b == 0:
                    P.op("dve", lambda e, p1=p1, s_=s_, hs=hs: e.tensor_tensor(out=acc[:, hs], in0=p1[:], in1=s_[:],
                                                                              op=ALU.mult), reads=[p1, s_], writes=[acc])
                else:
                    P.op("dve", lambda e, p1=p1, s_=s_: e.tensor_tensor(out=tmp[:], in0=p1[:], in1=s_[:], op=ALU.mult),
                         reads=[p1, s_], writes=[tmp])
                    P.op("pool", lambda e, hs=hs: e.tensor_tensor(out=acc[:, hs], in0=acc[:, hs], in1=tmp[:],
                                                                  op=ALU.add), reads=[acc, tmp], writes=[acc])
        P.op("act", lambda e, j=j: e.copy(out=mT[:, j, :], in_=acc[:]), reads=[acc], writes=[mT])
    wo_view = Y[:, 0:8, :].rearrange("p a (b c) -> p (a b) c", c=512)
    wout_v = wout_d.rearrange("(c p) n -> p c n", p=128)
    xr = [P.sbuf(f"xr{i}", [128, 512]) for i in range(2)]
    ot = [P.sbuf(f"oo{i}", [128, 512]) for i in range(2)]
    cnt = 0
    for cb in range(4):
        cs = slice(cb * 512, (cb + 1) * 512)
        P.dma("pool", [(wo_view[:, k, :], wout_v[:, k, cs]) for k in range(16)], Y, writes=[Y])
        for i in range(TPC // 128):
            ps = pb[cnt % 2]
            r_ = xr[cnt % 2]
            o_ = ot[cnt % 2]
            cnt += 1
            ts = slice(i * 128, (i + 1) * 128)
            P.load("sp", r_, r_[:], x_d[ts, cs])
            for k in range(16):
                P.op("pe", lambda e, ps=ps, k=k, ts=ts: e.matmul(ps[:], mT[:, k, ts], wo_view[:, k, :],
                                                                start=(k == 0), stop=(k == 15)),
                     reads=[mT, Y], writes=[ps])
            P.op("dve", lambda e, ps=ps, r_=r_, o_=o_: e.tensor_tensor(out=o_[:], in0=ps[:], in1=r_[:], op=ALU.add),
                 reads=[ps, r_], writes=[o_])
            P.store("sp", o_, out_d[ts, cs], o_[:])
    return P


def run_C2(x, norm_g, w_in_l, yTs, w_br, w_o):
    P = build_C2()
    common = dict(gb=bcast_rows(norm_g), ident=np.eye(128, dtype=np.float32),
                  wg=np.ascontiguousarray(w_in_l[:, 8216:]), wbr=np.ascontiguousarray(w_br),
                  wout=np.ascontiguousarray(w_o))
    maps = [dict(common, x=np.ascontiguousarray(x[c * TPC:(c + 1) * TPC]), yT=yTs[c]) for c in range(NCORES)]
    res = run(P, maps)
    return np.concatenate([r["xnew"] for r in res], axis=0)


def build_F():
    P = Prog()
    x_d = P.dram_in("x", [TPC, D])
    gb_d = P.dram_in("gb", [128, D])
    out_d = P.dram_out("y", [TPC, D])
    gb = P.sbuf("gb", [128, D])
    P.load("sp", gb, gb[:], gb_d)
    xt = [P.sbuf(f"xt{i}", [128, D]) for i in range(2)]
    yt = [P.sbuf(f"yt{i}", [128, D]) for i in range(2)]
    junk = P.sbuf("junk", [128, D], BF16)
    ss = [P.sbuf(f"ss{i}", [128, 16]) for i in range(2)]
    for i in range(TPC // 128):
        b = i % 2
        ts = slice(i * 128, (i + 1) * 128)
        P.load("sp", xt[b], xt[b][:], x_d[ts, :])
        P.op("act", lambda e, b=b: e.activation(out=junk[:], in_=xt[b][:], func=AF.Square, accum_out=ss[b][:, 0:1]),
             reads=[xt[b]], writes=[junk, ss[b]])
        P.op("act", lambda e, b=b: e.activation(out=ss[b][:, 1:2], in_=ss[b][:, 0:1], func=AF.Sqrt, scale=1.0 / D,
                                                bias=EPS), reads=[ss[b]], writes=[ss[b]])
        P.op("dve", lambda e, b=b: e.reciprocal(out=ss[b][:, 1:2], in_=ss[b][:, 1:2]), reads=[ss[b]], writes=[ss[b]])
        P.op("dve", lambda e, b=b: e.scalar_tensor_tensor(out=yt[b][:], in0=xt[b][:], scalar=ss[b][:, 1:2], in1=gb[:],
                                                          op0=ALU.mult, op1=ALU.mult),
             reads=[xt[b], ss[b], gb], writes=[yt[b]])
        P.store("sp", yt[b], out_d[ts, :], yt[b][:])
    return P


def run_F(x, g):
    P = build_F()
    maps = [dict(x=np.ascontiguousarray(x[c * TPC:(c + 1) * TPC]), gb=bcast_rows(g)) for c in range(NCORES)]
    res = run(P, maps)
    return np.concatenate([r["y"] for r in res], axis=0)


def layer_forward(xs, L, inp):
    g = lambda k: np.asarray(inp[k][L], np.float32)
    w_in_l = g("w_in")
    hA = run_A(xs, g("norm_g"), w_in_l, g("attn_q_norm"), g("attn_k_norm"))
    yc = run_ATT(hA)
    yf, yb = run_S5(hA, g("ssm_a_re"), g("ssm_a_im"), g("ssm_log_step"), g("ssm_b_re"), g("ssm_b_im"),
                    g("ssm_c_re"), g("ssm_c_im"))
    of, ob = run_DN(hA, g("dn_conv"), g("dn_a_log"), g("dn_dt_bias"))
    yTs = run_C1(xs, g("norm_g"), w_in_l, np.asarray(inp["mem"], np.float32)[0], g("mem_norm_g"), g("w_mem_kv"),
                 hA, yf, yb, g("ssm_d"), g("ssm_w_glu"), g("ssm_b_glu"), of, ob, g("dn_norm_g"), yc)
    return run_C2(xs, g("norm_g"), w_in_l, yTs, g("w_branch"), g("w_out"))


def kernel(**inp):
    xs = np.asarray(inp["x"], np.float32)[0]
    for L in range(2):
        xs = layer_forward(xs, L, inp)
    out = run_F(xs, np.asarray(inp["final_norm_g"], np.float32))
    return out[None].astype(np.float32)
```

```python
from contextlib import ExitStack
import os
import numpy as np
import concourse.bass as bass
import concourse.mybir as mybir
from concourse.bass_utils import run_bass_kernel_spmd

F32 = mybir.dt.float32
BF16 = mybir.dt.bfloat16
I32 = mybir.dt.int32
AF = mybir.ActivationFunctionType
ALU = mybir.AluOpType
AX = mybir.AxisListType

NCORES = 8
D = 2048
SEQ = 8192
TPC = SEQ // NCORES
EPS = 1e-6
TWO_PI = float(2 * np.pi)


class Buf:
    def __init__(self, name, t=None):
        self.name = name
        self.t = t
        self.w = None
        self.r = []
        self.dsem = None
        self.dcnt = 0

    def __getitem__(self, k):
        return self.t[k]


class Prog:
    ENG = ("sp", "act", "dve", "pool", "pe")

    def __init__(self):
        self.nc = bass.Bass("TRN2", target_bir_lowering=False)
        self.ctx = ExitStack()
        self.streams = {e: [] for e in self.ENG}
        self.seq = {e: 0 for e in self.ENG}
        self.esem = {e: self.ctx.enter_context(self.nc.semaphore("es_" + e)) for e in self.ENG}
        self.store_bufs = []
        self.nid = 0

    def dram_in(self, name, shape, dt=F32):
        return self.nc.dram_tensor(name, list(shape), dt, kind="ExternalInput").ap()

    def dram_out(self, name, shape, dt=F32):
        return self.nc.dram_tensor(name, list(shape), dt, kind="ExternalOutput").ap()

    def sbuf(self, name, shape, dt=F32):
        t = self.ctx.enter_context(self.nc.sbuf_tensor("sb_" + name, list(shape), dt))
        esz = 2 if dt == BF16 else 4
        nbytes = int(np.prod(shape[1:])) * esz
        rem = (-nbytes) % 64
        if rem > 32:
            self.ctx.enter_context(self.nc.sbuf_tensor("pad_" + name, [shape[0], 8], F32))
        elif 0 < rem <= 32 and ((nbytes + 31) // 32 * 32) % 64 != 0:
            self.ctx.enter_context(self.nc.sbuf_tensor("pad_" + name, [shape[0], 8], F32))
        return Buf(name, t)

    def psum(self, name, shape, dt=F32):
        t = self.ctx.enter_context(self.nc.psum_tensor("ps_" + name, list(shape), dt))
        return Buf(name, t)

    def _deps(self, reads, writes):
        toks = []
        for b in reads:
            if b.w is not None:
                toks.append((b.w[0], b.w[1], "raw:" + str(b.w[2])))
        for b in writes:
            if b.w is not None:
                toks.append(b.w)
            toks.extend(b.r)
        return toks

    def op(self, eng, fn, reads=(), writes=()):
        toks = self._deps(reads, writes)
        self.seq[eng] += 1
        tok = (self.esem[eng], self.seq[eng], eng)
        self.streams[eng].append((toks, fn, (self.esem[eng], 1)))
        for b in reads:
            b.r.append(tok)
        for b in writes:
            b.w = tok
            b.r = []

    def dma(self, eng, pairs, owner, reads=(), writes=()):
        if owner.dsem is None:
            self.nid += 1
            owner.dsem = self.ctx.enter_context(self.nc.semaphore("ds%d" % self.nid))
        toks = self._deps(reads, writes)
        owner.dcnt += len(pairs)
        tok = (owner.dsem, 16 * owner.dcnt, "dma")

        def fn(e, pairs=pairs):
            return [e.dma_start(out=o, in_=i) for (o, i) in pairs]

        self.streams[eng].append((toks, fn, (owner.dsem, 16)))
        for b in reads:
            b.r.append(tok)
        for b in writes:
            b.w = tok
            b.r = []

    def load(self, eng, buf, out_ap, in_ap):
        self.dma(eng, [(out_ap, in_ap)], buf, writes=[buf])

    def store(self, eng, buf, out_ap, in_ap):
        if buf not in self.store_bufs:
            self.store_bufs.append(buf)
        self.dma(eng, [(out_ap, in_ap)], buf, reads=[buf])

    def finish(self):
        final = [(b.dsem, 16 * b.dcnt, "dma") for b in self.store_bufs]
        self.streams["sp"].append((final, None, None))
        streams = self.streams

        def emit(name, e):
            waited = {}
            for toks, fn, inc in streams[name]:
                for (sem, val, teng) in toks:
                    if teng == name or (teng == "raw:" + name and name == "pe"):
                        continue
                    k = id(sem)
                    if waited.get(k, 0) >= val:
                        continue
                    e.wait_ge(sem, val)
                    waited[k] = val
                if fn is None:
                    continue
                ins = fn(e)
                if isinstance(ins, list):
                    for i_ in ins:
                        i_.then_inc(inc[0], inc[1])
                else:
                    ins.then_inc(inc[0], inc[1])

        with self.nc.Block() as block:
            @block.sync
            def _(e):
                emit("sp", e)

            @block.scalar
            def _(e):
                emit("act", e)

            @block.vector
            def _(e):
                emit("dve", e)

            @block.gpsimd
            def _(e):
                emit("pool", e)

            @block.tensor
            def _(e):
                emit("pe", e)
        self.ctx.close()
        return self.nc


class Rec:
    def __init__(self):
        self.groups = []
        self.cur = None

    def _add(self, call):
        if self.cur is not None:
            self.cur.append(call)
        else:
            self.groups.append([call])

    def op(self, *a, **k):
        self._add(("op", a, k))

    def dma(self, *a, **k):
        self._add(("dma", a, k))

    def load(self, *a, **k):
        self._add(("load", a, k))

    def store(self, *a, **k):
        self._add(("store", a, k))

    def begin(self):
        self.cur = []

    def end(self):
        self.groups.append(self.cur)
        self.cur = None


def replay_rr(P, recs):
    idx = [0] * len(recs)
    while any(idx[i] < len(r.groups) for i, r in enumerate(recs)):
        for i, r in enumerate(recs):
            if idx[i] < len(r.groups):
                for (kind, a, k) in r.groups[idx[i]]:
                    getattr(P, kind)(*a, **k)
                idx[i] += 1


def run(prog, in_maps):
    nc = prog.finish()
    n = int(os.environ.get("DBG_CORES", NCORES))
    if os.environ.get("DBG_TRACE"):
        res = run_bass_kernel_spmd(nc, in_maps[:n], core_ids=list(range(n)), trace=True)
        print("DBG_TRACE exec_time_ns", res.exec_time_ns, flush=True)
    else:
        res = run_bass_kernel_spmd(nc, in_maps[:n], core_ids=list(range(n)))
    out = list(res.results)
    while len(out) < NCORES:
        out.append(out[0])
    return out


def emit_norm_T(P, x_dram, gb, ident, xnT, ntiles, tag, single=False, bufs=None):
    if bufs is None:
        nb = 1 if single else 2
        bufs = dict(xt=[P.sbuf(f"{tag}_xt{i}", [128, D]) for i in range(nb)],
                    junk=P.sbuf(f"{tag}_junk", [128, D], BF16),
                    xn=[P.sbuf(f"{tag}_xn{i}", [128, D]) for i in range(nb)],
                    ss=[P.sbuf(f"{tag}_ss{i}", [128, 16]) for i in range(nb)],
                    tp=[P.psum(f"{tag}_tp{i}", [128, 512]) for i in range(2)])
    xt, junk, xn, ss, tp = bufs["xt"], bufs["junk"], bufs["xn"], bufs["ss"], bufs["tp"]
    nb = len(xt)
    tcount = 0
    for i in range(ntiles):
        b = i % nb
        P.load("sp", xt[b], xt[b][:], x_dram[i * 128:(i + 1) * 128, :])
        P.op("dve", lambda e, b=b: e.memset(ss[b][:], 0.0), writes=[ss[b]])
        P.op("act", lambda e, b=b: e.activation(out=junk[:], in_=xt[b][:], func=AF.Square,
                                                accum_out=ss[b][:, 0:1]),
             reads=[xt[b]], writes=[junk, ss[b]])
        P.op("act", lambda e, b=b: e.activation(out=ss[b][:, 1:2], in_=ss[b][:, 0:1], func=AF.Sqrt,
                                                scale=1.0 / D, bias=EPS),
             reads=[ss[b]], writes=[ss[b]])
        P.op("dve", lambda e, b=b: e.reciprocal(out=ss[b][:, 1:2], in_=ss[b][:, 1:2]),
             reads=[ss[b]], writes=[ss[b]])
        P.op("dve", lambda e, b=b: e.scalar_tensor_tensor(out=xn[b][:], in0=xt[b][:], scalar=ss[b][:, 1:2],
                                                          in1=gb[:], op0=ALU.mult, op1=ALU.mult),
             reads=[xt[b], ss[b], gb], writes=[xn[b]])
        for kk in range(4):
            pb = tp[tcount % 2]
            tcount += 1
            for j in range(4):
                k = kk * 4 + j
                P.op("pe", lambda e, b=b, k=k, j=j, pb=pb: e.transpose(pb[:, j * 128:(j + 1) * 128],
                                                                      xn[b][:, k * 128:(k + 1) * 128], ident[:]),
                     reads=[xn[b], ident], writes=[pb])
            eng = "act" if kk % 2 == 0 else "dve"
            if eng == "act":
                P.op("act", lambda e, kk=kk, i=i, pb=pb: e.copy(
                    out=xnT[:, kk * 4:(kk + 1) * 4, i * 128:(i + 1) * 128],
                    in_=pb[:].rearrange("p (a b) -> p a b", a=4)), reads=[pb], writes=[xnT])
            else:
                P.op("dve", lambda e, kk=kk, i=i, pb=pb: e.tensor_copy(
                    out=xnT[:, kk * 4:(kk + 1) * 4, i * 128:(i + 1) * 128],
                    in_=pb[:].rearrange("p (a b) -> p a b", a=4)), reads=[pb], writes=[xnT])

    return bufs


NA = 4632
NA_MAIN = 4608


def build_A():
    P = Prog()
    x_d = P.dram_in("x", [TPC, D])
    gb_d = P.dram_in("gb", [128, D])
    w_d = P.dram_in("wA", [D, NA])
    id_d = P.dram_in("ident", [128, 128])
    pos_d = P.dram_in("pos", [128, TPC // 128, 64])
    frq_d = P.dram_in("frq", [128, 64])
    qg_d = P.dram_in("qg", [128, 128])
    kg_d = P.dram_in("kg", [128, 128])
    out_d = P.dram_out("hA", [TPC, NA])
    NT = TPC // 128

    ident = P.sbuf("ident", [128, 128])
    gb = P.sbuf("gb", [128, D])
    qg = P.sbuf("qg", [128, 128])
    kg = P.sbuf("kg", [128, 128])
    pos = P.sbuf("pos", [128, NT, 64])
    frq = P.sbuf("frq", [128, 64])
    P.load("sp", ident, ident[:], id_d)
    P.load("sp", gb, gb[:], gb_d)
    P.load("sp", qg, qg[:], qg_d)
    P.load("sp", kg, kg[:], kg_d)
    P.load("sp", pos, pos[:], pos_d)
    P.load("sp", frq, frq[:], frq_d)

    ang = P.sbuf("ang", [128, NT, 64])
    tmpf = P.sbuf("tmpf", [128, NT, 64])
    tmpi = P.sbuf("tmpi", [128, NT, 64], I32)
    cosT = P.sbuf("cosT", [128, NT, 64])
    sinT = P.sbuf("sinT", [128, NT, 64])
    P.op("dve", lambda e: e.tensor_tensor(out=ang[:], in0=pos[:], in1=frq[:].unsqueeze(1).to_broadcast([128, NT, 64]),
                                          op=ALU.mult), reads=[pos, frq], writes=[ang])
    for (dst, shift) in ((sinT, 0.0), (cosT, float(np.pi / 2))):
        if shift != 0.0:
            P.op("dve", lambda e, shift=shift: e.tensor_scalar(out=ang[:], in0=ang[:], scalar1=shift, scalar2=None,
                                                                op0=ALU.add), reads=[ang], writes=[ang])
        P.op("dve", lambda e: e.tensor_scalar(out=tmpf[:], in0=ang[:], scalar1=1.0 / TWO_PI, scalar2=None,
                                              op0=ALU.mult), reads=[ang], writes=[tmpf])
        P.op("dve", lambda e: e.tensor_copy(out=tmpi[:], in_=tmpf[:]), reads=[tmpf], writes=[tmpi])
        P.op("dve", lambda e: e.tensor_copy(out=tmpf[:], in_=tmpi[:]), reads=[tmpi], writes=[tmpf])
        P.op("dve", lambda e: e.scalar_tensor_tensor(out=tmpf[:], in0=tmpf[:], scalar=-TWO_PI, in1=ang[:],
                                                     op0=ALU.mult, op1=ALU.add), reads=[tmpf, ang], writes=[tmpf])
        P.op("act", lambda e, dst=dst: e.activation(out=dst[:], in_=tmpf[:], func=AF.Sin),
             reads=[tmpf], writes=[dst])

    xnT = P.sbuf("xnT", [128, 16, TPC], BF16)
    emit_norm_T(P, x_d, gb, ident, xnT, NT, "nA")

    wblk = [P.sbuf(f"wblk{i}", [128, 16, 512], BF16) for i in range(2)]
    pp = [P.psum(f"pp{i}", [128, 512]) for i in range(2)]
    ot = [P.sbuf(f"ot{i}", [128, 512]) for i in range(3)]
    ss4 = P.sbuf("ss4", [128, 8])
    junk2 = P.sbuf("junk2", [128, 128])
    t1 = P.sbuf("rt1", [128, 4, 64])
    t2 = P.sbuf("rt2", [128, 4, 64])
    qn = P.sbuf("qn", [128, 512])
    w_v = w_d.rearrange("(c p) n -> p c n", p=128)
    nblk = 10
    cnt = 0
    for cb in range(nblk):
        wb = wblk[cb % 2]
        c0 = cb * 512
        ncol = 512 if cb < 9 else NA - NA_MAIN
        P.dma("pool", [(wb[:, k, 0:ncol], w_v[:, k, c0:c0 + ncol]) for k in range(16)], wb, writes=[wb])
        for i in range(NT):
            ps = pp[cnt % 2]
            o = ot[cnt % 3]
            cnt += 1
            for k in range(16):
                P.op("pe", lambda e, ps=ps, k=k, i=i, wb=wb, ncol=ncol: e.matmul(
                    ps[:, 0:ncol], xnT[:, k, i * 128:(i + 1) * 128], wb[:, k, 0:ncol],
                    start=(k == 0), stop=(k == 15)), reads=[xnT, wb], writes=[ps])
            if cb in (6, 7) or cb == 8:
                nh = 4 if cb in (6, 7) else 2
                g = qg if cb in (6, 7) else kg
                sc = (128.0 ** -0.5) if cb in (6, 7) else 1.0
                P.op("dve", lambda e: e.memset(ss4[:], 0.0), writes=[ss4])
                for h in range(nh):
                    P.op("act", lambda e, ps=ps, h=h: e.activation(out=junk2[:], in_=ps[:, h * 128:(h + 1) * 128],
                                                                   func=AF.Square, accum_out=ss4[:, h:h + 1]),
                         reads=[ps], writes=[junk2, ss4])
                P.op("act", lambda e, nh=nh: e.activation(out=ss4[:, 4:4 + nh], in_=ss4[:, 0:nh], func=AF.Sqrt,
                                                          scale=1.0 / 128, bias=EPS), reads=[ss4], writes=[ss4])
                P.op("dve", lambda e, nh=nh: e.reciprocal(out=ss4[:, 4:4 + nh], in_=ss4[:, 4:4 + nh]),
                     reads=[ss4], writes=[ss4])
                if sc != 1.0:
                    P.op("dve", lambda e, nh=nh, sc=sc: e.tensor_scalar(out=ss4[:, 4:4 + nh], in0=ss4[:, 4:4 + nh],
                                                                        scalar1=sc, scalar2=None, op0=ALU.mult),
                         reads=[ss4], writes=[ss4])
                for h in range(nh):
                    P.op("dve", lambda e, ps=ps, h=h, g=g: e.scalar_tensor_tensor(
                        out=qn[:, h * 128:(h + 1) * 128], in0=ps[:, h * 128:(h + 1) * 128],
                        scalar=ss4[:, 4 + h:5 + h], in1=g[:], op0=ALU.mult, op1=ALU.mult),
                         reads=[ps, ss4, g], writes=[qn])
                if nh < 4:
                    P.op("act", lambda e, ps=ps, o=o: e.copy(out=o[:, 256:512], in_=ps[:, 256:512]),
                         reads=[ps], writes=[o])
                W = nh * 128
                qv = qn[:, 0:W].rearrange("p (h i two) -> p h i two", h=nh, two=2)
                ov = o[:, 0:W].rearrange("p (h i two) -> p h i two", h=nh, two=2)
                x0 = qv[:, :, :, 0]
                x1 = qv[:, :, :, 1]
                cb_ = cosT[:, i, :].unsqueeze(1).to_broadcast([128, nh, 64])
                sb_ = sinT[:, i, :].unsqueeze(1).to_broadcast([128, nh, 64])
                a1 = t1[:, 0:nh, :]
                a2 = t2[:, 0:nh, :]
                P.op("dve", lambda e, x0=x0, cb_=cb_, a1=a1: e.tensor_tensor(out=a1, in0=x0, in1=cb_, op=ALU.mult),
                     reads=[qn, cosT], writes=[t1])
                P.op("pool", lambda e, x1=x1, sb_=sb_, a2=a2: e.tensor_tensor(out=a2, in0=x1, in1=sb_, op=ALU.mult),
                     reads=[qn, sinT], writes=[t2])
                P.op("dve", lambda e, ov=ov, a1=a1, a2=a2: e.tensor_tensor(out=ov[:, :, :, 0], in0=a1, in1=a2,
                                                                          op=ALU.subtract),
                     reads=[t1, t2], writes=[o])
                P.op("dve", lambda e, x0=x0, sb_=sb_, a1=a1: e.tensor_tensor(out=a1, in0=x0, in1=sb_, op=ALU.mult),
                     reads=[qn, sinT], writes=[t1])
                P.op("pool", lambda e, x1=x1, cb_=cb_, a2=a2: e.tensor_tensor(out=a2, in0=x1, in1=cb_, op=ALU.mult),
                     reads=[qn, cosT], writes=[t2])
                P.op("dve", lambda e, ov=ov, a1=a1, a2=a2: e.tensor_tensor(out=ov[:, :, :, 1], in0=a1, in1=a2,
                                                                          op=ALU.add),
                     reads=[t1, t2], writes=[o])
            else:
                if cnt % 2 == 0:
                    P.op("act", lambda e, ps=ps, o=o, ncol=ncol: e.copy(out=o[:, 0:ncol], in_=ps[:, 0:ncol]),
                         reads=[ps], writes=[o])
                else:
                    P.op("dve", lambda e, ps=ps, o=o, ncol=ncol: e.tensor_copy(out=o[:, 0:ncol], in_=ps[:, 0:ncol]),
                         reads=[ps], writes=[o])
            P.store("sp", o, out_d[i * 128:(i + 1) * 128, c0:c0 + ncol], o[:, 0:ncol])
    return P


COLS_A = np.concatenate([
    np.arange(0, 768),
    np.arange(1536, 3840),
    np.arange(4632, 6168),
    np.arange(3840, 3864),
])


def rope_consts():
    t = np.arange(SEQ)
    row = (t // 64).astype(np.float32)
    col = (t % 64).astype(np.float32)
    pos = np.concatenate([np.repeat(row[:, None], 32, 1), np.repeat(col[:, None], 32, 1)], axis=1)
    freqs = (10000.0 ** (-np.arange(0, 64, 2, dtype=np.float32) / 64)).astype(np.float32)
    frq = np.concatenate([freqs, freqs])[None, :].repeat(128, 0).astype(np.float32)
    return pos.astype(np.float32), frq


def bcast_rows(v, n=128):
    return np.ascontiguousarray(np.broadcast_to(np.asarray(v, np.float32)[None, :], (n, v.shape[0])))


def run_A(x, norm_g, w_in_l, qn_g, kn_g):
    P = build_A()
    pos, frq = rope_consts()
    wA = np.ascontiguousarray(w_in_l[:, COLS_A])
    common = dict(gb=bcast_rows(norm_g), wA=wA, ident=np.eye(128, dtype=np.float32), frq=frq,
                  qg=bcast_rows(qn_g), kg=bcast_rows(kn_g))
    maps = []
    for c in range(NCORES):
        pc = pos[c * TPC:(c + 1) * TPC].reshape(TPC // 128, 128, 64).transpose(1, 0, 2)
        maps.append(dict(common, x=np.ascontiguousarray(x[c * TPC:(c + 1) * TPC]), pos=np.ascontiguousarray(pc)))
    res = run(P, maps)
    return np.concatenate([r["hA"] for r in res], axis=0)


def build_ATT():
    P = Prog()
    qT_d = P.dram_in("qT", [128, SEQ])
    kT_d = P.dram_in("kT", [128, SEQ])
    v_d = P.dram_in("v", [128, SEQ // 128, 128])
    out_d = P.dram_out("oT", [128, SEQ])
    qT = P.sbuf("qT", [128, SEQ], BF16)
    kT = P.sbuf("kT", [128, SEQ], BF16)
    v = P.sbuf("v", [128, SEQ // 128, 128], BF16)
    ones = P.sbuf("ones", [128, 128], BF16)
    P.op("dve", lambda e: e.memset(ones[:], 1.0), writes=[ones])
    for j in range(4):
        sl = slice(j * 2048, (j + 1) * 2048)
        P.dma("pool", [(kT[:, sl], kT_d[:, sl])], kT, writes=[kT])
        P.dma("pool", [(qT[:, sl], qT_d[:, sl])], qT, writes=[qT])
        P.dma("pool", [(v[:, j * 16:(j + 1) * 16, :], v_d[:, j * 16:(j + 1) * 16, :])], v, writes=[v])
    ps_s = [P.psum(f"s{i}", [128, 512]) for i in range(2)]
    ps_o = [P.psum(f"o{i}", [128, 512]) for i in range(2)]
    ps_d = [P.psum(f"d{i}", [128, 512]) for i in range(2)]
    pt = [P.sbuf(f"pt{i}", [128, 512], BF16) for i in range(3)]
    rd = P.sbuf("rd", [128, 512])
    ot = [P.sbuf(f"ot{i}", [128, 512]) for i in range(2)]
    NKT = SEQ // 128
    NQB = SEQ // 512
    ps_s = ps_s + [P.psum("s2", [128, 512])]
    steps = [(qb, kt) for qb in range(NQB) for kt in range(NKT)]

    def emit_S(idx):
        qb, kt = steps[idx]
        s_ = ps_s[idx % 3]
        P.op("pe", lambda e, s_=s_, kt=kt, qb=qb: e.matmul(s_[:], kT[:, kt * 128:(kt + 1) * 128],
                                                        qT[:, qb * 512:(qb + 1) * 512], start=True, stop=True),
             reads=[kT, qT], writes=[s_])

    emit_S(0)
    emit_S(1)
    for idx, (qb, kt) in enumerate(steps):
        po = ps_o[qb % 2]
        pd = ps_d[qb % 2]
        s_ = ps_s[idx % 3]
        p = pt[idx % 3]
        P.op("act", lambda e, s_=s_, p=p: e.activation(out=p[:], in_=s_[:], func=AF.Exp), reads=[s_], writes=[p])
        if idx + 2 < len(steps):
            emit_S(idx + 2)
        P.op("pe", lambda e, po=po, p=p, kt=kt: e.matmul(po[:], v[:, kt, :], p[:], start=(kt == 0),
                                                      stop=(kt == NKT - 1)), reads=[v, p], writes=[po])
        P.op("pe", lambda e, pd=pd, p=p, kt=kt: e.matmul(pd[:], ones[:], p[:], start=(kt == 0),
                                                      stop=(kt == NKT - 1)), reads=[ones, p], writes=[pd])
        if kt == NKT - 1:
            o = ot[qb % 2]
            P.op("dve", lambda e, pd=pd: e.reciprocal(out=rd[:], in_=pd[:]), reads=[pd], writes=[rd])
            P.op("dve", lambda e, po=po, o=o: e.tensor_tensor(out=o[:], in0=po[:], in1=rd[:], op=ALU.mult),
                 reads=[po, rd], writes=[o])
            P.store("sp", o, out_d[:, qb * 512:(qb + 1) * 512], o[:])
    return P


def run_ATT(hA):
    P = build_ATT()
    q = hA[:, 3072:4096]
    k = hA[:, 4096:4352]
    vv = hA[:, 4352:4608]
    maps = []
    for c in range(NCORES):
        kv = c // 4
        maps.append(dict(
            qT=np.ascontiguousarray(q[:, c * 128:(c + 1) * 128].T),
            kT=np.ascontiguousarray(k[:, kv * 128:(kv + 1) * 128].T),
            v=np.ascontiguousarray(vv[:, kv * 128:(kv + 1) * 128].reshape(SEQ // 128, 128, 128).transpose(1, 0, 2)),
        ))
    res = run(P, maps)
    return np.concatenate([r["oT"].T for r in res], axis=1)


S5C = 512


def _range_reduce_sin(P, dst, src, tmpf, tmpi, shift):
    P.op("dve", lambda e: e.tensor_scalar(out=tmpf[:], in0=src[:], scalar1=shift, scalar2=1.0 / TWO_PI,
                                          op0=ALU.add, op1=ALU.mult), reads=[src], writes=[tmpf])
    P.op("dve", lambda e: e.tensor_copy(out=tmpi[:], in_=tmpf[:]), reads=[tmpf], writes=[tmpi])
    P.op("dve", lambda e: e.tensor_copy(out=tmpf[:], in_=tmpi[:]), reads=[tmpi], writes=[tmpf])
    P.op("dve", lambda e: e.scalar_tensor_tensor(out=tmpf[:], in0=tmpf[:], scalar=-TWO_PI, in1=src[:],
                                                 op0=ALU.mult, op1=ALU.add), reads=[tmpf, src], writes=[tmpf])
    if shift != 0.0:
        P.op("dve", lambda e: e.tensor_scalar(out=tmpf[:], in0=tmpf[:], scalar1=shift, scalar2=None, op0=ALU.add),
             reads=[tmpf], writes=[tmpf])
    P.op("act", lambda e: e.activation(out=dst[:], in_=tmpf[:], func=AF.Sin), reads=[tmpf], writes=[dst])


def build_S5():
    P = Prog()
    NU = 6
    NCH = SEQ // S5C
    uT_d = P.dram_in("uT", [NU, 32, SEQ])
    prm_d = P.dram_in("prm", [NU, 128, 3])
    b_d = P.dram_in("bmat", [NU, 128, 64])
    c_d = P.dram_in("cmat", [NU, 128, 64])
    jidx_d = P.dram_in("jidx", [128, S5C + 1])
    id_d = P.dram_in("ident", [128, 128])
    out_d = P.dram_out("yT", [NU, 32, SEQ])

    ident = P.sbuf("ident", [128, 128])
    jidx = P.sbuf("jidx", [128, S5C + 1])
    P.load("sp", ident, ident[:], id_d)
    P.load("sp", jidx, jidx[:], jidx_d)
    pst = P.psum("pst", [32, 256])
    ps_re = [P.psum(f"psre{i}", [128, S5C]) for i in range(2)]
    ps_im = [P.psum(f"psim{i}", [128, S5C]) for i in range(2)]
    ps_y = [P.psum(f"psy{i}", [32, S5C]) for i in range(2)]

    def alloc_unit(ui):
        prm = P.sbuf(f"prm_u{ui}", [128, 3])
        bm = P.sbuf(f"bm_u{ui}", [128, 64])
        cm = P.sbuf(f"cm_u{ui}", [128, 64])
        sc = P.sbuf(f"sc_u{ui}", [128, 16])
        s1f = P.sbuf(f"s1f_u{ui}", [128, 1])
        s1i = P.sbuf(f"s1i_u{ui}", [128, 1], I32)
        th = P.sbuf(f"th_u{ui}", [128, 1])
        ph = P.sbuf(f"ph_u{ui}", [128, S5C + 1])
        tmpf = P.sbuf(f"tmpf_u{ui}", [128, S5C + 1])
        tmpi = P.sbuf(f"tmpi_u{ui}", [128, S5C + 1], I32)
        Pc = P.sbuf(f"Pc_u{ui}", [128, S5C + 1])
        Ps = P.sbuf(f"Ps_u{ui}", [128, S5C + 1])
        rt = P.sbuf(f"rt_u{ui}", [128, S5C])
        bb = P.sbuf(f"bb_u{ui}", [128, 64])
        bt1 = P.sbuf(f"bt1_u{ui}", [128, 32])
        BT = P.sbuf(f"BT_u{ui}", [32, 256], BF16)
        CT = P.sbuf(f"CT_u{ui}", [128, 96], BF16)
        ut = [P.sbuf(f"ut{i}_u{ui}", [32, S5C], BF16) for i in range(2)]
        m = [P.sbuf(f"m{i}_u{ui}", [128, S5C]) for i in range(4)]
        cre = P.sbuf(f"cre__u{ui}", [128, S5C])
        cim = P.sbuf(f"cim__u{ui}", [128, S5C])
        zre = [P.sbuf(f"zre{i}_u{ui}", [128, S5C]) for i in range(2)]
        zim = [P.sbuf(f"zim{i}_u{ui}", [128, S5C]) for i in range(2)]
        nn = [P.sbuf(f"nn{i}_u{ui}", [128, S5C], BF16) for i in range(4)]
        init = P.sbuf(f"init_u{ui}", [128, 4])
        yt = [P.sbuf(f"yt{i}_u{ui}", [32, S5C]) for i in range(2)]
        return dict(prm=prm, bm=bm, cm=cm, sc=sc, s1f=s1f, s1i=s1i, th=th, ph=ph, tmpf=tmpf, tmpi=tmpi, Pc=Pc, Ps=Ps, rt=rt, bb=bb, bt1=bt1, BT=BT, CT=CT, ut=ut, m=m, cre=cre, cim=cim, zre=zre, zim=zim, nn=nn, init=init, yt=yt)

    RS = [alloc_unit(0), alloc_unit(1)]

    def col(t, j):
        return t[:, j:j + 1]

    def emit_unit(Q, u, R):
        prm, bm, cm, sc, s1f, s1i, th, ph, tmpf, tmpi, Pc, Ps, rt, bb, bt1, BT, CT, ut, m, cre, cim, zre, zim, nn, init, yt = (R["prm"], R["bm"], R["cm"], R["sc"], R["s1f"], R["s1i"], R["th"], R["ph"], R["tmpf"], R["tmpi"], R["Pc"], R["Ps"], R["rt"], R["bb"], R["bt1"], R["BT"], R["CT"], R["ut"], R["m"], R["cre"], R["cim"], R["zre"], R["zim"], R["nn"], R["init"], R["yt"])
        Q.load("sp", prm, prm[:], prm_d[u])
        Q.load("sp", bm, bm[:], b_d[u])
        Q.load("sp", cm, cm[:], c_d[u])
        Q.op("act", lambda e: e.activation(out=col(sc, 0), in_=col(prm, 2), func=AF.Exp), reads=[prm], writes=[sc])
        Q.op("dve", lambda e: e.tensor_tensor(out=col(sc, 10), in0=col(prm, 0), in1=col(sc, 0), op=ALU.mult),
             reads=[prm, sc], writes=[sc])
        Q.op("act", lambda e: e.activation(out=col(sc, 1), in_=col(sc, 10), func=AF.Exp), reads=[sc], writes=[sc])
        Q.op("dve", lambda e: e.tensor_tensor(out=col(sc, 2), in0=col(prm, 1), in1=col(sc, 0), op=ALU.mult),
             reads=[prm, sc], writes=[sc])
        Q.op("dve", lambda e: e.tensor_scalar(out=s1f[:], in0=col(sc, 2), scalar1=1.0 / TWO_PI, scalar2=None,
                                              op0=ALU.mult), reads=[sc], writes=[s1f])
        Q.op("dve", lambda e: e.tensor_copy(out=s1i[:], in_=s1f[:]), reads=[s1f], writes=[s1i])
        Q.op("dve", lambda e: e.tensor_copy(out=s1f[:], in_=s1i[:]), reads=[s1i], writes=[s1f])
        Q.op("dve", lambda e: e.scalar_tensor_tensor(out=th[:], in0=s1f[:], scalar=-TWO_PI, in1=col(sc, 2),
                                                     op0=ALU.mult, op1=ALU.add), reads=[s1f, sc], writes=[th])
        Q.op("dve", lambda e: e.tensor_scalar(out=ph[:], in0=jidx[:], scalar1=th[:, 0:1], scalar2=None,
                                              op0=ALU.mult), reads=[jidx, th], writes=[ph])
        _range_reduce_sin(Q, Ps, ph, tmpf, tmpi, 0.0)
        _range_reduce_sin(Q, Pc, ph, tmpf, tmpi, float(np.pi / 2))
        Q.op("dve", lambda e: e.tensor_tensor(out=col(sc, 5), in0=col(sc, 1), in1=col(Pc, 1), op=ALU.mult),
             reads=[sc, Pc], writes=[sc])
        Q.op("dve", lambda e: e.tensor_scalar(out=col(sc, 5), in0=col(sc, 5), scalar1=-1.0, scalar2=None,
                                              op0=ALU.add), reads=[sc], writes=[sc])
        Q.op("dve", lambda e: e.tensor_tensor(out=col(sc, 6), in0=col(sc, 1), in1=col(Ps, 1), op=ALU.mult),
             reads=[sc, Ps], writes=[sc])
        Q.op("dve", lambda e: e.tensor_tensor(out=col(sc, 7), in0=col(prm, 0), in1=col(prm, 0), op=ALU.mult),
             reads=[prm], writes=[sc])
        Q.op("dve", lambda e: e.scalar_tensor_tensor(out=col(sc, 7), in0=col(prm, 1), scalar=col(prm, 1),
                                                     in1=col(sc, 7), op0=ALU.mult, op1=ALU.add),
             reads=[prm, sc], writes=[sc])
        Q.op("dve", lambda e: e.reciprocal(out=col(sc, 7), in_=col(sc, 7)), reads=[sc], writes=[sc])
        Q.op("dve", lambda e: e.tensor_tensor(out=col(sc, 10), in0=col(sc, 5), in1=col(prm, 0), op=ALU.mult),
             reads=[sc, prm], writes=[sc])
        Q.op("dve", lambda e: e.scalar_tensor_tensor(out=col(sc, 10), in0=col(sc, 6), scalar=col(prm, 1),
                                                     in1=col(sc, 10), op0=ALU.mult, op1=ALU.add),
             reads=[sc, prm], writes=[sc])
        Q.op("dve", lambda e: e.tensor_tensor(out=col(sc, 8), in0=col(sc, 10), in1=col(sc, 7), op=ALU.mult),
             reads=[sc], writes=[sc])
        Q.op("dve", lambda e: e.tensor_tensor(out=col(sc, 11), in0=col(sc, 5), in1=col(prm, 1), op=ALU.mult),
             reads=[sc, prm], writes=[sc])
        Q.op("dve", lambda e: e.scalar_tensor_tensor(out=col(sc, 11), in0=col(sc, 6), scalar=col(prm, 0),
                                                     in1=col(sc, 11), op0=ALU.mult, op1=ALU.subtract),
             reads=[sc, prm], writes=[sc])
        Q.op("dve", lambda e: e.tensor_tensor(out=col(sc, 9), in0=col(sc, 11), in1=col(sc, 7), op=ALU.mult),
             reads=[sc], writes=[sc])
        Q.op("dve", lambda e: e.tensor_scalar(out=bt1[:], in0=bm[:, 32:64], scalar1=col(sc, 9), scalar2=None,
                                              op0=ALU.mult), reads=[bm, sc], writes=[bt1])
        Q.op("dve", lambda e: e.scalar_tensor_tensor(out=bb[:, 0:32], in0=bm[:, 0:32], scalar=col(sc, 8),
                                                     in1=bt1[:], op0=ALU.mult, op1=ALU.subtract),
             reads=[bm, sc, bt1], writes=[bb])
        Q.op("dve", lambda e: e.tensor_scalar(out=bt1[:], in0=bm[:, 0:32], scalar1=col(sc, 9), scalar2=None,
                                              op0=ALU.mult), reads=[bm, sc, bb], writes=[bt1])
        Q.op("dve", lambda e: e.scalar_tensor_tensor(out=bb[:, 32:64], in0=bm[:, 32:64], scalar=col(sc, 8),
                                                     in1=bt1[:], op0=ALU.mult, op1=ALU.add),
             reads=[bm, sc, bt1], writes=[bb])
        Q.begin()
        Q.op("pe", lambda e: e.transpose(pst[:, 0:128], bb[:, 0:32], ident[:]), reads=[bb, ident], writes=[pst])
        Q.op("pe", lambda e: e.transpose(pst[:, 128:256], bb[:, 32:64], ident[:]), reads=[bb, ident], writes=[pst])
        Q.op("dve", lambda e: e.tensor_copy(out=BT[:], in_=pst[:]), reads=[pst], writes=[BT])
        Q.end()
        Q.op("dve", lambda e: e.tensor_copy(out=CT[:, 0:32], in_=cm[:, 0:32]), reads=[cm], writes=[CT])
        Q.op("dve", lambda e: e.tensor_scalar(out=CT[:, 32:96], in0=cm[:, 0:64], scalar1=-1.0, scalar2=None,
                                              op0=ALU.mult), reads=[cm], writes=[CT])
        Q.op("dve", lambda e: e.tensor_scalar(out=rt[:], in0=jidx[:, 0:S5C], scalar1=0.0, scalar2=col(sc, 1),
                                              op0=ALU.mult, op1=ALU.add), reads=[jidx, sc], writes=[rt])
        for ch in range(NCH):
            b = ch % 2
            tsl = slice(ch * S5C, (ch + 1) * S5C)
            Q.dma("pool", [(ut[b][:], uT_d[u, :, tsl])], ut[b], writes=[ut[b]])
            pr, pi_ = ps_re[b], ps_im[b]
            PcS, PsS = Pc[:, 0:S5C], Ps[:, 0:S5C]
            Q.begin()
            Q.op("pe", lambda e, pr=pr, b=b: e.matmul(pr[:], BT[:, 0:128], ut[b][:], start=True, stop=True),
                 reads=[BT, ut[b]], writes=[pr])
            Q.op("pe", lambda e, pi_=pi_, b=b: e.matmul(pi_[:], BT[:, 128:256], ut[b][:], start=True, stop=True),
                 reads=[BT, ut[b]], writes=[pi_])
            Q.op("dve", lambda e, pr=pr: e.tensor_tensor(out=m[0][:], in0=pr[:], in1=PcS, op=ALU.mult),
                 reads=[pr, Pc], writes=[m[0]])
            Q.op("dve", lambda e, pi_=pi_: e.tensor_tensor(out=m[1][:], in0=pi_[:], in1=PsS, op=ALU.mult),
                 reads=[pi_, Ps], writes=[m[1]])
            Q.op("dve", lambda e, pi_=pi_: e.tensor_tensor(out=m[2][:], in0=pi_[:], in1=PcS, op=ALU.mult),
                 reads=[pi_, Pc], writes=[m[2]])
            Q.op("dve", lambda e, pr=pr: e.tensor_tensor(out=m[3][:], in0=pr[:], in1=PsS, op=ALU.mult),
                 reads=[pr, Ps], writes=[m[3]])
            Q.end()
            Q.op("pool", lambda e: e.tensor_tensor(out=cre[:], in0=m[0][:], in1=m[1][:], op=ALU.add),
                 reads=[m[0], m[1]], writes=[cre])
            Q.op("pool", lambda e: e.tensor_tensor(out=cim[:], in0=m[2][:], in1=m[3][:], op=ALU.subtract),
                 reads=[m[2], m[3]], writes=[cim])
            if ch == 0:
                Q.op("dve", lambda e: e.memset(init[:], 0.0), writes=[init])
            else:
                pzr, pzi = zre[1 - b], zim[1 - b]
                L = S5C - 1
                Q.op("dve", lambda e, pzi=pzi: e.tensor_tensor(out=col(init, 2), in0=col(pzi, L), in1=col(Ps, S5C),
                                                              op=ALU.mult), reads=[pzi, Ps], writes=[init])
                Q.op("dve", lambda e, pzr=pzr: e.scalar_tensor_tensor(out=col(init, 0), in0=col(pzr, L),
                                                                     scalar=col(Pc, S5C), in1=col(init, 2),
                                                                     op0=ALU.mult, op1=ALU.subtract),
                     reads=[pzr, Pc, init], writes=[init])
                Q.op("dve", lambda e, pzr=pzr: e.tensor_tensor(out=col(init, 3), in0=col(pzr, L), in1=col(Ps, S5C),
                                                              op=ALU.mult), reads=[pzr, Ps], writes=[init])
                Q.op("dve", lambda e, pzi=pzi: e.scalar_tensor_tensor(out=col(init, 1), in0=col(pzi, L),
                                                                     scalar=col(Pc, S5C), in1=col(init, 3),
                                                                     op0=ALU.mult, op1=ALU.add),
                     reads=[pzi, Pc, init], writes=[init])
            zr, zi = zre[b], zim[b]
            Q.op("dve", lambda e, zr=zr: e.tensor_tensor_scan(out=zr[:], data0=rt[:], data1=cre[:],
                                                             initial=col(init, 0), op0=ALU.mult, op1=ALU.add),
                 reads=[rt, cre, init], writes=[zr])
            Q.op("dve", lambda e, zi=zi: e.tensor_tensor_scan(out=zi[:], data0=rt[:], data1=cim[:],
                                                             initial=col(init, 1), op0=ALU.mult, op1=ALU.add),
                 reads=[rt, cim, init], writes=[zi])
            Q.op("pool", lambda e, zr=zr: e.tensor_tensor(out=nn[0][:], in0=zr[:], in1=PcS, op=ALU.mult),
                 reads=[zr, Pc], writes=[nn[0]])
            Q.op("pool", lambda e, zi=zi: e.tensor_tensor(out=nn[1][:], in0=zi[:], in1=PsS, op=ALU.mult),
                 reads=[zi, Ps], writes=[nn[1]])
            Q.op("pool", lambda e, zi=zi: e.tensor_tensor(out=nn[2][:], in0=zi[:], in1=PcS, op=ALU.mult),
                 reads=[zi, Pc], writes=[nn[2]])
            Q.op("dve", lambda e, zr=zr: e.tensor_tensor(out=nn[3][:], in0=zr[:], in1=PsS, op=ALU.mult),
                 reads=[zr, Ps], writes=[nn[3]])
            py = ps_y[b]
            lts = [CT[:, 0:32], CT[:, 32:64], CT[:, 64:96], CT[:, 64:96]]
            Q.begin()
            for q in range(4):
                Q.op("pe", lambda e, py=py, q=q, lt=lts[q]: e.matmul(py[:], lt, nn[q][:], start=(q == 0),
                                                                    stop=(q == 3)), reads=[CT, nn[q]], writes=[py])
            y = yt[b]
            Q.op("act", lambda e, py=py, y=y: e.copy(out=y[:], in_=py[:]), reads=[py], writes=[y])
            Q.end()
            Q.store("sp", y, out_d[u, :, tsl], y[:])
    for u0 in range(0, NU, 2):
        recs = []
        for i_ in range(2):
            Q = Rec()
            emit_unit(Q, u0 + i_, RS[i_])
            recs.append(Q)
        replay_rr(P, recs)
    return P


def run_S5(hA, a_re, a_im, log_step, b_re, b_im, c_re, c_im):
    P = build_S5()
    u = hA[:, 0:768]
    uT = np.ascontiguousarray(u.T)
    uTr = np.ascontiguousarray(uT[:, ::-1])
    jidx = bcast_rows(np.arange(S5C + 1, dtype=np.float32))
    maps = []
    for c in range(NCORES):
        uTc = np.zeros((6, 32, SEQ), np.float32)
        prm = np.zeros((6, 128, 3), np.float32)
        bmat = np.zeros((6, 128, 64), np.float32)
        cmat = np.zeros((6, 128, 64), np.float32)
        for pq in range(3):
            for d in range(2):
                un = pq * 2 + d
                for gg in range(2):
                    g = c * 6 + pq * 2 + gg
                    src = uTr if d == 1 else uT
                    uTc[un, gg * 16:(gg + 1) * 16] = src[g * 16:(g + 1) * 16]
                    rs = slice(gg * 64, (gg + 1) * 64)
                    prm[un, rs, 0] = a_re[d, g]
                    prm[un, rs, 1] = a_im[d, g]
                    prm[un, rs, 2] = log_step[d, g]
                    bmat[un, rs, gg * 16:(gg + 1) * 16] = b_re[d, g]
                    bmat[un, rs, 32 + gg * 16:32 + (gg + 1) * 16] = b_im[d, g]
                    cmat[un, rs, gg * 16:(gg + 1) * 16] = c_re[d, g].T
                    cmat[un, rs, 32 + gg * 16:32 + (gg + 1) * 16] = c_im[d, g].T
        maps.append(dict(uT=uTc, prm=prm, bmat=bmat, cmat=cmat, jidx=jidx, ident=np.eye(128, dtype=np.float32)))
    res = run(P, maps)
    yf = np.zeros((SEQ, 768), np.float32)
    yb = np.zeros((SEQ, 768), np.float32)
    for c in range(NCORES):
        yT = res[c]["yT"]
        for pq in range(3):
            cs = slice((c * 6 + pq * 2) * 16, (c * 6 + pq * 2 + 2) * 16)
            yf[:, cs] = yT[pq * 2].T
            yb[:, cs] = yT[pq * 2 + 1].T[::-1]
    return yf, yb


DNC = 128
NDC = SEQ // DNC
DN_K = int(os.environ.get('DN_K', 2))


def build_DN():
    P = Prog()
    NU = 2
    xin_d = P.dram_in("xin", [NU, 3, 128, SEQ + 4])
    cw_d = P.dram_in("cw", [NU, 128, 15])
    ab_d = P.dram_in("ab", [NU, 128, 2, NDC])
    hp_d = P.dram_in("hp", [NU, 128, 2])
    id_d = P.dram_in("ident", [128, 128])
    tri_d = P.dram_in("triu", [128, 128])
    mb_d = P.dram_in("maskb", [128, 128])
    m0_d = P.dram_in("msk0", [128, 128])
    mT_d = P.dram_in("mskT", [128, 6, 128])
    out_d = P.dram_out("o", [NU, SEQ, 128])

    ident = P.sbuf("ident", [128, 128])
    identb = P.sbuf("identb", [128, 128], BF16)
    triu = P.sbuf("triu", [128, 128])
    maskb = P.sbuf("maskb", [128, 128])
    onesf = P.sbuf("onesf", [128, 128])
    P.load("sp", ident, ident[:], id_d)
    P.load("sp", triu, triu[:], tri_d)
    P.load("sp", maskb, maskb[:], mb_d)
    onesb = P.sbuf("onesb", [128, 128], BF16)
    triub = P.sbuf("triub", [128, 128], BF16)
    P.op("dve", lambda e: e.memset(onesf[:], 1.0), writes=[onesf])
    P.op("dve", lambda e: e.memset(onesb[:], 1.0), writes=[onesb])
    P.op("dve", lambda e: e.tensor_copy(out=triub[:], in_=triu[:]), reads=[triu], writes=[triub])
    P.op("dve", lambda e: e.tensor_copy(out=identb[:], in_=ident[:]), reads=[ident], writes=[identb])

    qT = P.sbuf("qT", [128, SEQ], BF16)
    kT = P.sbuf("kT", [128, SEQ], BF16)
    vT = P.sbuf("vT", [128, SEQ], BF16)
    cw = P.sbuf("cw", [128, 15])
    ab = P.sbuf("ab", [128, 2, NDC])
    hp = P.sbuf("hp", [128, 16])
    PW = 2048
    xp = [P.sbuf(f"xp{i}", [128, PW + 4]) for i in range(2)]
    acc = P.sbuf("acc", [128, PW])
    sq = P.sbuf("sq", [128, PW], BF16)
    ghl = P.sbuf("ghl", [128, 2, NDC], BF16)
    gtmp = P.sbuf("gtmp", [128, NDC])
    dgh = P.sbuf("dgh", [128, 128], BF16)
    dgl = P.sbuf("dgl", [128, 128], BF16)
    dgt = P.sbuf("dgt", [128, 128])
    rn = P.sbuf("rn", [128, 512])
    pss = [P.psum(f"pss{i}", [128, 512]) for i in range(2)]

    tb = {n: P.sbuf("tb_" + n, [128, NDC]) for n in ("g", "beta", "gc", "egc", "negegc", "egl", "ekd", "tmp")}
    pt64 = pss[0]

    ptr = P.psum("ptr", [128, 256], BF16)
    KV = P.sbuf("KV", [128, 256], BF16)
    dg = P.sbuf("dg", [128, 128])
    kTc = P.sbuf("kTc", [128, 128], BF16)
    qTc = P.sbuf("qTc", [128, 128], BF16)
    Winvw = P.sbuf("Winvw", [128, 128], BF16)
    pg = P.psum("pg", [128, 128])
    xe = P.sbuf("xe", [128, 128])
    E = P.sbuf("E", [128, 128])
    Es = P.sbuf("Es", [128, 128])
    pkk = P.psum("pkk", [128, 256])
    AT = P.sbuf("AT", [128, 128], BF16)
    X = [P.sbuf(f"X{i}", [128, 128], BF16) for i in range(2)]
    XT = [P.sbuf(f"XT{i}", [128, 128], BF16) for i in range(2)]
    W = [P.sbuf(f"W{i}", [128, 128]) for i in range(2)]
    Wb = [P.sbuf(f"Wb{i}", [128, 128], BF16) for i in range(2)]
    Xw = [P.sbuf(f"Xw{i}", [128, 128], BF16) for i in range(2)]
    XTw = [P.sbuf(f"XTw{i}", [128, 128], BF16) for i in range(2)]
    x0f = P.sbuf("x0f", [128, 128])
    UT = P.sbuf("UT", [128, 128])
    G32 = P.sbuf("G32", [128, 128])
    Gm = P.sbuf("Gm", [128, 128], BF16)
    Gw = P.sbuf("Gw", [128, 128], BF16)
    GTw = P.sbuf("GTw", [128, 128], BF16)
    Ysb = P.sbuf("Ysb", [128, 128], BF16)
    CT = [P.sbuf(f"CTl{i}", [128, 128], BF16) for i in range(6)]
    msk0 = P.sbuf("msk0", [128, 128])
    mskT = P.sbuf("mskT", [128, 6, 128])
    P.load("sp", msk0, msk0[:], m0_d)
    P.load("sp", mskT, mskT[:], mT_d)
    pX = P.psum("pX", [128, 256])
    pW = P.psum("pW", [128, 128])
    S = P.sbuf("S", [128, 128])
    Sb = P.sbuf("Sb", [128, 128], BF16)
    pks = P.psum("pks", [128, 256])
    Rp = P.sbuf("Rp", [128, 128], BF16)
    vnew = P.sbuf("vnew", [128, 128], BF16)
    oq = P.sbuf("oq", [128, 128])
    ot = [P.sbuf(f"ot{i}", [128, 128]) for i in range(2)]
    Kd = P.sbuf("Kd", [128, 128], BF16)
    pgs = pss[0]
    TT = []
    for i in range(DN_K):
        T = {n: P.sbuf(f"T{i}_{n}", [128, 128]) for n in ("dg", "dgt", "xe", "E", "Es", "x0f", "UT", "G32")}
        T.update({n: P.sbuf(f"T{i}_{n}", [128, 128], BF16) for n in ("dgh", "dgl", "kTc", "qTc", "Gm", "GTw", "Ysb")})
        T["CT"] = [P.sbuf(f"T{i}_CT{l}", [128, 128], BF16) for l in range(6)]
        TT.append(T)
    ORing = [(P.sbuf(f"O{i}_KV", [128, 256], BF16), P.sbuf(f"O{i}_AT", [128, 128], BF16),
              P.sbuf(f"O{i}_Gw", [128, 128], BF16)) for i in range(2 * DN_K)]

    def col(t, j):
        return t[:, j:j + 1]

    for u in range(NU):
        P.load("sp", cw, cw[:], cw_d[u])
        P.load("sp", ab, ab[:], ab_d[u])
        P.load("sp", hp, hp[:, 0:2], hp_d[u])
        cnt = 0
        for ti, dst in enumerate((qT, kT, vT)):
            for pc in range(SEQ // PW):
                x_ = xp[cnt % 2]
                cnt += 1
                P.load("sp", x_, x_[:], xin_d[u, ti, :, pc * PW:pc * PW + PW + 4])
                P.op("act", lambda e, x_=x_, ti=ti: e.activation(out=acc[:], in_=x_[:, 0:PW], func=AF.Copy,
                                                                scale=col(cw, ti * 5)), reads=[x_, cw], writes=[acc])
                for k in range(1, 5):
                    P.op("dve", lambda e, x_=x_, ti=ti, k=k: e.scalar_tensor_tensor(
                        out=acc[:], in0=x_[:, k:k + PW], scalar=col(cw, ti * 5 + k), in1=acc[:],
                        op0=ALU.mult, op1=ALU.add), reads=[x_, cw, acc], writes=[acc])
                dsl = slice(pc * PW, (pc + 1) * PW)
                if ti == 2:
                    P.op("act", lambda e, dsl=dsl: e.activation(out=vT[:, dsl], in_=acc[:], func=AF.Silu),
                         reads=[acc], writes=[vT])
                    continue
                P.op("act", lambda e: e.activation(out=acc[:], in_=acc[:], func=AF.Silu), reads=[acc], writes=[acc])
                P.op("pool", lambda e: e.tensor_tensor(out=sq[:], in0=acc[:], in1=acc[:], op=ALU.mult),
                     reads=[acc], writes=[sq])
                for j in range(PW // 512):
                    ps = pss[j % 2]
                    js = slice(j * 512, (j + 1) * 512)
                    P.op("pe", lambda e, ps=ps, js=js: e.matmul(ps[:], onesb[:], sq[:, js], start=True, stop=True),
                         reads=[onesb, sq], writes=[ps])
                    P.op("act", lambda e, ps=ps: e.activation(out=rn[:], in_=ps[:], func=AF.Sqrt, bias=EPS),
                         reads=[ps], writes=[rn])
                    P.op("dve", lambda e: e.reciprocal(out=rn[:], in_=rn[:]), reads=[rn], writes=[rn])
                    scl = (128.0 ** -0.5) if ti == 0 else 1.0
                    P.op("dve", lambda e, js=js, dst=dst, pc=pc, j=j, scl=scl: e.scalar_tensor_tensor(
                        out=dst[:, pc * PW + j * 512:pc * PW + (j + 1) * 512], in0=acc[:, js], scalar=scl, in1=rn[:],
                        op0=ALU.mult, op1=ALU.mult), reads=[acc, rn], writes=[dst])
        P.op("act", lambda e: e.activation(out=tb["tmp"][:], in_=ab[:, 0, :], func=AF.Exp, bias=col(hp, 1)),
             reads=[ab, hp], writes=[tb["tmp"]])
        P.op("act", lambda e: e.activation(out=tb["tmp"][:], in_=tb["tmp"][:], func=AF.Ln, bias=1.0),
             reads=[tb["tmp"]], writes=[tb["tmp"]])
        P.op("act", lambda e: e.activation(out=col(hp, 2), in_=col(hp, 0), func=AF.Exp), reads=[hp], writes=[hp])
        P.op("dve", lambda e: e.tensor_scalar(out=col(hp, 3), in0=col(hp, 2), scalar1=-1.0, scalar2=None,
                                              op0=ALU.mult), reads=[hp], writes=[hp])
        P.op("dve", lambda e: e.tensor_scalar(out=tb["g"][:], in0=tb["tmp"][:], scalar1=col(hp, 3), scalar2=None,
                                              op0=ALU.mult), reads=[tb["tmp"], hp], writes=[tb["g"]])
        P.op("act", lambda e: e.activation(out=tb["beta"][:], in_=ab[:, 1, :], func=AF.Sigmoid),
             reads=[ab], writes=[tb["beta"]])
        P.op("dve", lambda e: e.tensor_copy(out=ghl[:, 0, :], in_=tb["g"][:]), reads=[tb["g"]], writes=[ghl])
        P.op("dve", lambda e: e.tensor_copy(out=gtmp[:], in_=ghl[:, 0, :]), reads=[ghl], writes=[gtmp])
        P.op("dve", lambda e: e.tensor_tensor(out=ghl[:, 1, :], in0=tb["g"][:], in1=gtmp[:], op=ALU.subtract),
             reads=[tb["g"], gtmp], writes=[ghl])
        for hl in range(2):
            P.op("pe", lambda e, hl=hl: e.matmul(pt64[:, 0:NDC], triub[:], ghl[:, hl, :], start=(hl == 0),
                                                 stop=(hl == 1)), reads=[triub, ghl], writes=[pt64])
        for hl in range(2):
            P.op("pe", lambda e, hl=hl: e.matmul(pt64[:, NDC:2 * NDC], onesb[:], ghl[:, hl, :], start=(hl == 0),
                                                 stop=(hl == 1)), reads=[onesb, ghl], writes=[pt64])
        P.op("dve", lambda e: e.tensor_copy(out=tb["gc"][:], in_=pt64[:, 0:NDC]), reads=[pt64], writes=[tb["gc"]])
        P.op("dve", lambda e: e.tensor_copy(out=gtmp[:], in_=pt64[:, NDC:2 * NDC]), reads=[pt64], writes=[gtmp])
        P.op("act", lambda e: e.activation(out=tb["egc"][:], in_=tb["gc"][:], func=AF.Exp),
             reads=[tb["gc"]], writes=[tb["egc"]])
        P.op("dve", lambda e: e.tensor_scalar(out=tb["negegc"][:], in0=tb["egc"][:], scalar1=-1.0, scalar2=None,
                                              op0=ALU.mult), reads=[tb["egc"]], writes=[tb["negegc"]])
        P.op("act", lambda e: e.activation(out=tb["egl"][:], in_=gtmp[:], func=AF.Exp),
             reads=[gtmp], writes=[tb["egl"]])
        P.op("dve", lambda e: e.tensor_tensor(out=tb["tmp"][:], in0=gtmp[:], in1=tb["gc"][:],
                                              op=ALU.subtract), reads=[gtmp, tb["gc"]], writes=[tb["tmp"]])
        P.op("act", lambda e: e.activation(out=tb["ekd"][:], in_=tb["tmp"][:], func=AF.Exp),
             reads=[tb["tmp"]], writes=[tb["ekd"]])
        P.op("dve", lambda e: e.memset(S[:], 0.0), writes=[S])
        P.op("dve", lambda e: e.memset(Sb[:], 0.0), writes=[Sb])
        def pre(c, T, O):
            csl = slice(c * DNC, (c + 1) * DNC)
            gcc, bec = col(tb["gc"], c), col(tb["beta"], c)
            KV_, AT_, Gw_ = O
            P.op("pe", lambda e: e.transpose(ptr[:, 0:128], kT[:, csl], identb[:]), reads=[kT, identb], writes=[ptr])
            P.op("pe", lambda e: e.transpose(ptr[:, 128:256], vT[:, csl], identb[:]), reads=[vT, identb], writes=[ptr])
            P.op("act", lambda e: e.copy(out=KV_[:], in_=ptr[:]), reads=[ptr], writes=[KV_])
            yield
            P.op("dve", lambda e: e.tensor_scalar(out=T["dg"][:], in0=ident[:], scalar1=gcc, scalar2=None,
                                                  op0=ALU.mult), reads=[ident, tb["gc"]], writes=[T["dg"]])
            yield
            P.op("dve", lambda e: e.tensor_copy(out=T["dgh"][:], in_=T["dg"][:]), reads=[T["dg"]], writes=[T["dgh"]])
            yield
            P.op("pool", lambda e: e.tensor_copy(out=T["dgt"][:], in_=T["dgh"][:]), reads=[T["dgh"]], writes=[T["dgt"]])
            yield
            P.op("pool", lambda e: e.tensor_tensor(out=T["dgl"][:], in0=T["dg"][:], in1=T["dgt"][:], op=ALU.subtract),
                 reads=[T["dg"], T["dgt"]], writes=[T["dgl"]])
            yield
            P.op("pe", lambda e: e.matmul(pg[:], onesb[:], T["dgh"][:], start=True, stop=False),
                 reads=[onesb, T["dgh"]], writes=[pg])
            P.op("pe", lambda e: e.matmul(pg[:], onesb[:], T["dgl"][:], start=False, stop=True),
                 reads=[onesb, T["dgl"]], writes=[pg])
            P.op("dve", lambda e: e.tensor_scalar(out=T["xe"][:], in0=pg[:], scalar1=gcc, scalar2=0.0,
                                                  op0=ALU.subtract, op1=ALU.min), reads=[pg, tb["gc"]], writes=[T["xe"]])
            yield
            P.op("pool", lambda e: e.tensor_tensor(out=T["xe"][:], in0=T["xe"][:], in1=maskb[:], op=ALU.add),
                 reads=[T["xe"], maskb], writes=[T["xe"]])
            yield
            P.op("act", lambda e: e.activation(out=T["E"][:], in_=T["xe"][:], func=AF.Exp), reads=[T["xe"]], writes=[T["E"]])
            yield
            P.op("pool", lambda e: e.tensor_copy(out=T["kTc"][:], in_=kT[:, csl]), reads=[kT], writes=[T["kTc"]])
            yield
            P.op("pool", lambda e: e.tensor_copy(out=T["qTc"][:], in_=qT[:, csl]), reads=[qT], writes=[T["qTc"]])
            yield
            P.op("pe", lambda e: e.matmul(pkk[:, 0:128], kT[:, csl], T["kTc"][:], start=True, stop=True),
                 reads=[kT, T["kTc"]], writes=[pkk])
            P.op("pe", lambda e: e.matmul(pkk[:, 128:256], kT[:, csl], T["qTc"][:], start=True, stop=True),
                 reads=[kT, T["qTc"]], writes=[pkk])
            P.op("pool", lambda e: e.tensor_tensor(out=T["Es"][:], in0=T["E"][:], in1=ident[:], op=ALU.subtract),
                 reads=[T["E"], ident], writes=[T["Es"]])
            P.op("dve", lambda e: e.scalar_tensor_tensor(out=T["x0f"][:], in0=pkk[:, 0:128], scalar=bec,
                                                         in1=T["Es"][:], op0=ALU.mult, op1=ALU.mult),
                 reads=[pkk, tb["beta"], T["Es"]], writes=[T["x0f"]])
            P.op("dve", lambda e: e.tensor_tensor(out=AT_[:], in0=pkk[:, 128:256], in1=T["E"][:], op=ALU.mult),
                 reads=[pkk, T["E"]], writes=[AT_])
            yield
            P.op("pe", lambda e: e.transpose(pss[1][:, 0:128], T["x0f"][:], ident[:]), reads=[T["x0f"], ident],
                 writes=[pss[1]])
            P.op("dve", lambda e: e.tensor_copy(out=T["UT"][:], in_=pss[1][:, 0:128]), reads=[pss[1]], writes=[T["UT"]])
            yield
            for lv in range(1, 7):
                eng = "pool" if lv % 2 else "dve"
                P.op(eng, lambda e, lv=lv: e.tensor_tensor(out=T["CT"][lv - 1][:], in0=T["UT"][:],
                                                           in1=mskT[:, lv - 1, :], op=ALU.mult),
                     reads=[T["UT"], mskT], writes=[T["CT"][lv - 1]])
                yield
            P.op("dve", lambda e: e.tensor_tensor(out=T["G32"][:], in0=T["x0f"][:], in1=msk0[:], op=ALU.mult),
                 reads=[T["x0f"], msk0], writes=[T["G32"]])
            yield
            P.op("dve", lambda e: e.tensor_tensor(out=T["G32"][:], in0=ident[:], in1=T["G32"][:], op=ALU.subtract),
                 reads=[ident, T["G32"]], writes=[T["G32"]])
            yield
            P.op("act", lambda e: e.copy(out=T["Gm"][:], in_=T["G32"][:]), reads=[T["G32"]], writes=[T["Gm"]])
            yield
            P.op("pool", lambda e: e.tensor_copy(out=Gw_[:], in_=T["G32"][:]), reads=[T["G32"]], writes=[Gw_])
            yield
            for lv in range(1, 7):
                P.op("pe", lambda e, lv=lv: e.matmul(pX[:, 0:128], T["CT"][lv - 1][:], T["Gm"][:], start=True, stop=True),
                     reads=[T["CT"][lv - 1], T["Gm"]], writes=[pX])
                P.op("pe", lambda e: e.transpose(ptr[:, 0:128], Gw_[:], identb[:]), reads=[Gw_, identb], writes=[ptr])
                P.op("act", lambda e: e.copy(out=T["Ysb"][:], in_=pX[:, 0:128]), reads=[pX], writes=[T["Ysb"]])
                P.op("act", lambda e: e.copy(out=T["GTw"][:], in_=ptr[:, 0:128]), reads=[ptr], writes=[T["GTw"]])
                yield
                P.op("pe", lambda e: e.matmul(pW[:], T["GTw"][:], T["Ysb"][:], start=True, stop=True),
                     reads=[T["GTw"], T["Ysb"]], writes=[pW])
                P.op("dve", lambda e: e.tensor_tensor(out=T["G32"][:], in0=T["G32"][:], in1=pW[:], op=ALU.subtract),
                     reads=[T["G32"], pW], writes=[T["G32"]])
                yield
                if lv < 6:
                    P.op("act", lambda e: e.copy(out=T["Gm"][:], in_=T["G32"][:]), reads=[T["G32"]], writes=[T["Gm"]])
                    yield
                P.op("pool", lambda e: e.tensor_copy(out=Gw_[:], in_=T["G32"][:]), reads=[T["G32"]], writes=[Gw_])
                yield

        def seq(c, O):
            csl = slice(c * DNC, (c + 1) * DNC)
            bec = col(tb["beta"], c)
            KV_, AT_, Gw_ = O
            P.op("pe", lambda e: e.matmul(pks[:, 0:128], kT[:, csl], Sb[:], start=True, stop=True),
                 reads=[kT, Sb], writes=[pks])
            P.op("pe", lambda e: e.matmul(pks[:, 128:256], qT[:, csl], Sb[:], start=True, stop=True),
                 reads=[qT, Sb], writes=[pks])
            yield
            P.op("dve", lambda e: e.scalar_tensor_tensor(out=Rp[:], in0=pks[:, 0:128], scalar=col(tb["negegc"], c),
                                                         in1=KV_[:, 128:256], op0=ALU.mult, op1=ALU.add),
                 reads=[pks, tb["negegc"], KV_], writes=[Rp])
            yield
            P.op("pe", lambda e: e.matmul(pgs[:, 0:128], Gw_[:], Rp[:], start=True, stop=True), reads=[Gw_, Rp], writes=[pgs])
            yield
            P.op("dve", lambda e: e.tensor_scalar(out=vnew[:], in0=pgs[:, 0:128], scalar1=bec, scalar2=None, op0=ALU.mult),
                 reads=[pgs, tb["beta"]], writes=[vnew])
            yield
            P.op("pe", lambda e: e.matmul(pgs[:, 0:128], AT_[:], vnew[:], start=True, stop=True), reads=[AT_, vnew], writes=[pgs])
            yield
            P.op("dve", lambda e: e.tensor_scalar(out=oq[:], in0=pks[:, 128:256], scalar1=col(tb["egc"], c),
                                                  scalar2=None, op0=ALU.mult), reads=[pks, tb["egc"]], writes=[oq])
            yield
            o = ot[c % 2]
            P.op("dve", lambda e: e.tensor_tensor(out=o[:], in0=pgs[:, 0:128], in1=oq[:], op=ALU.add),
                 reads=[pgs, oq], writes=[o])
            P.store("sp", o, out_d[u, csl, :], o[:])
            yield
            P.op("pool", lambda e: e.tensor_scalar(out=Kd[:], in0=KV_[:, 0:128], scalar1=col(tb["ekd"], c),
                                                   scalar2=None, op0=ALU.mult), reads=[KV_, tb["ekd"]], writes=[Kd])
            yield
            P.op("pe", lambda e: e.matmul(pks[:, 0:128], Kd[:], vnew[:], start=True, stop=True),
                 reads=[Kd, vnew], writes=[pks])
            yield
            P.op("dve", lambda e: e.scalar_tensor_tensor(out=S[:], in0=S[:], scalar=col(tb["egl"], c),
                                                         in1=pks[:, 0:128], op0=ALU.mult, op1=ALU.add),
                 reads=[S, tb["egl"], pks], writes=[S])
            yield
            P.op("act", lambda e: e.copy(out=Sb[:], in_=S[:]), reads=[S], writes=[Sb])
            yield

        def seq_group(c0):
            for cc in range(c0, c0 + DN_K):
                yield from seq(cc, ORing[cc % (2 * DN_K)])

        def round_robin(gens):
            gens = list(gens)
            while gens:
                for g_ in list(gens):
                    try:
                        next(g_)
                    except StopIteration:
                        gens.remove(g_)

        nch = int(os.environ.get('DN_NCH', NDC))
        for p in range(nch // DN_K):
            gens = [pre(DN_K * p + i, TT[i], ORing[(DN_K * p + i) % (2 * DN_K)]) for i in range(DN_K)]
            if p >= 1:
                gens.append(seq_group(DN_K * (p - 1)))
            round_robin(gens)
        if nch >= DN_K:
            round_robin([seq_group(nch - DN_K)])
    return P


DN_UNITS = [(h, d) for h in range(6) for d in range(2)]


def run_DN(hA, conv_w, a_log, dt_bias):
    P = build_DN()
    qkv = hA[:, 768:3072]
    da = hA[:, 4608:4620]
    db = hA[:, 4620:4632]
    ii = np.arange(128)
    triu = (ii[:, None] <= ii[None, :]).astype(np.float32)
    maskb = np.where(ii[None, :] >= ii[:, None], 0.0, -30000.0).astype(np.float32)
    msk0 = np.zeros((128, 128), np.float32)
    mskT = np.zeros((128, 6, 128), np.float32)
    for lv in range(7):
        b = 1 << lv
        jj, i2 = np.meshgrid(ii, ii, indexing="ij")
        m = ((jj // (2 * b) == i2 // (2 * b)) & (jj % (2 * b) < b) & (i2 % (2 * b) >= b)).astype(np.float32)
        if lv == 0:
            msk0 = m
        else:
            mskT[:, lv - 1, :] = m.T
    units = DN_UNITS + DN_UNITS[:4]
    maps = []
    for c in range(NCORES):
        xin = np.zeros((2, 3, 128, SEQ + 4), np.float32)
        cw = np.zeros((2, 128, 15), np.float32)
        ab = np.zeros((2, 128, 2, NDC), np.float32)
        hp = np.zeros((2, 128, 2), np.float32)
        for s in range(2):
            h, d = units[c * 2 + s]
            for ti in range(3):
                cs = slice(ti * 768 + h * 128, ti * 768 + (h + 1) * 128)
                xt = qkv[:, cs].T
                w = conv_w[cs]
                if d == 1:
                    xt = xt[:, ::-1]
                    w = w[:, ::-1]
                xin[s, ti, :, 2:2 + SEQ] = xt
                cw[s, :, ti * 5:(ti + 1) * 5] = w
            av = da[:, d * 6 + h]
            bv = db[:, d * 6 + h]
            if d == 1:
                av = av[::-1]
                bv = bv[::-1]
            ab[s, :, 0, :] = av.reshape(NDC, 128).T
            ab[s, :, 1, :] = bv.reshape(NDC, 128).T
            hp[s, :, 0] = a_log[d, h]
            hp[s, :, 1] = dt_bias[d, h]
        maps.append(dict(xin=xin, cw=cw, ab=ab, hp=hp, ident=np.eye(128, dtype=np.float32), triu=triu, maskb=maskb,
                         msk0=msk0, mskT=mskT))
    res = run(P, maps)
    of = np.zeros((SEQ, 768), np.float32)
    ob = np.zeros((SEQ, 768), np.float32)
    for idx, (h, d) in enumerate(DN_UNITS):
        o = res[idx // 2]["o"][idx % 2]
        if d == 0:
            of[:, h * 128:(h + 1) * 128] = o
        else:
            ob[:, h * 128:(h + 1) * 128] = o[::-1]
    return of, ob


COLS_Z = np.concatenate([np.arange(768, 1536), np.arange(3864, 4632), np.arange(6168, 7192),
                         np.arange(7704, 8216), np.arange(7192, 7704)])
NZ = 3584


def build_C1():
    P = Prog()
    x_d = P.dram_in("x", [TPC, D])
    gb_d = P.dram_in("gb", [128, D])
    id_d = P.dram_in("ident", [128, 128])
    wz_d = P.dram_in("wz", [D, NZ])
    mem_d = P.dram_in("mem", [256, D])
    mgb_d = P.dram_in("mgb", [128, D])
    wkv_d = P.dram_in("wkv", [D, 1024])
    s5_d = P.dram_in("s5", [3, 768, TPC])
    sd_d = P.dram_in("sd", [128, 16])
    wglu_d = P.dram_in("wglu", [768, 768])
    dn_d = P.dram_in("dn", [2, 768, TPC])
    oc_d = P.dram_in("oc", [1024, TPC])
    y_d = P.dram_out("yT", [24, 128, TPC], BF16)

    ident = P.sbuf("ident", [128, 128])
    gb = P.sbuf("gb", [128, D])
    sd = P.sbuf("sd", [128, 16])
    onesb = P.sbuf("onesb", [128, 128], BF16)
    P.load("sp", ident, ident[:], id_d)
    P.load("sp", gb, gb[:], gb_d)
    P.load("sp", sd, sd[:], sd_d)
    P.op("dve", lambda e: e.memset(onesb[:], 1.0), writes=[onesb])
    xnT = P.sbuf("xnT", [128, 16, TPC], BF16)
    nb = emit_norm_T(P, x_d, gb, ident, xnT, TPC // 128, "nC", single=True)
    memnT = P.sbuf("memnT", [128, 16, 256], BF16)
    P.load("sp", gb, gb[:], mgb_d)
    emit_norm_T(P, mem_d, gb, ident, memnT, 2, "nC", bufs=nb)

    pp = [P.psum(f"pp{i}", [128, 512]) for i in range(2)]
    pa = [P.psum(f"pa{i}", [128, 512]) for i in range(2)]
    po = P.psum("po", [128, 512])
    pd = P.psum("pd", [128, 512])

    wk = P.sbuf("wk", [128, 16, 512], BF16)
    KmT = P.sbuf("KmT", [128, 4, 256], BF16)
    Vm = P.sbuf("Vm", [128, 2, 512], BF16)
    wkv_v = wkv_d.rearrange("(c p) n -> p c n", p=128)
    P.dma("pool", [(wk[:, k, :], wkv_v[:, k, 0:512]) for k in range(16)], wk, writes=[wk])
    for h in range(4):
        ps = pp[h % 2]
        for k in range(16):
            P.op("pe", lambda e, ps=ps, k=k, h=h: e.matmul(ps[:, 0:256], wk[:, k, h * 128:(h + 1) * 128],
                                                          memnT[:, k, :], start=(k == 0), stop=(k == 15)),
                 reads=[wk, memnT], writes=[ps])
        P.op("act", lambda e, ps=ps, h=h: e.copy(out=KmT[:, h, :], in_=ps[:, 0:256]), reads=[ps], writes=[KmT])
    P.dma("pool", [(wk[:, k, :], wkv_v[:, k, 512:1024]) for k in range(16)], wk, writes=[wk])
    for mt in range(2):
        ps = pp[mt % 2]
        for k in range(16):
            P.op("pe", lambda e, ps=ps, k=k, mt=mt: e.matmul(ps[:], memnT[:, k, mt * 128:(mt + 1) * 128],
                                                            wk[:, k, :], start=(k == 0), stop=(k == 15)),
                 reads=[wk, memnT], writes=[ps])
        P.op("act", lambda e, ps=ps, mt=mt: e.copy(out=Vm[:, mt, :], in_=ps[:]), reads=[ps], writes=[Vm])

    wj = [P.sbuf(f"wj{i}", [128, 16, 128], BF16) for i in range(2)]
    sz = [P.sbuf(f"sz{i}", [128, TPC], BF16) for i in range(2)]
    wz_v = wz_d.rearrange("(c p) n -> p c n", p=128)
    cnt = {"w": 0, "p": 0}
    stg = [nb["xt"][0], nb["xn"][0]]

    def projT(col0, dst_ap_fn, func, dst_buf):
        w = wj[cnt["w"] % 2]
        cnt["w"] += 1
        P.dma("pool", [(w[:, k, :], wz_v[:, k, col0:col0 + 128]) for k in range(16)], w, writes=[w])
        for half in range(2):
            ps = pp[cnt["p"] % 2]
            cnt["p"] += 1
            hs = slice(half * 512, (half + 1) * 512)
            for k in range(16):
                P.op("pe", lambda e, ps=ps, k=k, w=w, hs=hs: e.matmul(ps[:], w[:, k, :], xnT[:, k, hs],
                                                                     start=(k == 0), stop=(k == 15)),
                     reads=[w, xnT], writes=[ps])
            P.op("act", lambda e, ps=ps, hs=hs: e.activation(out=dst_ap_fn(hs), in_=ps[:], func=func),
                 reads=[ps], writes=[dst_buf])

    def proj_silu(col0):
        z = sz[cnt["w"] % 2]
        projT(col0, lambda hs, z=z: z[:, hs], AF.Silu, z)
        return z

    f1 = [P.sbuf(f"f1_{i}", [128, TPC]) for i in range(2)]
    f2 = [P.sbuf(f"f2_{i}", [128, TPC]) for i in range(2)]
    f3 = P.sbuf("f3", [128, TPC])
    f4 = P.sbuf("f4", [128, TPC])
    yo = [P.sbuf(f"yo{i}", [128, TPC], BF16) for i in range(2)]
    sqb = P.sbuf("sqb", [128, TPC], BF16)
    rn = P.sbuf("rn", [128, 512])
    sg = P.sbuf("sg", [128, 512])
    ycnt = {"n": 0}

    def next_yo():
        y = yo[ycnt["n"] % 2]
        ycnt["n"] += 1
        return y

    GY = P.sbuf("GY", [128, 6, TPC], BF16)
    wglu = P.sbuf("wglu", [128, 6, 768], BF16)
    wglu_v = wglu_d.rearrange("(c p) n -> p c n", p=128)
    P.dma("pool", [(wglu[:, k, :], wglu_v[:, k, :]) for k in range(6)], wglu, writes=[wglu])
    for j in range(6):
        a, b_ = f1[j % 2], f2[j % 2]
        rs = slice(j * 128, (j + 1) * 128)
        P.load("sp", a, a[:], s5_d[0, rs, :])
        P.load("sp", b_, b_[:], s5_d[1, rs, :])
        P.load("sp", f3, f3[:], s5_d[2, rs, :])
        P.op("pool", lambda e, a=a, b_=b_: e.tensor_tensor(out=a[:], in0=a[:], in1=b_[:], op=ALU.add),
             reads=[a, b_], writes=[a])
        P.op("dve", lambda e, a=a, j=j: e.scalar_tensor_tensor(out=a[:], in0=f3[:], scalar=sd[:, j:j + 1], in1=a[:],
                                                              op0=ALU.mult, op1=ALU.add), reads=[f3, sd, a], writes=[a])
        P.op("pool", lambda e, a=a, b_=b_: e.tensor_tensor(out=b_[:], in0=a[:], in1=a[:], op=ALU.mult),
             reads=[a], writes=[b_])
        P.op("dve", lambda e, b_=b_: e.tensor_scalar(out=b_[:], in0=b_[:], scalar1=0.044715, scalar2=1.0,
                                                     op0=ALU.mult, op1=ALU.add), reads=[b_], writes=[b_])
        P.op("pool", lambda e, a=a, b_=b_: e.tensor_tensor(out=b_[:], in0=b_[:], in1=a[:], op=ALU.mult),
             reads=[a, b_], writes=[b_])
        P.op("act", lambda e, b_=b_: e.activation(out=f4[:], in_=b_[:], func=AF.Sigmoid, scale=1.5957691216057308),
             reads=[b_], writes=[f4])
        P.op("dve", lambda e, a=a, j=j: e.tensor_tensor(out=GY[:, j, :], in0=a[:], in1=f4[:], op=ALU.mult),
             reads=[a, f4], writes=[GY])
    for j in range(6):
        z = proj_silu(0 + j * 128)
        y = next_yo()
        for half in range(2):
            hs = slice(half * 512, (half + 1) * 512)
            ps = pa[half]
            for k in range(6):
                P.op("pe", lambda e, ps=ps, k=k, j=j, hs=hs: e.matmul(ps[:], wglu[:, k, j * 128:(j + 1) * 128],
                                                                     GY[:, k, hs], start=(k == 0), stop=(k == 5)),
                     reads=[wglu, GY], writes=[ps])
            P.op("act", lambda e, ps=ps, j=j: e.activation(out=sg[:], in_=ps[:], func=AF.Sigmoid,
                                                           bias=sd[:, 6 + j:7 + j]), reads=[ps, sd], writes=[sg])
            P.op("dve", lambda e, j=j, hs=hs: e.tensor_tensor(out=sg[:], in0=sg[:], in1=GY[:, j, hs], op=ALU.mult),
                 reads=[sg, GY], writes=[sg])
            P.op("dve", lambda e, y=y, z=z, hs=hs: e.tensor_tensor(out=y[:, hs], in0=sg[:], in1=z[:, hs], op=ALU.mult),
                 reads=[sg, z], writes=[y])
        P.store("sp", y, y_d[j], y[:])
    for h in range(6):
        a, b_ = f1[h % 2], f2[h % 2]
        rs = slice(h * 128, (h + 1) * 128)
        P.load("sp", a, a[:], dn_d[0, rs, :])
        P.load("sp", b_, b_[:], dn_d[1, rs, :])
        P.op("pool", lambda e, a=a, b_=b_: e.tensor_tensor(out=a[:], in0=a[:], in1=b_[:], op=ALU.add),
             reads=[a, b_], writes=[a])
        P.op("pool", lambda e, a=a: e.tensor_tensor(out=sqb[:], in0=a[:], in1=a[:], op=ALU.mult),
             reads=[a], writes=[sqb])
        z = proj_silu(768 + h * 128)
        y = next_yo()
        for half in range(2):
            hs = slice(half * 512, (half + 1) * 512)
            ps = pa[half]
            P.op("pe", lambda e, ps=ps, hs=hs: e.matmul(ps[:], onesb[:], sqb[:, hs], start=True, stop=True),
                 reads=[onesb, sqb], writes=[ps])
            P.op("act", lambda e, ps=ps: e.activation(out=rn[:], in_=ps[:], func=AF.Sqrt, scale=1.0 / 128, bias=EPS),
                 reads=[ps], writes=[rn])
            P.op("dve", lambda e: e.reciprocal(out=rn[:], in_=rn[:]), reads=[rn], writes=[rn])
            P.op("dve", lambda e, a=a, hs=hs: e.scalar_tensor_tensor(out=rn[:], in0=a[:, hs], scalar=sd[:, 12:13],
                                                                    in1=rn[:], op0=ALU.mult, op1=ALU.mult),
                 reads=[a, sd, rn], writes=[rn])
            P.op("dve", lambda e, y=y, z=z, hs=hs: e.tensor_tensor(out=y[:, hs], in0=rn[:], in1=z[:, hs], op=ALU.mult),
                 reads=[rn, z], writes=[y])
        P.store("sp", y, y_d[6 + h], y[:])
    for c in range(8):
        a = f1[c % 2]
        P.load("sp", a, a[:], oc_d[c * 128:(c + 1) * 128, :])
        z = proj_silu(1536 + c * 128)
        y = next_yo()
        P.op("dve", lambda e, a=a, y=y, z=z: e.tensor_tensor(out=y[:], in0=a[:], in1=z[:], op=ALU.mult),
             reads=[a, z], writes=[y])
        P.store("sp", y, y_d[12 + c], y[:])
    QmT = P.sbuf("QmT", [128, TPC], BF16)
    pm = [P.sbuf(f"pm{i}", [128, 512], BF16) for i in range(2)]
    pc = 0
    for h in range(4):
        projT(3072 + h * 128, lambda hs: QmT[:, hs], AF.Copy, QmT)
        z = proj_silu(2560 + h * 128)
        y = next_yo()
        for half in range(2):
            hs = slice(half * 512, (half + 1) * 512)
            for mt in range(2):
                ps = pa[mt]
                p_ = pm[pc % 2]
                pc += 1
                P.op("pe", lambda e, ps=ps, h=h, mt=mt, hs=hs: e.matmul(ps[:], KmT[:, h, mt * 128:(mt + 1) * 128],
                                                                       QmT[:, hs], start=True, stop=True),
                     reads=[KmT, QmT], writes=[ps])
                P.op("act", lambda e, ps=ps, p_=p_: e.activation(out=p_[:], in_=ps[:], func=AF.Exp, scale=128.0 ** -0.5),
                     reads=[ps], writes=[p_])
                P.op("pe", lambda e, p_=p_, h=h, mt=mt: e.matmul(po[:], Vm[:, mt, h * 128:(h + 1) * 128], p_[:],
                                                                start=(mt == 0), stop=(mt == 1)),
                     reads=[Vm, p_], writes=[po])
                P.op("pe", lambda e, p_=p_, mt=mt: e.matmul(pd[:], onesb[:], p_[:], start=(mt == 0), stop=(mt == 1)),
                     reads=[onesb, p_], writes=[pd])
            P.op("dve", lambda e: e.reciprocal(out=rn[:], in_=pd[:]), reads=[pd], writes=[rn])
            P.op("dve", lambda e: e.tensor_tensor(out=rn[:], in0=po[:], in1=rn[:], op=ALU.mult),
                 reads=[po, rn], writes=[rn])
            P.op("dve", lambda e, y=y, z=z, hs=hs: e.tensor_tensor(out=y[:, hs], in0=rn[:], in1=z[:, hs], op=ALU.mult),
                 reads=[rn, z], writes=[y])
        P.store("sp", y, y_d[20 + h], y[:])
    return P


def run_C1(x, norm_g, w_in_l, mem, mem_g, w_kv, hA, yf, yb, ssm_d, w_glu, b_glu, of, ob, dn_g, yc):
    P = build_C1()
    sd = np.zeros((128, 16), np.float32)
    sd[:, 0:6] = np.asarray(ssm_d, np.float32).reshape(6, 128).T
    sd[:, 6:12] = np.asarray(b_glu, np.float32).reshape(6, 128).T
    sd[:, 12] = np.asarray(dn_g, np.float32)
    common = dict(gb=bcast_rows(norm_g), ident=np.eye(128, dtype=np.float32),
                  wz=np.ascontiguousarray(w_in_l[:, COLS_Z]), mem=np.ascontiguousarray(mem),
                  mgb=bcast_rows(mem_g), wkv=np.ascontiguousarray(w_kv), sd=sd,
                  wglu=np.ascontiguousarray(w_glu))
    maps = []
    for c in range(NCORES):
        ts = slice(c * TPC, (c + 1) * TPC)
        s5 = np.stack([yf[ts].T, yb[ts].T, hA[ts, 0:768].T]).astype(np.float32)
        dn = np.stack([of[ts].T, ob[ts].T]).astype(np.float32)
        maps.append(dict(common, x=np.ascontiguousarray(x[ts]), s5=np.ascontiguousarray(s5),
                         dn=np.ascontiguousarray(dn), oc=np.ascontiguousarray(yc[ts].T)))
    res = run(P, maps)
    return [r["yT"] for r in res]


BR_CHUNKS = [(0, 6), (6, 12), (12, 20), (20, 24)]


def build_C2():
    P = Prog()
    x_d = P.dram_in("x", [TPC, D])
    gb_d = P.dram_in("gb", [128, D])
    id_d = P.dram_in("ident", [128, 128])
    wg_d = P.dram_in("wg", [D, 4 * D])
    y_d = P.dram_in("yT", [24, 128, TPC], BF16)
    wbr_d = P.dram_in("wbr", [3072, D])
    wout_d = P.dram_in("wout", [D, D])
    out_d = P.dram_out("xnew", [TPC, D])

    ident = P.sbuf("ident", [128, 128])
    gb = P.sbuf("gb", [128, D])
    P.load("sp", ident, ident[:], id_d)
    P.load("sp", gb, gb[:], gb_d)
    xnT = P.sbuf("xnT", [128, 16, TPC], BF16)
    nb = emit_norm_T(P, x_d, gb, ident, xnT, TPC // 128, "nD", single=True)
    stg = [nb["xt"][0], nb["xn"][0]]
    stg_v = [t_[:].rearrange("p (k c) -> p k c", c=128) for t_ in stg]
    gb_v = gb[:].rearrange("p (k c) -> p k c", c=128)
    Y = P.sbuf("Y", [128, 24, TPC], BF16)
    for k in range(24):
        P.dma("sp", [(Y[:, k, :], y_d[k])], Y, writes=[Y])
    mT = P.sbuf("mT", [128, 16, TPC], BF16)
    wbr = P.sbuf("wbr", [128, 24, 128], BF16)
    wg = [P.sbuf(f"wg{i}", [128, 16, 128], BF16) for i in range(2)]
    pb = [P.psum(f"pb{i}", [128, 512]) for i in range(2)]
    pg = [P.psum(f"pg{i}", [128, 512]) for i in range(2)]
    sg = [P.sbuf(f"sg{i}", [128, 512]) for i in range(2)]
    acc = P.sbuf("acc", [128, TPC])
    tmp = P.sbuf("tmpm", [128, 512])
    wbr_v = wbr_d.rearrange("(c p) n -> p c n", p=128)
    wg_v = wg_d.rearrange("(c p) n -> p c n", p=128)
    cnt = 0
    wcnt = 0
    for j in range(16):
        js = slice(j * 128, (j + 1) * 128)
        P.dma("sp", [(gb_v[:, k, :], wbr_v[:, k, js]) for k in range(16)], gb, writes=[gb])
        P.op("dve", lambda e: e.tensor_copy(out=wbr[:, 0:16, :], in_=gb_v), reads=[gb], writes=[wbr])
        P.dma("sp", [(gb_v[:, k, :], wbr_v[:, 16 + k, js]) for k in range(8)], gb, writes=[gb])
        P.op("dve", lambda e: e.tensor_copy(out=wbr[:, 16:24, :], in_=gb_v[:, 0:8, :]), reads=[gb], writes=[wbr])
        for b in range(4):
            w = wg[wcnt % 2]
            g0 = b * D + j * 128
            sb_, sv_ = stg[wcnt % 2], stg_v[wcnt % 2]
            wcnt += 1
            P.dma("sp", [(sv_[:, k, :], wg_v[:, k, g0:g0 + 128]) for k in range(16)], sb_, writes=[sb_])
            P.op("pool", lambda e, w=w, sv_=sv_: e.tensor_copy(out=w[:], in_=sv_), reads=[sb_], writes=[w])
            k0, k1 = BR_CHUNKS[b]
            for half in range(2):
                hs = slice(half * 512, (half + 1) * 512)
                p1, p2, s_ = pb[cnt % 2], pg[cnt % 2], sg[cnt % 2]
                cnt += 1
                for k in range(16):
                    P.op("pe", lambda e, p2=p2, k=k, w=w, hs=hs: e.matmul(p2[:], w[:, k, :], xnT[:, k, hs],
                                                                         start=(k == 0), stop=(k == 15)),
                         reads=[w, xnT], writes=[p2])
                for k in range(k0, k1):
                    P.op("pe", lambda e, p1=p1, k=k, hs=hs, k0=k0, k1=k1: e.matmul(p1[:], wbr[:, k, :], Y[:, k, hs],
                                                                                  start=(k == k0), stop=(k == k1 - 1)),
                         reads=[wbr, Y], writes=[p1])
                P.op("act", lambda e, p2=p2, s_=s_: e.activation(out=s_[:], in_=p2[:], func=AF.Sigmoid),
                     reads=[p2], writes=[s_])
                if b == 0:
                    P.op("dve", lambda e, p1=p1, s_=s_, hs=hs: e.tensor_tensor(out=acc[:, hs], in0=p1[:], in1=s_[:],
                                                                              op=ALU.mult), reads=[p1, s_], writes=[acc])
                else:
                    P.op("dve", lambda e, p1=p1, s_=s_: e.tensor_tensor(out=tmp[:], in0=p1[:], in1=s_[:], op=ALU.mult),
                         reads=[p1, s_], writes=[tmp])
                    P.op("pool", lambda e, hs=hs: e.tensor_tensor(out=acc[:, hs], in0=acc[:, hs], in1=tmp[:],
                                                                  op=ALU.add), reads=[acc, tmp], writes=[acc])
        P.op("act", lambda e, j=j: e.copy(out=mT[:, j, :], in_=acc[:]), reads=[acc], writes=[mT])
    wo_view = Y[:, 0:8, :].rearrange("p a (b c) -> p (a b) c", c=512)
    wout_v = wout_d.rearrange("(c p) n -> p c n", p=128)
    xr = [P.sbuf(f"xr{i}", [128, 512]) for i in range(2)]
    ot = [P.sbuf(f"oo{i}", [128, 512]) for i in range(2)]
    cnt = 0
    for cb in range(4):
        cs = slice(cb * 512, (cb + 1) * 512)
        P.dma("pool", [(wo_view[:, k, :], wout_v[:, k, cs]) for k in range(16)], Y, writes=[Y])
        for i in range(TPC // 128):
            ps = pb[cnt % 2]
            r_ = xr[cnt % 2]
            o_ = ot[cnt % 2]
            cnt += 1
            ts = slice(i * 128, (i + 1) * 128)
            P.load("sp", r_, r_[:], x_d[ts, cs])
            for k in range(16):
                P.op("pe", lambda e, ps=ps, k=k, ts=ts: e.matmul(ps[:], mT[:, k, ts], wo_view[:, k, :],
                                                                start=(k == 0), stop=(k == 15)),
                     reads=[mT, Y], writes=[ps])
            P.op("dve", lambda e, ps=ps, r_=r_, o_=o_: e.tensor_tensor(out=o_[:], in0=ps[:], in1=r_[:], op=ALU.add),
                 reads=[ps, r_], writes=[o_])
            P.store("sp", o_, out_d[ts, cs], o_[:])
    return P


def run_C2(x, norm_g, w_in_l, yTs, w_br, w_o):
    P = build_C2()
    common = dict(gb=bcast_rows(norm_g), ident=np.eye(128, dtype=np.float32),
                  wg=np.ascontiguousarray(w_in_l[:, 8216:]), wbr=np.ascontiguousarray(w_br),
                  wout=np.ascontiguousarray(w_o))
    maps = [dict(common, x=np.ascontiguousarray(x[c * TPC:(c + 1) * TPC]), yT=yTs[c]) for c in range(NCORES)]
    res = run(P, maps)
    return np.concatenate([r["xnew"] for r in res], axis=0)


def build_F():
    P = Prog()
    x_d = P.dram_in("x", [TPC, D])
    gb_d = P.dram_in("gb", [128, D])
    out_d = P.dram_out("y", [TPC, D])
    gb = P.sbuf("gb", [128, D])
    P.load("sp", gb, gb[:], gb_d)
    xt = [P.sbuf(f"xt{i}", [128, D]) for i in range(2)]
    yt = [P.sbuf(f"yt{i}", [128, D]) for i in range(2)]
    junk = P.sbuf("junk", [128, D], BF16)
    ss = [P.sbuf(f"ss{i}", [128, 16]) for i in range(2)]
    for i in range(TPC // 128):
        b = i % 2
        ts = slice(i * 128, (i + 1) * 128)
        P.load("sp", xt[b], xt[b][:], x_d[ts, :])
        P.op("act", lambda e, b=b: e.activation(out=junk[:], in_=xt[b][:], func=AF.Square, accum_out=ss[b][:, 0:1]),
             reads=[xt[b]], writes=[junk, ss[b]])
        P.op("act", lambda e, b=b: e.activation(out=ss[b][:, 1:2], in_=ss[b][:, 0:1], func=AF.Sqrt, scale=1.0 / D,
                                                bias=EPS), reads=[ss[b]], writes=[ss[b]])
        P.op("dve", lambda e, b=b: e.reciprocal(out=ss[b][:, 1:2], in_=ss[b][:, 1:2]), reads=[ss[b]], writes=[ss[b]])
        P.op("dve", lambda e, b=b: e.scalar_tensor_tensor(out=yt[b][:], in0=xt[b][:], scalar=ss[b][:, 1:2], in1=gb[:],
                                                          op0=ALU.mult, op1=ALU.mult),
             reads=[xt[b], ss[b], gb], writes=[yt[b]])
        P.store("sp", yt[b], out_d[ts, :], yt[b][:])
    return P


def run_F(x, g):
    P = build_F()
    maps = [dict(x=np.ascontiguousarray(x[c * TPC:(c + 1) * TPC]), gb=bcast_rows(g)) for c in range(NCORES)]
    res = run(P, maps)
    return np.concatenate([r["y"] for r in res], axis=0)


def layer_forward(xs, L, inp):
    g = lambda k: np.asarray(inp[k][L], np.float32)
    w_in_l = g("w_in")
    hA = run_A(xs, g("norm_g"), w_in_l, g("attn_q_norm"), g("attn_k_norm"))
    yc = run_ATT(hA)
    yf, yb = run_S5(hA, g("ssm_a_re"), g("ssm_a_im"), g("ssm_log_step"), g("ssm_b_re"), g("ssm_b_im"),
                    g("ssm_c_re"), g("ssm_c_im"))
    of, ob = run_DN(hA, g("dn_conv"), g("dn_a_log"), g("dn_dt_bias"))
    yTs = run_C1(xs, g("norm_g"), w_in_l, np.asarray(inp["mem"], np.float32)[0], g("mem_norm_g"), g("w_mem_kv"),
                 hA, yf, yb, g("ssm_d"), g("ssm_w_glu"), g("ssm_b_glu"), of, ob, g("dn_norm_g"), yc)
    return run_C2(xs, g("norm_g"), w_in_l, yTs, g("w_branch"), g("w_out"))


def kernel(**inp):
    xs = np.asarray(inp["x"], np.float32)[0]
    for L in range(2):
        xs = layer_forward(xs, L, inp)
    out = run_F(xs, np.asarray(inp["final_norm_g"], np.float32))
    return out[None].astype(np.float32)
```

```python
from contextlib import ExitStack
import os
import numpy as np
import concourse.bass as bass
import concourse.mybir as mybir
from concourse.bass_utils import run_bass_kernel_spmd

F32 = mybir.dt.float32
BF16 = mybir.dt.bfloat16
I32 = mybir.dt.int32
AF = mybir.ActivationFunctionType
ALU = mybir.AluOpType
AX = mybir.AxisListType

NCORES = 8
D = 2048
SEQ = 8192
TPC = SEQ // NCORES
EPS = 1e-6
TWO_PI = float(2 * np.pi)


class Buf:
    def __init__(self, name, t=None):
        self.name = name
        self.t = t
        self.w = None
        self.r = []
        self.dsem = None
        self.dcnt = 0

    def __getitem__(self, k):
        return self.t[k]


class Prog:
    ENG = ("sp", "act", "dve", "pool", "pe")

    def __init__(self):
        self.nc = bass.Bass("TRN2", target_bir_lowering=False)
        self.ctx = ExitStack()
        self.streams = {e: [] for e in self.ENG}
        self.seq = {e: 0 for e in self.ENG}
        self.esem = {e: self.ctx.enter_context(self.nc.semaphore("es_" + e)) for e in self.ENG}
        self.store_bufs = []
        self.nid = 0

    def dram_in(self, name, shape, dt=F32):
        return self.nc.dram_tensor(name, list(shape), dt, kind="ExternalInput").ap()

    def dram_out(self, name, shape, dt=F32):
        return self.nc.dram_tensor(name, list(shape), dt, kind="ExternalOutput").ap()

    def sbuf(self, name, shape, dt=F32):
        t = self.ctx.enter_context(self.nc.sbuf_tensor("sb_" + name, list(shape), dt))
        esz = 2 if dt == BF16 else 4
        nbytes = int(np.prod(shape[1:])) * esz
        rem = (-nbytes) % 64
        if rem > 32:
            self.ctx.enter_context(self.nc.sbuf_tensor("pad_" + name, [shape[0], 8], F32))
        elif 0 < rem <= 32 and ((nbytes + 31) // 32 * 32) % 64 != 0:
            self.ctx.enter_context(self.nc.sbuf_tensor("pad_" + name, [shape[0], 8], F32))
        return Buf(name, t)

    def psum(self, name, shape, dt=F32):
        t = self.ctx.enter_context(self.nc.psum_tensor("ps_" + name, list(shape), dt))
        return Buf(name, t)

    def _deps(self, reads, writes):
        toks = []
        for b in reads:
            if b.w is not None:
                toks.append((b.w[0], b.w[1], "raw:" + str(b.w[2])))
        for b in writes:
            if b.w is not None:
                toks.append(b.w)
            toks.extend(b.r)
        return toks

    def op(self, eng, fn, reads=(), writes=()):
        toks = self._deps(reads, writes)
        self.seq[eng] += 1
        tok = (self.esem[eng], self.seq[eng], eng)
        self.streams[eng].append((toks, fn, (self.esem[eng], 1)))
        for b in reads:
            b.r.append(tok)
        for b in writes:
            b.w = tok
            b.r = []

    def dma(self, eng, pairs, owner, reads=(), writes=()):
        if owner.dsem is None:
            self.nid += 1
            owner.dsem = self.ctx.enter_context(self.nc.semaphore("ds%d" % self.nid))
        toks = self._deps(reads, writes)
        owner.dcnt += len(pairs)
        tok = (owner.dsem, 16 * owner.dcnt, "dma")

        def fn(e, pairs=pairs):
            return [e.dma_start(out=o, in_=i) for (o, i) in pairs]

        self.streams[eng].append((toks, fn, (owner.dsem, 16)))
        for b in reads:
            b.r.append(tok)
        for b in writes:
            b.w = tok
            b.r = []

    def load(self, eng, buf, out_ap, in_ap):
        self.dma(eng, [(out_ap, in_ap)], buf, writes=[buf])

    def store(self, eng, buf, out_ap, in_ap):
        if buf not in self.store_bufs:
            self.store_bufs.append(buf)
        self.dma(eng, [(out_ap, in_ap)], buf, reads=[buf])

    def finish(self):
        final = [(b.dsem, 16 * b.dcnt, "dma") for b in self.store_bufs]
        self.streams["sp"].append((final, None, None))
        streams = self.streams

        def emit(name, e):
            waited = {}
            for toks, fn, inc in streams[name]:
                for (sem, val, teng) in toks:
                    if teng == name or (teng == "raw:" + name and name == "pe"):
                        continue
                    k = id(sem)
                    if waited.get(k, 0) >= val:
                        continue
                    e.wait_ge(sem, val)
                    waited[k] = val
                if fn is None:
                    continue
                ins = fn(e)
                if isinstance(ins, list):
                    for i_ in ins:
                        i_.then_inc(inc[0], inc[1])
                else:
                    ins.then_inc(inc[0], inc[1])

        with self.nc.Block() as block:
            @block.sync
            def _(e):
                emit("sp", e)

            @block.scalar
            def _(e):
                emit("act", e)

            @block.vector
            def _(e):
                emit("dve", e)

            @block.gpsimd
            def _(e):
                emit("pool", e)

            @block.tensor
            def _(e):
                emit("pe", e)
        self.ctx.close()
        return self.nc


class Rec:
    def __init__(self):
        self.groups = []
        self.cur = None

    def _add(self, call):
        if self.cur is not None:
            self.cur.append(call)
        else:
            self.groups.append([call])

    def op(self, *a, **k):
        self._add(("op", a, k))

    def dma(self, *a, **k):
        self._add(("dma", a, k))

    def load(self, *a, **k):
        self._add(("load", a, k))

    def store(self, *a, **k):
        self._add(("store", a, k))

    def begin(self):
        self.cur = []

    def end(self):
        self.groups.append(self.cur)
        self.cur = None


def replay_rr(P, recs):
    idx = [0] * len(recs)
    while any(idx[i] < len(r.groups) for i, r in enumerate(recs)):
        for i, r in enumerate(recs):
            if idx[i] < len(r.groups):
                for (kind, a, k) in r.groups[idx[i]]:
                    getattr(P, kind)(*a, **k)
                idx[i] += 1


def run(prog, in_maps):
    nc = prog.finish()
    n = int(os.environ.get("DBG_CORES", NCORES))
    if os.environ.get("DBG_TRACE"):
        res = run_bass_kernel_spmd(nc, in_maps[:n], core_ids=list(range(n)), trace=True)
        print("DBG_TRACE exec_time_ns", res.exec_time_ns, flush=True)
    else:
        res = run_bass_kernel_spmd(nc, in_maps[:n], core_ids=list(range(n)))
    out = list(res.results)
    while len(out) < NCORES:
        out.append(out[0])
    return out


def emit_norm_T(P, x_dram, gb, ident, xnT, ntiles, tag, single=False, bufs=None):
    if bufs is None:
        nb = 1 if single else 2
        bufs = dict(xt=[P.sbuf(f"{tag}_xt{i}", [128, D]) for i in range(nb)],
                    junk=P.sbuf(f"{tag}_junk", [128, D], BF16),
                    xn=[P.sbuf(f"{tag}_xn{i}", [128, D]) for i in range(nb)],
                    ss=[P.sbuf(f"{tag}_ss{i}", [128, 16]) for i in range(nb)],
                    tp=[P.psum(f"{tag}_tp{i}", [128, 512]) for i in range(2)])
    xt, junk, xn, ss, tp = bufs["xt"], bufs["junk"], bufs["xn"], bufs["ss"], bufs["tp"]
    nb = len(xt)
    tcount = 0
    for i in range(ntiles):
        b = i % nb
        P.load("sp", xt[b], xt[b][:], x_dram[i * 128:(i + 1) * 128, :])
        P.op("dve", lambda e, b=b: e.memset(ss[b][:], 0.0), writes=[ss[b]])
        P.op("act", lambda e, b=b: e.activation(out=junk[:], in_=xt[b][:], func=AF.Square,
                                                accum_out=ss[b][:, 0:1]),
             reads=[xt[b]], writes=[junk, ss[b]])
        P.op("act", lambda e, b=b: e.activation(out=ss[b][:, 1:2], in_=ss[b][:, 0:1], func=AF.Sqrt,
                                                scale=1.0 / D, bias=EPS),
             reads=[ss[b]], writes=[ss[b]])
        P.op("dve", lambda e, b=b: e.reciprocal(out=ss[b][:, 1:2], in_=ss[b][:, 1:2]),
             reads=[ss[b]], writes=[ss[b]])
        P.op("dve", lambda e, b=b: e.scalar_tensor_tensor(out=xn[b][:], in0=xt[b][:], scalar=ss[b][:, 1:2],
                                                          in1=gb[:], op0=ALU.mult, op1=ALU.mult),
             reads=[xt[b], ss[b], gb], writes=[xn[b]])
        for kk in range(4):
            pb = tp[tcount % 2]
            tcount += 1
            for j in range(4):
                k = kk * 4 + j
                P.op("pe", lambda e, b=b, k=k, j=j, pb=pb: e.transpose(pb[:, j * 128:(j + 1) * 128],
                                                                      xn[b][:, k * 128:(k + 1) * 128], ident[:]),
                     reads=[xn[b], ident], writes=[pb])
            eng = "act" if kk % 2 == 0 else "dve"
            if eng == "act":
                P.op("act", lambda e, kk=kk, i=i, pb=pb: e.copy(
                    out=xnT[:, kk * 4:(kk + 1) * 4, i * 128:(i + 1) * 128],
                    in_=pb[:].rearrange("p (a b) -> p a b", a=4)), reads=[pb], writes=[xnT])
            else:
                P.op("dve", lambda e, kk=kk, i=i, pb=pb: e.tensor_copy(
                    out=xnT[:, kk * 4:(kk + 1) * 4, i * 128:(i + 1) * 128],
                    in_=pb[:].rearrange("p (a b) -> p a b", a=4)), reads=[pb], writes=[xnT])

    return bufs


NA = 4632
NA_MAIN = 4608


def build_A():
    P = Prog()
    x_d = P.dram_in("x", [TPC, D])
    gb_d = P.dram_in("gb", [128, D])
    w_d = P.dram_in("wA", [D, NA])
    id_d = P.dram_in("ident", [128, 128])
    pos_d = P.dram_in("pos", [128, TPC // 128, 64])
    frq_d = P.dram_in("frq", [128, 64])
    qg_d = P.dram_in("qg", [128, 128])
    kg_d = P.dram_in("kg", [128, 128])
    out_d = P.dram_out("hA", [TPC, NA])
    NT = TPC // 128

    ident = P.sbuf("ident", [128, 128])
    gb = P.sbuf("gb", [128, D])
    qg = P.sbuf("qg", [128, 128])
    kg = P.sbuf("kg", [128, 128])
    pos = P.sbuf("pos", [128, NT, 64])
    frq = P.sbuf("frq", [128, 64])
    P.load("sp", ident, ident[:], id_d)
    P.load("sp", gb, gb[:], gb_d)
    P.load("sp", qg, qg[:], qg_d)
    P.load("sp", kg, kg[:], kg_d)
    P.load("sp", pos, pos[:], pos_d)
    P.load("sp", frq, frq[:], frq_d)

    ang = P.sbuf("ang", [128, NT, 64])
    tmpf = P.sbuf("tmpf", [128, NT, 64])
    tmpi = P.sbuf("tmpi", [128, NT, 64], I32)
    cosT = P.sbuf("cosT", [128, NT, 64])
    sinT = P.sbuf("sinT", [128, NT, 64])
    P.op("dve", lambda e: e.tensor_tensor(out=ang[:], in0=pos[:], in1=frq[:].unsqueeze(1).to_broadcast([128, NT, 64]),
                                          op=ALU.mult), reads=[pos, frq], writes=[ang])
    for (dst, shift) in ((sinT, 0.0), (cosT, float(np.pi / 2))):
        if shift != 0.0:
            P.op("dve", lambda e, shift=shift: e.tensor_scalar(out=ang[:], in0=ang[:], scalar1=shift, scalar2=None,
                                                                op0=ALU.add), reads=[ang], writes=[ang])
        P.op("dve", lambda e: e.tensor_scalar(out=tmpf[:], in0=ang[:], scalar1=1.0 / TWO_PI, scalar2=None,
                                              op0=ALU.mult), reads=[ang], writes=[tmpf])
        P.op("dve", lambda e: e.tensor_copy(out=tmpi[:], in_=tmpf[:]), reads=[tmpf], writes=[tmpi])
        P.op("dve", lambda e: e.tensor_copy(out=tmpf[:], in_=tmpi[:]), reads=[tmpi], writes=[tmpf])
        P.op("dve", lambda e: e.scalar_tensor_tensor(out=tmpf[:], in0=tmpf[:], scalar=-TWO_PI, in1=ang[:],
                                                     op0=ALU.mult, op1=ALU.add), reads=[tmpf, ang], writes=[tmpf])
        P.op("act", lambda e, dst=dst: e.activation(out=dst[:], in_=tmpf[:], func=AF.Sin),
             reads=[tmpf], writes=[dst])

    xnT = P.sbuf("xnT", [128, 16, TPC], BF16)
    emit_norm_T(P, x_d, gb, ident, xnT, NT, "nA")

    wblk = [P.sbuf(f"wblk{i}", [128, 16, 512], BF16) for i in range(2)]
    pp = [P.psum(f"pp{i}", [128, 512]) for i in range(2)]
    ot = [P.sbuf(f"ot{i}", [128, 512]) for i in range(3)]
    ss4 = P.sbuf("ss4", [128, 8])
    junk2 = P.sbuf("junk2", [128, 128])
    t1 = P.sbuf("rt1", [128, 4, 64])
    t2 = P.sbuf("rt2", [128, 4, 64])
    qn = P.sbuf("qn", [128, 512])
    w_v = w_d.rearrange("(c p) n -> p c n", p=128)
    nblk = 10
    cnt = 0
    for cb in range(nblk):
        wb = wblk[cb % 2]
        c0 = cb * 512
        ncol = 512 if cb < 9 else NA - NA_MAIN
        P.dma("pool", [(wb[:, k, 0:ncol], w_v[:, k, c0:c0 + ncol]) for k in range(16)], wb, writes=[wb])
        for i in range(NT):
            ps = pp[cnt % 2]
            o = ot[cnt % 3]
            cnt += 1
            for k in range(16):
                P.op("pe", lambda e, ps=ps, k=k, i=i, wb=wb, ncol=ncol: e.matmul(
                    ps[:, 0:ncol], xnT[:, k, i * 128:(i + 1) * 128], wb[:, k, 0:ncol],
                    start=(k == 0), stop=(k == 15)), reads=[xnT, wb], writes=[ps])
            if cb in (6, 7) or cb == 8:
                nh = 4 if cb in (6, 7) else 2
                g = qg if cb in (6, 7) else kg
                sc = (128.0 ** -0.5) if cb in (6, 7) else 1.0
                P.op("dve", lambda e: e.memset(ss4[:], 0.0), writes=[ss4])
                for h in range(nh):
                    P.op("act", lambda e, ps=ps, h=h: e.activation(out=junk2[:], in_=ps[:, h * 128:(h + 1) * 128],
                                                                   func=AF.Square, accum_out=ss4[:, h:h + 1]),
                         reads=[ps], writes=[junk2, ss4])
                P.op("act", lambda e, nh=nh: e.activation(out=ss4[:, 4:4 + nh], in_=ss4[:, 0:nh], func=AF.Sqrt,
                                                          scale=1.0 / 128, bias=EPS), reads=[ss4], writes=[ss4])
                P.op("dve", lambda e, nh=nh: e.reciprocal(out=ss4[:, 4:4 + nh], in_=ss4[:, 4:4 + nh]),
                     reads=[ss4], writes=[ss4])
                if sc != 1.0:
                    P.op("dve", lambda e, nh=nh, sc=sc: e.tensor_scalar(out=ss4[:, 4:4 + nh], in0=ss4[:, 4:4 + nh],
                                                                        scalar1=sc, scalar2=None, op0=ALU.mult),
                         reads=[ss4], writes=[ss4])
                for h in range(nh):
                    P.op("dve", lambda e, ps=ps, h=h, g=g: e.scalar_tensor_tensor(
                        out=qn[:, h * 128:(h + 1) * 128], in0=ps[:, h * 128:(h + 1) * 128],
                        scalar=ss4[:, 4 + h:5 + h], in1=g[:], op0=ALU.mult, op1=ALU.mult),
                         reads=[ps, ss4, g], writes=[qn])
                if nh < 4:
                    P.op("act", lambda e, ps=ps, o=o: e.copy(out=o[:, 256:512], in_=ps[:, 256:512]),
                         reads=[ps], writes=[o])
                W = nh * 128
                qv = qn[:, 0:W].rearrange("p (h i two) -> p h i two", h=nh, two=2)
                ov = o[:, 0:W].rearrange("p (h i two) -> p h i two", h=nh, two=2)
                x0 = qv[:, :, :, 0]
                x1 = qv[:, :, :, 1]
                cb_ = cosT[:, i, :].unsqueeze(1).to_broadcast([128, nh, 64])
                sb_ = sinT[:, i, :].unsqueeze(1).to_broadcast([128, nh, 64])
                a1 = t1[:, 0:nh, :]
                a2 = t2[:, 0:nh, :]
                P.op("dve", lambda e, x0=x0, cb_=cb_, a1=a1: e.tensor_tensor(out=a1, in0=x0, in1=cb_, op=ALU.mult),
                     reads=[qn, cosT], writes=[t1])
                P.op("pool", lambda e, x1=x1, sb_=sb_, a2=a2: e.tensor_tensor(out=a2, in0=x1, in1=sb_, op=ALU.mult),
                     reads=[qn, sinT], writes=[t2])
                P.op("dve", lambda e, ov=ov, a1=a1, a2=a2: e.tensor_tensor(out=ov[:, :, :, 0], in0=a1, in1=a2,
                                                                          op=ALU.subtract),
                     reads=[t1, t2], writes=[o])
                P.op("dve", lambda e, x0=x0, sb_=sb_, a1=a1: e.tensor_tensor(out=a1, in0=x0, in1=sb_, op=ALU.mult),
                     reads=[qn, sinT], writes=[t1])
                P.op("pool", lambda e, x1=x1, cb_=cb_, a2=a2: e.tensor_tensor(out=a2, in0=x1, in1=cb_, op=ALU.mult),
                     reads=[qn, cosT], writes=[t2])
                P.op("dve", lambda e, ov=ov, a1=a1, a2=a2: e.tensor_tensor(out=ov[:, :, :, 1], in0=a1, in1=a2,
                                                                          op=ALU.add),
                     reads=[t1, t2], writes=[o])
            else:
                if cnt % 2 == 0:
                    P.op("act", lambda e, ps=ps, o=o, ncol=ncol: e.copy(out=o[:, 0:ncol], in_=ps[:, 0:ncol]),
                         reads=[ps], writes=[o])
                else:
                    P.op("dve", lambda e, ps=ps, o=o, ncol=ncol: e.tensor_copy(out=o[:, 0:ncol], in_=ps[:, 0:ncol]),
                         reads=[ps], writes=[o])
            P.store("sp", o, out_d[i * 128:(i + 1) * 128, c0:c0 + ncol], o[:, 0:ncol])
    return P


COLS_A = np.concatenate([
    np.arange(0, 768),
    np.arange(1536, 3840),
    np.arange(4632, 6168),
    np.arange(3840, 3864),
])


def rope_consts():
    t = np.arange(SEQ)
    row = (t // 64).astype(np.float32)
    col = (t % 64).astype(np.float32)
    pos = np.concatenate([np.repeat(row[:, None], 32, 1), np.repeat(col[:, None], 32, 1)], axis=1)
    freqs = (10000.0 ** (-np.arange(0, 64, 2, dtype=np.float32) / 64)).astype(np.float32)
    frq = np.concatenate([freqs, freqs])[None, :].repeat(128, 0).astype(np.float32)
    return pos.astype(np.float32), frq


def bcast_rows(v, n=128):
    return np.ascontiguousarray(np.broadcast_to(np.asarray(v, np.float32)[None, :], (n, v.shape[0])))


def run_A(x, norm_g, w_in_l, qn_g, kn_g):
    P = build_A()
    pos, frq = rope_consts()
    wA = np.ascontiguousarray(w_in_l[:, COLS_A])
    common = dict(gb=bcast_rows(norm_g), wA=wA, ident=np.eye(128, dtype=np.float32), frq=frq,
                  qg=bcast_rows(qn_g), kg=bcast_rows(kn_g))
    maps = []
    for c in range(NCORES):
        pc = pos[c * TPC:(c + 1) * TPC].reshape(TPC // 128, 128, 64).transpose(1, 0, 2)
        maps.append(dict(common, x=np.ascontiguousarray(x[c * TPC:(c + 1) * TPC]), pos=np.ascontiguousarray(pc)))
    res = run(P, maps)
    return np.concatenate([r["hA"] for r in res], axis=0)


def build_ATT():
    P = Prog()
    qT_d = P.dram_in("qT", [128, SEQ])
    kT_d = P.dram_in("kT", [128, SEQ])
    v_d = P.dram_in("v", [128, SEQ // 128, 128])
    out_d = P.dram_out("oT", [128, SEQ])
    qT = P.sbuf("qT", [128, SEQ], BF16)
    kT = P.sbuf("kT", [128, SEQ], BF16)
    v = P.sbuf("v", [128, SEQ // 128, 128], BF16)
    ones = P.sbuf("ones", [128, 128], BF16)
    P.op("dve", lambda e: e.memset(ones[:], 1.0), writes=[ones])
    for j in range(4):
        sl = slice(j * 2048, (j + 1) * 2048)
        P.dma("pool", [(kT[:, sl], kT_d[:, sl])], kT, writes=[kT])
        P.dma("pool", [(qT[:, sl], qT_d[:, sl])], qT, writes=[qT])
        P.dma("pool", [(v[:, j * 16:(j + 1) * 16, :], v_d[:, j * 16:(j + 1) * 16, :])], v, writes=[v])
    ps_s = [P.psum(f"s{i}", [128, 512]) for i in range(2)]
    ps_o = [P.psum(f"o{i}", [128, 512]) for i in range(2)]
    ps_d = [P.psum(f"d{i}", [128, 512]) for i in range(2)]
    pt = [P.sbuf(f"pt{i}", [128, 512], BF16) for i in range(3)]
    rd = P.sbuf("rd", [128, 512])
    ot = [P.sbuf(f"ot{i}", [128, 512]) for i in range(2)]
    NKT = SEQ // 128
    NQB = SEQ // 512
    ps_s = ps_s + [P.psum("s2", [128, 512])]
    steps = [(qb, kt) for qb in range(NQB) for kt in range(NKT)]

    def emit_S(idx):
        qb, kt = steps[idx]
        s_ = ps_s[idx % 3]
        P.op("pe", lambda e, s_=s_, kt=kt, qb=qb: e.matmul(s_[:], kT[:, kt * 128:(kt + 1) * 128],
                                                        qT[:, qb * 512:(qb + 1) * 512], start=True, stop=True),
             reads=[kT, qT], writes=[s_])

    emit_S(0)
    emit_S(1)
    for idx, (qb, kt) in enumerate(steps):
        po = ps_o[qb % 2]
        pd = ps_d[qb % 2]
        s_ = ps_s[idx % 3]
        p = pt[idx % 3]
        P.op("act", lambda e, s_=s_, p=p: e.activation(out=p[:], in_=s_[:], func=AF.Exp), reads=[s_], writes=[p])
        if idx + 2 < len(steps):
            emit_S(idx + 2)
        P.op("pe", lambda e, po=po, p=p, kt=kt: e.matmul(po[:], v[:, kt, :], p[:], start=(kt == 0),
                                                      stop=(kt == NKT - 1)), reads=[v, p], writes=[po])
        P.op("pe", lambda e, pd=pd, p=p, kt=kt: e.matmul(pd[:], ones[:], p[:], start=(kt == 0),
                                                      stop=(kt == NKT - 1)), reads=[ones, p], writes=[pd])
        if kt == NKT - 1:
            o = ot[qb % 2]
            P.op("dve", lambda e, pd=pd: e.reciprocal(out=rd[:], in_=pd[:]), reads=[pd], writes=[rd])
            P.op("dve", lambda e, po=po, o=o: e.tensor_tensor(out=o[:], in0=po[:], in1=rd[:], op=ALU.mult),
                 reads=[po, rd], writes=[o])
            P.store("sp", o, out_d[:, qb * 512:(qb + 1) * 512], o[:])
    return P


def run_ATT(hA):
    P = build_ATT()
    q = hA[:, 3072:4096]
    k = hA[:, 4096:4352]
    vv = hA[:, 4352:4608]
    maps = []
    for c in range(NCORES):
        kv = c // 4
        maps.append(dict(
            qT=np.ascontiguousarray(q[:, c * 128:(c + 1) * 128].T),
            kT=np.ascontiguousarray(k[:, kv * 128:(kv + 1) * 128].T),
            v=np.ascontiguousarray(vv[:, kv * 128:(kv + 1) * 128].reshape(SEQ // 128, 128, 128).transpose(1, 0, 2)),
        ))
    res = run(P, maps)
    return np.concatenate([r["oT"].T for r in res], axis=1)


S5C = 512


def _range_reduce_sin(P, dst, src, tmpf, tmpi, shift):
    P.op("dve", lambda e: e.tensor_scalar(out=tmpf[:], in0=src[:], scalar1=shift, scalar2=1.0 / TWO_PI,
                                          op0=ALU.add, op1=ALU.mult), reads=[src], writes=[tmpf])
    P.op("dve", lambda e: e.tensor_copy(out=tmpi[:], in_=tmpf[:]), reads=[tmpf], writes=[tmpi])
    P.op("dve", lambda e: e.tensor_copy(out=tmpf[:], in_=tmpi[:]), reads=[tmpi], writes=[tmpf])
    P.op("dve", lambda e: e.scalar_tensor_tensor(out=tmpf[:], in0=tmpf[:], scalar=-TWO_PI, in1=src[:],
                                                 op0=ALU.mult, op1=ALU.add), reads=[tmpf, src], writes=[tmpf])
    if shift != 0.0:
        P.op("dve", lambda e: e.tensor_scalar(out=tmpf[:], in0=tmpf[:], scalar1=shift, scalar2=None, op0=ALU.add),
             reads=[tmpf], writes=[tmpf])
    P.op("act", lambda e: e.activation(out=dst[:], in_=tmpf[:], func=AF.Sin), reads=[tmpf], writes=[dst])


def build_S5():
    P = Prog()
    NU = 6
    NCH = SEQ // S5C
    uT_d = P.dram_in("uT", [NU, 32, SEQ])
    prm_d = P.dram_in("prm", [NU, 128, 3])
    b_d = P.dram_in("bmat", [NU, 128, 64])
    c_d = P.dram_in("cmat", [NU, 128, 64])
    jidx_d = P.dram_in("jidx", [128, S5C + 1])
    id_d = P.dram_in("ident", [128, 128])
    out_d = P.dram_out("yT", [NU, 32, SEQ])

    ident = P.sbuf("ident", [128, 128])
    jidx = P.sbuf("jidx", [128, S5C + 1])
    P.load("sp", ident, ident[:], id_d)
    P.load("sp", jidx, jidx[:], jidx_d)
    pst = P.psum("pst", [32, 256])
    ps_re = [P.psum(f"psre{i}", [128, S5C]) for i in range(2)]
    ps_im = [P.psum(f"psim{i}", [128, S5C]) for i in range(2)]
    ps_y = [P.psum(f"psy{i}", [32, S5C]) for i in range(2)]

    def alloc_unit(ui):
        prm = P.sbuf(f"prm_u{ui}", [128, 3])
        bm = P.sbuf(f"bm_u{ui}", [128, 64])
        cm = P.sbuf(f"cm_u{ui}", [128, 64])
        sc = P.sbuf(f"sc_u{ui}", [128, 16])
        s1f = P.sbuf(f"s1f_u{ui}", [128, 1])
        s1i = P.sbuf(f"s1i_u{ui}", [128, 1], I32)
        th = P.sbuf(f"th_u{ui}", [128, 1])
        ph = P.sbuf(f"ph_u{ui}", [128, S5C + 1])
        tmpf = P.sbuf(f"tmpf_u{ui}", [128, S5C + 1])
        tmpi = P.sbuf(f"tmpi_u{ui}", [128, S5C + 1], I32)
        Pc = P.sbuf(f"Pc_u{ui}", [128, S5C + 1])
        Ps = P.sbuf(f"Ps_u{ui}", [128, S5C + 1])
        rt = P.sbuf(f"rt_u{ui}", [128, S5C])
        bb = P.sbuf(f"bb_u{ui}", [128, 64])
        bt1 = P.sbuf(f"bt1_u{ui}", [128, 32])
        BT = P.sbuf(f"BT_u{ui}", [32, 256], BF16)
        CT = P.sbuf(f"CT_u{ui}", [128, 96], BF16)
        ut = [P.sbuf(f"ut{i}_u{ui}", [32, S5C], BF16) for i in range(2)]
        m = [P.sbuf(f"m{i}_u{ui}", [128, S5C]) for i in range(4)]
        cre = P.sbuf(f"cre__u{ui}", [128, S5C])
        cim = P.sbuf(f"cim__u{ui}", [128, S5C])
        zre = [P.sbuf(f"zre{i}_u{ui}", [128, S5C]) for i in range(2)]
        zim = [P.sbuf(f"zim{i}_u{ui}", [128, S5C]) for i in range(2)]
        nn = [P.sbuf(f"nn{i}_u{ui}", [128, S5C], BF16) for i in range(4)]
        init = P.sbuf(f"init_u{ui}", [128, 4])
        yt = [P.sbuf(f"yt{i}_u{ui}", [32, S5C]) for i in range(2)]
        return dict(prm=prm, bm=bm, cm=cm, sc=sc, s1f=s1f, s1i=s1i, th=th, ph=ph, tmpf=tmpf, tmpi=tmpi, Pc=Pc, Ps=Ps, rt=rt, bb=bb, bt1=bt1, BT=BT, CT=CT, ut=ut, m=m, cre=cre, cim=cim, zre=zre, zim=zim, nn=nn, init=init, yt=yt)

    RS = [alloc_unit(0), alloc_unit(1)]

    def col(t, j):
        return t[:, j:j + 1]

    def emit_unit(Q, u, R):
        prm, bm, cm, sc, s1f, s1i, th, ph, tmpf, tmpi, Pc, Ps, rt, bb, bt1, BT, CT, ut, m, cre, cim, zre, zim, nn, init, yt = (R["prm"], R["bm"], R["cm"], R["sc"], R["s1f"], R["s1i"], R["th"], R["ph"], R["tmpf"], R["tmpi"], R["Pc"], R["Ps"], R["rt"], R["bb"], R["bt1"], R["BT"], R["CT"], R["ut"], R["m"], R["cre"], R["cim"], R["zre"], R["zim"], R["nn"], R["init"], R["yt"])
        Q.load("sp", prm, prm[:], prm_d[u])
        Q.load("sp", bm, bm[:], b_d[u])
        Q.load("sp", cm, cm[:], c_d[u])
        Q.op("act", lambda e: e.activation(out=col(sc, 0), in_=col(prm, 2), func=AF.Exp), reads=[prm], writes=[sc])
        Q.op("dve", lambda e: e.tensor_tensor(out=col(sc, 10), in0=col(prm, 0), in1=col(sc, 0), op=ALU.mult),
             reads=[prm, sc], writes=[sc])
        Q.op("act", lambda e: e.activation(out=col(sc, 1), in_=col(sc, 10), func=AF.Exp), reads=[sc], writes=[sc])
        Q.op("dve", lambda e: e.tensor_tensor(out=col(sc, 2), in0=col(prm, 1), in1=col(sc, 0), op=ALU.mult),
             reads=[prm, sc], writes=[sc])
        Q.op("dve", lambda e: e.tensor_scalar(out=s1f[:], in0=col(sc, 2), scalar1=1.0 / TWO_PI, scalar2=None,
                                              op0=ALU.mult), reads=[sc], writes=[s1f])
        Q.op("dve", lambda e: e.tensor_copy(out=s1i[:], in_=s1f[:]), reads=[s1f], writes=[s1i])
        Q.op("dve", lambda e: e.tensor_copy(out=s1f[:], in_=s1i[:]), reads=[s1i], writes=[s1f])
        Q.op("dve", lambda e: e.scalar_tensor_tensor(out=th[:], in0=s1f[:], scalar=-TWO_PI, in1=col(sc, 2),
                                                     op0=ALU.mult, op1=ALU.add), reads=[s1f, sc], writes=[th])
        Q.op("dve", lambda e: e.tensor_scalar(out=ph[:], in0=jidx[:], scalar1=th[:, 0:1], scalar2=None,
                                              op0=ALU.mult), reads=[jidx, th], writes=[ph])
        _range_reduce_sin(Q, Ps, ph, tmpf, tmpi, 0.0)
        _range_reduce_sin(Q, Pc, ph, tmpf, tmpi, float(np.pi / 2))
        Q.op("dve", lambda e: e.tensor_tensor(out=col(sc, 5), in0=col(sc, 1), in1=col(Pc, 1), op=ALU.mult),
             reads=[sc, Pc], writes=[sc])
        Q.op("dve", lambda e: e.tensor_scalar(out=col(sc, 5), in0=col(sc, 5), scalar1=-1.0, scalar2=None,
                                              op0=ALU.add), reads=[sc], writes=[sc])
        Q.op("dve", lambda e: e.tensor_tensor(out=col(sc, 6), in0=col(sc, 1), in1=col(Ps, 1), op=ALU.mult),
             reads=[sc, Ps], writes=[sc])
        Q.op("dve", lambda e: e.tensor_tensor(out=col(sc, 7), in0=col(prm, 0), in1=col(prm, 0), op=ALU.mult),
             reads=[prm], writes=[sc])
        Q.op("dve", lambda e: e.scalar_tensor_tensor(out=col(sc, 7), in0=col(prm, 1), scalar=col(prm, 1),
                                                     in1=col(sc, 7), op0=ALU.mult, op1=ALU.add),
             reads=[prm, sc], writes=[sc])
        Q.op("dve", lambda e: e.reciprocal(out=col(sc, 7), in_=col(sc, 7)), reads=[sc], writes=[sc])
        Q.op("dve", lambda e: e.tensor_tensor(out=col(sc, 10), in0=col(sc, 5), in1=col(prm, 0), op=ALU.mult),
             reads=[sc, prm], writes=[sc])
        Q.op("dve", lambda e: e.scalar_tensor_tensor(out=col(sc, 10), in0=col(sc, 6), scalar=col(prm, 1),
                                                     in1=col(sc, 10), op0=ALU.mult, op1=ALU.add),
             reads=[sc, prm], writes=[sc])
        Q.op("dve", lambda e: e.tensor_tensor(out=col(sc, 8), in0=col(sc, 10), in1=col(sc, 7), op=ALU.mult),
             reads=[sc], writes=[sc])
        Q.op("dve", lambda e: e.tensor_tensor(out=col(sc, 11), in0=col(sc, 5), in1=col(prm, 1), op=ALU.mult),
             reads=[sc, prm], writes=[sc])
        Q.op("dve", lambda e: e.scalar_tensor_tensor(out=col(sc, 11), in0=col(sc, 6), scalar=col(prm, 0),
                                                     in1=col(sc, 11), op0=ALU.mult, op1=ALU.subtract),
             reads=[sc, prm], writes=[sc])
        Q.op("dve", lambda e: e.tensor_tensor(out=col(sc, 9), in0=col(sc, 11), in1=col(sc, 7), op=ALU.mult),
             reads=[sc], writes=[sc])
        Q.op("dve", lambda e: e.tensor_scalar(out=bt1[:], in0=bm[:, 32:64], scalar1=col(sc, 9), scalar2=None,
                                              op0=ALU.mult), reads=[bm, sc], writes=[bt1])
        Q.op("dve", lambda e: e.scalar_tensor_tensor(out=bb[:, 0:32], in0=bm[:, 0:32], scalar=col(sc, 8),
                                                     in1=bt1[:], op0=ALU.mult, op1=ALU.subtract),
             reads=[bm, sc, bt1], writes=[bb])
        Q.op("dve", lambda e: e.tensor_scalar(out=bt1[:], in0=bm[:, 0:32], scalar1=col(sc, 9), scalar2=None,
                                              op0=ALU.mult), reads=[bm, sc, bb], writes=[bt1])
        Q.op("dve", lambda e: e.scalar_tensor_tensor(out=bb[:, 32:64], in0=bm[:, 32:64], scalar=col(sc, 8),
                                                     in1=bt1[:], op0=ALU.mult, op1=ALU.add),
             reads=[bm, sc, bt1], writes=[bb])
        Q.begin()
        Q.op("pe", lambda e: e.transpose(pst[:, 0:128], bb[:, 0:32], ident[:]), reads=[bb, ident], writes=[pst])
        Q.op("pe", lambda e: e.transpose(pst[:, 128:256], bb[:, 32:64], ident[:]), reads=[bb, ident], writes=[pst])
        Q.op("dve", lambda e: e.tensor_copy(out=BT[:], in_=pst[:]), reads=[pst], writes=[BT])
        Q.end()
        Q.op("dve", lambda e: e.tensor_copy(out=CT[:, 0:32], in_=cm[:, 0:32]), reads=[cm], writes=[CT])
        Q.op("dve", lambda e: e.tensor_scalar(out=CT[:, 32:96], in0=cm[:, 0:64], scalar1=-1.0, scalar2=None,
                                              op0=ALU.mult), reads=[cm], writes=[CT])
        Q.op("dve", lambda e: e.tensor_scalar(out=rt[:], in0=jidx[:, 0:S5C], scalar1=0.0, scalar2=col(sc, 1),
                                              op0=ALU.mult, op1=ALU.add), reads=[jidx, sc], writes=[rt])
        for ch in range(NCH):
            b = ch % 2
            tsl = slice(ch * S5C, (ch + 1) * S5C)
            Q.dma("pool", [(ut[b][:], uT_d[u, :, tsl])], ut[b], writes=[ut[b]])
            pr, pi_ = ps_re[b], ps_im[b]
            PcS, PsS = Pc[:, 0:S5C], Ps[:, 0:S5C]
            Q.begin()
            Q.op("pe", lambda e, pr=pr, b=b: e.matmul(pr[:], BT[:, 0:128], ut[b][:], start=True, stop=True),
                 reads=[BT, ut[b]], writes=[pr])
            Q.op("pe", lambda e, pi_=pi_, b=b: e.matmul(pi_[:], BT[:, 128:256], ut[b][:], start=True, stop=True),
                 reads=[BT, ut[b]], writes=[pi_])
            Q.op("dve", lambda e, pr=pr: e.tensor_tensor(out=m[0][:], in0=pr[:], in1=PcS, op=ALU.mult),
                 reads=[pr, Pc], writes=[m[0]])
            Q.op("dve", lambda e, pi_=pi_: e.tensor_tensor(out=m[1][:], in0=pi_[:], in1=PsS, op=ALU.mult),
                 reads=[pi_, Ps], writes=[m[1]])
            Q.op("dve", lambda e, pi_=pi_: e.tensor_tensor(out=m[2][:], in0=pi_[:], in1=PcS, op=ALU.mult),
                 reads=[pi_, Pc], writes=[m[2]])
            Q.op("dve", lambda e, pr=pr: e.tensor_tensor(out=m[3][:], in0=pr[:], in1=PsS, op=ALU.mult),
                 reads=[pr, Ps], writes=[m[3]])
            Q.end()
            Q.op("pool", lambda e: e.tensor_tensor(out=cre[:], in0=m[0][:], in1=m[1][:], op=ALU.add),
                 reads=[m[0], m[1]], writes=[cre])
            Q.op("pool", lambda e: e.tensor_tensor(out=cim[:], in0=m[2][:], in1=m[3][:], op=ALU.subtract),
                 reads=[m[2], m[3]], writes=[cim])
            if ch == 0:
                Q.op("dve", lambda e: e.memset(init[:], 0.0), writes=[init])
            else:
                pzr, pzi = zre[1 - b], zim[1 - b]
                L = S5C - 1
                Q.op("dve", lambda e, pzi=pzi: e.tensor_tensor(out=col(init, 2), in0=col(pzi, L), in1=col(Ps, S5C),
                                                              op=ALU.mult), reads=[pzi, Ps], writes=[init])
                Q.op("dve", lambda e, pzr=pzr: e.scalar_tensor_tensor(out=col(init, 0), in0=col(pzr, L),
                                                                     scalar=col(Pc, S5C), in1=col(init, 2),
                                                                     op0=ALU.mult, op1=ALU.subtract),
                     reads=[pzr, Pc, init], writes=[init])
                Q.op("dve", lambda e, pzr=pzr: e.tensor_tensor(out=col(init, 3), in0=col(pzr, L), in1=col(Ps, S5C),
                                                              op=ALU.mult), reads=[pzr, Ps], writes=[init])
                Q.op("dve", lambda e, pzi=pzi: e.scalar_tensor_tensor(out=col(init, 1), in0=col(pzi, L),
                                                                     scalar=col(Pc, S5C), in1=col(init, 3),
                                                                     op0=ALU.mult, op1=ALU.add),
                     reads=[pzi, Pc, init], writes=[init])
            zr, zi = zre[b], zim[b]
            Q.op("dve", lambda e, zr=zr: e.tensor_tensor_scan(out=zr[:], data0=rt[:], data1=cre[:],
                                                             initial=col(init, 0), op0=ALU.mult, op1=ALU.add),
                 reads=[rt, cre, init], writes=[zr])
            Q.op("dve", lambda e, zi=zi: e.tensor_tensor_scan(out=zi[:], data0=rt[:], data1=cim[:],
                                                             initial=col(init, 1), op0=ALU.mult, op1=ALU.add),
                 reads=[rt, cim, init], writes=[zi])
            Q.op("pool", lambda e, zr=zr: e.tensor_tensor(out=nn[0][:], in0=zr[:], in1=PcS, op=ALU.mult),
                 reads=[zr, Pc], writes=[nn[0]])
            Q.op("pool", lambda e, zi=zi: e.tensor_tensor(out=nn[1][:], in0=zi[:], in1=PsS, op=ALU.mult),
                 reads=[zi, Ps], writes=[nn[1]])
            Q.op("pool", lambda e, zi=zi: e.tensor_tensor(out=nn[2][:], in0=zi[:], in1=PcS, op=ALU.mult),
                 reads=[zi, Pc], writes=[nn[2]])
            Q.op("dve", lambda e, zr=zr: e.tensor_tensor(out=nn[3][:], in0=zr[:], in1=PsS, op=ALU.mult),
                 reads=[zr, Ps], writes=[nn[3]])
            py = ps_y[b]
            lts = [CT[:, 0:32], CT[:, 32:64], CT[:, 64:96], CT[:, 64:96]]
            Q.begin()
            for q in range(4):
                Q.op("pe", lambda e, py=py, q=q, lt=lts[q]: e.matmul(py[:], lt, nn[q][:], start=(q == 0),
                                                                    stop=(q == 3)), reads=[CT, nn[q]], writes=[py])
            y = yt[b]
            Q.op("act", lambda e, py=py, y=y: e.copy(out=y[:], in_=py[:]), reads=[py], writes=[y])
            Q.end()
            Q.store("sp", y, out_d[u, :, tsl], y[:])
    for u0 in range(0, NU, 2):
        recs = []
        for i_ in range(2):
            Q = Rec()
            emit_unit(Q, u0 + i_, RS[i_])
            recs.append(Q)
        replay_rr(P, recs)
    return P


def run_S5(hA, a_re, a_im, log_step, b_re, b_im, c_re, c_im):
    P = build_S5()
    u = hA[:, 0:768]
    uT = np.ascontiguousarray(u.T)
    uTr = np.ascontiguousarray(uT[:, ::-1])
    jidx = bcast_rows(np.arange(S5C + 1, dtype=np.float32))
    maps = []
    for c in range(NCORES):
        uTc = np.zeros((6, 32, SEQ), np.float32)
        prm = np.zeros((6, 128, 3), np.float32)
        bmat = np.zeros((6, 128, 64), np.float32)
        cmat = np.zeros((6, 128, 64), np.float32)
        for pq in range(3):
            for d in range(2):
                un = pq * 2 + d
                for gg in range(2):
                    g = c * 6 + pq * 2 + gg
                    src = uTr if d == 1 else uT
                    uTc[un, gg * 16:(gg + 1) * 16] = src[g * 16:(g + 1) * 16]
                    rs = slice(gg * 64, (gg + 1) * 64)
                    prm[un, rs, 0] = a_re[d, g]
                    prm[un, rs, 1] = a_im[d, g]
                    prm[un, rs, 2] = log_step[d, g]
                    bmat[un, rs, gg * 16:(gg + 1) * 16] = b_re[d, g]
                    bmat[un, rs, 32 + gg * 16:32 + (gg + 1) * 16] = b_im[d, g]
                    cmat[un, rs, gg * 16:(gg + 1) * 16] = c_re[d, g].T
                    cmat[un, rs, 32 + gg * 16:32 + (gg + 1) * 16] = c_im[d, g].T
        maps.append(dict(uT=uTc, prm=prm, bmat=bmat, cmat=cmat, jidx=jidx, ident=np.eye(128, dtype=np.float32)))
    res = run(P, maps)
    yf = np.zeros((SEQ, 768), np.float32)
    yb = np.zeros((SEQ, 768), np.float32)
    for c in range(NCORES):
        yT = res[c]["yT"]
        for pq in range(3):
            cs = slice((c * 6 + pq * 2) * 16, (c * 6 + pq * 2 + 2) * 16)
            yf[:, cs] = yT[pq * 2].T
            yb[:, cs] = yT[pq * 2 + 1].T[::-1]
    return yf, yb


DNC = 128
NDC = SEQ // DNC
DN_K = int(os.environ.get('DN_K', 2))


def build_DN():
    P = Prog()
    NU = 2
    xin_d = P.dram_in("xin", [NU, 3, 128, SEQ + 4])
    cw_d = P.dram_in("cw", [NU, 128, 15])
    ab_d = P.dram_in("ab", [NU, 128, 2, NDC])
    hp_d = P.dram_in("hp", [NU, 128, 2])
    id_d = P.dram_in("ident", [128, 128])
    tri_d = P.dram_in("triu", [128, 128])
    mb_d = P.dram_in("maskb", [128, 128])
    m0_d = P.dram_in("msk0", [128, 128])
    mT_d = P.dram_in("mskT", [128, 6, 128])
    out_d = P.dram_out("o", [NU, SEQ, 128])

    ident = P.sbuf("ident", [128, 128])
    identb = P.sbuf("identb", [128, 128], BF16)
    triu = P.sbuf("triu", [128, 128])
    maskb = P.sbuf("maskb", [128, 128])
    onesf = P.sbuf("onesf", [128, 128])
    P.load("sp", ident, ident[:], id_d)
    P.load("sp", triu, triu[:], tri_d)
    P.load("sp", maskb, maskb[:], mb_d)
    onesb = P.sbuf("onesb", [128, 128], BF16)
    triub = P.sbuf("triub", [128, 128], BF16)
    P.op("dve", lambda e: e.memset(onesf[:], 1.0), writes=[onesf])
    P.op("dve", lambda e: e.memset(onesb[:], 1.0), writes=[onesb])
    P.op("dve", lambda e: e.tensor_copy(out=triub[:], in_=triu[:]), reads=[triu], writes=[triub])
    P.op("dve", lambda e: e.tensor_copy(out=identb[:], in_=ident[:]), reads=[ident], writes=[identb])

    qT = P.sbuf("qT", [128, SEQ], BF16)
    kT = P.sbuf("kT", [128, SEQ], BF16)
    vT = P.sbuf("vT", [128, SEQ], BF16)
    cw = P.sbuf("cw", [128, 15])
    ab = P.sbuf("ab", [128, 2, NDC])
    hp = P.sbuf("hp", [128, 16])
    PW = 2048
    xp = [P.sbuf(f"xp{i}", [128, PW + 4]) for i in range(2)]
    acc = P.sbuf("acc", [128, PW])
    sq = P.sbuf("sq", [128, PW], BF16)
    ghl = P.sbuf("ghl", [128, 2, NDC], BF16)
    gtmp = P.sbuf("gtmp", [128, NDC])
    dgh = P.sbuf("dgh", [128, 128], BF16)
    dgl = P.sbuf("dgl", [128, 128], BF16)
    dgt = P.sbuf("dgt", [128, 128])
    rn = P.sbuf("rn", [128, 512])
    pss = [P.psum(f"pss{i}", [128, 512]) for i in range(2)]

    tb = {n: P.sbuf("tb_" + n, [128, NDC]) for n in ("g", "beta", "gc", "egc", "negegc", "egl", "ekd", "tmp")}
    pt64 = pss[0]

    ptr = P.psum("ptr", [128, 256], BF16)
    KV = P.sbuf("KV", [128, 256], BF16)
    dg = P.sbuf("dg", [128, 128])
    kTc = P.sbuf("kTc", [128, 128], BF16)
    qTc = P.sbuf("qTc", [128, 128], BF16)
    Winvw = P.sbuf("Winvw", [128, 128], BF16)
    pg = P.psum("pg", [128, 128])
    xe = P.sbuf("xe", [128, 128])
    E = P.sbuf("E", [128, 128])
    Es = P.sbuf("Es", [128, 128])
    pkk = P.psum("pkk", [128, 256])
    AT = P.sbuf("AT", [128, 128], BF16)
    X = [P.sbuf(f"X{i}", [128, 128], BF16) for i in range(2)]
    XT = [P.sbuf(f"XT{i}", [128, 128], BF16) for i in range(2)]
    W = [P.sbuf(f"W{i}", [128, 128]) for i in range(2)]
    Wb = [P.sbuf(f"Wb{i}", [128, 128], BF16) for i in range(2)]
    Xw = [P.sbuf(f"Xw{i}", [128, 128], BF16) for i in range(2)]
    XTw = [P.sbuf(f"XTw{i}", [128, 128], BF16) for i in range(2)]
    x0f = P.sbuf("x0f", [128, 128])
    UT = P.sbuf("UT", [128, 128])
    G32 = P.sbuf("G32", [128, 128])
    Gm = P.sbuf("Gm", [128, 128], BF16)
    Gw = P.sbuf("Gw", [128, 128], BF16)
    GTw = P.sbuf("GTw", [128, 128], BF16)
    Ysb = P.sbuf("Ysb", [128, 128], BF16)
    CT = [P.sbuf(f"CTl{i}", [128, 128], BF16) for i in range(6)]
    msk0 = P.sbuf("msk0", [128, 128])
    mskT = P.sbuf("mskT", [128, 6, 128])
    P.load("sp", msk0, msk0[:], m0_d)
    P.load("sp", mskT, mskT[:], mT_d)
    pX = P.psum("pX", [128, 256])
    pW = P.psum("pW", [128, 128])
    S = P.sbuf("S", [128, 128])
    Sb = P.sbuf("Sb", [128, 128], BF16)
    pks = P.psum("pks", [128, 256])
    Rp = P.sbuf("Rp", [128, 128], BF16)
    vnew = P.sbuf("vnew", [128, 128], BF16)
    oq = P.sbuf("oq", [128, 128])
    ot = [P.sbuf(f"ot{i}", [128, 128]) for i in range(2)]
    Kd = P.sbuf("Kd", [128, 128], BF16)
    pgs = pss[0]
    TT = []
    for i in range(DN_K):
        T = {n: P.sbuf(f"T{i}_{n}", [128, 128]) for n in ("dg", "dgt", "xe", "E", "Es", "x0f", "UT", "G32")}
        T.update({n: P.sbuf(f"T{i}_{n}", [128, 128], BF16) for n in ("dgh", "dgl", "kTc", "qTc", "Gm", "GTw", "Ysb")})
        T["CT"] = P.sbuf(f"T{i}_CT", [128, 6, 128], BF16)
        TT.append(T)
    ORing = [(P.sbuf(f"O{i}_KV", [128, 256], BF16), P.sbuf(f"O{i}_AT", [128, 128], BF16),
              P.sbuf(f"O{i}_Gw", [128, 128], BF16)) for i in range(2 * DN_K)]

    def col(t, j):
        return t[:, j:j + 1]

    for u in range(NU):
        P.load("sp", cw, cw[:], cw_d[u])
        P.load("sp", ab, ab[:], ab_d[u])
        P.load("sp", hp, hp[:, 0:2], hp_d[u])
        cnt = 0
        for ti, dst in enumerate((qT, kT, vT)):
            for pc in range(SEQ // PW):
                x_ = xp[cnt % 2]
                cnt += 1
                P.load("sp", x_, x_[:], xin_d[u, ti, :, pc * PW:pc * PW + PW + 4])
                P.op("act", lambda e, x_=x_, ti=ti: e.activation(out=acc[:], in_=x_[:, 0:PW], func=AF.Copy,
                                                                scale=col(cw, ti * 5)), reads=[x_, cw], writes=[acc])
                for k in range(1, 5):
                    P.op("dve", lambda e, x_=x_, ti=ti, k=k: e.scalar_tensor_tensor(
                        out=acc[:], in0=x_[:, k:k + PW], scalar=col(cw, ti * 5 + k), in1=acc[:],
                        op0=ALU.mult, op1=ALU.add), reads=[x_, cw, acc], writes=[acc])
                dsl = slice(pc * PW, (pc + 1) * PW)
                if ti == 2:
                    P.op("act", lambda e, dsl=dsl: e.activation(out=vT[:, dsl], in_=acc[:], func=AF.Silu),
                         reads=[acc], writes=[vT])
                    continue
                P.op("act", lambda e: e.activation(out=acc[:], in_=acc[:], func=AF.Silu), reads=[acc], writes=[acc])
                P.op("pool", lambda e: e.tensor_tensor(out=sq[:], in0=acc[:], in1=acc[:], op=ALU.mult),
                     reads=[acc], writes=[sq])
                for j in range(PW // 512):
                    ps = pss[j % 2]
                    js = slice(j * 512, (j + 1) * 512)
                    P.op("pe", lambda e, ps=ps, js=js: e.matmul(ps[:], onesb[:], sq[:, js], start=True, stop=True),
                         reads=[onesb, sq], writes=[ps])
                    P.op("act", lambda e, ps=ps: e.activation(out=rn[:], in_=ps[:], func=AF.Sqrt, bias=EPS),
                         reads=[ps], writes=[rn])
                    P.op("dve", lambda e: e.reciprocal(out=rn[:], in_=rn[:]), reads=[rn], writes=[rn])
                    scl = (128.0 ** -0.5) if ti == 0 else 1.0
                    P.op("dve", lambda e, js=js, dst=dst, pc=pc, j=j, scl=scl: e.scalar_tensor_tensor(
                        out=dst[:, pc * PW + j * 512:pc * PW + (j + 1) * 512], in0=acc[:, js], scalar=scl, in1=rn[:],
                        op0=ALU.mult, op1=ALU.mult), reads=[acc, rn], writes=[dst])
        P.op("act", lambda e: e.activation(out=tb["tmp"][:], in_=ab[:, 0, :], func=AF.Exp, bias=col(hp, 1)),
             reads=[ab, hp], writes=[tb["tmp"]])
        P.op("act", lambda e: e.activation(out=tb["tmp"][:], in_=tb["tmp"][:], func=AF.Ln, bias=1.0),
             reads=[tb["tmp"]], writes=[tb["tmp"]])
        P.op("act", lambda e: e.activation(out=col(hp, 2), in_=col(hp, 0), func=AF.Exp), reads=[hp], writes=[hp])
        P.op("dve", lambda e: e.tensor_scalar(out=col(hp, 3), in0=col(hp, 2), scalar1=-1.0, scalar2=None,
                                              op0=ALU.mult), reads=[hp], writes=[hp])
        P.op("dve", lambda e: e.tensor_scalar(out=tb["g"][:], in0=tb["tmp"][:], scalar1=col(hp, 3), scalar2=None,
                                              op0=ALU.mult), reads=[tb["tmp"], hp], writes=[tb["g"]])
        P.op("act", lambda e: e.activation(out=tb["beta"][:], in_=ab[:, 1, :], func=AF.Sigmoid),
             reads=[ab], writes=[tb["beta"]])
        P.op("dve", lambda e: e.tensor_copy(out=ghl[:, 0, :], in_=tb["g"][:]), reads=[tb["g"]], writes=[ghl])
        P.op("dve", lambda e: e.tensor_copy(out=gtmp[:], in_=ghl[:, 0, :]), reads=[ghl], writes=[gtmp])
        P.op("dve", lambda e: e.tensor_tensor(out=ghl[:, 1, :], in0=tb["g"][:], in1=gtmp[:], op=ALU.subtract),
             reads=[tb["g"], gtmp], writes=[ghl])
        for hl in range(2):
            P.op("pe", lambda e, hl=hl: e.matmul(pt64[:, 0:NDC], triub[:], ghl[:, hl, :], start=(hl == 0),
                                                 stop=(hl == 1)), reads=[triub, ghl], writes=[pt64])
        for hl in range(2):
            P.op("pe", lambda e, hl=hl: e.matmul(pt64[:, NDC:2 * NDC], onesb[:], ghl[:, hl, :], start=(hl == 0),
                                                 stop=(hl == 1)), reads=[onesb, ghl], writes=[pt64])
        P.op("dve", lambda e: e.tensor_copy(out=tb["gc"][:], in_=pt64[:, 0:NDC]), reads=[pt64], writes=[tb["gc"]])
        P.op("dve", lambda e: e.tensor_copy(out=gtmp[:], in_=pt64[:, NDC:2 * NDC]), reads=[pt64], writes=[gtmp])
        P.op("act", lambda e: e.activation(out=tb["egc"][:], in_=tb["gc"][:], func=AF.Exp),
             reads=[tb["gc"]], writes=[tb["egc"]])
        P.op("dve", lambda e: e.tensor_scalar(out=tb["negegc"][:], in0=tb["egc"][:], scalar1=-1.0, scalar2=None,
                                              op0=ALU.mult), reads=[tb["egc"]], writes=[tb["negegc"]])
        P.op("act", lambda e: e.activation(out=tb["egl"][:], in_=gtmp[:], func=AF.Exp),
             reads=[gtmp], writes=[tb["egl"]])
        P.op("dve", lambda e: e.tensor_tensor(out=tb["tmp"][:], in0=gtmp[:], in1=tb["gc"][:],
                                              op=ALU.subtract), reads=[gtmp, tb["gc"]], writes=[tb["tmp"]])
        P.op("act", lambda e: e.activation(out=tb["ekd"][:], in_=tb["tmp"][:], func=AF.Exp),
             reads=[tb["tmp"]], writes=[tb["ekd"]])
        P.op("dve", lambda e: e.memset(S[:], 0.0), writes=[S])
        P.op("dve", lambda e: e.memset(Sb[:], 0.0), writes=[Sb])
        def pre(c, T, O):
            csl = slice(c * DNC, (c + 1) * DNC)
            gcc, bec = col(tb["gc"], c), col(tb["beta"], c)
            KV_, AT_, Gw_ = O
            P.op("pe", lambda e: e.transpose(ptr[:, 0:128], kT[:, csl], identb[:]), reads=[kT, identb], writes=[ptr])
            P.op("pe", lambda e: e.transpose(ptr[:, 128:256], vT[:, csl], identb[:]), reads=[vT, identb], writes=[ptr])
            P.op("act", lambda e: e.copy(out=KV_[:], in_=ptr[:]), reads=[ptr], writes=[KV_])
            yield
            P.op("dve", lambda e: e.tensor_scalar(out=T["dg"][:], in0=ident[:], scalar1=gcc, scalar2=None,
                                                  op0=ALU.mult), reads=[ident, tb["gc"]], writes=[T["dg"]])
            yield
            P.op("dve", lambda e: e.tensor_copy(out=T["dgh"][:], in_=T["dg"][:]), reads=[T["dg"]], writes=[T["dgh"]])
            yield
            P.op("pool", lambda e: e.tensor_copy(out=T["dgt"][:], in_=T["dgh"][:]), reads=[T["dgh"]], writes=[T["dgt"]])
            yield
            P.op("pool", lambda e: e.tensor_tensor(out=T["dgl"][:], in0=T["dg"][:], in1=T["dgt"][:], op=ALU.subtract),
                 reads=[T["dg"], T["dgt"]], writes=[T["dgl"]])
            yield
            P.op("pe", lambda e: e.matmul(pg[:], onesb[:], T["dgh"][:], start=True, stop=False),
                 reads=[onesb, T["dgh"]], writes=[pg])
            P.op("pe", lambda e: e.matmul(pg[:], onesb[:], T["dgl"][:], start=False, stop=True),
                 reads=[onesb, T["dgl"]], writes=[pg])
            P.op("dve", lambda e: e.tensor_scalar(out=T["xe"][:], in0=pg[:], scalar1=gcc, scalar2=0.0,
                                                  op0=ALU.subtract, op1=ALU.min), reads=[pg, tb["gc"]], writes=[T["xe"]])
            yield
            P.op("pool", lambda e: e.tensor_tensor(out=T["xe"][:], in0=T["xe"][:], in1=maskb[:], op=ALU.add),
                 reads=[T["xe"], maskb], writes=[T["xe"]])
            yield
            P.op("act", lambda e: e.activation(out=T["E"][:], in_=T["xe"][:], func=AF.Exp), reads=[T["xe"]], writes=[T["E"]])
            yield
            P.op("pool", lambda e: e.tensor_copy(out=T["kTc"][:], in_=kT[:, csl]), reads=[kT], writes=[T["kTc"]])
            yield
            P.op("pool", lambda e: e.tensor_copy(out=T["qTc"][:], in_=qT[:, csl]), reads=[qT], writes=[T["qTc"]])
            yield
            P.op("pe", lambda e: e.matmul(pkk[:, 0:128], kT[:, csl], T["kTc"][:], start=True, stop=True),
                 reads=[kT, T["kTc"]], writes=[pkk])
            P.op("pe", lambda e: e.matmul(pkk[:, 128:256], kT[:, csl], T["qTc"][:], start=True, stop=True),
                 reads=[kT, T["qTc"]], writes=[pkk])
            P.op("pool", lambda e: e.tensor_tensor(out=T["Es"][:], in0=T["E"][:], in1=ident[:], op=ALU.subtract),
                 reads=[T["E"], ident], writes=[T["Es"]])
            P.op("dve", lambda e: e.scalar_tensor_tensor(out=T["x0f"][:], in0=pkk[:, 0:128], scalar=bec,
                                                         in1=T["Es"][:], op0=ALU.mult, op1=ALU.mult),
                 reads=[pkk, tb["beta"], T["Es"]], writes=[T["x0f"]])
            P.op("dve", lambda e: e.tensor_tensor(out=AT_[:], in0=pkk[:, 128:256], in1=T["E"][:], op=ALU.mult),
                 reads=[pkk, T["E"]], writes=[AT_])
            yield
            P.op("pe", lambda e: e.transpose(pss[1][:, 0:128], T["x0f"][:], ident[:]), reads=[T["x0f"], ident],
                 writes=[pss[1]])
            P.op("dve", lambda e: e.tensor_copy(out=T["UT"][:], in_=pss[1][:, 0:128]), reads=[pss[1]], writes=[T["UT"]])
            yield
            P.op("dve", lambda e: e.tensor_tensor(out=T["CT"][:], in0=mskT[:],
                                                  in1=T["UT"][:].unsqueeze(1).to_broadcast([128, 6, 128]),
                                                  op=ALU.mult), reads=[T["UT"], mskT], writes=[T["CT"]])
            yield
            P.op("dve", lambda e: e.tensor_tensor(out=T["G32"][:], in0=T["x0f"][:], in1=msk0[:], op=ALU.mult),
                 reads=[T["x0f"], msk0], writes=[T["G32"]])
            yield
            P.op("dve", lambda e: e.tensor_tensor(out=T["G32"][:], in0=ident[:], in1=T["G32"][:], op=ALU.subtract),
                 reads=[ident, T["G32"]], writes=[T["G32"]])
            yield
            P.op("act", lambda e: e.copy(out=T["Gm"][:], in_=T["G32"][:]), reads=[T["G32"]], writes=[T["Gm"]])
            yield
            P.op("pool", lambda e: e.tensor_copy(out=Gw_[:], in_=T["G32"][:]), reads=[T["G32"]], writes=[Gw_])
            yield
            for lv in range(1, 7):
                P.op("pe", lambda e, lv=lv: e.matmul(pX[:, 0:128], T["CT"][:, lv - 1, :], T["Gm"][:], start=True, stop=True),
                     reads=[T["CT"], T["Gm"]], writes=[pX])
                P.op("pe", lambda e: e.transpose(ptr[:, 0:128], Gw_[:], identb[:]), reads=[Gw_, identb], writes=[ptr])
                P.op("act", lambda e: e.copy(out=T["Ysb"][:], in_=pX[:, 0:128]), reads=[pX], writes=[T["Ysb"]])
                P.op("act", lambda e: e.copy(out=T["GTw"][:], in_=ptr[:, 0:128]), reads=[ptr], writes=[T["GTw"]])
                yield
                P.op("pe", lambda e: e.matmul(pW[:], T["GTw"][:], T["Ysb"][:], start=True, stop=True),
                     reads=[T["GTw"], T["Ysb"]], writes=[pW])
                P.op("dve", lambda e: e.tensor_tensor(out=T["G32"][:], in0=T["G32"][:], in1=pW[:], op=ALU.subtract),
                     reads=[T["G32"], pW], writes=[T["G32"]])
                yield
                if lv < 6:
                    P.op("act", lambda e: e.copy(out=T["Gm"][:], in_=T["G32"][:]), reads=[T["G32"]], writes=[T["Gm"]])
                    yield
                P.op("pool", lambda e: e.tensor_copy(out=Gw_[:], in_=T["G32"][:]), reads=[T["G32"]], writes=[Gw_])
                yield

        def seq(c, O):
            csl = slice(c * DNC, (c + 1) * DNC)
            bec = col(tb["beta"], c)
            KV_, AT_, Gw_ = O
            P.op("pe", lambda e: e.matmul(pks[:, 0:128], kT[:, csl], Sb[:], start=True, stop=True),
                 reads=[kT, Sb], writes=[pks])
            P.op("pe", lambda e: e.matmul(pks[:, 128:256], qT[:, csl], Sb[:], start=True, stop=True),
                 reads=[qT, Sb], writes=[pks])
            yield
            P.op("dve", lambda e: e.scalar_tensor_tensor(out=Rp[:], in0=pks[:, 0:128], scalar=col(tb["negegc"], c),
                                                         in1=KV_[:, 128:256], op0=ALU.mult, op1=ALU.add),
                 reads=[pks, tb["negegc"], KV_], writes=[Rp])
            yield
            P.op("pe", lambda e: e.matmul(pgs[:, 0:128], Gw_[:], Rp[:], start=True, stop=True), reads=[Gw_, Rp], writes=[pgs])
            yield
            P.op("dve", lambda e: e.tensor_scalar(out=vnew[:], in0=pgs[:, 0:128], scalar1=bec, scalar2=None, op0=ALU.mult),
                 reads=[pgs, tb["beta"]], writes=[vnew])
            yield
            P.op("pe", lambda e: e.matmul(pgs[:, 0:128], AT_[:], vnew[:], start=True, stop=True), reads=[AT_, vnew], writes=[pgs])
            yield
            P.op("dve", lambda e: e.tensor_scalar(out=oq[:], in0=pks[:, 128:256], scalar1=col(tb["egc"], c),
                                                  scalar2=None, op0=ALU.mult), reads=[pks, tb["egc"]], writes=[oq])
            yield
            o = ot[c % 2]
            P.op("dve", lambda e: e.tensor_tensor(out=o[:], in0=pgs[:, 0:128], in1=oq[:], op=ALU.add),
                 reads=[pgs, oq], writes=[o])
            P.store("sp", o, out_d[u, csl, :], o[:])
            yield
            P.op("pool", lambda e: e.tensor_scalar(out=Kd[:], in0=KV_[:, 0:128], scalar1=col(tb["ekd"], c),
                                                   scalar2=None, op0=ALU.mult), reads=[KV_, tb["ekd"]], writes=[Kd])
            yield
            P.op("pe", lambda e: e.matmul(pks[:, 0:128], Kd[:], vnew[:], start=True, stop=True),
                 reads=[Kd, vnew], writes=[pks])
            yield
            P.op("dve", lambda e: e.scalar_tensor_tensor(out=S[:], in0=S[:], scalar=col(tb["egl"], c),
                                                         in1=pks[:, 0:128], op0=ALU.mult, op1=ALU.add),
                 reads=[S, tb["egl"], pks], writes=[S])
            yield
            P.op("act", lambda e: e.copy(out=Sb[:], in_=S[:]), reads=[S], writes=[Sb])
            yield

        def seq_group(c0):
            for cc in range(c0, c0 + DN_K):
                yield from seq(cc, ORing[cc % (2 * DN_K)])

        def round_robin(gens):
            gens = list(gens)
            while gens:
                for g_ in list(gens):
                    try:
                        next(g_)
                    except StopIteration:
                        gens.remove(g_)

        nch = int(os.environ.get('DN_NCH', NDC))
        for p in range(nch // DN_K):
            gens = [pre(DN_K * p + i, TT[i], ORing[(DN_K * p + i) % (2 * DN_K)]) for i in range(DN_K)]
            if p >= 1:
                gens.append(seq_group(DN_K * (p - 1)))
            round_robin(gens)
        if nch >= DN_K:
            round_robin([seq_group(nch - DN_K)])
    return P


DN_UNITS = [(h, d) for h in range(6) for d in range(2)]


def run_DN(hA, conv_w, a_log, dt_bias):
    P = build_DN()
    qkv = hA[:, 768:3072]
    da = hA[:, 4608:4620]
    db = hA[:, 4620:4632]
    ii = np.arange(128)
    triu = (ii[:, None] <= ii[None, :]).astype(np.float32)
    maskb = np.where(ii[None, :] >= ii[:, None], 0.0, -30000.0).astype(np.float32)
    msk0 = np.zeros((128, 128), np.float32)
    mskT = np.zeros((128, 6, 128), np.float32)
    for lv in range(7):
        b = 1 << lv
        jj, i2 = np.meshgrid(ii, ii, indexing="ij")
        m = ((jj // (2 * b) == i2 // (2 * b)) & (jj % (2 * b) < b) & (i2 % (2 * b) >= b)).astype(np.float32)
        if lv == 0:
            msk0 = m
        else:
            mskT[:, lv - 1, :] = m.T
    units = DN_UNITS + DN_UNITS[:4]
    maps = []
    for c in range(NCORES):
        xin = np.zeros((2, 3, 128, SEQ + 4), np.float32)
        cw = np.zeros((2, 128, 15), np.float32)
        ab = np.zeros((2, 128, 2, NDC), np.float32)
        hp = np.zeros((2, 128, 2), np.float32)
        for s in range(2):
            h, d = units[c * 2 + s]
            for ti in range(3):
                cs = slice(ti * 768 + h * 128, ti * 768 + (h + 1) * 128)
                xt = qkv[:, cs].T
                w = conv_w[cs]
                if d == 1:
                    xt = xt[:, ::-1]
                    w = w[:, ::-1]
                xin[s, ti, :, 2:2 + SEQ] = xt
                cw[s, :, ti * 5:(ti + 1) * 5] = w
            av = da[:, d * 6 + h]
            bv = db[:, d * 6 + h]
            if d == 1:
                av = av[::-1]
                bv = bv[::-1]
            ab[s, :, 0, :] = av.reshape(NDC, 128).T
            ab[s, :, 1, :] = bv.reshape(NDC, 128).T
            hp[s, :, 0] = a_log[d, h]
            hp[s, :, 1] = dt_bias[d, h]
        maps.append(dict(xin=xin, cw=cw, ab=ab, hp=hp, ident=np.eye(128, dtype=np.float32), triu=triu, maskb=maskb,
                         msk0=msk0, mskT=mskT))
    res = run(P, maps)
    of = np.zeros((SEQ, 768), np.float32)
    ob = np.zeros((SEQ, 768), np.float32)
    for idx, (h, d) in enumerate(DN_UNITS):
        o = res[idx // 2]["o"][idx % 2]
        if d == 0:
            of[:, h * 128:(h + 1) * 128] = o
        else:
            ob[:, h * 128:(h + 1) * 128] = o[::-1]
    return of, ob


COLS_Z = np.concatenate([np.arange(768, 1536), np.arange(3864, 4632), np.arange(6168, 7192),
                         np.arange(7704, 8216), np.arange(7192, 7704)])
NZ = 3584


def build_C1():
    P = Prog()
    x_d = P.dram_in("x", [TPC, D])
    gb_d = P.dram_in("gb", [128, D])
    id_d = P.dram_in("ident", [128, 128])
    wz_d = P.dram_in("wz", [D, NZ])
    mem_d = P.dram_in("mem", [256, D])
    mgb_d = P.dram_in("mgb", [128, D])
    wkv_d = P.dram_in("wkv", [D, 1024])
    s5_d = P.dram_in("s5", [3, 768, TPC])
    sd_d = P.dram_in("sd", [128, 16])
    wglu_d = P.dram_in("wglu", [768, 768])
    dn_d = P.dram_in("dn", [2, 768, TPC])
    oc_d = P.dram_in("oc", [1024, TPC])
    y_d = P.dram_out("yT", [24, 128, TPC], BF16)

    ident = P.sbuf("ident", [128, 128])
    gb = P.sbuf("gb", [128, D])
    sd = P.sbuf("sd", [128, 16])
    onesb = P.sbuf("onesb", [128, 128], BF16)
    P.load("sp", ident, ident[:], id_d)
    P.load("sp", gb, gb[:], gb_d)
    P.load("sp", sd, sd[:], sd_d)
    P.op("dve", lambda e: e.memset(onesb[:], 1.0), writes=[onesb])
    xnT = P.sbuf("xnT", [128, 16, TPC], BF16)
    nb = emit_norm_T(P, x_d, gb, ident, xnT, TPC // 128, "nC", single=True)
    memnT = P.sbuf("memnT", [128, 16, 256], BF16)
    P.load("sp", gb, gb[:], mgb_d)
    emit_norm_T(P, mem_d, gb, ident, memnT, 2, "nC", bufs=nb)

    pp = [P.psum(f"pp{i}", [128, 512]) for i in range(2)]
    pa = [P.psum(f"pa{i}", [128, 512]) for i in range(2)]
    po = P.psum("po", [128, 512])
    pd = P.psum("pd", [128, 512])

    wk = P.sbuf("wk", [128, 16, 512], BF16)
    KmT = P.sbuf("KmT", [128, 4, 256], BF16)
    Vm = P.sbuf("Vm", [128, 2, 512], BF16)
    wkv_v = wkv_d.rearrange("(c p) n -> p c n", p=128)
    P.dma("pool", [(wk[:, k, :], wkv_v[:, k, 0:512]) for k in range(16)], wk, writes=[wk])
    for h in range(4):
        ps = pp[h % 2]
        for k in range(16):
            P.op("pe", lambda e, ps=ps, k=k, h=h: e.matmul(ps[:, 0:256], wk[:, k, h * 128:(h + 1) * 128],
                                                          memnT[:, k, :], start=(k == 0), stop=(k == 15)),
                 reads=[wk, memnT], writes=[ps])
        P.op("act", lambda e, ps=ps, h=h: e.copy(out=KmT[:, h, :], in_=ps[:, 0:256]), reads=[ps], writes=[KmT])
    P.dma("pool", [(wk[:, k, :], wkv_v[:, k, 512:1024]) for k in range(16)], wk, writes=[wk])
    for mt in range(2):
        ps = pp[mt % 2]
        for k in range(16):
            P.op("pe", lambda e, ps=ps, k=k, mt=mt: e.matmul(ps[:], memnT[:, k, mt * 128:(mt + 1) * 128],
                                                            wk[:, k, :], start=(k == 0), stop=(k == 15)),
                 reads=[wk, memnT], writes=[ps])
        P.op("act", lambda e, ps=ps, mt=mt: e.copy(out=Vm[:, mt, :], in_=ps[:]), reads=[ps], writes=[Vm])

    wj = [P.sbuf(f"wj{i}", [128, 16, 128], BF16) for i in range(2)]
    sz = [P.sbuf(f"sz{i}", [128, TPC], BF16) for i in range(2)]
    wz_v = wz_d.rearrange("(c p) n -> p c n", p=128)
    cnt = {"w": 0, "p": 0}
    stg = [nb["xt"][0], nb["xn"][0]]

    def projT(col0, dst_ap_fn, func, dst_buf):
        w = wj[cnt["w"] % 2]
        cnt["w"] += 1
        P.dma("pool", [(w[:, k, :], wz_v[:, k, col0:col0 + 128]) for k in range(16)], w, writes=[w])
        for half in range(2):
            ps = pp[cnt["p"] % 2]
            cnt["p"] += 1
            hs = slice(half * 512, (half + 1) * 512)
            for k in range(16):
                P.op("pe", lambda e, ps=ps, k=k, w=w, hs=hs: e.matmul(ps[:], w[:, k, :], xnT[:, k, hs],
                                                                     start=(k == 0), stop=(k == 15)),
                     reads=[w, xnT], writes=[ps])
            P.op("act", lambda e, ps=ps, hs=hs: e.activation(out=dst_ap_fn(hs), in_=ps[:], func=func),
                 reads=[ps], writes=[dst_buf])

    def proj_silu(col0):
        z = sz[cnt["w"] % 2]
        projT(col0, lambda hs, z=z: z[:, hs], AF.Silu, z)
        return z

    f1 = [P.sbuf(f"f1_{i}", [128, TPC]) for i in range(2)]
    f2 = [P.sbuf(f"f2_{i}", [128, TPC]) for i in range(2)]
    f3 = P.sbuf("f3", [128, TPC])
    f4 = P.sbuf("f4", [128, TPC])
    yo = [P.sbuf(f"yo{i}", [128, TPC], BF16) for i in range(2)]
    sqb = P.sbuf("sqb", [128, TPC], BF16)
    rn = P.sbuf("rn", [128, 512])
    sg = P.sbuf("sg", [128, 512])
    ycnt = {"n": 0}

    def next_yo():
        y = yo[ycnt["n"] % 2]
        ycnt["n"] += 1
        return y

    GY = P.sbuf("GY", [128, 6, TPC], BF16)
    wglu = P.sbuf("wglu", [128, 6, 768], BF16)
    wglu_v = wglu_d.rearrange("(c p) n -> p c n", p=128)
    P.dma("pool", [(wglu[:, k, :], wglu_v[:, k, :]) for k in range(6)], wglu, writes=[wglu])
    for j in range(6):
        a, b_ = f1[j % 2], f2[j % 2]
        rs = slice(j * 128, (j + 1) * 128)
        P.load("sp", a, a[:], s5_d[0, rs, :])
        P.load("sp", b_, b_[:], s5_d[1, rs, :])
        P.load("sp", f3, f3[:], s5_d[2, rs, :])
        P.op("pool", lambda e, a=a, b_=b_: e.tensor_tensor(out=a[:], in0=a[:], in1=b_[:], op=ALU.add),
             reads=[a, b_], writes=[a])
        P.op("dve", lambda e, a=a, j=j: e.scalar_tensor_tensor(out=a[:], in0=f3[:], scalar=sd[:, j:j + 1], in1=a[:],
                                                              op0=ALU.mult, op1=ALU.add), reads=[f3, sd, a], writes=[a])
        P.op("pool", lambda e, a=a, b_=b_: e.tensor_tensor(out=b_[:], in0=a[:], in1=a[:], op=ALU.mult),
             reads=[a], writes=[b_])
        P.op("dve", lambda e, b_=b_: e.tensor_scalar(out=b_[:], in0=b_[:], scalar1=0.044715, scalar2=1.0,
                                                     op0=ALU.mult, op1=ALU.add), reads=[b_], writes=[b_])
        P.op("pool", lambda e, a=a, b_=b_: e.tensor_tensor(out=b_[:], in0=b_[:], in1=a[:], op=ALU.mult),
             reads=[a, b_], writes=[b_])
        P.op("act", lambda e, b_=b_: e.activation(out=f4[:], in_=b_[:], func=AF.Sigmoid, scale=1.5957691216057308),
             reads=[b_], writes=[f4])
        P.op("dve", lambda e, a=a, j=j: e.tensor_tensor(out=GY[:, j, :], in0=a[:], in1=f4[:], op=ALU.mult),
             reads=[a, f4], writes=[GY])
    for j in range(6):
        z = proj_silu(0 + j * 128)
        y = next_yo()
        for half in range(2):
            hs = slice(half * 512, (half + 1) * 512)
            ps = pa[half]
            for k in range(6):
                P.op("pe", lambda e, ps=ps, k=k, j=j, hs=hs: e.matmul(ps[:], wglu[:, k, j * 128:(j + 1) * 128],
                                                                     GY[:, k, hs], start=(k == 0), stop=(k == 5)),
                     reads=[wglu, GY], writes=[ps])
            P.op("act", lambda e, ps=ps, j=j: e.activation(out=sg[:], in_=ps[:], func=AF.Sigmoid,
                                                           bias=sd[:, 6 + j:7 + j]), reads=[ps, sd], writes=[sg])
            P.op("dve", lambda e, j=j, hs=hs: e.tensor_tensor(out=sg[:], in0=sg[:], in1=GY[:, j, hs], op=ALU.mult),
                 reads=[sg, GY], writes=[sg])
            P.op("dve", lambda e, y=y, z=z, hs=hs: e.tensor_tensor(out=y[:, hs], in0=sg[:], in1=z[:, hs], op=ALU.mult),
                 reads=[sg, z], writes=[y])
        P.store("sp", y, y_d[j], y[:])
    for h in range(6):
        a, b_ = f1[h % 2], f2[h % 2]
        rs = slice(h * 128, (h + 1) * 128)
        P.load("sp", a, a[:], dn_d[0, rs, :])
        P.load("sp", b_, b_[:], dn_d[1, rs, :])
        P.op("pool", lambda e, a=a, b_=b_: e.tensor_tensor(out=a[:], in0=a[:], in1=b_[:], op=ALU.add),
             reads=[a, b_], writes=[a])
        P.op("pool", lambda e, a=a: e.tensor_tensor(out=sqb[:], in0=a[:], in1=a[:], op=ALU.mult),
             reads=[a], writes=[sqb])
        z = proj_silu(768 + h * 128)
        y = next_yo()
        for half in range(2):
            hs = slice(half * 512, (half + 1) * 512)
            ps = pa[half]
            P.op("pe", lambda e, ps=ps, hs=hs: e.matmul(ps[:], onesb[:], sqb[:, hs], start=True, stop=True),
                 reads=[onesb, sqb], writes=[ps])
            P.op("act", lambda e, ps=ps: e.activation(out=rn[:], in_=ps[:], func=AF.Sqrt, scale=1.0 / 128, bias=EPS),
                 reads=[ps], writes=[rn])
            P.op("dve", lambda e: e.reciprocal(out=rn[:], in_=rn[:]), reads=[rn], writes=[rn])
            P.op("dve", lambda e, a=a, hs=hs: e.scalar_tensor_tensor(out=rn[:], in0=a[:, hs], scalar=sd[:, 12:13],
                                                                    in1=rn[:], op0=ALU.mult, op1=ALU.mult),
                 reads=[a, sd, rn], writes=[rn])
            P.op("dve", lambda e, y=y, z=z, hs=hs: e.tensor_tensor(out=y[:, hs], in0=rn[:], in1=z[:, hs], op=ALU.mult),
                 reads=[rn, z], writes=[y])
        P.store("sp", y, y_d[6 + h], y[:])
    for c in range(8):
        a = f1[c % 2]
        P.load("sp", a, a[:], oc_d[c * 128:(c + 1) * 128, :])
        z = proj_silu(1536 + c * 128)
        y = next_yo()
        P.op("dve", lambda e, a=a, y=y, z=z: e.tensor_tensor(out=y[:], in0=a[:], in1=z[:], op=ALU.mult),
             reads=[a, z], writes=[y])
        P.store("sp", y, y_d[12 + c], y[:])
    QmT = P.sbuf("QmT", [128, TPC], BF16)
    pm = [P.sbuf(f"pm{i}", [128, 512], BF16) for i in range(2)]
    pc = 0
    for h in range(4):
        projT(3072 + h * 128, lambda hs: QmT[:, hs], AF.Copy, QmT)
        z = proj_silu(2560 + h * 128)
        y = next_yo()
        for half in range(2):
            hs = slice(half * 512, (half + 1) * 512)
            for mt in range(2):
                ps = pa[mt]
                p_ = pm[pc % 2]
                pc += 1
                P.op("pe", lambda e, ps=ps, h=h, mt=mt, hs=hs: e.matmul(ps[:], KmT[:, h, mt * 128:(mt + 1) * 128],
                                                                       QmT[:, hs], start=True, stop=True),
                     reads=[KmT, QmT], writes=[ps])
                P.op("act", lambda e, ps=ps, p_=p_: e.activation(out=p_[:], in_=ps[:], func=AF.Exp, scale=128.0 ** -0.5),
                     reads=[ps], writes=[p_])
                P.op("pe", lambda e, p_=p_, h=h, mt=mt: e.matmul(po[:], Vm[:, mt, h * 128:(h + 1) * 128], p_[:],
                                                                start=(mt == 0), stop=(mt == 1)),
                     reads=[Vm, p_], writes=[po])
                P.op("pe", lambda e, p_=p_, mt=mt: e.matmul(pd[:], onesb[:], p_[:], start=(mt == 0), stop=(mt == 1)),
                     reads=[onesb, p_], writes=[pd])
            P.op("dve", lambda e: e.reciprocal(out=rn[:], in_=pd[:]), reads=[pd], writes=[rn])
            P.op("dve", lambda e: e.tensor_tensor(out=rn[:], in0=po[:], in1=rn[:], op=ALU.mult),
                 reads=[po, rn], writes=[rn])
            P.op("dve", lambda e, y=y, z=z, hs=hs: e.tensor_tensor(out=y[:, hs], in0=rn[:], in1=z[:, hs], op=ALU.mult),
                 reads=[rn, z], writes=[y])
        P.store("sp", y, y_d[20 + h], y[:])
    return P


def run_C1(x, norm_g, w_in_l, mem, mem_g, w_kv, hA, yf, yb, ssm_d, w_glu, b_glu, of, ob, dn_g, yc):
    P = build_C1()
    sd = np.zeros((128, 16), np.float32)
    sd[:, 0:6] = np.asarray(ssm_d, np.float32).reshape(6, 128).T
    sd[:, 6:12] = np.asarray(b_glu, np.float32).reshape(6, 128).T
    sd[:, 12] = np.asarray(dn_g, np.float32)
    common = dict(gb=bcast_rows(norm_g), ident=np.eye(128, dtype=np.float32),
                  wz=np.ascontiguousarray(w_in_l[:, COLS_Z]), mem=np.ascontiguousarray(mem),
                  mgb=bcast_rows(mem_g), wkv=np.ascontiguousarray(w_kv), sd=sd,
                  wglu=np.ascontiguousarray(w_glu))
    maps = []
    for c in range(NCORES):
        ts = slice(c * TPC, (c + 1) * TPC)
        s5 = np.stack([yf[ts].T, yb[ts].T, hA[ts, 0:768].T]).astype(np.float32)
        dn = np.stack([of[ts].T, ob[ts].T]).astype(np.float32)
        maps.append(dict(common, x=np.ascontiguousarray(x[ts]), s5=np.ascontiguousarray(s5),
                         dn=np.ascontiguousarray(dn), oc=np.ascontiguousarray(yc[ts].T)))
    res = run(P, maps)
    return [r["yT"] for r in res]


BR_CHUNKS = [(0, 6), (6, 12), (12, 20), (20, 24)]


def build_C2():
    P = Prog()
    x_d = P.dram_in("x", [TPC, D])
    gb_d = P.dram_in("gb", [128, D])
    id_d = P.dram_in("ident", [128, 128])
    wg_d = P.dram_in("wg", [D, 4 * D])
    y_d = P.dram_in("yT", [24, 128, TPC], BF16)
    wbr_d = P.dram_in("wbr", [3072, D])
    wout_d = P.dram_in("wout", [D, D])
    out_d = P.dram_out("xnew", [TPC, D])

    ident = P.sbuf("ident", [128, 128])
    gb = P.sbuf("gb", [128, D])
    P.load("sp", ident, ident[:], id_d)
    P.load("sp", gb, gb[:], gb_d)
    xnT = P.sbuf("xnT", [128, 16, TPC], BF16)
    nb = emit_norm_T(P, x_d, gb, ident, xnT, TPC // 128, "nD", single=True)
    stg = [nb["xt"][0], nb["xn"][0]]
    stg_v = [t_[:].rearrange("p (k c) -> p k c", c=128) for t_ in stg]
    gb_v = gb[:].rearrange("p (k c) -> p k c", c=128)
    Y = P.sbuf("Y", [128, 24, TPC], BF16)
    for k in range(24):
        P.dma("sp", [(Y[:, k, :], y_d[k])], Y, writes=[Y])
    mT = P.sbuf("mT", [128, 16, TPC], BF16)
    wbr = P.sbuf("wbr", [128, 24, 128], BF16)
    wg = [P.sbuf(f"wg{i}", [128, 16, 128], BF16) for i in range(2)]
    pb = [P.psum(f"pb{i}", [128, 512]) for i in range(2)]
    pg = [P.psum(f"pg{i}", [128, 512]) for i in range(2)]
    sg = [P.sbuf(f"sg{i}", [128, 512]) for i in range(2)]
    acc = P.sbuf("acc", [128, TPC])
    tmp = P.sbuf("tmpm", [128, 512])
    wbr_v = wbr_d.rearrange("(c p) n -> p c n", p=128)
    wg_v = wg_d.rearrange("(c p) n -> p c n", p=128)
    cnt = 0
    wcnt = 0
    for j in range(16):
        js = slice(j * 128, (j + 1) * 128)
        P.dma("sp", [(gb_v[:, k, :], wbr_v[:, k, js]) for k in range(16)], gb, writes=[gb])
        P.op("dve", lambda e: e.tensor_copy(out=wbr[:, 0:16, :], in_=gb_v), reads=[gb], writes=[wbr])
        P.dma("sp", [(gb_v[:, k, :], wbr_v[:, 16 + k, js]) for k in range(8)], gb, writes=[gb])
        P.op("dve", lambda e: e.tensor_copy(out=wbr[:, 16:24, :], in_=gb_v[:, 0:8, :]), reads=[gb], writes=[wbr])
        for b in range(4):
            w = wg[wcnt % 2]
            g0 = b * D + j * 128
            sb_, sv_ = stg[wcnt % 2], stg_v[wcnt % 2]
            wcnt += 1
            P.dma("sp", [(sv_[:, k, :], wg_v[:, k, g0:g0 + 128]) for k in range(16)], sb_, writes=[sb_])
            P.op("pool", lambda e, w=w, sv_=sv_: e.tensor_copy(out=w[:], in_=sv_), reads=[sb_], writes=[w])
            k0, k1 = BR_CHUNKS[b]
            for half in range(2):
                hs = slice(half * 512, (half + 1) * 512)
                p1, p2, s_ = pb[cnt % 2], pg[cnt % 2], sg[cnt % 2]
                cnt += 1
                for k in range(16):
                    P.op("pe", lambda e, p2=p2, k=k, w=w, hs=hs: e.matmul(p2[:], w[:, k, :], xnT[:, k, hs],
                                                                         start=(k == 0), stop=(k == 15)),
                         reads=[w, xnT], writes=[p2])
                for k in range(k0, k1):
                    P.op("pe", lambda e, p1=p1, k=k, hs=hs, k0=k0, k1=k1: e.matmul(p1[:], wbr[:, k, :], Y[:, k, hs],
                                                                                  start=(k == k0), stop=(k == k1 - 1)),
                         reads=[wbr, Y], writes=[p1])
                P.op("act", lambda e, p2=p2, s_=s_: e.activation(out=s_[:], in_=p2[:], func=AF.Sigmoid),
                     reads=[p2], writes=[s_])
                if b == 0:
                    P.op("dve", lambda e, p1=p1, s_=s_, hs=hs: e.tensor_tensor(out=acc[:, hs], in0=p1[:], in1=s_[:],
                                                                              op=ALU.mult), reads=[p1, s_], writes=[acc])
                else:
                    P.op("dve", lambda e, p1=p1, s_=s_: e.tensor_tensor(out=tmp[:], in0=p1[:], in1=s_[:], op=ALU.mult),
                         reads=[p1, s_], writes=[tmp])
                    P.op("pool", lambda e, hs=hs: e.tensor_tensor(out=acc[:, hs], in0=acc[:, hs], in1=tmp[:],
                                                                  op=ALU.add), reads=[acc, tmp], writes=[acc])
        P.op("act", lambda e, j=j: e.copy(out=mT[:, j, :], in_=acc[:]), reads=[acc], writes=[mT])
    wo_view = Y[:, 0:8, :].rearrange("p a (b c) -> p (a b) c", c=512)
    wout_v = wout_d.rearrange("(c p) n -> p c n", p=128)
    xr = [P.sbuf(f"xr{i}", [128, 512]) for i in range(2)]
    ot = [P.sbuf(f"oo{i}", [128, 512]) for i in range(2)]
    cnt = 0
    for cb in range(4):
        cs = slice(cb * 512, (cb + 1) * 512)
        P.dma("pool", [(wo_view[:, k, :], wout_v[:, k, cs]) for k in range(16)], Y, writes=[Y])
        for i in range(TPC // 128):
            ps = pb[cnt % 2]
            r_ = xr[cnt % 2]
            o_ = ot[cnt % 2]
            cnt += 1
            ts = slice(i * 128, (i + 1) * 128)
            P.load("sp", r_, r_[:], x_d[ts, cs])
            for k in range(16):
                P.op("pe", lambda e, ps=ps, k=k, ts=ts: e.matmul(ps[:], mT[:, k, ts], wo_view[:, k, :],
                                                                start=(k == 0), stop=(k == 15)),
                     reads=[mT, Y], writes=[ps])
            P.op("dve", lambda e, ps=ps, r_=r_, o_=o_: e.tensor_tensor(out=o_[:], in0=ps[:], in1=r_[:], op=ALU.add),
                 reads=[ps, r_], writes=[o_])
            P.store("sp", o_, out_d[ts, cs], o_[:])
    return P


def run_C2(x, norm_g, w_in_l, yTs, w_br, w_o):
    P = build_C2()
    common = dict(gb=bcast_rows(norm_g), ident=np.eye(128, dtype=np.float32),
                  wg=np.ascontiguousarray(w_in_l[:, 8216:]), wbr=np.ascontiguousarray(w_br),
                  wout=np.ascontiguousarray(w_o))
    maps = [dict(common, x=np.ascontiguousarray(x[c * TPC:(c + 1) * TPC]), yT=yTs[c]) for c in range(NCORES)]
    res = run(P, maps)
    return np.concatenate([r["xnew"] for r in res], axis=0)


def build_F():
    P = Prog()
    x_d = P.dram_in("x", [TPC, D])
    gb_d = P.dram_in("gb", [128, D])
    out_d = P.dram_out("y", [TPC, D])
    gb = P.sbuf("gb", [128, D])
    P.load("sp", gb, gb[:], gb_d)
    xt = [P.sbuf(f"xt{i}", [128, D]) for i in range(2)]
    yt = [P.sbuf(f"yt{i}", [128, D]) for i in range(2)]
    junk = P.sbuf("junk", [128, D], BF16)
    ss = [P.sbuf(f"ss{i}", [128, 16]) for i in range(2)]
    for i in range(TPC // 128):
        b = i % 2
        ts = slice(i * 128, (i + 1) * 128)
        P.load("sp", xt[b], xt[b][:], x_d[ts, :])
        P.op("act", lambda e, b=b: e.activation(out=junk[:], in_=xt[b][:], func=AF.Square, accum_out=ss[b][:, 0:1]),
             reads=[xt[b]], writes=[junk, ss[b]])
        P.op("act", lambda e, b=b: e.activation(out=ss[b][:, 1:2], in_=ss[b][:, 0:1], func=AF.Sqrt, scale=1.0 / D,
                                                bias=EPS), reads=[ss[b]], writes=[ss[b]])
        P.op("dve", lambda e, b=b: e.reciprocal(out=ss[b][:, 1:2], in_=ss[b][:, 1:2]), reads=[ss[b]], writes=[ss[b]])
        P.op("dve", lambda e, b=b: e.scalar_tensor_tensor(out=yt[b][:], in0=xt[b][:], scalar=ss[b][:, 1:2], in1=gb[:],
                                                          op0=ALU.mult, op1=ALU.mult),
             reads=[xt[b], ss[b], gb], writes=[yt[b]])
        P.store("sp", yt[b], out_d[ts, :], yt[b][:])
    return P


def run_F(x, g):
    P = build_F()
    maps = [dict(x=np.ascontiguousarray(x[c * TPC:(c + 1) * TPC]), gb=bcast_rows(g)) for c in range(NCORES)]
    res = run(P, maps)
    return np.concatenate([r["y"] for r in res], axis=0)


def layer_forward(xs, L, inp):
    g = lambda k: np.asarray(inp[k][L], np.float32)
    w_in_l = g("w_in")
    hA = run_A(xs, g("norm_g"), w_in_l, g("attn_q_norm"), g("attn_k_norm"))
    yc = run_ATT(hA)
    yf, yb = run_S5(hA, g("ssm_a_re"), g("ssm_a_im"), g("ssm_log_step"), g("ssm_b_re"), g("ssm_b_im"),
                    g("ssm_c_re"), g("ssm_c_im"))
    of, ob = run_DN(hA, g("dn_conv"), g("dn_a_log"), g("dn_dt_bias"))
    yTs = run_C1(xs, g("norm_g"), w_in_l, np.asarray(inp["mem"], np.float32)[0], g("mem_norm_g"), g("w_mem_kv"),
                 hA, yf, yb, g("ssm_d"), g("ssm_w_glu"), g("ssm_b_glu"), of, ob, g("dn_norm_g"), yc)
    return run_C2(xs, g("norm_g"), w_in_l, yTs, g("w_branch"), g("w_out"))


def kernel(**inp):
    xs = np.asarray(inp["x"], np.float32)[0]
    for L in range(2):
        xs = layer_forward(xs, L, inp)
    out = run_F(xs, np.asarray(inp["final_norm_g"], np.float32))
    return out[None].astype(np.float32)
```
